# Optimizing a Trainium2 kernel written in Bass

```python
import math
import jax, jax.numpy as jnp
from jax import lax
import numpy as np

D_MODEL = 1024
BATCH = 8
SEQ = 2048
DEPTH = 1
DEC_BATCH = 128
DEC_SEQ = 4
PAST_LEN = 2048
PAGE_SIZE = 128

MIX_WIDTH = D_MODEL
ATTN_WIDTH = MIX_WIDTH // 2
CONV_WIDTH = MIX_WIDTH - ATTN_WIDTH
HEAD_DIM = 64
N_DIFF_HEADS = ATTN_WIDTH // (2 * HEAD_DIM)
CONV_KERNEL = 31
D_FF = ((8 * D_MODEL // 3 + 127) // 128) * 128
FFN_KERNEL = 3
Q_BLOCK = 128
LN_EPS = 1e-5
ALPHA = (2 * DEPTH) ** 0.25
BETA = (8 * DEPTH) ** -0.25
IN_COLS = 3 * ATTN_WIDTH + 2 * CONV_WIDTH

kernel_name = 'hymba_diffattn_conformer_convffn_deepnorm_step'


def layer_norm(x, g, b):
    xf = x.astype(jnp.float32)
    mu = jnp.mean(xf, axis=-1, keepdims=True)
    var = jnp.mean(jnp.square(xf - mu), axis=-1, keepdims=True)
    y = (xf - mu) * lax.rsqrt(var + LN_EPS) * g.astype(jnp.float32) + b.astype(jnp.float32)
    return y.astype(x.dtype)


def alibi_slopes(n_heads):
    return 2.0 ** (-8.0 * jnp.arange(1, n_heads + 1, dtype=jnp.float32) / n_heads)


def causal_dwconv(buf, u, w, b):
    full = jnp.concatenate([buf.astype(u.dtype), u], axis=1)
    y = lax.conv_general_dilated(full, w[:, None, :].astype(u.dtype), window_strides=(1,), padding='VALID',
                                 dimension_numbers=('NWC', 'WIO', 'NWC'), feature_group_count=u.shape[-1])
    return y + b.astype(u.dtype), full[:, full.shape[1] - (w.shape[0] - 1):]


def diff_attend(q, k, v, q_pos, k_pos, lam):
    bsz, tq = q.shape[:2]
    tk = k.shape[1]
    s = jnp.einsum('bqnd,bknd->bnqk', q.astype(jnp.float32), k.astype(jnp.float32)) * (HEAD_DIM ** -0.5)
    s = s.reshape(bsz, N_DIFF_HEADS, 2, tq, tk)
    dist = q_pos[:, None] - k_pos[None, :]
    bias = -alibi_slopes(N_DIFF_HEADS)[:, None, None, None] * dist.astype(jnp.float32)
    s = jnp.where(dist >= 0, s + bias, -jnp.inf)
    p = jax.nn.softmax(s, axis=-1)
    a = p[:, :, 0] - lam * p[:, :, 1]
    return jnp.einsum('bhqk,bkhe->bqhe', a, v.astype(jnp.float32))


def prompt_diff_attention(q, k, v, lam):
    bsz, seq = q.shape[:2]
    n_blk = seq // Q_BLOCK
    pos = jnp.arange(seq, dtype=jnp.int32)
    q_blk = q.reshape(bsz, n_blk, Q_BLOCK, 2 * N_DIFF_HEADS, HEAD_DIM).swapaxes(0, 1)
    p_blk = pos.reshape(n_blk, Q_BLOCK)
    o = lax.map(lambda qp: diff_attend(qp[0], k, v, qp[1], pos, lam), (q_blk, p_blk))
    return o.swapaxes(0, 1).reshape(bsz, seq, N_DIFF_HEADS, 2 * HEAD_DIM)


def sample_diff_attention(q, k, v, past_k, past_v, lam):
    past = past_k.shape[1]
    n_new = q.shape[1]
    k_all = jnp.concatenate([past_k.astype(k.dtype), k], axis=1)
    v_all = jnp.concatenate([past_v.astype(v.dtype), v], axis=1)
    q_pos = past + jnp.arange(n_new, dtype=jnp.int32)
    k_pos = jnp.arange(past + n_new, dtype=jnp.int32)
    return diff_attend(q, k_all, v_all, q_pos, k_pos, lam)


def decoder_layer(x, c, depth_idx, past_kv, conv_buf, ffn_buf,
                  w_ada, b_ada, w_in, lambda_q1, lambda_k1, lambda_q2, lambda_k2, subln_g,
                  conv_w, conv_b, conv_ln_g, conv_ln_b, w_out, ln1_g, ln1_b,
                  w_up, ffn_conv_w, ffn_conv_b, w_down, ln2_g, ln2_b):
    bsz, seq = x.shape[:2]
    mod = jax.nn.silu(c) @ w_ada + b_ada
    shift1, scale1, gate1, shift2, scale2, gate2 = jnp.split(mod[:, None, :], 6, axis=-1)

    h = x * (1 + scale1) + shift1
    z = h @ w_in
    q, k, v, glu = jnp.split(z, [ATTN_WIDTH, 2 * ATTN_WIDTH, 3 * ATTN_WIDTH], axis=-1)
    q = q.reshape(bsz, seq, 2 * N_DIFF_HEADS, HEAD_DIM)
    k = k.reshape(bsz, seq, 2 * N_DIFF_HEADS, HEAD_DIM)
    v = v.reshape(bsz, seq, N_DIFF_HEADS, 2 * HEAD_DIM)

    lambda_init = 0.8 - 0.6 * math.exp(-0.3 * depth_idx)
    lam = (jnp.exp(jnp.sum(lambda_q1.astype(jnp.float32) * lambda_k1.astype(jnp.float32)))
           - jnp.exp(jnp.sum(lambda_q2.astype(jnp.float32) * lambda_k2.astype(jnp.float32))) + lambda_init)
    if past_kv is None:
        o = prompt_diff_attention(q, k, v, lam)
    else:
        o = sample_diff_attention(q, k, v, past_kv[0], past_kv[1], lam)
    o = o * lax.rsqrt(jnp.mean(jnp.square(o), axis=-1, keepdims=True) + LN_EPS)
    o = o * subln_g.astype(jnp.float32) * (1.0 - lambda_init)
    attn_out = o.reshape(bsz, seq, ATTN_WIDTH).astype(x.dtype)

    glu_a, glu_g = jnp.split(glu, 2, axis=-1)
    u = glu_a * jax.nn.sigmoid(glu_g)
    cv, new_conv_buf = causal_dwconv(conv_buf, u, conv_w, conv_b)
    conv_out = jax.nn.silu(layer_norm(cv, conv_ln_g, conv_ln_b))

    mix = jnp.concatenate([attn_out, conv_out], axis=-1) @ w_out
    x = layer_norm(ALPHA * x + gate1 * mix, ln1_g, ln1_b)

    h = x * (1 + scale2) + shift2
    up, new_ffn_buf = causal_dwconv(ffn_buf, h @ w_up, ffn_conv_w, ffn_conv_b)
    up_a, up_b = jnp.split(up, 2, axis=-1)
    f = (jax.nn.silu(up_a) * up_b) @ w_down
    x = layer_norm(ALPHA * x + gate2 * f, ln2_g, ln2_b)
    return x, k, v, new_conv_buf, new_ffn_buf


def setup_inputs(seed: int = 0) -> dict:
    key = jax.random.key(seed)
    ks = jax.random.split(key, 32)

    def nrm(k, shape, scale):
        return scale * jax.random.normal(k, shape, jnp.float32)

    d = D_MODEL
    n_pages = PAST_LEN // PAGE_SIZE
    n_used = DEC_BATCH * n_pages
    n_phys = n_used + max(1, n_used // 4)
    page_table = jax.random.permutation(ks[6], n_phys)[:n_used].reshape(DEC_BATCH, n_pages).astype(jnp.int32)
    col_scale = jnp.concatenate([jnp.ones((2 * ATTN_WIDTH,), jnp.float32),
                                 jnp.full((ATTN_WIDTH + CONV_WIDTH,), BETA, jnp.float32),
                                 jnp.ones((CONV_WIDTH,), jnp.float32)])
    return {
        'x_prompt': nrm(ks[0], (BATCH, SEQ, d), 1.0),
        'x_sample': nrm(ks[1], (DEC_BATCH, DEC_SEQ, d), 1.0),
        'c_prompt': nrm(ks[2], (BATCH, d), 1.0),
        'c_sample': nrm(ks[3], (DEC_BATCH, d), 1.0),
        'cache_k': nrm(ks[4], (DEPTH, n_phys, PAGE_SIZE, 2 * N_DIFF_HEADS, HEAD_DIM), 1.0),
        'cache_v': nrm(ks[5], (DEPTH, n_phys, PAGE_SIZE, N_DIFF_HEADS, 2 * HEAD_DIM), BETA),
        'page_table': page_table,
        'state_conv': nrm(ks[7], (DEPTH, DEC_BATCH, CONV_KERNEL - 1, CONV_WIDTH), 0.5 * BETA),
        'state_ffn': nrm(ks[8], (DEPTH, DEC_BATCH, FFN_KERNEL - 1, 2 * D_FF), BETA),
        'ln_emb_g': 1.0 + nrm(ks[9], (d,), 0.02),
        'ln_emb_b': nrm(ks[10], (d,), 0.02),
        'w_ada': nrm(ks[11], (DEPTH, d, 6 * d), 0.5 * d ** -0.5),
        'b_ada': nrm(ks[12], (DEPTH, 6 * d), 0.01),
        'w_in': nrm(ks[13], (DEPTH, d, IN_COLS), d ** -0.5) * col_scale,
        'lambda_q1': nrm(ks[14], (DEPTH, HEAD_DIM), 0.1),
        'lambda_k1': nrm(ks[15], (DEPTH, HEAD_DIM), 0.1),
        'lambda_q2': nrm(ks[16], (DEPTH, HEAD_DIM), 0.1),
        'lambda_k2': nrm(ks[17], (DEPTH, HEAD_DIM), 0.1),
        'subln_g': 1.0 + nrm(ks[18], (DEPTH, 2 * HEAD_DIM), 0.02),
        'conv_w': nrm(ks[19], (DEPTH, CONV_KERNEL, CONV_WIDTH), CONV_KERNEL ** -0.5),
        'conv_b': nrm(ks[20], (DEPTH, CONV_WIDTH), 0.02),
        'conv_ln_g': 1.0 + nrm(ks[21], (DEPTH, CONV_WIDTH), 0.02),
        'conv_ln_b': nrm(ks[22], (DEPTH, CONV_WIDTH), 0.02),
        'w_out': nrm(ks[23], (DEPTH, MIX_WIDTH, d), BETA * MIX_WIDTH ** -0.5),
        'ln1_g': 1.0 + nrm(ks[24], (DEPTH, d), 0.02),
        'ln1_b': nrm(ks[25], (DEPTH, d), 0.02),
        'w_up': nrm(ks[26], (DEPTH, d, 2 * D_FF), BETA * d ** -0.5),
        'ffn_conv_w': nrm(ks[27], (DEPTH, FFN_KERNEL, 2 * D_FF), FFN_KERNEL ** -0.5),
        'ffn_conv_b': nrm(ks[28], (DEPTH, 2 * D_FF), 0.02),
        'w_down': nrm(ks[29], (DEPTH, D_FF, d), BETA * D_FF ** -0.5),
        'ln2_g': 1.0 + nrm(ks[30], (DEPTH, d), 0.02),
        'ln2_b': nrm(ks[31], (DEPTH, d), 0.02),
    }


def reference(x_prompt, x_sample, c_prompt, c_sample, cache_k, cache_v, page_table, state_conv, state_ffn,
              ln_emb_g, ln_emb_b, w_ada, b_ada, w_in, lambda_q1, lambda_k1, lambda_q2, lambda_k2, subln_g,
              conv_w, conv_b, conv_ln_g, conv_ln_b, w_out, ln1_g, ln1_b,
              w_up, ffn_conv_w, ffn_conv_b, w_down, ln2_g, ln2_b):
    xp = layer_norm(x_prompt, ln_emb_g, ln_emb_b)
    xs = layer_norm(x_sample, ln_emb_g, ln_emb_b)
    bsz = x_prompt.shape[0]
    n_seq = x_sample.shape[0]
    kp, vp, cp, fp = [], [], [], []
    ks_, vs_, cs_, fs_ = [], [], [], []
    for l in range(DEPTH):
        wl = (w_ada[l], b_ada[l], w_in[l], lambda_q1[l], lambda_k1[l], lambda_q2[l], lambda_k2[l], subln_g[l],
              conv_w[l], conv_b[l], conv_ln_g[l], conv_ln_b[l], w_out[l], ln1_g[l], ln1_b[l],
              w_up[l], ffn_conv_w[l], ffn_conv_b[l], w_down[l], ln2_g[l], ln2_b[l])
        zero_conv = jnp.zeros((bsz, CONV_KERNEL - 1, CONV_WIDTH), xp.dtype)
        zero_ffn = jnp.zeros((bsz, FFN_KERNEL - 1, 2 * D_FF), xp.dtype)
        xp, k_new, v_new, c_new, f_new = decoder_layer(xp, c_prompt, l, None, zero_conv, zero_ffn, *wl)
        kp.append(k_new)
        vp.append(v_new)
        cp.append(c_new)
        fp.append(f_new)
        past_k = cache_k[l][page_table].reshape(n_seq, -1, 2 * N_DIFF_HEADS, HEAD_DIM)
        past_v = cache_v[l][page_table].reshape(n_seq, -1, N_DIFF_HEADS, 2 * HEAD_DIM)
        xs, k_new, v_new, c_new, f_new = decoder_layer(xs, c_sample, l, (past_k, past_v),
                                                       state_conv[l], state_ffn[l], *wl)
        ks_.append(k_new)
        vs_.append(v_new)
        cs_.append(c_new)
        fs_.append(f_new)
    return (xp, xs, jnp.stack(kp), jnp.stack(vp), jnp.stack(cp), jnp.stack(fp),
            jnp.stack(ks_), jnp.stack(vs_), jnp.stack(cs_), jnp.stack(fs_))
```

```python
import math
from contextlib import ExitStack

import numpy as np
import concourse.bass as bass
import concourse.mybir as mybir
from concourse.bass_utils import run_bass_kernel_spmd

F32 = mybir.dt.float32
BF16 = mybir.dt.bfloat16
I32 = mybir.dt.int32
AF = mybir.ActivationFunctionType
ALU = mybir.AluOpType
AX = mybir.AxisListType

D = 1024
T = 2048
NBLK = 16
NS = 16
ST = 64
DFF = 2816
NCH = 22
EPS = 1e-5
ALPHA = 2.0 ** 0.25
LAMBDA_INIT = 0.8 - 0.6 * math.exp(0.0)
SLOPES = [2.0 ** (-8.0 * (i + 1) / 4) for i in range(4)]
NCORES = 8
NPHYS = 2560
STOP_AFTER = None
DEBUG_X1 = False


class Sched:
    NDMA = 12

    def __init__(self, nc, es):
        self.nc = nc
        self.engs = {'pe': nc.tensor, 'act': nc.scalar, 'dve': nc.vector, 'pool': nc.gpsimd, 'sp': nc.sync}
        self.sem = {}
        self.cnt = {}
        for e in ['pe', 'act', 'dve', 'pool']:
            self.sem[e] = es.enter_context(nc.semaphore("s_" + e))
            self.cnt[e] = 0
        self.dsem = {}
        self.dcnt = {}
        self.dnext = {}
        for q in ['sp', 'pool']:
            self.dsem[q] = [es.enter_context(nc.semaphore("d_%s%d" % (q, i))) for i in range(self.NDMA)]
            self.dcnt[q] = [0] * self.NDMA
            self.dnext[q] = 0
        self.seen = {e: {} for e in self.engs}
        self.reg = {}
        self.pend = {e: ([], []) for e in self.engs}
        self.semobj = {}
        for e in self.sem:
            self.semobj[('c', e)] = self.sem[e]
        for q in self.dsem:
            for i, s in enumerate(self.dsem[q]):
                self.semobj[('d', q, i)] = s

    def _r(self, k):
        if k not in self.reg:
            self.reg[k] = [None, []]
        return self.reg[k]

    def _wait(self, e, tok):
        sk, val = tok
        if self.seen[e].get(sk, 0) >= val:
            return
        self.engs[e].wait_ge(self.semobj[sk], val)
        self.seen[e][sk] = val

    def _deps(self, e, reads, writes):
        own = ('c', e)
        deps = {}

        def add(tok, same_ok):
            if tok is None:
                return
            if tok[0] == own and same_ok:
                return
            if deps.get(tok[0], 0) < tok[1]:
                deps[tok[0]] = tok[1]
        for k in reads:
            add(self._r(k)[0], False)
        for k in writes:
            r = self._r(k)
            add(r[0], True)
            for t in r[1]:
                add(t, True)
        for sk, v in deps.items():
            self._wait(e, (sk, v))

    def op(self, e, fn, reads=(), writes=(), inc=True):
        reads = list(reads)
        writes = list(writes)
        self._deps(e, reads, writes)
        ins = fn(self.engs[e])
        pr, pw = self.pend[e]
        if not inc:
            pr.extend(reads)
            pw.extend(writes)
            return ins
        self.cnt[e] += 1
        ins.then_inc(self.sem[e], 1)
        tok = (('c', e), self.cnt[e])
        for k in reads + pr:
            self._r(k)[1].append(tok)
        for k in writes + pw:
            r = self._r(k)
            r[0] = tok
            r[1] = []
        self.pend[e] = ([], [])
        return ins

    def dma(self, q, fn, reads=(), writes=()):
        reads = list(reads)
        writes = list(writes)
        self._deps(q, reads, writes)
        i = self.dnext[q]
        self.dnext[q] = (i + 1) % self.NDMA
        sk = ('d', q, i)
        if self.dcnt[q][i] > 0:
            self._wait(q, (sk, self.dcnt[q][i]))
        ins = fn(self.engs[q])
        self.dcnt[q][i] += 16
        ins.then_inc(self.dsem[q][i], 16)
        tok = (sk, self.dcnt[q][i])
        for k in reads:
            self._r(k)[1].append(tok)
        for k in writes:
            r = self._r(k)
            r[0] = tok
            r[1] = []
        return tok

    def barrier(self):
        toks = [(('c', e), self.cnt[e]) for e in self.sem if self.cnt[e] > 0]
        for q in self.dsem:
            for i in range(self.NDMA):
                if self.dcnt[q][i] > 0:
                    toks.append((('d', q, i), self.dcnt[q][i]))
        for e in self.engs:
            for t in toks:
                if t[0] == ('c', e):
                    continue
                self._wait(e, t)
        self.reg = {}

    def finish(self):
        for q in self.dsem:
            for i in range(self.NDMA):
                if self.dcnt[q][i] > 0:
                    self._wait('sp', (('d', q, i), self.dcnt[q][i]))


def build_nc():
    nc = bass.Bass("TRN2", target_bir_lowering=False)

    def din(name, shape, dt=F32):
        return nc.dram_tensor(name, list(shape), dt, kind="ExternalInput").ap()

    def dout(name, shape, dt=F32):
        return nc.dram_tensor(name, list(shape), dt, kind="ExternalOutput").ap()

    x_p = din("x_p", [T, D])
    x_s = din("x_s", [ST, D])
    c_all = din("c_all", [17, D])
    cache_k = din("cache_k", [NPHYS * 8, 16 * 512])
    cache_v = din("cache_v", [NPHYS * 8, 16 * 512])
    pt_lay = din("pt_lay", [128, NS], I32)
    st_conv = din("st_conv", [NS * 30, 512])
    st_ffn = din("st_ffn", [NS * 2, 2 * DFF])
    ln_emb_g = din("ln_emb_g", [1, D]); ln_emb_b = din("ln_emb_b", [1, D])
    w_ada = din("w_ada", [D, 6 * D]); b_ada = din("b_ada", [1, 6 * D])
    w_in = din("w_in", [D, 2560])
    lam_in = din("lam_in", [1, 256])
    subln_g = din("subln_g", [128, 1])
    conv_w = din("conv_w", [128, 4, 31])
    conv_b = din("conv_b", [128, 4])
    conv_ln_g = din("conv_ln_g", [128, 4]); conv_ln_b = din("conv_ln_b", [128, 4])
    w_out = din("w_out", [D, D])
    ln1_g = din("ln1_g", [1, D]); ln1_b = din("ln1_b", [1, D])
    w_up = din("w_up", [D, 2 * DFF])
    ffn_cw = din("ffn_cw", [128, 44, 3]); ffn_cb = din("ffn_cb", [128, 44])
    w_down = din("w_down", [DFF, D])
    ln2_g = din("ln2_g", [1, D]); ln2_b = din("ln2_b", [1, D])
    c_ident = din("c_ident", [128, 128])
    c_maskT = din("c_maskT", [128, 128])
    c_auglp = din("c_auglp", [3, 4 * 128]); c_augrp = din("c_augrp", [3, 4096])
    c_augls = din("c_augls", [3, 128]); c_augrs = din("c_augrs", [3, 256])
    c_bnew = din("c_bnew", [64, 256]); c_mnew = din("c_mnew", [64, 256])
    c_selp = din("c_selp", [17, 128]); c_sels = din("c_sels", [17, 64])
    c_pm8 = din("c_pm8", [128, 1])

    y_p = dout("y_p", [T, D]); y_s = dout("y_s", [ST, D])
    k_p = dout("k_p", [T, 512]); v_p = dout("v_p", [T, 512])
    conv_p = dout("conv_p", [30, 512]); ffn_p = dout("ffn_p", [2, 2 * DFF])
    k_s = dout("k_s", [ST, 512]); v_s = dout("v_s", [ST, 512])
    conv_s = dout("conv_s", [NS * 30, 512]); ffn_s = dout("ffn_s", [NS * 2, 2 * DFF])
    x1_scr = nc.dram_tensor("x1_scr", [T + ST, D], F32, kind=("ExternalOutput" if DEBUG_X1 else "Internal")).ap()

    es_outer = ExitStack()
    with es_outer as es:
        S = Sched(nc, es)

        def sb(name, shape, dt=F32, stack=None):
            return (stack or es).enter_context(nc.sbuf_tensor(name, list(shape), dt))

        def ps(name, shape, dt=F32, stack=None):
            return (stack or es).enter_context(nc.psum_tensor(name, list(shape), dt))

        PS = [ps("psb%d" % i, [128, 512]) for i in range(8)]
        PSB = [p[:].bitcast(BF16) for p in PS]

        ident = sb("ident", [128, 128]); identb = sb("identb", [128, 128], BF16)
        onesb = sb("onesb", [128, 128], BF16); onesf = sb("onesf", [128, 128])
        maskT = sb("maskT", [128, 128]); maskTb = sb("maskTb", [128, 128], BF16)
        selp = sb("selp", [17, 128]); sels = sb("sels", [17, 64])
        modA = None
        modB = sb("modB", [17, 3 * D])
        neglam = sb("neglam", [128, 1]); sg8 = sb("sg8", [128, 1])
        sg8row = sb("sg8row", [128, 128])
        cw = sb("cw", [128, 4, 31]); cb = sb("cb", [128, 4]); clg = sb("clg", [128, 4]); clb = sb("clb", [128, 4])
        fcw = sb("fcw", [128, 44, 3]); fcb = sb("fcb", [128, 44])
        small = sb("small", [128, 64])
        epsc = sb("epsc", [128, 1])

        S.dma('sp', lambda e: e.dma_start(out=ident[:], in_=c_ident), writes=['ident'])
        S.dma('sp', lambda e: e.dma_start(out=maskT[:], in_=c_maskT), writes=['maskT'])
        S.dma('sp', lambda e: e.dma_start(out=selp[:], in_=c_selp), writes=['selp'])
        S.dma('sp', lambda e: e.dma_start(out=sels[:], in_=c_sels), writes=['sels'])
        S.dma('sp', lambda e: e.dma_start(out=sg8[:], in_=subln_g), writes=['sg8'])
        S.dma('sp', lambda e: e.dma_start(out=sg8row[:], in_=subln_g.rearrange("p o -> o p").partition_broadcast(128)), writes=['sg8row'])
        S.dma('sp', lambda e: e.dma_start(out=cw[:], in_=conv_w), writes=['cw'])
        S.dma('sp', lambda e: e.dma_start(out=cb[:], in_=conv_b), writes=['cb'])
        S.dma('sp', lambda e: e.dma_start(out=clg[:], in_=conv_ln_g), writes=['clg'])
        S.dma('sp', lambda e: e.dma_start(out=clb[:], in_=conv_ln_b), writes=['clb'])
        S.dma('sp', lambda e: e.dma_start(out=fcw[:], in_=ffn_cw), writes=['fcw'])
        S.dma('sp', lambda e: e.dma_start(out=fcb[:], in_=ffn_cb), writes=['fcb'])
        S.op('dve', lambda e: e.tensor_copy(out=identb[:], in_=ident[:]), reads=['ident'], writes=['identb'])
        S.op('dve', lambda e: e.tensor_copy(out=maskTb[:], in_=maskT[:]), reads=['maskT'], writes=['maskTb'])
        S.op('dve', lambda e: e.memset(onesb[:], 1.0), writes=['onesb'])
        S.op('dve', lambda e: e.memset(onesf[:], 1.0), writes=['onesf'])
        S.op('dve', lambda e: e.memset(epsc[:], EPS), writes=['epsc'])
        S.op('dve', lambda e: e.tensor_scalar(out=sg8[:], in0=sg8[:], scalar1=1.0 - LAMBDA_INIT, scalar2=None, op0=ALU.mult), reads=['sg8'], writes=['sg8'])
        S.op('dve', lambda e: e.tensor_scalar(out=sg8row[:], in0=sg8row[:], scalar1=1.0 - LAMBDA_INIT, scalar2=None, op0=ALU.mult), reads=['sg8row'], writes=['sg8row'])

        def rsqrt(out_ap, in_ap, reads, writes, scale=1.0):
            S.op('act', lambda e: e.activation(out=out_ap, in_=in_ap, func=AF.Ln, bias=epsc[:in_ap.shape[0], 0:1], scale=scale), reads=list(reads) + ['epsc'], writes=writes)
            S.op('act', lambda e: e.activation(out=out_ap, in_=out_ap, func=AF.Exp, scale=-0.5), reads=writes, writes=writes)

        def layer_norm_rows(P, src_ap, dst_ap, src_keys, dst_keys, col):
            st6 = small[:P, col:col + 12].rearrange("p (c s) -> p c s", s=6)
            mv = small[:P, col + 12:col + 14]
            rstd = small[:P, col + 14:col + 15]
            nmr = small[:P, col + 15:col + 16]
            k = ('small', col)
            for c in range(2):
                S.op('dve', lambda e, c=c: e.bn_stats(out=st6[:, c, :], in_=src_ap[:, c * 512:(c + 1) * 512]),
                     reads=src_keys, writes=[k], inc=(c == 1))
            S.op('dve', lambda e: e.bn_aggr(out=mv, in_=st6), reads=[k], writes=[k])
            rsqrt(rstd, mv[:, 1:2], [k], [k])
            S.op('dve', lambda e: e.scalar_tensor_tensor(out=nmr, in0=mv[:, 0:1], scalar=-1.0, in1=rstd, op0=ALU.mult, op1=ALU.mult),
                 reads=[k], writes=[k])
            S.op('act', lambda e: e.activation(out=dst_ap, in_=src_ap, func=AF.Identity, bias=nmr, scale=rstd),
                 reads=src_keys + [k], writes=dst_keys)

        def transpose_to(P, src_ap, src_keys, nchunk, dst_fn, dst_keys, banks, evac=('act', 'dve')):
            done = 0
            gi = 0
            while done < nchunk:
                n = min(4, nchunk - done)
                bank = banks[gi % len(banks)]
                pk = ('ps', bank)
                for j in range(n):
                    c = done + j
                    S.op('pe', lambda e, c=c, j=j: e.transpose(out=PS[bank][:, j * 128:j * 128 + P], in_=src_ap[:, c * 128:(c + 1) * 128], identity=ident[:P, :P]),
                         reads=src_keys + ['ident'], writes=[pk], inc=(j == n - 1))
                eng = evac[gi % len(evac)]
                src = PS[bank][:, 0:n * 128].rearrange("p (n t) -> p n t", t=128)[:, :, 0:P]
                dst = dst_fn(done, n)
                if eng == 'act':
                    S.op('act', lambda e, src=src, dst=dst: e.copy(out=dst, in_=src), reads=[pk], writes=dst_keys)
                else:
                    S.op('dve', lambda e, src=src, dst=dst: e.tensor_copy(out=dst, in_=src), reads=[pk], writes=dst_keys)
                done += n
                gi += 1

        def bcast_rows(dst, P, sel, modt, c0, add_one, key, bank):
            selk = 'selp' if sel is selp else 'sels'
            for hf in range(2):
                pk = ('ps', bank)
                S.op('pe', lambda e, hf=hf: e.matmul(PS[bank][:P, :], lhsT=sel[:, :P], rhs=modt[:, c0 + hf * 512:c0 + (hf + 1) * 512], start=True, stop=True),
                     reads=[selk, 'mod'], writes=[pk])
                if add_one:
                    S.op('dve', lambda e, hf=hf: e.tensor_scalar(out=dst[:P, hf * 512:(hf + 1) * 512], in0=PS[bank][:P, :], scalar1=1.0, scalar2=None, op0=ALU.add),
                         reads=[pk], writes=[key])
                else:
                    S.op('dve', lambda e, hf=hf: e.tensor_copy(out=dst[:P, hf * 512:(hf + 1) * 512], in_=PS[bank][:P, :]), reads=[pk], writes=[key])

        es_mix = ExitStack()
        with es_mix as em:
            modA = sb("modA", [17, 3 * D], stack=em)
            with ExitStack() as e0:
                ct = sb("ct", [17, D], stack=e0)
                cT = sb("cT", [128, 8, 17], BF16, stack=e0)
                bada = sb("bada", [17, 6 * D], stack=e0)
                wada = [sb("wada%d" % i, [128, 8, 512], BF16, stack=e0) for i in range(3)]
                lamt = sb("lamt", [128, 256], stack=e0)
                lamp = sb("lamp", [128, 128], stack=e0)
                lams = sb("lams", [128, 4], stack=e0)
                S.dma('sp', lambda e: e.dma_start(out=ct[:], in_=c_all), writes=['ct'])
                S.dma('sp', lambda e: e.dma_start(out=bada[:], in_=b_ada.partition_broadcast(17)), writes=['bada'])
                S.dma('sp', lambda e: e.dma_start(out=lamt[:], in_=lam_in.partition_broadcast(128)), writes=['lamt'])
                S.op('dve', lambda e: e.tensor_tensor(out=lamp[:].rearrange("p (a d) -> p a d", d=64), in0=lamt[:].rearrange("p (a d) -> p a d", d=64)[:, 0::2, :],
                                                      in1=lamt[:].rearrange("p (a d) -> p a d", d=64)[:, 1::2, :], op=ALU.mult), reads=['lamt'], writes=['lamp'])
                S.op('dve', lambda e: e.tensor_reduce(out=lams[:, 0:2], in_=lamp[:].rearrange("p (a d) -> p a d", d=64), axis=AX.X, op=ALU.add), reads=['lamp'], writes=['lams'])
                S.op('act', lambda e: e.activation(out=lams[:, 2:4], in_=lams[:, 0:2], func=AF.Exp), reads=['lams'], writes=['lams2'])
                S.op('dve', lambda e: e.tensor_tensor(out=neglam[:], in0=lams[:, 3:4], in1=lams[:, 2:3], op=ALU.subtract), reads=['lams2'], writes=['neglam'])
                S.op('dve', lambda e: e.tensor_scalar(out=neglam[:], in0=neglam[:], scalar1=-LAMBDA_INIT, scalar2=None, op0=ALU.add), reads=['neglam'], writes=['neglam'])
                S.op('act', lambda e: e.activation(out=ct[:], in_=ct[:], func=AF.Silu), reads=['ct'], writes=['ct'])
                transpose_to(17, ct[:], ['ct'], 8, lambda c0, n: cT[:, c0:c0 + n, :], ['cT'], [0, 1])
                wv = w_ada.rearrange("(k p) n -> p k n", p=128)
                for n in range(12):
                    wb = wada[n % 3]
                    wk = ('wada', n % 3)
                    S.dma('pool', lambda e, n=n, wb=wb: e.dma_start(out=wb[:], in_=wv[:, :, n * 512:(n + 1) * 512]), writes=[wk])
                    bank = 2 + (n % 2)
                    pk = ('ps', bank)
                    for k in range(8):
                        S.op('pe', lambda e, k=k, wb=wb, bank=bank: e.matmul(PS[bank][:17, :], lhsT=cT[:, k, :], rhs=wb[:, k, :], start=(k == 0), stop=(k == 7)),
                             reads=['cT', wk], writes=[pk], inc=(k == 7))
                    mt = modA if n < 6 else modB
                    off = (n % 6) * 512
                    S.op('dve', lambda e, mt=mt, off=off, bank=bank, n=n: e.tensor_tensor(out=mt[:, off:off + 512], in0=PS[bank][:17, :], in1=bada[:, n * 512:(n + 1) * 512], op=ALU.add),
                         reads=[pk, 'bada'], writes=['mod'])
            S.barrier()
            if STOP_AFTER == 0:
                S.dma('sp', lambda e: e.dma_start(out=y_s[0:17, :], in_=modA[:, 0:1024]), writes=[])
                S.dma('sp', lambda e: e.dma_start(out=y_s[17:34, :], in_=modB[:, 2048:3072]), writes=[])
                S.finish()
                return nc

            W_in = sb("W_in", [128, 8, 2560], BF16, stack=em)
            W_out = sb("W_out", [128, 8, D], BF16, stack=em)
            gE = sb("gE", [128, D], stack=em); bE = sb("bE", [128, D], stack=em)
            g1 = sb("g1", [128, D], stack=em); b1 = sb("b1", [128, D], stack=em)
            SC = sb("SC", [128, D], stack=em); SH = sb("SH", [128, D], stack=em); GT = sb("GT", [128, D], stack=em)
            xin = sb("xin", [128, D], stack=em); xp = sb("xp", [128, D], stack=em); htmp = sb("htmp", [128, D], stack=em)
            hT = sb("hT", [128, 8, 128], BF16, stack=em)
            QT = sb("QT", [128, 4, 128], BF16, stack=em)
            kvst = [sb("kvst%d" % i, [128, 512], stack=em) for i in range(2)]
            sig = sb("sig", [128, 128], stack=em)
            mixT = sb("mixT", [128, 8, 128], BF16, stack=em)
            cvt = sb("cvt", [128, 512], stack=em); cvn = sb("cvn", [128, 512], stack=em)

            w_in_v = w_in.rearrange("(k p) n -> p k n", p=128)
            for i in range(4):
                S.dma('pool', lambda e, i=i: e.dma_start(out=W_in[:, 2 * i:2 * i + 2, :], in_=w_in_v[:, 2 * i:2 * i + 2, :]), writes=['W_in'])
            S.dma('pool', lambda e: e.dma_start(out=W_out[:], in_=w_out.rearrange("(k p) n -> p k n", p=128)), writes=['W_out'])
            for (tl, src, key) in [(gE, ln_emb_g, 'gE'), (bE, ln_emb_b, 'bE'), (g1, ln1_g, 'g1'), (b1, ln1_b, 'b1')]:
                S.dma('sp', lambda e, tl=tl, src=src: e.dma_start(out=tl[:], in_=src.partition_broadcast(128)), writes=[key])

            def set_mod_tiles(P, sel, modt, scaleoff, shiftoff, gateoff):
                bcast_rows(SH, P, sel, modt, shiftoff, False, 'SH', 4)
                bcast_rows(SC, P, sel, modt, scaleoff, True, 'SC', 5)
                bcast_rows(GT, P, sel, modt, gateoff, False, 'GT', 4)

            def front(P, x_src, kout, vout, kT_dst, vbf_dst, u_dst):
                S.dma('sp', lambda e: e.dma_start(out=xin[:P, :], in_=x_src), writes=['xin'])
                layer_norm_rows(P, xin[:P, :], xp[:P, :], ['xin'], ['xp'], 0)
                S.op('dve', lambda e: e.tensor_tensor(out=xp[:P, :], in0=xp[:P, :], in1=gE[:P, :], op=ALU.mult), reads=['xp', 'gE'], writes=['xp'])
                S.op('dve', lambda e: e.tensor_tensor(out=xp[:P, :], in0=xp[:P, :], in1=bE[:P, :], op=ALU.add), reads=['xp', 'bE'], writes=['xp'])
                S.op('dve', lambda e: e.tensor_tensor(out=htmp[:P, :], in0=xp[:P, :], in1=SC[:P, :], op=ALU.mult), reads=['xp', 'SC'], writes=['htmp'])
                S.op('dve', lambda e: e.tensor_tensor(out=htmp[:P, :], in0=htmp[:P, :], in1=SH[:P, :], op=ALU.add), reads=['htmp', 'SH'], writes=['htmp'])
                transpose_to(P, htmp[:P, :], ['htmp'], 8, lambda c0, n: hT[:, c0:c0 + n, 0:P], ['hT'], [0, 1])
                for wi, (c0, dst) in enumerate([(512, kout), (1024, vout)]):
                    bank = 2 + wi
                    pk = ('ps', bank)
                    for k in range(8):
                        S.op('pe', lambda e, k=k, c0=c0, bank=bank: e.matmul(PS[bank][:P, :], lhsT=hT[:, k, 0:P], rhs=W_in[:, k, c0:c0 + 512], start=(k == 0), stop=(k == 7)),
                             reads=['hT', 'W_in'], writes=[pk], inc=(k == 7))
                    st = kvst[wi]
                    sk = ('kvst', wi)
                    S.op('act', lambda e, st=st, bank=bank: e.copy(out=st[:P, :], in_=PS[bank][:P, :]), reads=[pk], writes=[sk])
                    S.dma('sp', lambda e, st=st, dst=dst: e.dma_start(out=dst, in_=st[:P, :]), reads=[sk], writes=[])
                    if wi == 1:
                        vbf_dst(st, sk)
                for pr in range(4):
                    for wi, c0 in enumerate([0, 512]):
                        bank = 4 + ((2 * pr + wi) % 2)
                        pk = ('ps', bank)
                        for k in range(8):
                            S.op('pe', lambda e, k=k, c0=c0, pr=pr, bank=bank: e.matmul(PS[bank][:, 0:P], lhsT=W_in[:, k, c0 + 128 * pr:c0 + 128 * pr + 128], rhs=hT[:, k, 0:P], start=(k == 0), stop=(k == 7)),
                                 reads=['hT', 'W_in'], writes=[pk], inc=(k == 7))
                        if wi == 0:
                            S.op('act', lambda e, pr=pr, bank=bank: e.copy(out=QT[:, pr, 0:P], in_=PS[bank][:, 0:P]), reads=[pk], writes=['QT'])
                        else:
                            kT_dst(pr, bank, pk)
                for c in range(4):
                    pa = ('ps', 6)
                    pg = ('ps', 7)
                    for k in range(8):
                        S.op('pe', lambda e, k=k, c=c: e.matmul(PS[6][:, 0:P], lhsT=W_in[:, k, 1536 + 128 * c:1536 + 128 * c + 128], rhs=hT[:, k, 0:P], start=(k == 0), stop=(k == 7)),
                             reads=['hT', 'W_in'], writes=[pa], inc=(k == 7))
                    for k in range(8):
                        S.op('pe', lambda e, k=k, c=c: e.matmul(PS[7][:, 0:P], lhsT=W_in[:, k, 2048 + 128 * c:2048 + 128 * c + 128], rhs=hT[:, k, 0:P], start=(k == 0), stop=(k == 7)),
                             reads=['hT', 'W_in'], writes=[pg], inc=(k == 7))
                    S.op('act', lambda e: e.activation(out=sig[:, 0:P], in_=PS[7][:, 0:P], func=AF.Sigmoid), reads=[pg], writes=['sig'])
                    u_dst(c, pa)

            def attn_epilogue_tok(P, obank0, col0):
                pks = [('ps', obank0 + i) for i in range(4)]
                for hb in range(4):
                    ov = PS[obank0 + hb][:P, :].rearrange("p (h w) -> p h w", w=256)
                    S.op('dve', lambda e, hb=hb, ov=ov: e.reciprocal(out=rs8[:P, 2 * hb:2 * hb + 2], in_=ov[:, :, 128:129].rearrange("p h o -> p (h o)")), reads=[pks[hb]], writes=['rs8'])
                    S.op('dve', lambda e, hb=hb, ov=ov: e.tensor_tensor(out=osb[:P, 2 * hb:2 * hb + 2, :], in0=ov[:, :, 0:128],
                                                                 in1=rs8[:P, 2 * hb:2 * hb + 2].unsqueeze(2).to_broadcast([P, 2, 128]), op=ALU.mult), reads=[pks[hb], 'rs8'], writes=['osb'])
                S.op('dve', lambda e: e.scalar_tensor_tensor(out=o4[:P], in0=osb[:P, 1::2, :], scalar=neglam[:P, 0:1], in1=osb[:P, 0::2, :], op0=ALU.mult, op1=ALU.add),
                     reads=['osb', 'neglam'], writes=['o4'])
                S.op('pool', lambda e: e.tensor_tensor(out=osq[:P], in0=o4[:P], in1=o4[:P], op=ALU.mult), reads=['o4'], writes=['osq'])
                S.op('dve', lambda e: e.tensor_reduce(out=rs8[:P, 8:12], in_=osq[:P], axis=AX.X, op=ALU.add), reads=['osq'], writes=['rs8b'])
                rsqrt(rs8[:P, 12:16], rs8[:P, 8:12], ['rs8b'], ['rs8c'], scale=1.0 / 128)
                S.op('dve', lambda e: e.tensor_tensor(out=o4[:P], in0=o4[:P], in1=rs8[:P, 12:16].unsqueeze(2).to_broadcast([P, 4, 128]), op=ALU.mult), reads=['o4', 'rs8c'], writes=['o4'])
                S.op('dve', lambda e: e.tensor_tensor(out=o4[:P], in0=o4[:P], in1=sg8row[:P, :].unsqueeze(1).to_broadcast([P, 4, 128]), op=ALU.mult), reads=['o4', 'sg8row'], writes=['o4'])
                transpose_to(P, o4[:P].rearrange("p a e -> p (a e)"), ['o4'], 4, lambda c0, n: mixT[:, c0:c0 + n, col0:col0 + P], ['mixT'], [0], evac=('act',))

            def conv_ln_silu(P, cv_fn, cv_keys):
                pk = ('ps', 1)
                for c in range(4):
                    S.op('pe', lambda e, c=c: e.transpose(out=PS[1][:P, c * 128:(c + 1) * 128], in_=cv_fn(c), identity=ident[:]),
                         reads=cv_keys + ['ident'], writes=[pk], inc=(c == 3))
                S.op('act', lambda e: e.copy(out=cvt[:P, :], in_=PS[1][:P, :]), reads=[pk], writes=['cvt'])
                st6 = small[:P, 16:22]
                mv = small[:P, 22:24]
                rstd = small[:P, 24:25]
                nmr = small[:P, 25:26]
                k = ('small', 16)
                S.op('dve', lambda e: e.bn_stats(out=st6, in_=cvt[:P, :]), reads=['cvt'], writes=[k])
                S.op('dve', lambda e: e.bn_aggr(out=mv, in_=st6), reads=[k], writes=[k])
                rsqrt(rstd, mv[:, 1:2], [k], [k])
                S.op('dve', lambda e: e.scalar_tensor_tensor(out=nmr, in0=mv[:, 0:1], scalar=-1.0, in1=rstd, op0=ALU.mult, op1=ALU.mult), reads=[k], writes=[k])
                S.op('act', lambda e: e.activation(out=cvn[:P, :], in_=cvt[:P, :], func=AF.Identity, bias=nmr, scale=rstd), reads=['cvt', k], writes=['cvn'])
                pk0 = ('ps', 0)
                for c in range(4):
                    S.op('pe', lambda e, c=c: e.transpose(out=PS[0][:, c * 128:c * 128 + P], in_=cvn[:P, c * 128:(c + 1) * 128], identity=ident[:P, :P]),
                         reads=['cvn', 'ident'], writes=[pk0], inc=(c == 3))
                for c in range(4):
                    S.op('act', lambda e, c=c: e.activation(out=mixT[:, 4 + c, 0:P], in_=PS[0][:, c * 128:c * 128 + P], func=AF.Silu, bias=clb[:, c:c + 1], scale=clg[:, c:c + 1]),
                         reads=[pk0, 'clg', 'clb'], writes=['mixT'])

            def out_ln1(P, row0):
                for hf in range(2):
                    bank = 2 + hf
                    pk = ('ps', bank)
                    for k in range(8):
                        S.op('pe', lambda e, k=k, hf=hf, bank=bank: e.matmul(PS[bank][:P, :], lhsT=mixT[:, k, 0:P], rhs=W_out[:, k, hf * 512:(hf + 1) * 512], start=(k == 0), stop=(k == 7)),
                             reads=['mixT', 'W_out'], writes=[pk], inc=(k == 7))
                    S.op('dve', lambda e, hf=hf, bank=bank: e.tensor_tensor(out=htmp[:P, hf * 512:(hf + 1) * 512], in0=PS[bank][:P, :], in1=GT[:P, hf * 512:(hf + 1) * 512], op=ALU.mult),
                         reads=[pk, 'GT'], writes=['htmp'])
                S.op('dve', lambda e: e.scalar_tensor_tensor(out=htmp[:P, :], in0=xp[:P, :], scalar=ALPHA, in1=htmp[:P, :], op0=ALU.mult, op1=ALU.add), reads=['xp', 'htmp'], writes=['htmp'])
                layer_norm_rows(P, htmp[:P, :], xin[:P, :], ['htmp'], ['xin'], 32)
                S.op('pool', lambda e: e.tensor_tensor(out=xin[:P, :], in0=xin[:P, :], in1=g1[:P, :], op=ALU.mult), reads=['xin', 'g1'], writes=['xin'])
                S.op('pool', lambda e: e.tensor_tensor(out=xin[:P, :], in0=xin[:P, :], in1=b1[:P, :], op=ALU.add), reads=['xin', 'b1'], writes=['xin'])
                S.dma('sp', lambda e: e.dma_start(out=x1_scr[row0:row0 + P, :], in_=xin[:P, :]), reads=['xin'], writes=['x1scr'])

            with ExitStack() as e1:
                kall = sb("kall", [128, 16, 512], BF16, stack=e1)
                vall = sb("vall", [128, 16, 512], BF16, stack=e1)
                ktp = [sb("ktp%d" % i, [128, 4, 128], BF16, stack=e1) for i in range(3)]
                PTs = sb("PTs", [128, 512], BF16, stack=e1)
                Us = sb("Us", [128, 4, NS, 34], stack=e1)
                KTs = sb("KTs", [128, 4, 64], BF16, stack=e1)
                Vsb = sb("Vsb", [64, 512], BF16, stack=e1)
                auglsb = sb("auglsb", [128, 128], BF16, stack=e1); augrsb = sb("augrsb", [128, 256], BF16, stack=e1)
                ucont = sb("ucont", [128, 4, NS, 30], stack=e1)
                bnew = sb("bnew", [64, 256], stack=e1); mnew = sb("mnew", [64, 256], stack=e1)
                pnewf = sb("pnewf", [64, 512], stack=e1); pnewT = sb("pnewT", [64, 512], BF16, stack=e1)
                ptl = sb("ptl", [128, NS], I32, stack=e1); ptf = sb("ptf", [128, NS], stack=e1)
                pm8 = sb("pm8", [128, 1], stack=e1); idx = sb("idx", [128, NS], I32, stack=e1)
                stc = sb("stc", [120, 512], stack=e1)
                osall = sb("osall", [128, 4, 64], stack=e1); ossq = sb("ossq", [128, 4, 64], stack=e1)
                rsum = sb("rsum", [128, 32], stack=e1); on32 = sb("on32", [128, 32], stack=e1)
                rstd_s = sb("rstd_s", [128, 256], stack=e1)
                cvs = sb("cvs", [128, 4, 64], stack=e1)
                acc = [sb("acc%d" % i, [128, 64], stack=e1) for i in range(2)]

                for r0 in (0, 64):
                    S.dma('pool', lambda e, r0=r0: e.dma_start(out=auglsb[r0:r0 + 3, :], in_=c_augls), writes=['auglsb'])
                    S.dma('pool', lambda e, r0=r0: e.dma_start(out=augrsb[r0:r0 + 3, :], in_=c_augrs), writes=['augrsb'])
                S.dma('sp', lambda e: e.dma_start(out=bnew[:], in_=c_bnew), writes=['bnew'])
                S.dma('sp', lambda e: e.dma_start(out=mnew[:], in_=c_mnew), writes=['mnew'])
                S.dma('sp', lambda e: e.dma_start(out=ptl[:], in_=pt_lay), writes=['ptl'])
                S.dma('sp', lambda e: e.dma_start(out=pm8[:], in_=c_pm8), writes=['pm8'])
                S.op('dve', lambda e: e.tensor_copy(out=ptf[:], in_=ptl[:]), reads=['ptl'], writes=['ptf'])
                S.op('dve', lambda e: e.tensor_scalar(out=ptf[:], in0=ptf[:], scalar1=8.0, scalar2=pm8[:, 0:1], op0=ALU.mult, op1=ALU.add), reads=['ptf', 'pm8'], writes=['ptf'])
                S.op('dve', lambda e: e.tensor_copy(out=idx[:], in_=ptf[:]), reads=['ptf'], writes=['idx'])

                def gather(b, which):
                    if which == 0:
                        S.dma('pool', lambda e: e.indirect_dma_start(out=kall[:].rearrange("p a f -> p (a f)"), out_offset=None, in_=cache_k,
                                                                    in_offset=bass.IndirectOffsetOnAxis(ap=idx[:, b:b + 1], axis=0)), reads=['idx'], writes=['kall'])
                    else:
                        S.dma('pool', lambda e: e.indirect_dma_start(out=vall[:].rearrange("p a f -> p (a f)"), out_offset=None, in_=cache_v,
                                                                    in_offset=bass.IndirectOffsetOnAxis(ap=idx[:, b:b + 1], axis=0)), reads=['idx'], writes=['vall'])
                if STOP_AFTER == 0.1:
                    S.barrier(); S.finish()
                    return nc
                gather(0, 0)
                gather(0, 1)
                if STOP_AFTER == 0.2:
                    S.barrier(); S.finish()
                    return nc

                for g in range(4):
                    S.dma('sp', lambda e, g=g: e.dma_start(out=stc[:, :], in_=st_conv[g * 120:(g + 1) * 120, :]), writes=['stc'])
                    pk = ('ps', 0)
                    for c in range(4):
                        S.op('pe', lambda e, c=c: e.transpose(out=PS[0][:, c * 128:c * 128 + 120], in_=stc[:, c * 128:(c + 1) * 128], identity=ident[:120, :120]),
                             reads=['stc', 'ident'], writes=[pk], inc=(c == 3))
                    for c in range(4):
                        S.op('dve', lambda e, c=c, g=g: e.tensor_copy(out=Us[:, c, 4 * g:4 * g + 4, 0:30], in_=PS[0][:, c * 128:c * 128 + 120].rearrange("p (b t) -> p b t", t=30)),
                             reads=[pk], writes=['Us'])

                if STOP_AFTER == 0.3:
                    S.barrier(); S.finish()
                    return nc
                set_mod_tiles(64, sels, modA, 1024, 0, 2048)
                if STOP_AFTER == 0.4:
                    S.barrier(); S.finish()
                    return nc

                def s_vbf(st, sk):
                    S.op('dve', lambda e: e.tensor_copy(out=Vsb[:, :], in_=st[:64, :]), reads=[sk], writes=['Vsb'])

                def s_kT(pr, bank, pk):
                    S.op('dve', lambda e: e.tensor_copy(out=KTs[:, pr, :], in_=PS[bank][:, 0:64]), reads=[pk], writes=['KTs'])

                def s_u(c, pa):
                    S.op('dve', lambda e: e.tensor_tensor(out=Us[:, c, :, 30:34], in0=PS[6][:, 0:64].rearrange("p (b t) -> p b t", t=4),
                                                          in1=sig[:, 0:64].rearrange("p (b t) -> p b t", t=4), op=ALU.mult), reads=[pa, 'sig'], writes=['Us'])

                front(64, x_s, k_s, v_s, s_kT, s_vbf, s_u)

                if STOP_AFTER == 0.5:
                    S.barrier(); S.finish()
                    return nc
                for hh in range(2):
                    bank = 4 if hh == 0 else 2
                    pk = ('ps', bank)
                    r0 = 64 * hh
                    for pr in range(4):
                        S.op('pe', lambda e, pr=pr, r0=r0, bank=bank: e.matmul(PS[bank][:64, pr * 64:(pr + 1) * 64], lhsT=KTs[r0:r0 + 64, pr, :], rhs=QT[r0:r0 + 64, pr, 0:64], start=True, stop=True),
                             reads=['KTs', 'QT'], writes=[pk], inc=(pr == 3))
                    S.op('dve', lambda e, hh=hh, bank=bank: e.scalar_tensor_tensor(out=pnewf[:, hh * 256:(hh + 1) * 256], in0=PS[bank][:64, 0:256], scalar=0.125, in1=bnew[:, :], op0=ALU.mult, op1=ALU.add),
                         reads=[pk, 'bnew'], writes=['pnewf'])
                S.op('act', lambda e: e.activation(out=pnewf[:], in_=pnewf[:], func=AF.Exp), reads=['pnewf'], writes=['pnewf'])
                pnv = pnewT[:].rearrange("p (b hh pr t) -> p b hh pr t", hh=2, pr=4, t=4)
                for hh in range(2):
                    S.op('dve', lambda e, hh=hh: e.tensor_tensor(out=pnv[:, :, hh, :, :].rearrange("p b pr t -> p pr b t"), in0=pnewf[:, hh * 256:(hh + 1) * 256].rearrange("p (pr b t) -> p pr b t", pr=4, t=4),
                                                                in1=mnew[:, :].rearrange("p (pr b t) -> p pr b t", pr=4, t=4), op=ALU.mult), reads=['pnewf', 'mnew'], writes=['pnewT'])
                if STOP_AFTER == 0.6:
                    S.barrier(); S.finish()
                    return nc

                SB = [5, 3]
                for b in range(NS):
                    for t16 in range(16):
                        tb = 6 + (t16 % 2)
                        tpk = ('ps', tb)
                        for pr in range(4):
                            S.op('pe', lambda e, pr=pr, t16=t16, tb=tb: e.transpose(out=PSB[tb][:, pr * 128:(pr + 1) * 128], in_=kall[:, t16, pr * 128:(pr + 1) * 128], identity=identb[:]),
                                 reads=['kall', 'identb'], writes=[tpk], inc=(pr == 3))
                        kt = ktp[t16 % 3]
                        kk = ('ktp', t16 % 3)
                        if t16 % 2 == 0:
                            S.op('act', lambda e, kt=kt, tb=tb: e.copy(out=kt[:].rearrange("p a k -> p (a k)"), in_=PSB[tb][:, 0:512]), reads=[tpk], writes=[kk])
                        else:
                            S.op('dve', lambda e, kt=kt, tb=tb: e.tensor_copy(out=kt[:].rearrange("p a k -> p (a k)"), in_=PSB[tb][:, 0:512]), reads=[tpk], writes=[kk])
                        for hh in range(2):
                            r0 = 64 * hh
                            for pr in range(4):
                                S.op('pe', lambda e, pr=pr, hh=hh, r0=r0, kt=kt, t16=t16, b=b: e.matmul(PS[SB[hh]][:, t16 * 16 + pr * 4:t16 * 16 + pr * 4 + 4], lhsT=kt[r0:r0 + 64, pr, :],
                                                                                               rhs=QT[r0:r0 + 64, pr, 4 * b:4 * b + 4], start=(t16 == 0 and pr == 0), stop=False, skip_group_check=True),
                                     reads=[kk, 'QT'], writes=[('ps', SB[hh])], inc=False)
                    if b + 1 < NS:
                        gather(b + 1, 0)
                    for hh in range(2):
                        r0 = 64 * hh
                        S.op('pe', lambda e, hh=hh, r0=r0: e.matmul(PS[SB[hh]][:, 0:256], lhsT=auglsb[r0:r0 + 3, :], rhs=augrsb[r0:r0 + 3, :], start=False, stop=True, skip_group_check=True),
                             reads=['auglsb', 'augrsb'], writes=[('ps', SB[hh])])
                    for hh in range(2):
                        S.op('act', lambda e, hh=hh: e.activation(out=PTs[:, hh * 256:(hh + 1) * 256], in_=PS[SB[hh]][:, 0:256], func=AF.Exp, scale=0.125), reads=[('ps', SB[hh])], writes=['PTs'])
                    pk4 = ('ps', 4)
                    for hh in range(2):
                        for t16 in range(16):
                            S.op('pe', lambda e, t16=t16, hh=hh: e.matmul(PS[4][:, hh * 16:(hh + 1) * 16], lhsT=onesb[:, :], rhs=PTs[:, hh * 256 + t16 * 16:hh * 256 + (t16 + 1) * 16], start=(t16 == 0), stop=False),
                                 reads=['PTs', 'onesb'], writes=[pk4], inc=False)
                        S.op('pe', lambda e, b=b, hh=hh: e.matmul(PS[4][:, hh * 16:(hh + 1) * 16], lhsT=onesb[0:64, :], rhs=pnv[:, b, hh, :, :], start=False, stop=True),
                             reads=['pnewT', 'onesb'], writes=[pk4], inc=False)
                    for dh in range(4):
                        for hh in range(2):
                            oc = 64 + dh * 8 + hh * 4
                            for t16 in range(16):
                                S.op('pe', lambda e, dh=dh, t16=t16, hh=hh, oc=oc: e.matmul(PS[4][:, oc:oc + 4], lhsT=vall[:, t16, dh * 128:(dh + 1) * 128],
                                                                                         rhs=PTs[:, hh * 256 + t16 * 16 + dh * 4:hh * 256 + t16 * 16 + dh * 4 + 4], start=(t16 == 0), stop=False),
                                     reads=['PTs', 'vall'], writes=[pk4], inc=False)
                            S.op('pe', lambda e, dh=dh, b=b, hh=hh, oc=oc: e.matmul(PS[4][:, oc:oc + 4], lhsT=Vsb[0:64, dh * 128:(dh + 1) * 128], rhs=pnv[:, b, hh, dh, :], start=False, stop=True),
                                 reads=['pnewT', 'Vsb'], writes=[pk4], inc=(dh == 3 and hh == 1))
                    if b + 1 < NS:
                        gather(b + 1, 1)
                    rsv = rsum[:].rearrange("p (d h q) -> p d h q", h=2, q=4)
                    for hh in range(2):
                        S.op('dve', lambda e, hh=hh: e.reciprocal(out=rsv[:, :, hh, :], in_=PS[4][:, hh * 16:(hh + 1) * 16].rearrange("p (d q) -> p d q", q=4)), reads=[pk4], writes=['rsum'])
                    S.op('dve', lambda e: e.tensor_tensor(out=on32[:], in0=PS[4][:, 64:96], in1=rsum[:], op=ALU.mult), reads=[pk4, 'rsum'], writes=['on32'])
                    onv = on32[:].rearrange("p (d h q) -> p d h q", h=2, q=4)
                    S.op('dve', lambda e, b=b: e.scalar_tensor_tensor(out=osall[:, :, 4 * b:4 * b + 4], in0=onv[:, :, 1, :], scalar=neglam[:, 0:1], in1=onv[:, :, 0, :], op0=ALU.mult, op1=ALU.add),
                         reads=['on32', 'neglam'], writes=['osall'])

                if STOP_AFTER == 0.7:
                    S.barrier(); S.finish()
                    return nc
                S.op('dve', lambda e: e.tensor_tensor(out=ossq[:], in0=osall[:], in1=osall[:], op=ALU.mult), reads=['osall'], writes=['ossq'])
                pk = ('ps', 6)
                S.op('pe', lambda e: e.matmul(PS[6][:, 0:256], lhsT=onesf[:, :], rhs=ossq[:].rearrange("p a t -> p (a t)"), start=True, stop=True), reads=['ossq', 'onesf'], writes=[pk])
                rsqrt(rstd_s[:], PS[6][:, 0:256], [pk], ['rstd_s'], scale=1.0 / 128)
                S.op('dve', lambda e: e.scalar_tensor_tensor(out=mixT[:, 0:4, 0:64], in0=osall[:], scalar=sg8[:, 0:1], in1=rstd_s[:].rearrange("p (a t) -> p a t", t=64), op0=ALU.mult, op1=ALU.mult),
                     reads=['osall', 'sg8', 'rstd_s'], writes=['mixT'])

                if STOP_AFTER == 0.8:
                    S.barrier(); S.finish()
                    return nc
                for c in range(4):
                    eng = 'dve'
                    a = acc[c % 2]
                    ak = ('acc', c % 2)
                    av = a[:, :].rearrange("p (b t) -> p b t", t=4)
                    S.op(eng, lambda e, c=c, av=av: e.tensor_scalar(out=av, in0=Us[:, c, :, 0:4], scalar1=cw[:, c, 0:1], scalar2=cb[:, c:c + 1], op0=ALU.mult, op1=ALU.add),
                         reads=['Us', 'cw', 'cb'], writes=[ak])
                    for j in range(1, 31):
                        dst = av if j < 30 else cvs[:, c, :].rearrange("p (b t) -> p b t", t=4)
                        S.op(eng, lambda e, c=c, j=j, av=av, dst=dst: e.scalar_tensor_tensor(out=dst, in0=Us[:, c, :, j:j + 4], scalar=cw[:, c, j:j + 1], in1=av, op0=ALU.mult, op1=ALU.add),
                             reads=['Us', ak], writes=[ak] if j < 30 else ['cvs'])
                conv_ln_silu(64, lambda c: cvs[:, c, :], ['cvs'])
                for c in range(4):
                    S.op('act', lambda e, c=c: e.copy(out=ucont[:, c, :, :], in_=Us[:, c, :, 4:34]), reads=['Us'], writes=['ucont'])
                for g in range(4):
                    pk = ('ps', 1)
                    for c in range(4):
                        S.op('pe', lambda e, c=c, g=g: e.transpose(out=PS[1][:120, c * 128:(c + 1) * 128], in_=ucont[:, c, 4 * g:4 * g + 4, :], identity=ident[:]),
                             reads=['ucont', 'ident'], writes=[pk], inc=(c == 3))
                    S.op('act', lambda e: e.copy(out=stc[:, :], in_=PS[1][:120, :]), reads=[pk], writes=['stc'])
                    S.dma('sp', lambda e, g=g: e.dma_start(out=conv_s[g * 120:(g + 1) * 120, :], in_=stc[:, :]), reads=['stc'], writes=[])
                if STOP_AFTER == 0.9:
                    S.barrier(); S.finish()
                    return nc
                out_ln1(64, T)
            S.barrier()
            if STOP_AFTER == 1:
                S.op('dve', lambda e: e.tensor_copy(out=htmp[:, :], in_=mixT[:].rearrange("p a t -> p (a t)")), writes=['htmp'])
                S.dma('sp', lambda e: e.dma_start(out=y_p[0:128, :], in_=htmp[:, :]), reads=['htmp'], writes=[])
                S.finish()
                return nc

            with ExitStack() as e2:
                KT = sb("KT", [128, 4, T], BF16, stack=e2)
                Vext = sb("Vext", [128, NBLK, 4, 130], BF16, stack=e2)
                auglpb = sb("auglpb", [128, 512], BF16, stack=e2); augrpb = sb("augrpb", [128, 4096], BF16, stack=e2)
                osb = sb("osb", [128, 8, 128], stack=e2); o4 = sb("o4", [128, 4, 128], stack=e2); osq = sb("osq", [128, 4, 128], stack=e2)
                rs8 = sb("rs8", [128, 16], stack=e2)
                U = sb("U", [128, 4, 30 + 128], stack=e2)
                PT = [sb("PT%d" % i, [128, 512], BF16, stack=e2) for i in range(3)]
                cva = [sb("cva%d" % i, [128, 128], stack=e2) for i in range(4)]
                cpo = sb("cpo", [30, 512], stack=e2)

                for r0 in (0, 64):
                    S.dma('pool', lambda e, r0=r0: e.dma_start(out=auglpb[r0:r0 + 3, :], in_=c_auglp), writes=['auglpb'])
                    S.dma('pool', lambda e, r0=r0: e.dma_start(out=augrpb[r0:r0 + 3, :], in_=c_augrp), writes=['augrpb'])
                S.op('dve', lambda e: e.memset(U[:], 0.0), writes=['U'])
                S.op('pool', lambda e: e.memset(Vext[:].rearrange("p a b c -> p (a b c)"), 1.0), writes=['Vext'])
                set_mod_tiles(128, selp, modA, 1024, 0, 2048)

                ptcnt = [0]
                for i in range(NBLK):
                    def p_vbf(st, sk, i=i):
                        S.op('pool', lambda e: e.tensor_copy(out=Vext[:, i, :, 0:128], in_=st[:, :].rearrange("p (a e) -> p a e", e=128)), reads=[sk], writes=['Vext'])

                    def p_kT(pr, bank, pk, i=i):
                        S.op('dve', lambda e: e.tensor_copy(out=KT[:, pr, i * 128:(i + 1) * 128], in_=PS[bank][:, 0:128]), reads=[pk], writes=['KT'])

                    def p_u(c, pa):
                        S.op('dve', lambda e: e.tensor_tensor(out=U[:, c, 30:158], in0=PS[6][:, 0:128], in1=sig[:, 0:128], op=ALU.mult), reads=[pa, 'sig'], writes=[('U', c)])

                    front(128, x_p[i * 128:(i + 1) * 128, :], k_p[i * 128:(i + 1) * 128, :], v_p[i * 128:(i + 1) * 128, :], p_kT, p_vbf, p_u)

                    for c in range(4):
                        eng = 'dve'
                        a = cva[c]
                        ak = ('cva', c)
                        S.op(eng, lambda e, c=c, a=a: e.tensor_scalar(out=a[:, :], in0=U[:, c, 0:128], scalar1=cw[:, c, 0:1], scalar2=cb[:, c:c + 1], op0=ALU.mult, op1=ALU.add),
                             reads=[('U', c), 'U', 'cw', 'cb'], writes=[ak])
                        for j in range(1, 31):
                            S.op(eng, lambda e, c=c, j=j, a=a: e.scalar_tensor_tensor(out=a[:, :], in0=U[:, c, j:j + 128], scalar=cw[:, c, j:j + 1], in1=a[:, :], op0=ALU.mult, op1=ALU.add),
                                 reads=[('U', c), ak], writes=[ak])
                    if i == NBLK - 1:
                        pk = ('ps', 1)
                        for c in range(4):
                            S.op('pe', lambda e, c=c: e.transpose(out=PS[1][:30, c * 128:(c + 1) * 128], in_=U[:, c, 128:158], identity=ident[:]),
                                 reads=[('U', c), 'U', 'ident'], writes=[pk], inc=(c == 3))
                        S.op('act', lambda e: e.copy(out=cpo[:, :], in_=PS[1][:30, :]), reads=[pk], writes=['cpo'])
                        S.dma('sp', lambda e: e.dma_start(out=conv_p, in_=cpo[:, :]), reads=['cpo'], writes=[])

                    obanks = [4, 5, 6, 7]
                    for h in range(8):
                        r0 = 64 * (h % 2)
                        pr = h // 2
                        ob = obanks[h // 2]
                        ocol = (h % 2) * 256
                        opk = ('ps', ob)
                        ngrp = (i + 4) // 4
                        for g in range(ngrp):
                            j0 = 4 * g
                            nb = min(4, i + 1 - j0)
                            sbank = 2 + (ptcnt[0] % 2)
                            spk = ('ps', sbank)
                            for jj in range(nb):
                                S.op('pe', lambda e, jj=jj, j0=j0, r0=r0, pr=pr, sbank=sbank: e.matmul(PS[sbank][:, jj * 128:(jj + 1) * 128], lhsT=KT[r0:r0 + 64, pr, (j0 + jj) * 128:(j0 + jj + 1) * 128],
                                                                                               rhs=QT[r0:r0 + 64, pr, 0:128], start=(jj == 0), stop=False, skip_group_check=True),
                                     reads=['KT', 'QT'], writes=[spk], inc=False)
                            g0 = j0 - i + 16
                            S.op('pe', lambda e, nb=nb, g0=g0, pr=pr, sbank=sbank, r0=r0: e.matmul(PS[sbank][:, 0:nb * 128], lhsT=auglpb[r0:r0 + 3, pr * 128:(pr + 1) * 128], rhs=augrpb[r0:r0 + 3, g0 * 128:(g0 + nb) * 128], start=False, stop=True, skip_group_check=True),
                                 reads=['auglpb', 'augrpb'], writes=[spk])
                            pt = PT[ptcnt[0] % 3]
                            ptk = ('PT', ptcnt[0] % 3)
                            ptcnt[0] += 1
                            S.op('act', lambda e, pt=pt, nb=nb, sbank=sbank: e.activation(out=pt[:, 0:nb * 128], in_=PS[sbank][:, 0:nb * 128], func=AF.Exp, scale=0.125), reads=[spk], writes=[ptk])
                            if j0 + nb - 1 == i:
                                S.op('dve', lambda e, pt=pt, nb=nb: e.tensor_tensor(out=pt[:, (nb - 1) * 128:nb * 128], in0=pt[:, (nb - 1) * 128:nb * 128], in1=maskTb[:, :], op=ALU.mult),
                                     reads=[ptk, 'maskTb'], writes=[ptk])
                            for jj in range(nb):
                                j = j0 + jj
                                S.op('pe', lambda e, pt=pt, jj=jj, j=j, pr=pr, ob=ob, ocol=ocol: e.matmul(PS[ob][:, ocol:ocol + 129], lhsT=pt[:, jj * 128:(jj + 1) * 128], rhs=Vext[:, j, pr, 0:129], start=(j == 0), stop=(j == i)),
                                     reads=[ptk, 'Vext'], writes=[opk], inc=(j == i))
                    attn_epilogue_tok(128, 4, 0)
                    conv_ln_silu(128, lambda c: cva[c][:, :], [('cva', c) for c in range(4)])
                    for c in range(4):
                        S.op('act', lambda e, c=c: e.copy(out=U[:, c, 0:30], in_=U[:, c, 128:158]), reads=[('U', c), 'U'], writes=[('U', c)])
                    out_ln1(128, i * 128)
            S.barrier()
            if STOP_AFTER == 2:
                S.finish()
                return nc

        with ExitStack() as ef:
            W_up = sb("W_up", [128, 8, 2 * DFF], BF16, stack=ef)
            W_dn = sb("W_dn", [128, NCH, D], BF16, stack=ef)
            g2 = sb("g2", [128, D], stack=ef); b2 = sb("b2", [128, D], stack=ef)
            SC2 = sb("SC2", [128, D], stack=ef); SH2 = sb("SH2", [128, D], stack=ef); GT2 = sb("GT2", [128, D], stack=ef)
            x1t = sb("x1t", [128, D], stack=ef); ht2 = sb("ht2", [128, D], stack=ef)
            h2T = sb("h2T", [128, 8, 128], BF16, stack=ef)
            gT = sb("gT", [128, NCH, 128], BF16, stack=ef)
            carry = sb("carry", [128, 44, 2], stack=ef)
            carrys = sb("carrys", [128, 44, NS, 2], stack=ef)
            ua = [sb("ua%d" % i, [128, 130], stack=ef) for i in range(4)]
            tt = [sb("tt%d" % i, [128, 128], stack=ef) for i in range(8)]
            uas = [sb("uas%d" % i, [128, NS, 6], stack=ef) for i in range(2)]
            sfc = [sb("sfc%d" % i, [32, 512], stack=ef) for i in range(2)]

            w_up_v = w_up.rearrange("(k p) n -> p k n", p=128)
            for i in range(8):
                S.dma('pool', lambda e, i=i: e.dma_start(out=W_up[:, i:i + 1, :], in_=w_up_v[:, i:i + 1, :]), writes=['W_up'])
            w_dn_v = w_down.rearrange("(c p) n -> p c n", p=128)
            for i in range(2):
                S.dma('pool', lambda e, i=i: e.dma_start(out=W_dn[:, 11 * i:11 * i + 11, :], in_=w_dn_v[:, 11 * i:11 * i + 11, :]), writes=['W_dn'])
            S.dma('sp', lambda e: e.dma_start(out=g2[:], in_=ln2_g.partition_broadcast(128)), writes=['g2'])
            S.dma('sp', lambda e: e.dma_start(out=b2[:], in_=ln2_b.partition_broadcast(128)), writes=['b2'])
            S.op('dve', lambda e: e.memset(carry[:], 0.0), writes=['carry'])
            for g in range(11):
                pk = ('ps', 0)
                sf = sfc[g % 2]
                sfk = ('sfc', g % 2)
                S.dma('sp', lambda e, g=g, sf=sf: e.dma_start(out=sf[:, :], in_=st_ffn[:, g * 512:(g + 1) * 512]), writes=[sfk])
                for c in range(4):
                    S.op('pe', lambda e, c=c, sf=sf: e.transpose(out=PS[0][:, c * 128:c * 128 + 32], in_=sf[:, c * 128:(c + 1) * 128], identity=ident[:32, :32]),
                         reads=[sfk, 'ident'], writes=[pk], inc=(c == 3))
                S.op('dve', lambda e, g=g: e.tensor_copy(out=carrys[:, 4 * g:4 * g + 4, :, :], in_=PS[0][:, :].rearrange("p (c x) -> p c x", x=128)[:, :, 0:32].rearrange("p c (b t) -> p c b t", t=2)),
                     reads=[pk], writes=['carrys'])

            def set_mod2(P, sel):
                for (dst, key, off, one, bank) in [(SH2, 'SH2', 0, False, 4), (SC2, 'SC2', 1024, True, 5), (GT2, 'GT2', 2048, False, 4)]:
                    bcast_rows(dst, P, sel, modB, off, one, key, bank)

            def ffn_block(P, row0, y_dst, sample, last):
                S.dma('sp', lambda e: e.dma_start(out=x1t[:P, :], in_=x1_scr[row0:row0 + P, :]), reads=['x1scr'], writes=['x1t'])
                S.op('dve', lambda e: e.tensor_tensor(out=ht2[:P, :], in0=x1t[:P, :], in1=SC2[:P, :], op=ALU.mult), reads=['x1t', 'SC2'], writes=['ht2'])
                S.op('dve', lambda e: e.tensor_tensor(out=ht2[:P, :], in0=ht2[:P, :], in1=SH2[:P, :], op=ALU.add), reads=['ht2', 'SH2'], writes=['ht2'])
                transpose_to(P, ht2[:P, :], ['ht2'], 8, lambda c0, n: h2T[:, c0:c0 + n, 0:P], ['h2T'], [0, 1])
                for c in range(NCH):
                    res = []
                    for half in range(2):
                        ch = c + NCH * half
                        bank = 2 + ((2 * c + half) % 4)
                        pk = ('ps', bank)
                        col = ch * 128
                        for k in range(8):
                            S.op('pe', lambda e, k=k, col=col, bank=bank: e.matmul(PS[bank][:, 0:P], lhsT=W_up[:, k, col:col + 128], rhs=h2T[:, k, 0:P], start=(k == 0), stop=(k == 7)),
                                 reads=['h2T', 'W_up'], writes=[pk], inc=(k == 7))
                        ti = (2 * c + half) % 4
                        t0 = tt[ti]; t1 = tt[4 + ti]
                        k0 = ('tt', ti); k1 = ('tt', 4 + ti)
                        if not sample:
                            u = ua[ti]
                            uk = ('ua', ti)
                            S.op('act', lambda e, u=u, bank=bank: e.copy(out=u[:, 2:130], in_=PS[bank][:, 0:128]), reads=[pk], writes=[uk])
                            S.op('pool', lambda e, u=u, ch=ch: e.tensor_copy(out=u[:, 0:2], in_=carry[:, ch, :]), reads=['carry'], writes=[uk])
                            S.op('act', lambda e, t0=t0, bank=bank, ch=ch: e.activation(out=t0[:, :], in_=PS[bank][:, 0:128], func=AF.Identity, bias=fcb[:, ch:ch + 1], scale=fcw[:, ch, 2:3]),
                                 reads=[pk, 'fcw', 'fcb'], writes=[k0])
                            S.op('dve', lambda e, t0=t0, t1=t1, u=u, ch=ch: e.scalar_tensor_tensor(out=t1[:, :], in0=u[:, 1:129], scalar=fcw[:, ch, 1:2], in1=t0[:, :], op0=ALU.mult, op1=ALU.add),
                                 reads=[uk, k0, 'fcw'], writes=[k1])
                            S.op('dve', lambda e, t0=t0, t1=t1, u=u, ch=ch: e.scalar_tensor_tensor(out=t0[:, :], in0=u[:, 0:128], scalar=fcw[:, ch, 0:1], in1=t1[:, :], op0=ALU.mult, op1=ALU.add),
                                 reads=[uk, k1, 'fcw'], writes=[k0])
                            S.op('pool', lambda e, u=u, ch=ch: e.tensor_copy(out=carry[:, ch, :], in_=u[:, 128:130]), reads=[uk], writes=['carry'])
                            res.append((t0[:, 0:P], k0))
                        else:
                            u = uas[half]
                            uk = ('uas', half)
                            S.op('act', lambda e, u=u, bank=bank: e.copy(out=u[:, :, 2:6], in_=PS[bank][:, 0:64].rearrange("p (b t) -> p b t", t=4)), reads=[pk], writes=[uk])
                            S.op('pool', lambda e, u=u, ch=ch: e.tensor_copy(out=u[:, :, 0:2], in_=carrys[:, ch, :, :]), reads=['carrys'], writes=[uk])
                            t0v = t0[:, 0:64].rearrange("p (b t) -> p b t", t=4)
                            t1v = t1[:, 0:64].rearrange("p (b t) -> p b t", t=4)
                            S.op('act', lambda e, t0=t0, bank=bank, ch=ch: e.activation(out=t0[:, 0:64], in_=PS[bank][:, 0:64], func=AF.Identity, bias=fcb[:, ch:ch + 1], scale=fcw[:, ch, 2:3]),
                                 reads=[pk, 'fcw', 'fcb'], writes=[k0])
                            S.op('dve', lambda e, t0v=t0v, t1v=t1v, u=u, ch=ch: e.scalar_tensor_tensor(out=t1v, in0=u[:, :, 1:5], scalar=fcw[:, ch, 1:2], in1=t0v, op0=ALU.mult, op1=ALU.add),
                                 reads=[uk, k0, 'fcw'], writes=[k1])
                            S.op('dve', lambda e, t0v=t0v, t1v=t1v, u=u, ch=ch: e.scalar_tensor_tensor(out=t0v, in0=u[:, :, 0:4], scalar=fcw[:, ch, 0:1], in1=t1v, op0=ALU.mult, op1=ALU.add),
                                 reads=[uk, k1, 'fcw'], writes=[k0])
                            S.op('pool', lambda e, u=u, ch=ch: e.tensor_copy(out=carrys[:, ch, :, :], in_=u[:, :, 4:6]), reads=[uk], writes=['carrys'])
                            res.append((t0[:, 0:P], k0))
                    (ca, ka), (cb_, kb) = res
                    S.op('act', lambda e, ca=ca: e.activation(out=ca, in_=ca, func=AF.Silu), reads=[ka], writes=[ka])
                    S.op('dve', lambda e, ca=ca, cb_=cb_, c=c: e.tensor_tensor(out=gT[:, c, 0:P], in0=ca, in1=cb_, op=ALU.mult), reads=[ka, kb], writes=['gT'])
                for hf in range(2):
                    bank = 6 + hf
                    pk = ('ps', bank)
                    for c in range(NCH):
                        S.op('pe', lambda e, c=c, hf=hf, bank=bank: e.matmul(PS[bank][:P, :], lhsT=gT[:, c, 0:P], rhs=W_dn[:, c, hf * 512:(hf + 1) * 512], start=(c == 0), stop=(c == NCH - 1)),
                             reads=['gT', 'W_dn'], writes=[pk], inc=(c == NCH - 1))
                    S.op('dve', lambda e, hf=hf, bank=bank: e.tensor_tensor(out=ht2[:P, hf * 512:(hf + 1) * 512], in0=PS[bank][:P, :], in1=GT2[:P, hf * 512:(hf + 1) * 512], op=ALU.mult),
                         reads=[pk, 'GT2'], writes=['ht2'])
                S.op('dve', lambda e: e.scalar_tensor_tensor(out=ht2[:P, :], in0=x1t[:P, :], scalar=ALPHA, in1=ht2[:P, :], op0=ALU.mult, op1=ALU.add), reads=['x1t', 'ht2'], writes=['ht2'])
                layer_norm_rows(P, ht2[:P, :], x1t[:P, :], ['ht2'], ['x1t'], 48)
                S.op('pool', lambda e: e.tensor_tensor(out=x1t[:P, :], in0=x1t[:P, :], in1=g2[:P, :], op=ALU.mult), reads=['x1t', 'g2'], writes=['x1t'])
                S.op('pool', lambda e: e.tensor_tensor(out=x1t[:P, :], in0=x1t[:P, :], in1=b2[:P, :], op=ALU.add), reads=['x1t', 'b2'], writes=['x1t'])
                S.dma('sp', lambda e: e.dma_start(out=y_dst, in_=x1t[:P, :]), reads=['x1t'], writes=[])

            def state_out(src_fn, nrow, dst):
                for g in range(11):
                    pk = ('ps', 1)
                    for c in range(4):
                        ch = 4 * g + c
                        S.op('pe', lambda e, c=c, ch=ch: e.transpose(out=PS[1][:nrow, c * 128:(c + 1) * 128], in_=src_fn(ch), identity=ident[:]),
                             reads=['carry', 'carrys', 'ident'], writes=[pk], inc=(c == 3))
                    sf = sfc[g % 2]
                    sfk = ('sfc', g % 2)
                    S.op('act', lambda e, sf=sf: e.copy(out=sf[:nrow, :], in_=PS[1][:nrow, :]), reads=[pk], writes=[sfk])
                    S.dma('sp', lambda e, g=g, sf=sf: e.dma_start(out=dst[:, g * 512:(g + 1) * 512], in_=sf[:nrow, :]), reads=[sfk], writes=[])

            set_mod2(64, sels)
            ffn_block(64, T, y_s, True, True)
            state_out(lambda ch: carrys[:, ch, :, :], 32, ffn_s)
            set_mod2(128, selp)
            for i in range(NBLK):
                ffn_block(128, i * 128, y_p[i * 128:(i + 1) * 128, :], False, i == NBLK - 1)
            state_out(lambda ch: carry[:, ch, :], 2, ffn_p)
            S.finish()
    return nc


def _consts():
    c = {}
    c["c_ident"] = np.eye(128, dtype=np.float32)
    k = np.arange(128)
    c["c_maskT"] = (k[:, None] <= k[None, :]).astype(np.float32)
    auglp = np.zeros((3, 4, 128), np.float32)
    for s, m in enumerate(SLOPES):
        auglp[0, s] = 8 * m * k
        auglp[1, s] = 8 * m
        auglp[2, s] = 1024 * m
    c["c_auglp"] = auglp.reshape(3, 512)
    augls = np.ones((3, 128), np.float32)
    augls[0] = k
    c["c_augls"] = augls
    augrp = np.zeros((3, 32, 128), np.float32)
    augrp[0] = 1.0
    augrp[1] = -k[None, :]
    augrp[2] = (np.arange(32) - 16)[:, None]
    c["c_augrp"] = augrp.reshape(3, 4096)
    augrs = np.zeros((3, 16, 4, 4), np.float32)
    mp = np.array(SLOPES, np.float32)
    augrs[0] = 128.0 * mp[None, :, None]
    augrs[1] = 8.0 * mp[None, :, None] * (np.arange(16)[:, None, None] - np.arange(4)[None, None, :])
    augrs[2] = -16384.0 * mp[None, :, None]
    c["c_augrs"] = augrs.reshape(3, 256)
    bnew = np.zeros((16, 4, 4, 16, 4), np.float32)
    mnew = np.zeros((16, 4, 4, 16, 4), np.float32)
    for b in range(16):
        for t1 in range(4):
            for t in range(t1, 4):
                for pr in range(4):
                    bnew[b, t1, pr, b, t] = -SLOPES[pr] * (t - t1)
                    mnew[b, t1, pr, b, t] = 1.0
    c["c_bnew"] = bnew.reshape(64, 256)
    c["c_mnew"] = mnew.reshape(64, 256)
    selp = np.zeros((17, 128), np.float32)
    selp[0] = 1.0
    sels = np.zeros((17, 64), np.float32)
    for b in range(16):
        sels[1 + b, 4 * b:4 * b + 4] = 1.0
    c["c_selp"] = selp
    c["c_sels"] = sels
    c["c_pm8"] = (k % 8).astype(np.float32).reshape(128, 1)
    return c


_NC = None
_LAST = None


def kernel(x_prompt, x_sample, c_prompt, c_sample, cache_k, cache_v, page_table, state_conv, state_ffn,
           ln_emb_g, ln_emb_b, w_ada, b_ada, w_in, lambda_q1, lambda_k1, lambda_q2, lambda_k2, subln_g,
           conv_w, conv_b, conv_ln_g, conv_ln_b, w_out, ln1_g, ln1_b,
           w_up, ffn_conv_w, ffn_conv_b, w_down, ln2_g, ln2_b):
    global _NC
    f = lambda a: np.ascontiguousarray(np.asarray(a, dtype=np.float32))
    if _NC is None:
        _NC = build_nc()
    nc = _NC
    shared = dict(_consts())
    shared["cache_k"] = f(cache_k).reshape(NPHYS * 8, 16 * 512)
    shared["cache_v"] = f(cache_v).reshape(NPHYS * 8, 16 * 512)
    shared["ln_emb_g"] = f(ln_emb_g).reshape(1, D); shared["ln_emb_b"] = f(ln_emb_b).reshape(1, D)
    shared["w_ada"] = f(w_ada)[0]; shared["b_ada"] = f(b_ada).reshape(1, 6 * D)
    shared["w_in"] = f(w_in)[0]
    shared["lam_in"] = np.concatenate([f(lambda_q1)[0], f(lambda_k1)[0], f(lambda_q2)[0], f(lambda_k2)[0]]).reshape(1, 256)
    shared["subln_g"] = f(subln_g).reshape(128, 1)
    shared["conv_w"] = np.ascontiguousarray(f(conv_w)[0].reshape(31, 4, 128).transpose(2, 1, 0))
    shared["conv_b"] = np.ascontiguousarray(f(conv_b)[0].reshape(4, 128).T)
    shared["conv_ln_g"] = np.ascontiguousarray(f(conv_ln_g)[0].reshape(4, 128).T)
    shared["conv_ln_b"] = np.ascontiguousarray(f(conv_ln_b)[0].reshape(4, 128).T)
    shared["w_out"] = f(w_out)[0]
    shared["ln1_g"] = f(ln1_g).reshape(1, D); shared["ln1_b"] = f(ln1_b).reshape(1, D)
    shared["w_up"] = f(w_up)[0]
    shared["ffn_cw"] = np.ascontiguousarray(f(ffn_conv_w)[0].reshape(3, 44, 128).transpose(2, 1, 0))
    shared["ffn_cb"] = np.ascontiguousarray(f(ffn_conv_b)[0].reshape(44, 128).T)
    shared["w_down"] = f(w_down)[0]
    shared["ln2_g"] = f(ln2_g).reshape(1, D); shared["ln2_b"] = f(ln2_b).reshape(1, D)
    xp = f(x_prompt); xs = f(x_sample); cp = f(c_prompt); cs = f(c_sample)
    pt = np.asarray(page_table, dtype=np.int32)
    sc = f(state_conv)[0]; sf = f(state_ffn)[0]
    in_maps = []
    for c in range(NCORES):
        m = dict(shared)
        m["x_p"] = xp[c]
        m["x_s"] = xs[NS * c:NS * (c + 1)].reshape(ST, D)
        m["c_all"] = np.concatenate([cp[c:c + 1], cs[NS * c:NS * (c + 1)]], axis=0)
        ptc = pt[NS * c:NS * (c + 1)]
        m["pt_lay"] = np.ascontiguousarray(np.repeat(ptc.T, 8, axis=0))
        m["st_conv"] = sc[NS * c:NS * (c + 1)].reshape(NS * 30, 512)
        m["st_ffn"] = sf[NS * c:NS * (c + 1)].reshape(NS * 2, 2 * DFF)
        in_maps.append(m)
    res = run_bass_kernel_spmd(nc, in_maps, core_ids=list(range(NCORES)))
    global _LAST
    _LAST = res
    R = res.results
    cat = lambda name: np.stack([R[c][name] for c in range(NCORES)], axis=0)
    y_prompt = cat("y_p")
    y_sample = cat("y_s").reshape(NCORES * NS, 4, D)
    k_prompt = cat("k_p").reshape(1, NCORES, T, 8, 64)
    v_prompt = cat("v_p").reshape(1, NCORES, T, 4, 128)
    conv_prompt = cat("conv_p").reshape(1, NCORES, 30, 512)
    ffn_prompt = cat("ffn_p").reshape(1, NCORES, 2, 2 * DFF)
    k_sample = cat("k_s").reshape(1, NCORES * NS, 4, 8, 64)
    v_sample = cat("v_s").reshape(1, NCORES * NS, 4, 4, 128)
    conv_sample = cat("conv_s").reshape(1, NCORES * NS, 30, 512)
    ffn_sample = cat("ffn_s").reshape(1, NCORES * NS, 2, 2 * DFF)
    return (y_prompt, y_sample, k_prompt, v_prompt, conv_prompt, ffn_prompt, k_sample, v_sample, conv_sample, ffn_sample)
```

```python
import math
from contextlib import ExitStack

import numpy as np
import concourse.bass as bass
import concourse.mybir as mybir
from concourse.bass_utils import run_bass_kernel_spmd

F32 = mybir.dt.float32
BF16 = mybir.dt.bfloat16
I32 = mybir.dt.int32
AF = mybir.ActivationFunctionType
ALU = mybir.AluOpType
AX = mybir.AxisListType

D = 1024
T = 2048
NBLK = 16
NS = 16
ST = 64
DFF = 2816
NCH = 22
EPS = 1e-5
ALPHA = 2.0 ** 0.25
LAMBDA_INIT = 0.8 - 0.6 * math.exp(0.0)
SLOPES = [2.0 ** (-8.0 * (i + 1) / 4) for i in range(4)]
NCORES = 8
NPHYS = 2560
STOP_AFTER = None
DEBUG_X1 = False


class Sched:
    NDMA = 12

    def __init__(self, nc, es):
        self.nc = nc
        self.engs = {'pe': nc.tensor, 'act': nc.scalar, 'dve': nc.vector, 'pool': nc.gpsimd, 'sp': nc.sync}
        self.sem = {}
        self.cnt = {}
        for e in ['pe', 'act', 'dve', 'pool']:
            self.sem[e] = es.enter_context(nc.semaphore("s_" + e))
            self.cnt[e] = 0
        self.dsem = {}
        self.dcnt = {}
        self.dnext = {}
        for q in ['sp', 'pool']:
            self.dsem[q] = [es.enter_context(nc.semaphore("d_%s%d" % (q, i))) for i in range(self.NDMA)]
            self.dcnt[q] = [0] * self.NDMA
            self.dnext[q] = 0
        self.seen = {e: {} for e in self.engs}
        self.reg = {}
        self.pend = {e: ([], []) for e in self.engs}
        self.semobj = {}
        for e in self.sem:
            self.semobj[('c', e)] = self.sem[e]
        for q in self.dsem:
            for i, s in enumerate(self.dsem[q]):
                self.semobj[('d', q, i)] = s

    def _r(self, k):
        if k not in self.reg:
            self.reg[k] = [None, []]
        return self.reg[k]

    def _wait(self, e, tok):
        sk, val = tok
        if self.seen[e].get(sk, 0) >= val:
            return
        self.engs[e].wait_ge(self.semobj[sk], val)
        self.seen[e][sk] = val

    def _deps(self, e, reads, writes):
        own = ('c', e)
        deps = {}

        def add(tok, same_ok):
            if tok is None:
                return
            if tok[0] == own and same_ok:
                return
            if deps.get(tok[0], 0) < tok[1]:
                deps[tok[0]] = tok[1]
        for k in reads:
            add(self._r(k)[0], False)
        for k in writes:
            r = self._r(k)
            add(r[0], True)
            for t in r[1]:
                add(t, True)
        for sk, v in deps.items():
            self._wait(e, (sk, v))

    def op(self, e, fn, reads=(), writes=(), inc=True):
        reads = list(reads)
        writes = list(writes)
        self._deps(e, reads, writes)
        ins = fn(self.engs[e])
        pr, pw = self.pend[e]
        if not inc:
            pr.extend(reads)
            pw.extend(writes)
            return ins
        self.cnt[e] += 1
        ins.then_inc(self.sem[e], 1)
        tok = (('c', e), self.cnt[e])
        for k in reads + pr:
            self._r(k)[1].append(tok)
        for k in writes + pw:
            r = self._r(k)
            r[0] = tok
            r[1] = []
        self.pend[e] = ([], [])
        return ins

    def dma(self, q, fn, reads=(), writes=()):
        reads = list(reads)
        writes = list(writes)
        self._deps(q, reads, writes)
        i = self.dnext[q]
        self.dnext[q] = (i + 1) % self.NDMA
        sk = ('d', q, i)
        if self.dcnt[q][i] > 0:
            self._wait(q, (sk, self.dcnt[q][i]))
        ins = fn(self.engs[q])
        self.dcnt[q][i] += 16
        ins.then_inc(self.dsem[q][i], 16)
        tok = (sk, self.dcnt[q][i])
        for k in reads:
            self._r(k)[1].append(tok)
        for k in writes:
            r = self._r(k)
            r[0] = tok
            r[1] = []
        return tok

    def barrier(self):
        toks = [(('c', e), self.cnt[e]) for e in self.sem if self.cnt[e] > 0]
        for q in self.dsem:
            for i in range(self.NDMA):
                if self.dcnt[q][i] > 0:
                    toks.append((('d', q, i), self.dcnt[q][i]))
        for e in self.engs:
            for t in toks:
                if t[0] == ('c', e):
                    continue
                self._wait(e, t)
        self.reg = {}

    def finish(self):
        for q in self.dsem:
            for i in range(self.NDMA):
                if self.dcnt[q][i] > 0:
                    self._wait('sp', (('d', q, i), self.dcnt[q][i]))


def build_nc():
    nc = bass.Bass("TRN2", target_bir_lowering=False)

    def din(name, shape, dt=F32):
        return nc.dram_tensor(name, list(shape), dt, kind="ExternalInput").ap()

    def dout(name, shape, dt=F32):
        return nc.dram_tensor(name, list(shape), dt, kind="ExternalOutput").ap()

    x_p = din("x_p", [T, D])
    x_s = din("x_s", [ST, D])
    c_all = din("c_all", [17, D])
    cache_k = din("cache_k", [NPHYS * 8, 16 * 512])
    cache_v = din("cache_v", [NPHYS * 8, 16 * 512])
    pt_lay = din("pt_lay", [128, NS], I32)
    st_conv = din("st_conv", [NS * 30, 512])
    st_ffn = din("st_ffn", [NS * 2, 2 * DFF])
    ln_emb_g = din("ln_emb_g", [1, D]); ln_emb_b = din("ln_emb_b", [1, D])
    w_ada = din("w_ada", [D, 6 * D]); b_ada = din("b_ada", [1, 6 * D])
    w_in = din("w_in", [D, 2560])
    lam_in = din("lam_in", [1, 256])
    subln_g = din("subln_g", [128, 1])
    conv_w = din("conv_w", [128, 4, 31])
    conv_b = din("conv_b", [128, 4])
    conv_ln_g = din("conv_ln_g", [128, 4]); conv_ln_b = din("conv_ln_b", [128, 4])
    w_out = din("w_out", [D, D])
    ln1_g = din("ln1_g", [1, D]); ln1_b = din("ln1_b", [1, D])
    w_up = din("w_up", [D, 2 * DFF])
    ffn_cw = din("ffn_cw", [128, 44, 3]); ffn_cb = din("ffn_cb", [128, 44])
    w_down = din("w_down", [DFF, D])
    ln2_g = din("ln2_g", [1, D]); ln2_b = din("ln2_b", [1, D])
    c_ident = din("c_ident", [128, 128])
    c_maskT = din("c_maskT", [128, 128])
    c_auglp = din("c_auglp", [3, 4 * 128]); c_augrp = din("c_augrp", [3, 4096])
    c_augls = din("c_augls", [3, 128]); c_augrs = din("c_augrs", [3, 256])
    c_bnew = din("c_bnew", [64, 256]); c_mnew = din("c_mnew", [64, 256])
    c_selp = din("c_selp", [17, 128]); c_sels = din("c_sels", [17, 64])
    c_pm8 = din("c_pm8", [128, 1])

    y_p = dout("y_p", [T, D]); y_s = dout("y_s", [ST, D])
    k_p = dout("k_p", [T, 512]); v_p = dout("v_p", [T, 512])
    conv_p = dout("conv_p", [30, 512]); ffn_p = dout("ffn_p", [2, 2 * DFF])
    k_s = dout("k_s", [ST, 512]); v_s = dout("v_s", [ST, 512])
    conv_s = dout("conv_s", [NS * 30, 512]); ffn_s = dout("ffn_s", [NS * 2, 2 * DFF])
    x1_scr = nc.dram_tensor("x1_scr", [T + ST, D], F32, kind=("ExternalOutput" if DEBUG_X1 else "Internal")).ap()

    mod_scr = nc.dram_tensor("mod_scr", [17, 6 * D], F32, kind="Internal").ap()

    es_outer = ExitStack()
    with es_outer as es:
        S = Sched(nc, es)

        def sb(name, shape, dt=F32, stack=None):
            return (stack or es).enter_context(nc.sbuf_tensor(name, list(shape), dt))

        def ps(name, shape, dt=F32, stack=None):
            return (stack or es).enter_context(nc.psum_tensor(name, list(shape), dt))

        PS = [ps("psb%d" % i, [128, 512]) for i in range(8)]
        PSB = [p[:].bitcast(BF16) for p in PS]

        ident = sb("ident", [128, 128]); identb = sb("identb", [128, 128], BF16)
        onesb = sb("onesb", [128, 128], BF16); onesf = sb("onesf", [128, 128])
        maskT = sb("maskT", [128, 128]); maskTb = sb("maskTb", [128, 128], BF16)
        selp = sb("selp", [17, 128]); sels = sb("sels", [17, 64])
        neglam = sb("neglam", [128, 1]); sg8 = sb("sg8", [128, 1])
        sg8row = sb("sg8row", [128, 128])
        cw = sb("cw", [128, 4, 31]); cb = sb("cb", [128, 4]); clg = sb("clg", [128, 4]); clb = sb("clb", [128, 4])
        fcw = sb("fcw", [128, 44, 3]); fcb = sb("fcb", [128, 44])
        small = sb("small", [128, 64])
        epsc = sb("epsc", [128, 1])

        S.dma('sp', lambda e: e.dma_start(out=ident[:], in_=c_ident), writes=['ident'])
        S.dma('sp', lambda e: e.dma_start(out=maskT[:], in_=c_maskT), writes=['maskT'])
        S.dma('sp', lambda e: e.dma_start(out=selp[:], in_=c_selp), writes=['selp'])
        S.dma('sp', lambda e: e.dma_start(out=sels[:], in_=c_sels), writes=['sels'])
        S.dma('sp', lambda e: e.dma_start(out=sg8[:], in_=subln_g), writes=['sg8'])
        S.dma('sp', lambda e: e.dma_start(out=sg8row[:], in_=subln_g.rearrange("p o -> o p").partition_broadcast(128)), writes=['sg8row'])
        S.dma('sp', lambda e: e.dma_start(out=cw[:], in_=conv_w), writes=['cw'])
        S.dma('sp', lambda e: e.dma_start(out=cb[:], in_=conv_b), writes=['cb'])
        S.dma('sp', lambda e: e.dma_start(out=clg[:], in_=conv_ln_g), writes=['clg'])
        S.dma('sp', lambda e: e.dma_start(out=clb[:], in_=conv_ln_b), writes=['clb'])
        S.dma('sp', lambda e: e.dma_start(out=fcw[:], in_=ffn_cw), writes=['fcw'])
        S.dma('sp', lambda e: e.dma_start(out=fcb[:], in_=ffn_cb), writes=['fcb'])
        S.op('dve', lambda e: e.tensor_copy(out=identb[:], in_=ident[:]), reads=['ident'], writes=['identb'])
        S.op('dve', lambda e: e.tensor_copy(out=maskTb[:], in_=maskT[:]), reads=['maskT'], writes=['maskTb'])
        S.op('dve', lambda e: e.memset(onesb[:], 1.0), writes=['onesb'])
        S.op('dve', lambda e: e.memset(onesf[:], 1.0), writes=['onesf'])
        S.op('dve', lambda e: e.memset(epsc[:], EPS), writes=['epsc'])
        S.op('dve', lambda e: e.tensor_scalar(out=sg8[:], in0=sg8[:], scalar1=1.0 - LAMBDA_INIT, scalar2=None, op0=ALU.mult), reads=['sg8'], writes=['sg8'])
        S.op('dve', lambda e: e.tensor_scalar(out=sg8row[:], in0=sg8row[:], scalar1=1.0 - LAMBDA_INIT, scalar2=None, op0=ALU.mult), reads=['sg8row'], writes=['sg8row'])

        def rsqrt(out_ap, in_ap, reads, writes, scale=1.0):
            S.op('act', lambda e: e.activation(out=out_ap, in_=in_ap, func=AF.Ln, bias=epsc[:in_ap.shape[0], 0:1], scale=scale), reads=list(reads) + ['epsc'], writes=writes)
            S.op('act', lambda e: e.activation(out=out_ap, in_=out_ap, func=AF.Exp, scale=-0.5), reads=writes, writes=writes)

        def layer_norm_rows(P, src_ap, dst_ap, src_keys, dst_keys, col):
            st6 = small[:P, col:col + 12].rearrange("p (c s) -> p c s", s=6)
            mv = small[:P, col + 12:col + 14]
            rstd = small[:P, col + 14:col + 15]
            nmr = small[:P, col + 15:col + 16]
            k = ('small', col)
            for c in range(2):
                S.op('dve', lambda e, c=c: e.bn_stats(out=st6[:, c, :], in_=src_ap[:, c * 512:(c + 1) * 512]),
                     reads=src_keys, writes=[k], inc=(c == 1))
            S.op('dve', lambda e: e.bn_aggr(out=mv, in_=st6), reads=[k], writes=[k])
            rsqrt(rstd, mv[:, 1:2], [k], [k])
            S.op('dve', lambda e: e.scalar_tensor_tensor(out=nmr, in0=mv[:, 0:1], scalar=-1.0, in1=rstd, op0=ALU.mult, op1=ALU.mult),
                 reads=[k], writes=[k])
            S.op('act', lambda e: e.activation(out=dst_ap, in_=src_ap, func=AF.Identity, bias=nmr, scale=rstd),
                 reads=src_keys + [k], writes=dst_keys)

        def transpose_to(P, src_ap, src_keys, nchunk, dst_fn, dst_keys, banks, evac=('act', 'dve')):
            done = 0
            gi = 0
            while done < nchunk:
                n = min(4, nchunk - done)
                bank = banks[gi % len(banks)]
                pk = ('ps', bank)
                for j in range(n):
                    c = done + j
                    S.op('pe', lambda e, c=c, j=j: e.transpose(out=PS[bank][:, j * 128:j * 128 + P], in_=src_ap[:, c * 128:(c + 1) * 128], identity=ident[:P, :P]),
                         reads=src_keys + ['ident'], writes=[pk], inc=(j == n - 1))
                eng = evac[gi % len(evac)]
                src = PS[bank][:, 0:n * 128].rearrange("p (n t) -> p n t", t=128)[:, :, 0:P]
                dst = dst_fn(done, n)
                if eng == 'act':
                    S.op('act', lambda e, src=src, dst=dst: e.copy(out=dst, in_=src), reads=[pk], writes=dst_keys)
                else:
                    S.op('dve', lambda e, src=src, dst=dst: e.tensor_copy(out=dst, in_=src), reads=[pk], writes=dst_keys)
                done += n
                gi += 1

        def bcast_rows(dst, P, sel, modt, c0, add_one, key, bank):
            selk = 'selp' if sel is selp else 'sels'
            for hf in range(2):
                pk = ('ps', bank)
                S.op('pe', lambda e, hf=hf: e.matmul(PS[bank][:P, :], lhsT=sel[:, :P], rhs=modt[:, c0 + hf * 512:c0 + (hf + 1) * 512], start=True, stop=True),
                     reads=[selk, 'mod'], writes=[pk])
                if add_one:
                    S.op('dve', lambda e, hf=hf: e.tensor_scalar(out=dst[:P, hf * 512:(hf + 1) * 512], in0=PS[bank][:P, :], scalar1=1.0, scalar2=None, op0=ALU.add),
                         reads=[pk], writes=[key])
                else:
                    S.op('dve', lambda e, hf=hf: e.tensor_copy(out=dst[:P, hf * 512:(hf + 1) * 512], in_=PS[bank][:P, :]), reads=[pk], writes=[key])

        es_mix = ExitStack()
        with es_mix as em:
            with ExitStack() as e0:
                modA = sb("modA", [17, 3 * D], stack=e0)
                modB = sb("modB", [17, 3 * D], stack=e0)
                ct = sb("ct", [17, D], stack=e0)
                cT = sb("cT", [128, 8, 17], BF16, stack=e0)
                bada = sb("bada", [17, 6 * D], stack=e0)
                wada = [sb("wada%d" % i, [128, 8, 512], BF16, stack=e0) for i in range(3)]
                lamt = sb("lamt", [128, 256], stack=e0)
                lamp = sb("lamp", [128, 128], stack=e0)
                lams = sb("lams", [128, 4], stack=e0)
                S.dma('sp', lambda e: e.dma_start(out=ct[:], in_=c_all), writes=['ct'])
                S.dma('sp', lambda e: e.dma_start(out=bada[:], in_=b_ada.partition_broadcast(17)), writes=['bada'])
                S.dma('sp', lambda e: e.dma_start(out=lamt[:], in_=lam_in.partition_broadcast(128)), writes=['lamt'])
                S.op('dve', lambda e: e.tensor_tensor(out=lamp[:].rearrange("p (a d) -> p a d", d=64), in0=lamt[:].rearrange("p (a d) -> p a d", d=64)[:, 0::2, :],
                                                      in1=lamt[:].rearrange("p (a d) -> p a d", d=64)[:, 1::2, :], op=ALU.mult), reads=['lamt'], writes=['lamp'])
                S.op('dve', lambda e: e.tensor_reduce(out=lams[:, 0:2], in_=lamp[:].rearrange("p (a d) -> p a d", d=64), axis=AX.X, op=ALU.add), reads=['lamp'], writes=['lams'])
                S.op('act', lambda e: e.activation(out=lams[:, 2:4], in_=lams[:, 0:2], func=AF.Exp), reads=['lams'], writes=['lams2'])
                S.op('dve', lambda e: e.tensor_tensor(out=neglam[:], in0=lams[:, 3:4], in1=lams[:, 2:3], op=ALU.subtract), reads=['lams2'], writes=['neglam'])
                S.op('dve', lambda e: e.tensor_scalar(out=neglam[:], in0=neglam[:], scalar1=-LAMBDA_INIT, scalar2=None, op0=ALU.add), reads=['neglam'], writes=['neglam'])
                S.op('act', lambda e: e.activation(out=ct[:], in_=ct[:], func=AF.Silu), reads=['ct'], writes=['ct'])
                transpose_to(17, ct[:], ['ct'], 8, lambda c0, n: cT[:, c0:c0 + n, :], ['cT'], [0, 1])
                wv = w_ada.rearrange("(k p) n -> p k n", p=128)
                for n in range(12):
                    wb = wada[n % 3]
                    wk = ('wada', n % 3)
                    S.dma('pool', lambda e, n=n, wb=wb: e.dma_start(out=wb[:], in_=wv[:, :, n * 512:(n + 1) * 512]), writes=[wk])
                    bank = 2 + (n % 2)
                    pk = ('ps', bank)
                    for k in range(8):
                        S.op('pe', lambda e, k=k, wb=wb, bank=bank: e.matmul(PS[bank][:17, :], lhsT=cT[:, k, :], rhs=wb[:, k, :], start=(k == 0), stop=(k == 7)),
                             reads=['cT', wk], writes=[pk], inc=(k == 7))
                    mt = modA if n < 6 else modB
                    off = (n % 6) * 512
                    S.op('dve', lambda e, mt=mt, off=off, bank=bank, n=n: e.tensor_tensor(out=mt[:, off:off + 512], in0=PS[bank][:17, :], in1=bada[:, n * 512:(n + 1) * 512], op=ALU.add),
                         reads=[pk, 'bada'], writes=['mod'])
                S.dma('sp', lambda e: e.dma_start(out=mod_scr[:, 0:3 * D], in_=modA[:, :]), reads=['mod'], writes=['mod_scr'])
                S.dma('sp', lambda e: e.dma_start(out=mod_scr[:, 3 * D:6 * D], in_=modB[:, :]), reads=['mod'], writes=['mod_scr'])
            S.barrier()

            W_in = sb("W_in", [128, 8, 2560], BF16, stack=em)
            W_out = sb("W_out", [128, 8, D], BF16, stack=em)
            gE = sb("gE", [128, D], stack=em); bE = sb("bE", [128, D], stack=em)
            g1 = sb("g1", [128, D], stack=em); b1 = sb("b1", [128, D], stack=em)
            SC = sb("SC", [128, D], stack=em); SH = sb("SH", [128, D], stack=em); GT = sb("GT", [128, D], stack=em)
            xin = sb("xin", [128, D], stack=em); xpb = [sb("xp%d" % i, [128, D], stack=em) for i in range(2)]; htmp = sb("htmp", [128, D], stack=em)
            hT = sb("hT", [128, 8, 128], BF16, stack=em)
            QTb = [sb("QT%d" % i, [128, 4, 128], BF16, stack=em) for i in range(2)]
            QT = QTb[0]
            kvst = [sb("kvst%d" % i, [128, 512], stack=em) for i in range(2)]
            sig = sb("sig", [128, 128], stack=em)
            mixT = sb("mixT", [128, 8, 128], BF16, stack=em)
            cvt = sb("cvt", [128, 512], stack=em); cvn = sb("cvn", [128, 512], stack=em)

            w_in_v = w_in.rearrange("(k p) n -> p k n", p=128)
            for i in range(4):
                S.dma('pool', lambda e, i=i: e.dma_start(out=W_in[:, 2 * i:2 * i + 2, :], in_=w_in_v[:, 2 * i:2 * i + 2, :]), writes=['W_in'])
            S.dma('pool', lambda e: e.dma_start(out=W_out[:], in_=w_out.rearrange("(k p) n -> p k n", p=128)), writes=['W_out'])
            for (tl, src, key) in [(gE, ln_emb_g, 'gE'), (bE, ln_emb_b, 'bE'), (g1, ln1_g, 'g1'), (b1, ln1_b, 'b1')]:
                S.dma('sp', lambda e, tl=tl, src=src: e.dma_start(out=tl[:], in_=src.partition_broadcast(128)), writes=[key])

            def set_mod_tiles(P, sel, modt, scaleoff, shiftoff, gateoff):
                bcast_rows(SH, P, sel, modt, shiftoff, False, 'SH', 4)
                bcast_rows(SC, P, sel, modt, scaleoff, True, 'SC', 5)
                bcast_rows(GT, P, sel, modt, gateoff, False, 'GT', 4)
                S.op('dve', lambda e: e.tensor_tensor(out=htmp[:P, :], in0=bE[:P, :], in1=SC[:P, :], op=ALU.mult), reads=['bE', 'SC'], writes=['htmp'])
                S.op('dve', lambda e: e.tensor_tensor(out=SH[:P, :], in0=SH[:P, :], in1=htmp[:P, :], op=ALU.add), reads=['SH', 'htmp'], writes=['SH'])
                S.op('dve', lambda e: e.tensor_tensor(out=SC[:P, :], in0=SC[:P, :], in1=gE[:P, :], op=ALU.mult), reads=['SC', 'gE'], writes=['SC'])

            def front(P, x_src, kout, vout, kT_dst, vbf_dst, u_dst, par=0):
                xp = xpb[par]
                xk = 'xp%d' % par
                QT = QTb[par]
                qk = 'QT%d' % par
                S.dma('sp', lambda e: e.dma_start(out=xin[:P, :], in_=x_src), writes=['xin'])
                layer_norm_rows(P, xin[:P, :], xp[:P, :], ['xin'], [xk], 0)
                S.op('dve', lambda e: e.tensor_tensor(out=htmp[:P, :], in0=xp[:P, :], in1=SC[:P, :], op=ALU.mult), reads=[xk, 'SC'], writes=['htmp'])
                S.op('dve', lambda e: e.tensor_tensor(out=htmp[:P, :], in0=htmp[:P, :], in1=SH[:P, :], op=ALU.add), reads=['htmp', 'SH'], writes=['htmp'])
                S.op('pool', lambda e: e.tensor_tensor(out=xp[:P, :], in0=xp[:P, :], in1=gE[:P, :], op=ALU.mult), reads=[xk, 'gE'], writes=[xk])
                S.op('pool', lambda e: e.tensor_tensor(out=xp[:P, :], in0=xp[:P, :], in1=bE[:P, :], op=ALU.add), reads=[xk, 'bE'], writes=[xk])
                yield
                transpose_to(P, htmp[:P, :], ['htmp'], 8, lambda c0, n: hT[:, c0:c0 + n, 0:P], ['hT'], [0, 1])
                yield
                for wi, (c0, dst) in enumerate([(512, kout), (1024, vout)]):
                    bank = 2 if wi == 0 else 0
                    pk = ('ps', bank)
                    for k in range(8):
                        S.op('pe', lambda e, k=k, c0=c0, bank=bank: e.matmul(PS[bank][:P, :], lhsT=hT[:, k, 0:P], rhs=W_in[:, k, c0:c0 + 512], start=(k == 0), stop=(k == 7)),
                             reads=['hT', 'W_in'], writes=[pk], inc=(k == 7))
                    st = kvst[wi]
                    sk = ('kvst', wi)
                    S.op('act', lambda e, st=st, bank=bank: e.copy(out=st[:P, :], in_=PS[bank][:P, :]), reads=[pk], writes=[sk])
                    S.dma('sp', lambda e, st=st, dst=dst: e.dma_start(out=dst, in_=st[:P, :]), reads=[sk], writes=[])
                    if wi == 1:
                        vbf_dst(st, sk)
                    yield
                for pr in range(4):
                    for wi, c0 in enumerate([0, 512]):
                        bank = 1 + ((2 * pr + wi) % 2)
                        pk = ('ps', bank)
                        for k in range(8):
                            S.op('pe', lambda e, k=k, c0=c0, pr=pr, bank=bank: e.matmul(PS[bank][:, 0:P], lhsT=W_in[:, k, c0 + 128 * pr:c0 + 128 * pr + 128], rhs=hT[:, k, 0:P], start=(k == 0), stop=(k == 7)),
                                 reads=['hT', 'W_in'], writes=[pk], inc=(k == 7))
                        if wi == 0:
                            S.op('act', lambda e, pr=pr, bank=bank: e.copy(out=QT[:, pr, 0:P], in_=PS[bank][:, 0:P]), reads=[pk], writes=[qk])
                        else:
                            kT_dst(pr, bank, pk)
                    yield
                for c in range(4):
                    pa = ('ps', 0)
                    pg = ('ps', 1)
                    for k in range(8):
                        S.op('pe', lambda e, k=k, c=c: e.matmul(PS[0][:, 0:P], lhsT=W_in[:, k, 1536 + 128 * c:1536 + 128 * c + 128], rhs=hT[:, k, 0:P], start=(k == 0), stop=(k == 7)),
                             reads=['hT', 'W_in'], writes=[pa], inc=(k == 7))
                    for k in range(8):
                        S.op('pe', lambda e, k=k, c=c: e.matmul(PS[1][:, 0:P], lhsT=W_in[:, k, 2048 + 128 * c:2048 + 128 * c + 128], rhs=hT[:, k, 0:P], start=(k == 0), stop=(k == 7)),
                             reads=['hT', 'W_in'], writes=[pg], inc=(k == 7))
                    S.op('act', lambda e: e.activation(out=sig[:, 0:P], in_=PS[1][:, 0:P], func=AF.Sigmoid), reads=[pg], writes=['sig'])
                    u_dst(c, pa)
                    yield

            def attn_epilogue_tok(P, obank0, col0):
                for hb in range(3):
                    nh = 3 if hb < 2 else 2
                    pkb = ('ps', obank0 + hb)
                    ov = PS[obank0 + hb][:P, 0:480].rearrange("p (h w) -> p h w", w=160)[:, 0:nh, :]
                    S.op('dve', lambda e, hb=hb, ov=ov, nh=nh: e.reciprocal(out=rs8[:P, 3 * hb:3 * hb + nh], in_=ov[:, :, 128:129].rearrange("p h o -> p (h o)")), reads=[pkb], writes=['rs8'])
                    S.op('dve', lambda e, hb=hb, ov=ov, nh=nh: e.tensor_tensor(out=osb[:P, 3 * hb:3 * hb + nh, :], in0=ov[:, :, 0:128],
                                                                        in1=rs8[:P, 3 * hb:3 * hb + nh].unsqueeze(2).to_broadcast([P, nh, 128]), op=ALU.mult), reads=[pkb, 'rs8'], writes=['osb'])
                S.op('dve', lambda e: e.scalar_tensor_tensor(out=o4[:P], in0=osb[:P, 1::2, :], scalar=neglam[:P, 0:1], in1=osb[:P, 0::2, :], op0=ALU.mult, op1=ALU.add),
                     reads=['osb', 'neglam'], writes=['o4'])
                S.op('pool', lambda e: e.tensor_tensor(out=osq[:P], in0=o4[:P], in1=o4[:P], op=ALU.mult), reads=['o4'], writes=['osq'])
                S.op('dve', lambda e: e.tensor_reduce(out=rs8[:P, 8:12], in_=osq[:P], axis=AX.X, op=ALU.add), reads=['osq'], writes=['rs8b'])
                rsqrt(rs8[:P, 12:16], rs8[:P, 8:12], ['rs8b'], ['rs8c'], scale=1.0 / 128)
                S.op('dve', lambda e: e.tensor_tensor(out=o4[:P], in0=o4[:P], in1=rs8[:P, 12:16].unsqueeze(2).to_broadcast([P, 4, 128]), op=ALU.mult), reads=['o4', 'rs8c'], writes=['o4'])
                S.op('dve', lambda e: e.tensor_tensor(out=o4[:P], in0=o4[:P], in1=sg8row[:P, :].unsqueeze(1).to_broadcast([P, 4, 128]), op=ALU.mult), reads=['o4', 'sg8row'], writes=['o4'])
                transpose_to(P, o4[:P].rearrange("p a e -> p (a e)"), ['o4'], 4, lambda c0, n: mixT[:, c0:c0 + n, col0:col0 + P], ['mixT'], [3], evac=('act',))

            def conv_ln_silu(P, cv_fn, cv_keys, bA=1, bB=0):
                pk = ('ps', bA)
                for c in range(4):
                    S.op('pe', lambda e, c=c: e.transpose(out=PS[bA][:P, c * 128:(c + 1) * 128], in_=cv_fn(c), identity=ident[:]),
                         reads=cv_keys + ['ident'], writes=[pk], inc=(c == 3))
                S.op('act', lambda e: e.copy(out=cvt[:P, :], in_=PS[bA][:P, :]), reads=[pk], writes=['cvt'])
                st6 = small[:P, 16:22]
                mv = small[:P, 22:24]
                rstd = small[:P, 24:25]
                nmr = small[:P, 25:26]
                k = ('small', 16)
                S.op('dve', lambda e: e.bn_stats(out=st6, in_=cvt[:P, :]), reads=['cvt'], writes=[k])
                S.op('dve', lambda e: e.bn_aggr(out=mv, in_=st6), reads=[k], writes=[k])
                rsqrt(rstd, mv[:, 1:2], [k], [k])
                S.op('dve', lambda e: e.scalar_tensor_tensor(out=nmr, in0=mv[:, 0:1], scalar=-1.0, in1=rstd, op0=ALU.mult, op1=ALU.mult), reads=[k], writes=[k])
                S.op('act', lambda e: e.activation(out=cvn[:P, :], in_=cvt[:P, :], func=AF.Identity, bias=nmr, scale=rstd), reads=['cvt', k], writes=['cvn'])
                pk0 = ('ps', bB)
                for c in range(4):
                    S.op('pe', lambda e, c=c: e.transpose(out=PS[bB][:, c * 128:c * 128 + P], in_=cvn[:P, c * 128:(c + 1) * 128], identity=ident[:P, :P]),
                         reads=['cvn', 'ident'], writes=[pk0], inc=(c == 3))
                for c in range(4):
                    S.op('act', lambda e, c=c: e.activation(out=mixT[:, 4 + c, 0:P], in_=PS[bB][:, c * 128:c * 128 + P], func=AF.Silu, bias=clb[:, c:c + 1], scale=clg[:, c:c + 1]),
                         reads=[pk0, 'clg', 'clb'], writes=['mixT'])

            def out_ln1(P, row0, par=0, rbuf=None, x1o=None, b0=2):
                xp = xpb[par]
                xk = 'xp%d' % par
                rk, ok = ('rbuf', 'x1o') if rbuf is not None else ('htmp', 'xin')
                if rbuf is None:
                    rbuf, x1o = htmp, xin
                for hf in range(2):
                    bank = b0 + hf
                    pk = ('ps', bank)
                    for k in range(8):
                        S.op('pe', lambda e, k=k, hf=hf, bank=bank: e.matmul(PS[bank][:P, :], lhsT=mixT[:, k, 0:P], rhs=W_out[:, k, hf * 512:(hf + 1) * 512], start=(k == 0), stop=(k == 7)),
                             reads=['mixT', 'W_out'], writes=[pk], inc=(k == 7))
                    S.op('dve', lambda e, hf=hf, bank=bank: e.tensor_tensor(out=rbuf[:P, hf * 512:(hf + 1) * 512], in0=PS[bank][:P, :], in1=GT[:P, hf * 512:(hf + 1) * 512], op=ALU.mult),
                         reads=[pk, 'GT'], writes=[rk])
                S.op('dve', lambda e: e.scalar_tensor_tensor(out=rbuf[:P, :], in0=xp[:P, :], scalar=ALPHA, in1=rbuf[:P, :], op0=ALU.mult, op1=ALU.add), reads=[xk, rk], writes=[rk])
                layer_norm_rows(P, rbuf[:P, :], x1o[:P, :], [rk], [ok], 32)
                S.op('dve', lambda e: e.tensor_tensor(out=x1o[:P, :], in0=x1o[:P, :], in1=g1[:P, :], op=ALU.mult), reads=[ok, 'g1'], writes=[ok])
                S.op('dve', lambda e: e.tensor_tensor(out=x1o[:P, :], in0=x1o[:P, :], in1=b1[:P, :], op=ALU.add), reads=[ok, 'b1'], writes=[ok])
                S.dma('sp', lambda e: e.dma_start(out=x1_scr[row0:row0 + P, :], in_=x1o[:P, :]), reads=[ok], writes=['x1scr'])

            with ExitStack() as e1:
                modS = sb("modS", [17, 3 * D], stack=e1)
                S.dma('sp', lambda e: e.dma_start(out=modS[:, :], in_=mod_scr[:, 0:3 * D]), writes=['mod'])
                kall = sb("kall", [128, 16, 512], BF16, stack=e1)
                vall = sb("vall", [128, 16, 512], BF16, stack=e1)
                ktp = [sb("ktp%d" % i, [128, 4, 128], BF16, stack=e1) for i in range(3)]
                PTs = sb("PTs", [128, 512], BF16, stack=e1)
                Us = sb("Us", [128, 4, NS, 34], stack=e1)
                KTs = sb("KTs", [128, 4, 64], BF16, stack=e1)
                Vsb = sb("Vsb", [64, 512], BF16, stack=e1)
                auglsb = sb("auglsb", [128, 128], BF16, stack=e1); augrsb = sb("augrsb", [128, 256], BF16, stack=e1)
                ucont = kall[:].rearrange("p a f -> p (a f)").bitcast(F32)[:, 0:4 * NS * 30].rearrange("p (c b t) -> p c b t", c=4, t=30)
                bnew = sb("bnew", [64, 256], stack=e1); mnew = sb("mnew", [64, 256], stack=e1)
                pnewf = sb("pnewf", [64, 512], stack=e1); pnewT = sb("pnewT", [64, 512], BF16, stack=e1)
                ptl = sb("ptl", [128, NS], I32, stack=e1); ptf = sb("ptf", [128, NS], stack=e1)
                pm8 = sb("pm8", [128, 1], stack=e1); idx = sb("idx", [128, NS], I32, stack=e1)
                stc = sb("stc", [120, 512], stack=e1)
                osall = sb("osall", [128, 4, 64], stack=e1); ossq = sb("ossq", [128, 4, 64], stack=e1)
                rsum = sb("rsum", [128, 32], stack=e1); on32 = sb("on32", [128, 32], stack=e1)
                rstd_s = sb("rstd_s", [128, 256], stack=e1)
                cvs = sb("cvs", [128, 4, 64], stack=e1)
                acc = [sb("acc%d" % i, [128, 64], stack=e1) for i in range(2)]
                osq_s = sb("osq_s", [128, 128], stack=e1)

                for r0 in (0, 64):
                    S.dma('pool', lambda e, r0=r0: e.dma_start(out=auglsb[r0:r0 + 3, :], in_=c_augls), writes=['auglsb'])
                    S.dma('pool', lambda e, r0=r0: e.dma_start(out=augrsb[r0:r0 + 3, :], in_=c_augrs), writes=['augrsb'])
                S.dma('sp', lambda e: e.dma_start(out=bnew[:], in_=c_bnew), writes=['bnew'])
                S.dma('sp', lambda e: e.dma_start(out=mnew[:], in_=c_mnew), writes=['mnew'])
                S.dma('sp', lambda e: e.dma_start(out=ptl[:], in_=pt_lay), writes=['ptl'])
                S.dma('sp', lambda e: e.dma_start(out=pm8[:], in_=c_pm8), writes=['pm8'])
                S.op('dve', lambda e: e.tensor_copy(out=ptf[:], in_=ptl[:]), reads=['ptl'], writes=['ptf'])
                S.op('dve', lambda e: e.tensor_scalar(out=ptf[:], in0=ptf[:], scalar1=8.0, scalar2=pm8[:, 0:1], op0=ALU.mult, op1=ALU.add), reads=['ptf', 'pm8'], writes=['ptf'])
                S.op('dve', lambda e: e.tensor_copy(out=idx[:], in_=ptf[:]), reads=['ptf'], writes=['idx'])

                def gather(b, which):
                    if which == 0:
                        S.dma('pool', lambda e: e.indirect_dma_start(out=kall[:].rearrange("p a f -> p (a f)"), out_offset=None, in_=cache_k,
                                                                    in_offset=bass.IndirectOffsetOnAxis(ap=idx[:, b:b + 1], axis=0)), reads=['idx'], writes=['kall'])
                    else:
                        S.dma('pool', lambda e: e.indirect_dma_start(out=vall[:].rearrange("p a f -> p (a f)"), out_offset=None, in_=cache_v,
                                                                    in_offset=bass.IndirectOffsetOnAxis(ap=idx[:, b:b + 1], axis=0)), reads=['idx'], writes=['vall'])
                if STOP_AFTER == 0.1:
                    S.barrier(); S.finish()
                    return nc
                gather(0, 0)
                gather(0, 1)
                if STOP_AFTER == 0.2:
                    S.barrier(); S.finish()
                    return nc

                for g in range(4):
                    S.dma('sp', lambda e, g=g: e.dma_start(out=stc[:, :], in_=st_conv[g * 120:(g + 1) * 120, :]), writes=['stc'])
                    pk = ('ps', 0)
                    for c in range(4):
                        S.op('pe', lambda e, c=c: e.transpose(out=PS[0][:, c * 128:c * 128 + 120], in_=stc[:, c * 128:(c + 1) * 128], identity=ident[:120, :120]),
                             reads=['stc', 'ident'], writes=[pk], inc=(c == 3))
                    for c in range(4):
                        S.op('dve', lambda e, c=c, g=g: e.tensor_copy(out=Us[:, c, 4 * g:4 * g + 4, 0:30], in_=PS[0][:, c * 128:c * 128 + 120].rearrange("p (b t) -> p b t", t=30)),
                             reads=[pk], writes=['Us'])

                if STOP_AFTER == 0.3:
                    S.barrier(); S.finish()
                    return nc
                set_mod_tiles(64, sels, modS, 1024, 0, 2048)
                if STOP_AFTER == 0.4:
                    S.barrier(); S.finish()
                    return nc

                def s_vbf(st, sk):
                    S.op('dve', lambda e: e.tensor_copy(out=Vsb[:, :], in_=st[:64, :]), reads=[sk], writes=['Vsb'])

                def s_kT(pr, bank, pk):
                    S.op('dve', lambda e: e.tensor_copy(out=KTs[:, pr, :], in_=PS[bank][:, 0:64]), reads=[pk], writes=['KTs'])

                def s_u(c, pa):
                    S.op('dve', lambda e: e.tensor_tensor(out=Us[:, c, :, 30:34], in0=PS[0][:, 0:64].rearrange("p (b t) -> p b t", t=4),
                                                          in1=sig[:, 0:64].rearrange("p (b t) -> p b t", t=4), op=ALU.mult), reads=[pa, 'sig'], writes=['Us'])

                for _ in front(64, x_s, k_s, v_s, s_kT, s_vbf, s_u):
                    pass

                if STOP_AFTER == 0.5:
                    S.barrier(); S.finish()
                    return nc
                for hh in range(2):
                    bank = 4 if hh == 0 else 2
                    pk = ('ps', bank)
                    r0 = 64 * hh
                    for pr in range(4):
                        S.op('pe', lambda e, pr=pr, r0=r0, bank=bank: e.matmul(PS[bank][:64, pr * 64:(pr + 1) * 64], lhsT=KTs[r0:r0 + 64, pr, :], rhs=QT[r0:r0 + 64, pr, 0:64], start=True, stop=True),
                             reads=['KTs', 'QT0'], writes=[pk], inc=(pr == 3))
                    S.op('dve', lambda e, hh=hh, bank=bank: e.scalar_tensor_tensor(out=pnewf[:, hh * 256:(hh + 1) * 256], in0=PS[bank][:64, 0:256], scalar=0.125, in1=bnew[:, :], op0=ALU.mult, op1=ALU.add),
                         reads=[pk, 'bnew'], writes=['pnewf'])
                S.op('act', lambda e: e.activation(out=pnewf[:], in_=pnewf[:], func=AF.Exp), reads=['pnewf'], writes=['pnewf'])
                pnv = pnewT[:].rearrange("p (b hh pr t) -> p b hh pr t", hh=2, pr=4, t=4)
                for hh in range(2):
                    S.op('dve', lambda e, hh=hh: e.tensor_tensor(out=pnv[:, :, hh, :, :].rearrange("p b pr t -> p pr b t"), in0=pnewf[:, hh * 256:(hh + 1) * 256].rearrange("p (pr b t) -> p pr b t", pr=4, t=4),
                                                                in1=mnew[:, :].rearrange("p (pr b t) -> p pr b t", pr=4, t=4), op=ALU.mult), reads=['pnewf', 'mnew'], writes=['pnewT'])
                if STOP_AFTER == 0.6:
                    S.barrier(); S.finish()
                    return nc

                SB = [5, 3]
                for b in range(NS):
                    for t16 in range(16):
                        tb = 6 + (t16 % 2)
                        tpk = ('ps', tb)
                        for pr in range(4):
                            S.op('pe', lambda e, pr=pr, t16=t16, tb=tb: e.transpose(out=PSB[tb][:, pr * 128:(pr + 1) * 128], in_=kall[:, t16, pr * 128:(pr + 1) * 128], identity=identb[:]),
                                 reads=['kall', 'identb'], writes=[tpk], inc=(pr == 3))
                        kt = ktp[t16 % 3]
                        kk = ('ktp', t16 % 3)
                        if t16 % 2 == 0:
                            S.op('act', lambda e, kt=kt, tb=tb: e.copy(out=kt[:].rearrange("p a k -> p (a k)"), in_=PSB[tb][:, 0:512]), reads=[tpk], writes=[kk])
                        else:
                            S.op('dve', lambda e, kt=kt, tb=tb: e.tensor_copy(out=kt[:].rearrange("p a k -> p (a k)"), in_=PSB[tb][:, 0:512]), reads=[tpk], writes=[kk])
                        for hh in range(2):
                            r0 = 64 * hh
                            for pr in range(4):
                                S.op('pe', lambda e, pr=pr, hh=hh, r0=r0, kt=kt, t16=t16, b=b: e.matmul(PS[SB[hh]][:, t16 * 16 + pr * 4:t16 * 16 + pr * 4 + 4], lhsT=kt[r0:r0 + 64, pr, :],
                                                                                               rhs=QT[r0:r0 + 64, pr, 4 * b:4 * b + 4], start=(t16 == 0 and pr == 0), stop=False, skip_group_check=True),
                                     reads=[kk, 'QT0'], writes=[('ps', SB[hh])], inc=False)
                    if b + 1 < NS:
                        gather(b + 1, 0)
                    for hh in range(2):
                        r0 = 64 * hh
                        S.op('pe', lambda e, hh=hh, r0=r0: e.matmul(PS[SB[hh]][:, 0:256], lhsT=auglsb[r0:r0 + 3, :], rhs=augrsb[r0:r0 + 3, :], start=False, stop=True, skip_group_check=True),
                             reads=['auglsb', 'augrsb'], writes=[('ps', SB[hh])])
                    for hh in range(2):
                        S.op('act', lambda e, hh=hh: e.activation(out=PTs[:, hh * 256:(hh + 1) * 256], in_=PS[SB[hh]][:, 0:256], func=AF.Exp, scale=0.125), reads=[('ps', SB[hh])], writes=['PTs'])
                    pk4 = ('ps', 4)
                    for hh in range(2):
                        for t16 in range(16):
                            S.op('pe', lambda e, t16=t16, hh=hh: e.matmul(PS[4][:, hh * 16:(hh + 1) * 16], lhsT=onesb[:, :], rhs=PTs[:, hh * 256 + t16 * 16:hh * 256 + (t16 + 1) * 16], start=(t16 == 0), stop=False),
                                 reads=['PTs', 'onesb'], writes=[pk4], inc=False)
                        S.op('pe', lambda e, b=b, hh=hh: e.matmul(PS[4][:, hh * 16:(hh + 1) * 16], lhsT=onesb[0:64, :], rhs=pnv[:, b, hh, :, :], start=False, stop=True),
                             reads=['pnewT', 'onesb'], writes=[pk4], inc=False)
                    for dh in range(4):
                        for hh in range(2):
                            oc = 64 + dh * 8 + hh * 4
                            for t16 in range(16):
                                S.op('pe', lambda e, dh=dh, t16=t16, hh=hh, oc=oc: e.matmul(PS[4][:, oc:oc + 4], lhsT=vall[:, t16, dh * 128:(dh + 1) * 128],
                                                                                         rhs=PTs[:, hh * 256 + t16 * 16 + dh * 4:hh * 256 + t16 * 16 + dh * 4 + 4], start=(t16 == 0), stop=False),
                                     reads=['PTs', 'vall'], writes=[pk4], inc=False)
                            S.op('pe', lambda e, dh=dh, b=b, hh=hh, oc=oc: e.matmul(PS[4][:, oc:oc + 4], lhsT=Vsb[0:64, dh * 128:(dh + 1) * 128], rhs=pnv[:, b, hh, dh, :], start=False, stop=True),
                                 reads=['pnewT', 'Vsb'], writes=[pk4], inc=(dh == 3 and hh == 1))
                    if b + 1 < NS:
                        gather(b + 1, 1)
                    rsv = rsum[:].rearrange("p (d h q) -> p d h q", h=2, q=4)
                    for hh in range(2):
                        S.op('dve', lambda e, hh=hh: e.reciprocal(out=rsv[:, :, hh, :], in_=PS[4][:, hh * 16:(hh + 1) * 16].rearrange("p (d q) -> p d q", q=4)), reads=[pk4], writes=['rsum'])
                    S.op('dve', lambda e: e.tensor_tensor(out=on32[:], in0=PS[4][:, 64:96], in1=rsum[:], op=ALU.mult), reads=[pk4, 'rsum'], writes=['on32'])
                    onv = on32[:].rearrange("p (d h q) -> p d h q", h=2, q=4)
                    S.op('dve', lambda e, b=b: e.scalar_tensor_tensor(out=osall[:, :, 4 * b:4 * b + 4], in0=onv[:, :, 1, :], scalar=neglam[:, 0:1], in1=onv[:, :, 0, :], op0=ALU.mult, op1=ALU.add),
                         reads=['on32', 'neglam'], writes=['osall'])

                if STOP_AFTER == 0.7:
                    S.barrier(); S.finish()
                    return nc
                S.op('dve', lambda e: e.tensor_tensor(out=ossq[:], in0=osall[:], in1=osall[:], op=ALU.mult), reads=['osall'], writes=['ossq'])
                pk = ('ps', 6)
                S.op('pe', lambda e: e.matmul(PS[6][:, 0:256], lhsT=onesf[:, :], rhs=ossq[:].rearrange("p a t -> p (a t)"), start=True, stop=True), reads=['ossq', 'onesf'], writes=[pk])
                rsqrt(rstd_s[:], PS[6][:, 0:256], [pk], ['rstd_s'], scale=1.0 / 128)
                S.op('dve', lambda e: e.scalar_tensor_tensor(out=mixT[:, 0:4, 0:64], in0=osall[:], scalar=sg8[:, 0:1], in1=rstd_s[:].rearrange("p (a t) -> p a t", t=64), op0=ALU.mult, op1=ALU.mult),
                     reads=['osall', 'sg8', 'rstd_s'], writes=['mixT'])

                if STOP_AFTER == 0.8:
                    S.barrier(); S.finish()
                    return nc
                acc4 = [acc[0][:, 0:64], acc[1][:, 0:64], osq_s[:, 0:64], osq_s[:, 64:128]]
                accv = [a.rearrange("p (b t) -> p b t", t=4) for a in acc4]
                for c in range(4):
                    S.op('dve', lambda e, c=c: e.tensor_scalar(out=accv[c], in0=Us[:, c, :, 0:4], scalar1=cw[:, c, 0:1], scalar2=cb[:, c:c + 1], op0=ALU.mult, op1=ALU.add),
                         reads=['Us', 'cw', 'cb'], writes=[('acc', c)])
                for j in range(1, 31):
                    for c in range(4):
                        dst = accv[c] if j < 30 else cvs[:, c, :].rearrange("p (b t) -> p b t", t=4)
                        S.op('dve', lambda e, c=c, j=j, dst=dst: e.scalar_tensor_tensor(out=dst, in0=Us[:, c, :, j:j + 4], scalar=cw[:, c, j:j + 1], in1=accv[c], op0=ALU.mult, op1=ALU.add),
                             reads=['Us', ('acc', c)], writes=[('acc', c)] if j < 30 else ['cvs'])
                conv_ln_silu(64, lambda c: cvs[:, c, :], ['cvs'])
                for c in range(4):
                    S.op('act', lambda e, c=c: e.copy(out=ucont[:, c, :, :], in_=Us[:, c, :, 4:34]), reads=['Us'], writes=['kall'])
                for g in range(4):
                    pk = ('ps', 1)
                    for c in range(4):
                        S.op('pe', lambda e, c=c, g=g: e.transpose(out=PS[1][:120, c * 128:(c + 1) * 128], in_=ucont[:, c, 4 * g:4 * g + 4, :], identity=ident[:]),
                             reads=['kall', 'ident'], writes=[pk], inc=(c == 3))
                    S.op('act', lambda e: e.copy(out=stc[:, :], in_=PS[1][:120, :]), reads=[pk], writes=['stc'])
                    S.dma('sp', lambda e, g=g: e.dma_start(out=conv_s[g * 120:(g + 1) * 120, :], in_=stc[:, :]), reads=['stc'], writes=[])
                if STOP_AFTER == 0.9:
                    S.barrier(); S.finish()
                    return nc
                out_ln1(64, T)
            S.barrier()
            if STOP_AFTER == 1:
                S.op('dve', lambda e: e.tensor_copy(out=htmp[:, :], in_=mixT[:].rearrange("p a t -> p (a t)")), writes=['htmp'])
                S.dma('sp', lambda e: e.dma_start(out=y_p[0:128, :], in_=htmp[:, :]), reads=['htmp'], writes=[])
                S.finish()
                return nc

            with ExitStack() as e2:
                with ExitStack() as et:
                    modP = sb("modP", [17, 3 * D], stack=et)
                    S.dma('sp', lambda e: e.dma_start(out=modP[:, :], in_=mod_scr[:, 0:3 * D]), writes=['mod'])
                    set_mod_tiles(128, selp, modP, 1024, 0, 2048)
                    S.barrier()
                KT = sb("KT", [128, 4, T], BF16, stack=e2)
                Vext = sb("Vext", [128, NBLK, 4, 130], BF16, stack=e2)
                auglpb = sb("auglpb", [128, 512], BF16, stack=e2); augrpb = sb("augrpb", [128, 4096], BF16, stack=e2)
                osb = sb("osb", [128, 8, 128], stack=e2); o4 = sb("o4", [128, 4, 128], stack=e2); osq = sb("osq", [128, 4, 128], stack=e2)
                rs8 = sb("rs8", [128, 16], stack=e2)
                Ub = [sb("U%d" % i, [128, 4, 30 + 128], stack=e2) for i in range(3)]
                PT = [sb("PT%d" % i, [128, 512], BF16, stack=e2) for i in range(3)]
                cva = [sb("cva%d" % i, [128, 128], stack=e2) for i in range(4)]
                cpo = sb("cpo", [30, 512], stack=e2)
                rbuf_p = sb("rbuf", [128, D], stack=e2); x1o_p = sb("x1o", [128, D], stack=e2)

                for r0 in (0, 64):
                    S.dma('pool', lambda e, r0=r0: e.dma_start(out=auglpb[r0:r0 + 3, :], in_=c_auglp), writes=['auglpb'])
                    S.dma('pool', lambda e, r0=r0: e.dma_start(out=augrpb[r0:r0 + 3, :], in_=c_augrp), writes=['augrpb'])
                S.op('dve', lambda e: e.memset(Ub[0][:], 0.0), writes=['U'])
                S.op('dve', lambda e: e.memset(Ub[1][:], 0.0), writes=['U'])
                S.op('dve', lambda e: e.memset(Ub[2][:], 0.0), writes=['U'])
                S.op('pool', lambda e: e.memset(Vext[:].rearrange("p a b c -> p (a b c)"), 1.0), writes=['Vext'])

                ptcnt = [0]

                def p_front(i):
                    par = i % 2
                    up_i = i % 3
                    un_i = (i + 1) % 3
                    Up = Ub[up_i]
                    Un = Ub[un_i]

                    def p_vbf(st, sk):
                        S.op('pool', lambda e: e.tensor_copy(out=Vext[:, i, :, 0:128], in_=st[:, :].rearrange("p (a e) -> p a e", e=128)), reads=[sk], writes=['Vext'])

                    def p_kT(pr, bank, pk):
                        S.op('dve', lambda e: e.tensor_copy(out=KT[:, pr, i * 128:(i + 1) * 128], in_=PS[bank][:, 0:128]), reads=[pk], writes=['KT'])

                    def p_u(c, pa):
                        S.op('dve', lambda e: e.tensor_tensor(out=Up[:, c, 30:158], in0=PS[0][:, 0:128], in1=sig[:, 0:128], op=ALU.mult), reads=[pa, 'sig'], writes=[('U', up_i, c)])
                        S.op('act', lambda e: e.copy(out=Un[:, c, 0:30], in_=Up[:, c, 128:158]), reads=[('U', up_i, c)], writes=[('U', un_i, c)])

                    yield from front(128, x_p[i * 128:(i + 1) * 128, :], k_p[i * 128:(i + 1) * 128, :], v_p[i * 128:(i + 1) * 128, :], p_kT, p_vbf, p_u, par=par)

                def p_back(i):
                    par = i % 2
                    ui = i % 3
                    U = Ub[ui]
                    QT = QTb[par]
                    qk = 'QT%d' % par
                    for c in range(4):
                        S.op('dve', lambda e, c=c: e.tensor_scalar(out=cva[c][:, :], in0=U[:, c, 0:128], scalar1=cw[:, c, 0:1], scalar2=cb[:, c:c + 1], op0=ALU.mult, op1=ALU.add),
                             reads=[('U', ui, c), 'U', 'cw', 'cb'], writes=[('cva', c)])
                    taps_left = list(range(1, 31))

                    def emit_taps(n):
                        for _ in range(n):
                            if not taps_left:
                                return
                            j = taps_left.pop(0)
                            for c in range(4):
                                S.op('dve', lambda e, c=c, j=j: e.scalar_tensor_tensor(out=cva[c][:, :], in0=U[:, c, j:j + 128], scalar=cw[:, c, j:j + 1], in1=cva[c][:, :], op0=ALU.mult, op1=ALU.add),
                                     reads=[('U', ui, c), ('cva', c)], writes=[('cva', c)])
                    emit_taps(2)
                    yield
                    if i == NBLK - 1:
                        pk = ('ps', 4)
                        for c in range(4):
                            S.op('pe', lambda e, c=c: e.transpose(out=PS[4][:30, c * 128:(c + 1) * 128], in_=U[:, c, 128:158], identity=ident[:]),
                                 reads=[('U', ui, c), 'U', 'ident'], writes=[pk], inc=(c == 3))
                        S.op('act', lambda e: e.copy(out=cpo[:, :], in_=PS[4][:30, :]), reads=[pk], writes=['cpo'])
                        S.dma('sp', lambda e: e.dma_start(out=conv_p, in_=cpo[:, :]), reads=['cpo'], writes=[])

                    for h in range(8):
                        r0 = 64 * (h % 2)
                        pr = h // 2
                        ob = 5 + h // 3
                        ocol = (h % 3) * 160
                        opk = ('ps', ob)
                        ngrp = (i + 4) // 4
                        for g in range(ngrp):
                            j0 = 4 * g
                            nb = min(4, i + 1 - j0)
                            sbank = 3 + (ptcnt[0] % 2)
                            spk = ('ps', sbank)
                            for jj in range(nb):
                                S.op('pe', lambda e, jj=jj, j0=j0, r0=r0, pr=pr, sbank=sbank: e.matmul(PS[sbank][:, jj * 128:(jj + 1) * 128], lhsT=KT[r0:r0 + 64, pr, (j0 + jj) * 128:(j0 + jj + 1) * 128],
                                                                                               rhs=QT[r0:r0 + 64, pr, 0:128], start=(jj == 0), stop=False, skip_group_check=True),
                                     reads=['KT', qk], writes=[spk], inc=False)
                            g0 = j0 - i + 16
                            S.op('pe', lambda e, nb=nb, g0=g0, pr=pr, sbank=sbank, r0=r0: e.matmul(PS[sbank][:, 0:nb * 128], lhsT=auglpb[r0:r0 + 3, pr * 128:(pr + 1) * 128], rhs=augrpb[r0:r0 + 3, g0 * 128:(g0 + nb) * 128], start=False, stop=True, skip_group_check=True),
                                 reads=['auglpb', 'augrpb'], writes=[spk])
                            pt = PT[ptcnt[0] % 3]
                            ptk = ('PT', ptcnt[0] % 3)
                            ptcnt[0] += 1
                            S.op('act', lambda e, pt=pt, nb=nb, sbank=sbank: e.activation(out=pt[:, 0:nb * 128], in_=PS[sbank][:, 0:nb * 128], func=AF.Exp, scale=0.125), reads=[spk], writes=[ptk])
                            if j0 + nb - 1 == i:
                                S.op('pool', lambda e, pt=pt, nb=nb: e.tensor_tensor(out=pt[:, (nb - 1) * 128:nb * 128], in0=pt[:, (nb - 1) * 128:nb * 128], in1=maskTb[:, :], op=ALU.mult),
                                     reads=[ptk, 'maskTb'], writes=[ptk])
                            for jj in range(nb):
                                j = j0 + jj
                                S.op('pe', lambda e, pt=pt, jj=jj, j=j, pr=pr, ob=ob, ocol=ocol: e.matmul(PS[ob][:, ocol:ocol + 129], lhsT=pt[:, jj * 128:(jj + 1) * 128], rhs=Vext[:, j, pr, 0:129], start=(j == 0), stop=(j == i)),
                                     reads=[ptk, 'Vext'], writes=[opk], inc=(j == i))
                        emit_taps(4)
                        yield
                    emit_taps(31)
                    yield
                    attn_epilogue_tok(128, 5, 0)
                    yield
                    conv_ln_silu(128, lambda c: cva[c][:, :], [('cva', c) for c in range(4)], bA=4, bB=3)
                    yield
                    out_ln1(128, i * 128, par=par, rbuf=rbuf_p, x1o=x1o_p, b0=3)

                def run_interleaved(gens):
                    gens = list(gens)
                    while gens:
                        for g in list(gens):
                            try:
                                next(g)
                            except StopIteration:
                                gens.remove(g)

                run_interleaved([p_front(0)])
                for i in range(NBLK):
                    gl = [p_back(i)]
                    if i + 1 < NBLK:
                        gl.append(p_front(i + 1))
                    run_interleaved(gl)
            S.barrier()
            if STOP_AFTER == 2:
                S.finish()
                return nc

        with ExitStack() as ef:
            W_up = sb("W_up", [128, 8, 2 * DFF], BF16, stack=ef)
            W_dn = sb("W_dn", [128, NCH, D], BF16, stack=ef)
            modB = sb("modBf", [17, 3 * D], stack=ef)
            S.dma('sp', lambda e: e.dma_start(out=modB[:, :], in_=mod_scr[:, 3 * D:6 * D]), writes=['mod'])
            g2 = sb("g2", [128, D], stack=ef); b2 = sb("b2", [128, D], stack=ef)
            SC2 = sb("SC2", [128, D], stack=ef); SH2 = sb("SH2", [128, D], stack=ef); GT2 = sb("GT2", [128, D], stack=ef)
            x1t = sb("x1t", [128, D], stack=ef); ht2 = sb("ht2", [128, D], stack=ef)
            h2T = sb("h2T", [128, 8, 128], BF16, stack=ef)
            gT = sb("gT", [128, NCH, 128], BF16, stack=ef)
            carry = sb("carry", [128, 44, 2], stack=ef)
            carrys = sb("carrys", [128, 44, NS, 2], stack=ef)
            ua = [sb("ua%d" % i, [128, 130], stack=ef) for i in range(4)]
            tt = [sb("tt%d" % i, [128, 128], stack=ef) for i in range(8)]
            uas = [sb("uas%d" % i, [128, NS, 6], stack=ef) for i in range(2)]
            sfc = [sb("sfc%d" % i, [32, 512], stack=ef) for i in range(2)]

            w_up_v = w_up.rearrange("(k p) n -> p k n", p=128)
            for i in range(8):
                S.dma('pool', lambda e, i=i: e.dma_start(out=W_up[:, i:i + 1, :], in_=w_up_v[:, i:i + 1, :]), writes=['W_up'])
            w_dn_v = w_down.rearrange("(c p) n -> p c n", p=128)
            for i in range(2):
                S.dma('pool', lambda e, i=i: e.dma_start(out=W_dn[:, 11 * i:11 * i + 11, :], in_=w_dn_v[:, 11 * i:11 * i + 11, :]), writes=['W_dn'])
            S.dma('sp', lambda e: e.dma_start(out=g2[:], in_=ln2_g.partition_broadcast(128)), writes=['g2'])
            S.dma('sp', lambda e: e.dma_start(out=b2[:], in_=ln2_b.partition_broadcast(128)), writes=['b2'])
            S.op('dve', lambda e: e.memset(carry[:], 0.0), writes=['carry'])
            for g in range(11):
                pk = ('ps', 0)
                sf = sfc[g % 2]
                sfk = ('sfc', g % 2)
                S.dma('sp', lambda e, g=g, sf=sf: e.dma_start(out=sf[:, :], in_=st_ffn[:, g * 512:(g + 1) * 512]), writes=[sfk])
                for c in range(4):
                    S.op('pe', lambda e, c=c, sf=sf: e.transpose(out=PS[0][:, c * 128:c * 128 + 32], in_=sf[:, c * 128:(c + 1) * 128], identity=ident[:32, :32]),
                         reads=[sfk, 'ident'], writes=[pk], inc=(c == 3))
                S.op('dve', lambda e, g=g: e.tensor_copy(out=carrys[:, 4 * g:4 * g + 4, :, :], in_=PS[0][:, :].rearrange("p (c x) -> p c x", x=128)[:, :, 0:32].rearrange("p c (b t) -> p c b t", t=2)),
                     reads=[pk], writes=['carrys'])

            def set_mod2(P, sel):
                for (dst, key, off, one, bank) in [(SH2, 'SH2', 0, False, 4), (SC2, 'SC2', 1024, True, 5), (GT2, 'GT2', 2048, False, 4)]:
                    bcast_rows(dst, P, sel, modB, off, one, key, bank)

            def ffn_block(P, row0, y_dst, sample, last):
                S.dma('sp', lambda e: e.dma_start(out=x1t[:P, :], in_=x1_scr[row0:row0 + P, :]), reads=['x1scr'], writes=['x1t'])
                S.op('dve', lambda e: e.tensor_tensor(out=ht2[:P, :], in0=x1t[:P, :], in1=SC2[:P, :], op=ALU.mult), reads=['x1t', 'SC2'], writes=['ht2'])
                S.op('dve', lambda e: e.tensor_tensor(out=ht2[:P, :], in0=ht2[:P, :], in1=SH2[:P, :], op=ALU.add), reads=['ht2', 'SH2'], writes=['ht2'])
                transpose_to(P, ht2[:P, :], ['ht2'], 8, lambda c0, n: h2T[:, c0:c0 + n, 0:P], ['h2T'], [0, 1])
                pend_sm = []

                def emit_sm(c, res):
                    (ca, ka), (cb_, kb) = res
                    S.op('act', lambda e: e.activation(out=ca, in_=ca, func=AF.Silu), reads=[ka], writes=[ka])
                    S.op('dve', lambda e: e.tensor_tensor(out=gT[:, c, 0:P], in0=ca, in1=cb_, op=ALU.mult), reads=[ka, kb], writes=['gT'])

                for c in range(NCH):
                    res = []
                    taps = []
                    for half in range(2):
                        ch = c + NCH * half
                        bank = 2 + ((2 * c + half) % 4)
                        pk = ('ps', bank)
                        col = ch * 128
                        for k in range(8):
                            S.op('pe', lambda e, k=k, col=col, bank=bank: e.matmul(PS[bank][:, 0:P], lhsT=W_up[:, k, col:col + 128], rhs=h2T[:, k, 0:P], start=(k == 0), stop=(k == 7)),
                                 reads=['h2T', 'W_up'], writes=[pk], inc=(k == 7))
                        ti = (2 * c + half) % 4
                        t0 = tt[ti]; t1 = tt[4 + ti]
                        k0 = ('tt', ti); k1 = ('tt', 4 + ti)
                        if not sample:
                            u = ua[ti]
                            uk = ('ua', ti)
                            S.op('act', lambda e, u=u, bank=bank: e.copy(out=u[:, 2:130], in_=PS[bank][:, 0:128]), reads=[pk], writes=[uk])
                            S.op('pool', lambda e, u=u, ch=ch: e.tensor_copy(out=u[:, 0:2], in_=carry[:, ch, :]), reads=['carry'], writes=[uk])
                            S.op('act', lambda e, t0=t0, bank=bank, ch=ch: e.activation(out=t0[:, :], in_=PS[bank][:, 0:128], func=AF.Identity, bias=fcb[:, ch:ch + 1], scale=fcw[:, ch, 2:3]),
                                 reads=[pk, 'fcw', 'fcb'], writes=[k0])
                            taps.append((t0, t1, u, ch, uk, k0, k1))
                            res.append((t0[:, 0:P], k0))
                        else:
                            u = uas[half]
                            uk = ('uas', half)
                            S.op('act', lambda e, u=u, bank=bank: e.copy(out=u[:, :, 2:6], in_=PS[bank][:, 0:64].rearrange("p (b t) -> p b t", t=4)), reads=[pk], writes=[uk])
                            S.op('pool', lambda e, u=u, ch=ch: e.tensor_copy(out=u[:, :, 0:2], in_=carrys[:, ch, :, :]), reads=['carrys'], writes=[uk])
                            t0v = t0[:, 0:64].rearrange("p (b t) -> p b t", t=4)
                            t1v = t1[:, 0:64].rearrange("p (b t) -> p b t", t=4)
                            S.op('act', lambda e, t0=t0, bank=bank, ch=ch: e.activation(out=t0[:, 0:64], in_=PS[bank][:, 0:64], func=AF.Identity, bias=fcb[:, ch:ch + 1], scale=fcw[:, ch, 2:3]),
                                 reads=[pk, 'fcw', 'fcb'], writes=[k0])
                            S.op('dve', lambda e, t0v=t0v, t1v=t1v, u=u, ch=ch: e.scalar_tensor_tensor(out=t1v, in0=u[:, :, 1:5], scalar=fcw[:, ch, 1:2], in1=t0v, op0=ALU.mult, op1=ALU.add),
                                 reads=[uk, k0, 'fcw'], writes=[k1])
                            S.op('dve', lambda e, t0v=t0v, t1v=t1v, u=u, ch=ch: e.scalar_tensor_tensor(out=t0v, in0=u[:, :, 0:4], scalar=fcw[:, ch, 0:1], in1=t1v, op0=ALU.mult, op1=ALU.add),
                                 reads=[uk, k1, 'fcw'], writes=[k0])
                            S.op('pool', lambda e, u=u, ch=ch: e.tensor_copy(out=carrys[:, ch, :, :], in_=u[:, :, 4:6]), reads=[uk], writes=['carrys'])
                            res.append((t0[:, 0:P], k0))
                    for (t0, t1, u, ch, uk, k0, k1) in taps:
                        S.op('dve', lambda e, t0=t0, t1=t1, u=u, ch=ch: e.scalar_tensor_tensor(out=t1[:, :], in0=u[:, 1:129], scalar=fcw[:, ch, 1:2], in1=t0[:, :], op0=ALU.mult, op1=ALU.add),
                             reads=[uk, k0, 'fcw'], writes=[k1])
                    for (t0, t1, u, ch, uk, k0, k1) in taps:
                        S.op('dve', lambda e, t0=t0, t1=t1, u=u, ch=ch: e.scalar_tensor_tensor(out=t0[:, :], in0=u[:, 0:128], scalar=fcw[:, ch, 0:1], in1=t1[:, :], op0=ALU.mult, op1=ALU.add),
                             reads=[uk, k1, 'fcw'], writes=[k0])
                        S.op('pool', lambda e, u=u, ch=ch: e.tensor_copy(out=carry[:, ch, :], in_=u[:, 128:130]), reads=[uk], writes=['carry'])
                    if pend_sm:
                        emit_sm(*pend_sm.pop())
                    pend_sm.append((c, res))
                emit_sm(*pend_sm.pop())
                for hf in range(2):
                    bank = 6 + hf
                    pk = ('ps', bank)
                    for c in range(NCH):
                        S.op('pe', lambda e, c=c, hf=hf, bank=bank: e.matmul(PS[bank][:P, :], lhsT=gT[:, c, 0:P], rhs=W_dn[:, c, hf * 512:(hf + 1) * 512], start=(c == 0), stop=(c == NCH - 1)),
                             reads=['gT', 'W_dn'], writes=[pk], inc=(c == NCH - 1))
                    S.op('dve', lambda e, hf=hf, bank=bank: e.tensor_tensor(out=ht2[:P, hf * 512:(hf + 1) * 512], in0=PS[bank][:P, :], in1=GT2[:P, hf * 512:(hf + 1) * 512], op=ALU.mult),
                         reads=[pk, 'GT2'], writes=['ht2'])
                S.op('dve', lambda e: e.scalar_tensor_tensor(out=ht2[:P, :], in0=x1t[:P, :], scalar=ALPHA, in1=ht2[:P, :], op0=ALU.mult, op1=ALU.add), reads=['x1t', 'ht2'], writes=['ht2'])
                layer_norm_rows(P, ht2[:P, :], x1t[:P, :], ['ht2'], ['x1t'], 48)
                S.op('pool', lambda e: e.tensor_tensor(out=x1t[:P, :], in0=x1t[:P, :], in1=g2[:P, :], op=ALU.mult), reads=['x1t', 'g2'], writes=['x1t'])
                S.op('pool', lambda e: e.tensor_tensor(out=x1t[:P, :], in0=x1t[:P, :], in1=b2[:P, :], op=ALU.add), reads=['x1t', 'b2'], writes=['x1t'])
                S.dma('sp', lambda e: e.dma_start(out=y_dst, in_=x1t[:P, :]), reads=['x1t'], writes=[])

            def state_out(src_fn, nrow, dst):
                for g in range(11):
                    pk = ('ps', 1)
                    for c in range(4):
                        ch = 4 * g + c
                        S.op('pe', lambda e, c=c, ch=ch: e.transpose(out=PS[1][:nrow, c * 128:(c + 1) * 128], in_=src_fn(ch), identity=ident[:]),
                             reads=['carry', 'carrys', 'ident'], writes=[pk], inc=(c == 3))
                    sf = sfc[g % 2]
                    sfk = ('sfc', g % 2)
                    S.op('act', lambda e, sf=sf: e.copy(out=sf[:nrow, :], in_=PS[1][:nrow, :]), reads=[pk], writes=[sfk])
                    S.dma('sp', lambda e, g=g, sf=sf: e.dma_start(out=dst[:, g * 512:(g + 1) * 512], in_=sf[:nrow, :]), reads=[sfk], writes=[])

            set_mod2(64, sels)
            ffn_block(64, T, y_s, True, True)
            state_out(lambda ch: carrys[:, ch, :, :], 32, ffn_s)
            set_mod2(128, selp)
            for i in range(NBLK):
                ffn_block(128, i * 128, y_p[i * 128:(i + 1) * 128, :], False, i == NBLK - 1)
            state_out(lambda ch: carry[:, ch, :], 2, ffn_p)
            S.finish()
    return nc


def _consts():
    c = {}
    c["c_ident"] = np.eye(128, dtype=np.float32)
    k = np.arange(128)
    c["c_maskT"] = (k[:, None] <= k[None, :]).astype(np.float32)
    auglp = np.zeros((3, 4, 128), np.float32)
    for s, m in enumerate(SLOPES):
        auglp[0, s] = 8 * m * k
        auglp[1, s] = 8 * m
        auglp[2, s] = 1024 * m
    c["c_auglp"] = auglp.reshape(3, 512)
    augls = np.ones((3, 128), np.float32)
    augls[0] = k
    c["c_augls"] = augls
    augrp = np.zeros((3, 32, 128), np.float32)
    augrp[0] = 1.0
    augrp[1] = -k[None, :]
    augrp[2] = (np.arange(32) - 16)[:, None]
    c["c_augrp"] = augrp.reshape(3, 4096)
    augrs = np.zeros((3, 16, 4, 4), np.float32)
    mp = np.array(SLOPES, np.float32)
    augrs[0] = 128.0 * mp[None, :, None]
    augrs[1] = 8.0 * mp[None, :, None] * (np.arange(16)[:, None, None] - np.arange(4)[None, None, :])
    augrs[2] = -16384.0 * mp[None, :, None]
    c["c_augrs"] = augrs.reshape(3, 256)
    bnew = np.zeros((16, 4, 4, 16, 4), np.float32)
    mnew = np.zeros((16, 4, 4, 16, 4), np.float32)
    for b in range(16):
        for t1 in range(4):
            for t in range(t1, 4):
                for pr in range(4):
                    bnew[b, t1, pr, b, t] = -SLOPES[pr] * (t - t1)
                    mnew[b, t1, pr, b, t] = 1.0
    c["c_bnew"] = bnew.reshape(64, 256)
    c["c_mnew"] = mnew.reshape(64, 256)
    selp = np.zeros((17, 128), np.float32)
    selp[0] = 1.0
    sels = np.zeros((17, 64), np.float32)
    for b in range(16):
        sels[1 + b, 4 * b:4 * b + 4] = 1.0
    c["c_selp"] = selp
    c["c_sels"] = sels
    c["c_pm8"] = (k % 8).astype(np.float32).reshape(128, 1)
    return c


_NC = None
_LAST = None


def kernel(x_prompt, x_sample, c_prompt, c_sample, cache_k, cache_v, page_table, state_conv, state_ffn,
           ln_emb_g, ln_emb_b, w_ada, b_ada, w_in, lambda_q1, lambda_k1, lambda_q2, lambda_k2, subln_g,
           conv_w, conv_b, conv_ln_g, conv_ln_b, w_out, ln1_g, ln1_b,
           w_up, ffn_conv_w, ffn_conv_b, w_down, ln2_g, ln2_b):
    global _NC
    f = lambda a: np.ascontiguousarray(np.asarray(a, dtype=np.float32))
    if _NC is None:
        _NC = build_nc()
    nc = _NC
    shared = dict(_consts())
    shared["cache_k"] = f(cache_k).reshape(NPHYS * 8, 16 * 512)
    shared["cache_v"] = f(cache_v).reshape(NPHYS * 8, 16 * 512)
    shared["ln_emb_g"] = f(ln_emb_g).reshape(1, D); shared["ln_emb_b"] = f(ln_emb_b).reshape(1, D)
    shared["w_ada"] = f(w_ada)[0]; shared["b_ada"] = f(b_ada).reshape(1, 6 * D)
    shared["w_in"] = f(w_in)[0]
    shared["lam_in"] = np.concatenate([f(lambda_q1)[0], f(lambda_k1)[0], f(lambda_q2)[0], f(lambda_k2)[0]]).reshape(1, 256)
    shared["subln_g"] = f(subln_g).reshape(128, 1)
    shared["conv_w"] = np.ascontiguousarray(f(conv_w)[0].reshape(31, 4, 128).transpose(2, 1, 0))
    shared["conv_b"] = np.ascontiguousarray(f(conv_b)[0].reshape(4, 128).T)
    shared["conv_ln_g"] = np.ascontiguousarray(f(conv_ln_g)[0].reshape(4, 128).T)
    shared["conv_ln_b"] = np.ascontiguousarray(f(conv_ln_b)[0].reshape(4, 128).T)
    shared["w_out"] = f(w_out)[0]
    shared["ln1_g"] = f(ln1_g).reshape(1, D); shared["ln1_b"] = f(ln1_b).reshape(1, D)
    shared["w_up"] = f(w_up)[0]
    shared["ffn_cw"] = np.ascontiguousarray(f(ffn_conv_w)[0].reshape(3, 44, 128).transpose(2, 1, 0))
    shared["ffn_cb"] = np.ascontiguousarray(f(ffn_conv_b)[0].reshape(44, 128).T)
    shared["w_down"] = f(w_down)[0]
    shared["ln2_g"] = f(ln2_g).reshape(1, D); shared["ln2_b"] = f(ln2_b).reshape(1, D)
    xp = f(x_prompt); xs = f(x_sample); cp = f(c_prompt); cs = f(c_sample)
    pt = np.asarray(page_table, dtype=np.int32)
    sc = f(state_conv)[0]; sf = f(state_ffn)[0]
    in_maps = []
    for c in range(NCORES):
        m = dict(shared)
        m["x_p"] = xp[c]
        m["x_s"] = xs[NS * c:NS * (c + 1)].reshape(ST, D)
        m["c_all"] = np.concatenate([cp[c:c + 1], cs[NS * c:NS * (c + 1)]], axis=0)
        ptc = pt[NS * c:NS * (c + 1)]
        m["pt_lay"] = np.ascontiguousarray(np.repeat(ptc.T, 8, axis=0))
        m["st_conv"] = sc[NS * c:NS * (c + 1)].reshape(NS * 30, 512)
        m["st_ffn"] = sf[NS * c:NS * (c + 1)].reshape(NS * 2, 2 * DFF)
        in_maps.append(m)
    res = run_bass_kernel_spmd(nc, in_maps, core_ids=list(range(NCORES)))
    global _LAST
    _LAST = res
    R = res.results
    cat = lambda name: np.stack([R[c][name] for c in range(NCORES)], axis=0)
    y_prompt = cat("y_p")
    y_sample = cat("y_s").reshape(NCORES * NS, 4, D)
    k_prompt = cat("k_p").reshape(1, NCORES, T, 8, 64)
    v_prompt = cat("v_p").reshape(1, NCORES, T, 4, 128)
    conv_prompt = cat("conv_p").reshape(1, NCORES, 30, 512)
    ffn_prompt = cat("ffn_p").reshape(1, NCORES, 2, 2 * DFF)
    k_sample = cat("k_s").reshape(1, NCORES * NS, 4, 8, 64)
    v_sample = cat("v_s").reshape(1, NCORES * NS, 4, 4, 128)
    conv_sample = cat("conv_s").reshape(1, NCORES * NS, 30, 512)
    ffn_sample = cat("ffn_s").reshape(1, NCORES * NS, 2, 2 * DFF)
    return (y_prompt, y_sample, k_prompt, v_prompt, conv_prompt, ffn_prompt, k_sample, v_sample, conv_sample, ffn_sample)
```

```python
import math
from contextlib import ExitStack

import numpy as np
import concourse.bass as bass
import concourse.mybir as mybir
from concourse.bass_utils import run_bass_kernel_spmd

F32 = mybir.dt.float32
BF16 = mybir.dt.bfloat16
I32 = mybir.dt.int32
AF = mybir.ActivationFunctionType
ALU = mybir.AluOpType
AX = mybir.AxisListType

D = 1024
T = 2048
NBLK = 16
NS = 16
ST = 64
DFF = 2816
NCH = 22
EPS = 1e-5
ALPHA = 2.0 ** 0.25
LAMBDA_INIT = 0.8 - 0.6 * math.exp(0.0)
SLOPES = [2.0 ** (-8.0 * (i + 1) / 4) for i in range(4)]
NCORES = 8
NPHYS = 2560
STOP_AFTER = None
DEBUG_X1 = False


class Sched:
    NDMA = 12

    def __init__(self, nc, es):
        self.nc = nc
        self.engs = {'pe': nc.tensor, 'act': nc.scalar, 'dve': nc.vector, 'pool': nc.gpsimd, 'sp': nc.sync}
        self.sem = {}
        self.cnt = {}
        for e in ['pe', 'act', 'dve', 'pool']:
            self.sem[e] = es.enter_context(nc.semaphore("s_" + e))
            self.cnt[e] = 0
        self.dsem = {}
        self.dcnt = {}
        self.dnext = {}
        for q in ['sp', 'pool']:
            self.dsem[q] = [es.enter_context(nc.semaphore("d_%s%d" % (q, i))) for i in range(self.NDMA)]
            self.dcnt[q] = [0] * self.NDMA
            self.dnext[q] = 0
        self.seen = {e: {} for e in self.engs}
        self.reg = {}
        self.pend = {e: ([], []) for e in self.engs}
        self.semobj = {}
        for e in self.sem:
            self.semobj[('c', e)] = self.sem[e]
        for q in self.dsem:
            for i, s in enumerate(self.dsem[q]):
                self.semobj[('d', q, i)] = s

    def _r(self, k):
        if k not in self.reg:
            self.reg[k] = [None, []]
        return self.reg[k]

    def _wait(self, e, tok):
        sk, val = tok
        if self.seen[e].get(sk, 0) >= val:
            return
        self.engs[e].wait_ge(self.semobj[sk], val)
        self.seen[e][sk] = val

    def _deps(self, e, reads, writes):
        own = ('c', e)
        deps = {}

        def add(tok, same_ok):
            if tok is None:
                return
            if tok[0] == own and same_ok:
                return
            if deps.get(tok[0], 0) < tok[1]:
                deps[tok[0]] = tok[1]
        for k in reads:
            add(self._r(k)[0], False)
        for k in writes:
            r = self._r(k)
            add(r[0], True)
            for t in r[1]:
                add(t, True)
        for sk, v in deps.items():
            self._wait(e, (sk, v))

    def op(self, e, fn, reads=(), writes=(), inc=True):
        reads = list(reads)
        writes = list(writes)
        self._deps(e, reads, writes)
        ins = fn(self.engs[e])
        pr, pw = self.pend[e]
        if not inc:
            pr.extend(reads)
            pw.extend(writes)
            return ins
        self.cnt[e] += 1
        ins.then_inc(self.sem[e], 1)
        tok = (('c', e), self.cnt[e])
        for k in reads + pr:
            self._r(k)[1].append(tok)
        for k in writes + pw:
            r = self._r(k)
            r[0] = tok
            r[1] = []
        self.pend[e] = ([], [])
        return ins

    def dma(self, q, fn, reads=(), writes=()):
        reads = list(reads)
        writes = list(writes)
        self._deps(q, reads, writes)
        i = self.dnext[q]
        self.dnext[q] = (i + 1) % self.NDMA
        sk = ('d', q, i)
        if self.dcnt[q][i] > 0:
            self._wait(q, (sk, self.dcnt[q][i]))
        ins = fn(self.engs[q])
        self.dcnt[q][i] += 16
        ins.then_inc(self.dsem[q][i], 16)
        tok = (sk, self.dcnt[q][i])
        for k in reads:
            self._r(k)[1].append(tok)
        for k in writes:
            r = self._r(k)
            r[0] = tok
            r[1] = []
        return tok

    def barrier(self):
        toks = [(('c', e), self.cnt[e]) for e in self.sem if self.cnt[e] > 0]
        for q in self.dsem:
            for i in range(self.NDMA):
                if self.dcnt[q][i] > 0:
                    toks.append((('d', q, i), self.dcnt[q][i]))
        for e in self.engs:
            for t in toks:
                if t[0] == ('c', e):
                    continue
                self._wait(e, t)
        self.reg = {}

    def finish(self):
        for q in self.dsem:
            for i in range(self.NDMA):
                if self.dcnt[q][i] > 0:
                    self._wait('sp', (('d', q, i), self.dcnt[q][i]))


def build_nc():
    nc = bass.Bass("TRN2", target_bir_lowering=False)

    def din(name, shape, dt=F32):
        return nc.dram_tensor(name, list(shape), dt, kind="ExternalInput").ap()

    def dout(name, shape, dt=F32):
        return nc.dram_tensor(name, list(shape), dt, kind="ExternalOutput").ap()

    x_p = din("x_p", [T, D])
    x_s = din("x_s", [ST, D])
    c_all = din("c_all", [17, D])
    cache_k = din("cache_k", [NPHYS * 8, 16 * 512])
    cache_v = din("cache_v", [NPHYS * 8, 16 * 512])
    pt_lay = din("pt_lay", [128, NS], I32)
    st_conv = din("st_conv", [NS * 30, 512])
    st_ffn = din("st_ffn", [NS * 2, 2 * DFF])
    ln_emb_g = din("ln_emb_g", [1, D]); ln_emb_b = din("ln_emb_b", [1, D])
    w_ada = din("w_ada", [D, 6 * D]); b_ada = din("b_ada", [1, 6 * D])
    w_in = din("w_in", [D, 2560])
    lam_in = din("lam_in", [1, 256])
    subln_g = din("subln_g", [128, 1])
    conv_w = din("conv_w", [128, 4, 31])
    conv_b = din("conv_b", [128, 4])
    conv_ln_g = din("conv_ln_g", [128, 4]); conv_ln_b = din("conv_ln_b", [128, 4])
    w_out = din("w_out", [D, D])
    ln1_g = din("ln1_g", [1, D]); ln1_b = din("ln1_b", [1, D])
    w_up = din("w_up", [D, 2 * DFF])
    ffn_cw = din("ffn_cw", [128, 44, 3]); ffn_cb = din("ffn_cb", [128, 44])
    w_down = din("w_down", [DFF, D])
    ln2_g = din("ln2_g", [1, D]); ln2_b = din("ln2_b", [1, D])
    c_ident = din("c_ident", [128, 128])
    c_maskT = din("c_maskT", [128, 128])
    c_auglp = din("c_auglp", [3, 4 * 128]); c_augrp = din("c_augrp", [3, 4096])
    c_augls = din("c_augls", [3, 128]); c_augrs = din("c_augrs", [3, 256])
    c_bnew = din("c_bnew", [64, 256]); c_mnew = din("c_mnew", [64, 256])
    c_selp = din("c_selp", [17, 128]); c_sels = din("c_sels", [17, 64])
    c_pm8 = din("c_pm8", [128, 1])

    y_p = dout("y_p", [T, D]); y_s = dout("y_s", [ST, D])
    k_p = dout("k_p", [T, 512]); v_p = dout("v_p", [T, 512])
    conv_p = dout("conv_p", [30, 512]); ffn_p = dout("ffn_p", [2, 2 * DFF])
    k_s = dout("k_s", [ST, 512]); v_s = dout("v_s", [ST, 512])
    conv_s = dout("conv_s", [NS * 30, 512]); ffn_s = dout("ffn_s", [NS * 2, 2 * DFF])
    x1_scr = nc.dram_tensor("x1_scr", [T + ST, D], F32, kind=("ExternalOutput" if DEBUG_X1 else "Internal")).ap()

    mod_scr = nc.dram_tensor("mod_scr", [17, 6 * D], F32, kind="Internal").ap()

    es_outer = ExitStack()
    with es_outer as es:
        S = Sched(nc, es)

        def sb(name, shape, dt=F32, stack=None):
            return (stack or es).enter_context(nc.sbuf_tensor(name, list(shape), dt))

        def ps(name, shape, dt=F32, stack=None):
            return (stack or es).enter_context(nc.psum_tensor(name, list(shape), dt))

        PS = [ps("psb%d" % i, [128, 512]) for i in range(8)]
        PSB = [p[:].bitcast(BF16) for p in PS]

        ident = sb("ident", [128, 128]); identb = sb("identb", [128, 128], BF16)
        onesb = sb("onesb", [128, 128], BF16); onesf = sb("onesf", [128, 128])
        maskT = sb("maskT", [128, 128]); maskTb = sb("maskTb", [128, 128], BF16)
        selp = sb("selp", [17, 128]); sels = sb("sels", [17, 64])
        neglam = sb("neglam", [128, 1]); sg8 = sb("sg8", [128, 1])
        sg8row = sb("sg8row", [128, 128])
        cw = sb("cw", [128, 4, 31]); cb = sb("cb", [128, 4]); clg = sb("clg", [128, 4]); clb = sb("clb", [128, 4])
        fcw = sb("fcw", [128, 44, 3]); fcb = sb("fcb", [128, 44])
        small = sb("small", [128, 64])
        epsc = sb("epsc", [128, 1])

        S.dma('sp', lambda e: e.dma_start(out=ident[:], in_=c_ident), writes=['ident'])
        S.dma('sp', lambda e: e.dma_start(out=maskT[:], in_=c_maskT), writes=['maskT'])
        S.dma('sp', lambda e: e.dma_start(out=selp[:], in_=c_selp), writes=['selp'])
        S.dma('sp', lambda e: e.dma_start(out=sels[:], in_=c_sels), writes=['sels'])
        S.dma('sp', lambda e: e.dma_start(out=sg8[:], in_=subln_g), writes=['sg8'])
        S.dma('sp', lambda e: e.dma_start(out=sg8row[:], in_=subln_g.rearrange("p o -> o p").partition_broadcast(128)), writes=['sg8row'])
        S.dma('sp', lambda e: e.dma_start(out=cw[:], in_=conv_w), writes=['cw'])
        S.dma('sp', lambda e: e.dma_start(out=cb[:], in_=conv_b), writes=['cb'])
        S.dma('sp', lambda e: e.dma_start(out=clg[:], in_=conv_ln_g), writes=['clg'])
        S.dma('sp', lambda e: e.dma_start(out=clb[:], in_=conv_ln_b), writes=['clb'])
        S.dma('sp', lambda e: e.dma_start(out=fcw[:], in_=ffn_cw), writes=['fcw'])
        S.dma('sp', lambda e: e.dma_start(out=fcb[:], in_=ffn_cb), writes=['fcb'])
        S.op('dve', lambda e: e.tensor_copy(out=identb[:], in_=ident[:]), reads=['ident'], writes=['identb'])
        S.op('dve', lambda e: e.tensor_copy(out=maskTb[:], in_=maskT[:]), reads=['maskT'], writes=['maskTb'])
        S.op('dve', lambda e: e.memset(onesb[:], 1.0), writes=['onesb'])
        S.op('dve', lambda e: e.memset(onesf[:], 1.0), writes=['onesf'])
        S.op('dve', lambda e: e.memset(epsc[:], EPS), writes=['epsc'])
        S.op('dve', lambda e: e.tensor_scalar(out=sg8[:], in0=sg8[:], scalar1=1.0 - LAMBDA_INIT, scalar2=None, op0=ALU.mult), reads=['sg8'], writes=['sg8'])
        S.op('dve', lambda e: e.tensor_scalar(out=sg8row[:], in0=sg8row[:], scalar1=1.0 - LAMBDA_INIT, scalar2=None, op0=ALU.mult), reads=['sg8row'], writes=['sg8row'])

        def rsqrt(out_ap, in_ap, reads, writes, scale=1.0):
            S.op('act', lambda e: e.activation(out=out_ap, in_=in_ap, func=AF.Ln, bias=epsc[:in_ap.shape[0], 0:1], scale=scale), reads=list(reads) + ['epsc'], writes=writes)
            S.op('act', lambda e: e.activation(out=out_ap, in_=out_ap, func=AF.Exp, scale=-0.5), reads=writes, writes=writes)

        def layer_norm_rows(P, src_ap, dst_ap, src_keys, dst_keys, col):
            st6 = small[:P, col:col + 12].rearrange("p (c s) -> p c s", s=6)
            mv = small[:P, col + 12:col + 14]
            rstd = small[:P, col + 14:col + 15]
            nmr = small[:P, col + 15:col + 16]
            k = ('small', col)
            for c in range(2):
                S.op('dve', lambda e, c=c: e.bn_stats(out=st6[:, c, :], in_=src_ap[:, c * 512:(c + 1) * 512]),
                     reads=src_keys, writes=[k], inc=(c == 1))
            S.op('dve', lambda e: e.bn_aggr(out=mv, in_=st6), reads=[k], writes=[k])
            rsqrt(rstd, mv[:, 1:2], [k], [k])
            S.op('dve', lambda e: e.scalar_tensor_tensor(out=nmr, in0=mv[:, 0:1], scalar=-1.0, in1=rstd, op0=ALU.mult, op1=ALU.mult),
                 reads=[k], writes=[k])
            S.op('act', lambda e: e.activation(out=dst_ap, in_=src_ap, func=AF.Identity, bias=nmr, scale=rstd),
                 reads=src_keys + [k], writes=dst_keys)

        def transpose_to(P, src_ap, src_keys, nchunk, dst_fn, dst_keys, banks, evac=('act', 'dve')):
            done = 0
            gi = 0
            while done < nchunk:
                n = min(4, nchunk - done)
                bank = banks[gi % len(banks)]
                pk = ('ps', bank)
                for j in range(n):
                    c = done + j
                    S.op('pe', lambda e, c=c, j=j: e.transpose(out=PS[bank][:, j * 128:j * 128 + P], in_=src_ap[:, c * 128:(c + 1) * 128], identity=ident[:P, :P]),
                         reads=src_keys + ['ident'], writes=[pk], inc=(j == n - 1))
                eng = evac[gi % len(evac)]
                src = PS[bank][:, 0:n * 128].rearrange("p (n t) -> p n t", t=128)[:, :, 0:P]
                dst = dst_fn(done, n)
                if eng == 'act':
                    S.op('act', lambda e, src=src, dst=dst: e.copy(out=dst, in_=src), reads=[pk], writes=dst_keys)
                else:
                    S.op('dve', lambda e, src=src, dst=dst: e.tensor_copy(out=dst, in_=src), reads=[pk], writes=dst_keys)
                done += n
                gi += 1

        def bcast_rows(dst, P, sel, modt, c0, add_one, key, bank):
            selk = 'selp' if sel is selp else 'sels'
            for hf in range(2):
                pk = ('ps', bank)
                S.op('pe', lambda e, hf=hf: e.matmul(PS[bank][:P, :], lhsT=sel[:, :P], rhs=modt[:, c0 + hf * 512:c0 + (hf + 1) * 512], start=True, stop=True),
                     reads=[selk, 'mod'], writes=[pk])
                if add_one:
                    S.op('dve', lambda e, hf=hf: e.tensor_scalar(out=dst[:P, hf * 512:(hf + 1) * 512], in0=PS[bank][:P, :], scalar1=1.0, scalar2=None, op0=ALU.add),
                         reads=[pk], writes=[key])
                else:
                    S.op('dve', lambda e, hf=hf: e.tensor_copy(out=dst[:P, hf * 512:(hf + 1) * 512], in_=PS[bank][:P, :]), reads=[pk], writes=[key])

        es_mix = ExitStack()
        with es_mix as em:
            with ExitStack() as e0:
                modA = sb("modA", [17, 3 * D], stack=e0)
                modB = sb("modB", [17, 3 * D], stack=e0)
                ct = sb("ct", [17, D], stack=e0)
                cT = sb("cT", [128, 8, 17], BF16, stack=e0)
                bada = sb("bada", [17, 6 * D], stack=e0)
                wada = [sb("wada%d" % i, [128, 8, 512], BF16, stack=e0) for i in range(3)]
                lamt = sb("lamt", [128, 256], stack=e0)
                lamp = sb("lamp", [128, 128], stack=e0)
                lams = sb("lams", [128, 4], stack=e0)
                S.dma('sp', lambda e: e.dma_start(out=ct[:], in_=c_all), writes=['ct'])
                S.dma('sp', lambda e: e.dma_start(out=bada[:], in_=b_ada.partition_broadcast(17)), writes=['bada'])
                S.dma('sp', lambda e: e.dma_start(out=lamt[:], in_=lam_in.partition_broadcast(128)), writes=['lamt'])
                S.op('dve', lambda e: e.tensor_tensor(out=lamp[:].rearrange("p (a d) -> p a d", d=64), in0=lamt[:].rearrange("p (a d) -> p a d", d=64)[:, 0::2, :],
                                                      in1=lamt[:].rearrange("p (a d) -> p a d", d=64)[:, 1::2, :], op=ALU.mult), reads=['lamt'], writes=['lamp'])
                S.op('dve', lambda e: e.tensor_reduce(out=lams[:, 0:2], in_=lamp[:].rearrange("p (a d) -> p a d", d=64), axis=AX.X, op=ALU.add), reads=['lamp'], writes=['lams'])
                S.op('act', lambda e: e.activation(out=lams[:, 2:4], in_=lams[:, 0:2], func=AF.Exp), reads=['lams'], writes=['lams2'])
                S.op('dve', lambda e: e.tensor_tensor(out=neglam[:], in0=lams[:, 3:4], in1=lams[:, 2:3], op=ALU.subtract), reads=['lams2'], writes=['neglam'])
                S.op('dve', lambda e: e.tensor_scalar(out=neglam[:], in0=neglam[:], scalar1=-LAMBDA_INIT, scalar2=None, op0=ALU.add), reads=['neglam'], writes=['neglam'])
                S.op('act', lambda e: e.activation(out=ct[:], in_=ct[:], func=AF.Silu), reads=['ct'], writes=['ct'])
                transpose_to(17, ct[:], ['ct'], 8, lambda c0, n: cT[:, c0:c0 + n, :], ['cT'], [0, 1])
                wv = w_ada.rearrange("(k p) n -> p k n", p=128)
                for n in range(12):
                    wb = wada[n % 3]
                    wk = ('wada', n % 3)
                    S.dma('pool', lambda e, n=n, wb=wb: e.dma_start(out=wb[:], in_=wv[:, :, n * 512:(n + 1) * 512]), writes=[wk])
                    bank = 2 + (n % 2)
                    pk = ('ps', bank)
                    for k in range(8):
                        S.op('pe', lambda e, k=k, wb=wb, bank=bank: e.matmul(PS[bank][:17, :], lhsT=cT[:, k, :], rhs=wb[:, k, :], start=(k == 0), stop=(k == 7)),
                             reads=['cT', wk], writes=[pk], inc=(k == 7))
                    mt = modA if n < 6 else modB
                    off = (n % 6) * 512
                    S.op('dve', lambda e, mt=mt, off=off, bank=bank, n=n: e.tensor_tensor(out=mt[:, off:off + 512], in0=PS[bank][:17, :], in1=bada[:, n * 512:(n + 1) * 512], op=ALU.add),
                         reads=[pk, 'bada'], writes=['mod'])
                S.dma('sp', lambda e: e.dma_start(out=mod_scr[:, 0:3 * D], in_=modA[:, :]), reads=['mod'], writes=['mod_scr'])
                S.dma('sp', lambda e: e.dma_start(out=mod_scr[:, 3 * D:6 * D], in_=modB[:, :]), reads=['mod'], writes=['mod_scr'])
            S.barrier()

            W_in = sb("W_in", [128, 8, 2560], BF16, stack=em)
            W_out = sb("W_out", [128, 8, D], BF16, stack=em)
            gE = sb("gE", [128, D], stack=em); bE = sb("bE", [128, D], stack=em)
            g1 = sb("g1", [128, D], stack=em); b1 = sb("b1", [128, D], stack=em)
            SC = sb("SC", [128, D], stack=em); SH = sb("SH", [128, D], stack=em); GT = sb("GT", [128, D], stack=em)
            xin = sb("xin", [128, D], stack=em); xpb = [sb("xp%d" % i, [128, D], stack=em) for i in range(2)]; htmp = sb("htmp", [128, D], stack=em)
            hT = sb("hT", [128, 8, 128], BF16, stack=em)
            QTb = [sb("QT%d" % i, [128, 4, 128], BF16, stack=em) for i in range(2)]
            QT = QTb[0]
            kvst = [sb("kvst%d" % i, [128, 512], stack=em) for i in range(2)]
            sig = sb("sig", [128, 128], stack=em)
            mixT = sb("mixT", [128, 8, 128], BF16, stack=em)
            cvt = sb("cvt", [128, 512], stack=em); cvn = sb("cvn", [128, 512], stack=em)

            w_in_v = w_in.rearrange("(k p) n -> p k n", p=128)
            for i in range(4):
                S.dma('pool', lambda e, i=i: e.dma_start(out=W_in[:, 2 * i:2 * i + 2, :], in_=w_in_v[:, 2 * i:2 * i + 2, :]), writes=['W_in'])
            S.dma('pool', lambda e: e.dma_start(out=W_out[:], in_=w_out.rearrange("(k p) n -> p k n", p=128)), writes=['W_out'])
            for (tl, src, key) in [(gE, ln_emb_g, 'gE'), (bE, ln_emb_b, 'bE'), (g1, ln1_g, 'g1'), (b1, ln1_b, 'b1')]:
                S.dma('sp', lambda e, tl=tl, src=src: e.dma_start(out=tl[:], in_=src.partition_broadcast(128)), writes=[key])

            def set_mod_tiles(P, sel, modt, scaleoff, shiftoff, gateoff):
                bcast_rows(SH, P, sel, modt, shiftoff, False, 'SH', 4)
                bcast_rows(SC, P, sel, modt, scaleoff, True, 'SC', 5)
                bcast_rows(GT, P, sel, modt, gateoff, False, 'GT', 4)
                S.op('dve', lambda e: e.tensor_tensor(out=htmp[:P, :], in0=bE[:P, :], in1=SC[:P, :], op=ALU.mult), reads=['bE', 'SC'], writes=['htmp'])
                S.op('dve', lambda e: e.tensor_tensor(out=SH[:P, :], in0=SH[:P, :], in1=htmp[:P, :], op=ALU.add), reads=['SH', 'htmp'], writes=['SH'])
                S.op('dve', lambda e: e.tensor_tensor(out=SC[:P, :], in0=SC[:P, :], in1=gE[:P, :], op=ALU.mult), reads=['SC', 'gE'], writes=['SC'])

            def front(P, x_src, kout, vout, kT_dst, vbf_dst, u_dst, par=0):
                xp = xpb[par]
                xk = 'xp%d' % par
                QT = QTb[par]
                qk = 'QT%d' % par
                S.dma('sp', lambda e: e.dma_start(out=xin[:P, :], in_=x_src), writes=['xin'])
                layer_norm_rows(P, xin[:P, :], xp[:P, :], ['xin'], [xk], 0)
                S.op('dve', lambda e: e.tensor_tensor(out=htmp[:P, :], in0=xp[:P, :], in1=SC[:P, :], op=ALU.mult), reads=[xk, 'SC'], writes=['htmp'])
                S.op('dve', lambda e: e.tensor_tensor(out=htmp[:P, :], in0=htmp[:P, :], in1=SH[:P, :], op=ALU.add), reads=['htmp', 'SH'], writes=['htmp'])
                S.op('pool', lambda e: e.tensor_tensor(out=xp[:P, :], in0=xp[:P, :], in1=gE[:P, :], op=ALU.mult), reads=[xk, 'gE'], writes=[xk])
                S.op('pool', lambda e: e.tensor_tensor(out=xp[:P, :], in0=xp[:P, :], in1=bE[:P, :], op=ALU.add), reads=[xk, 'bE'], writes=[xk])
                yield
                transpose_to(P, htmp[:P, :], ['htmp'], 8, lambda c0, n: hT[:, c0:c0 + n, 0:P], ['hT'], [0, 1])
                yield
                for wi, (c0, dst) in enumerate([(512, kout), (1024, vout)]):
                    bank = 2 if wi == 0 else 0
                    pk = ('ps', bank)
                    for k in range(8):
                        S.op('pe', lambda e, k=k, c0=c0, bank=bank: e.matmul(PS[bank][:P, :], lhsT=hT[:, k, 0:P], rhs=W_in[:, k, c0:c0 + 512], start=(k == 0), stop=(k == 7)),
                             reads=['hT', 'W_in'], writes=[pk], inc=(k == 7))
                    st = kvst[wi]
                    sk = ('kvst', wi)
                    S.op('act', lambda e, st=st, bank=bank: e.copy(out=st[:P, :], in_=PS[bank][:P, :]), reads=[pk], writes=[sk])
                    S.dma('sp', lambda e, st=st, dst=dst: e.dma_start(out=dst, in_=st[:P, :]), reads=[sk], writes=[])
                    if wi == 1:
                        vbf_dst(st, sk)
                    yield
                for pr in range(4):
                    for wi, c0 in enumerate([0, 512]):
                        bank = 1 + ((2 * pr + wi) % 2)
                        pk = ('ps', bank)
                        for k in range(8):
                            S.op('pe', lambda e, k=k, c0=c0, pr=pr, bank=bank: e.matmul(PS[bank][:, 0:P], lhsT=W_in[:, k, c0 + 128 * pr:c0 + 128 * pr + 128], rhs=hT[:, k, 0:P], start=(k == 0), stop=(k == 7)),
                                 reads=['hT', 'W_in'], writes=[pk], inc=(k == 7))
                        if wi == 0:
                            S.op('act', lambda e, pr=pr, bank=bank: e.copy(out=QT[:, pr, 0:P], in_=PS[bank][:, 0:P]), reads=[pk], writes=[qk])
                        else:
                            kT_dst(pr, bank, pk)
                    yield
                for c in range(4):
                    pa = ('ps', 0)
                    pg = ('ps', 1)
                    for k in range(8):
                        S.op('pe', lambda e, k=k, c=c: e.matmul(PS[0][:, 0:P], lhsT=W_in[:, k, 1536 + 128 * c:1536 + 128 * c + 128], rhs=hT[:, k, 0:P], start=(k == 0), stop=(k == 7)),
                             reads=['hT', 'W_in'], writes=[pa], inc=(k == 7))
                    for k in range(8):
                        S.op('pe', lambda e, k=k, c=c: e.matmul(PS[1][:, 0:P], lhsT=W_in[:, k, 2048 + 128 * c:2048 + 128 * c + 128], rhs=hT[:, k, 0:P], start=(k == 0), stop=(k == 7)),
                             reads=['hT', 'W_in'], writes=[pg], inc=(k == 7))
                    S.op('act', lambda e: e.activation(out=sig[:, 0:P], in_=PS[1][:, 0:P], func=AF.Sigmoid), reads=[pg], writes=['sig'])
                    u_dst(c, pa)
                    yield

            def attn_epilogue_tok(P, obank0, col0):
                for hb in range(3):
                    nh = 3 if hb < 2 else 2
                    pkb = ('ps', obank0 + hb)
                    ov = PS[obank0 + hb][:P, 0:480].rearrange("p (h w) -> p h w", w=160)[:, 0:nh, :]
                    S.op('dve', lambda e, hb=hb, ov=ov, nh=nh: e.reciprocal(out=rs8[:P, 3 * hb:3 * hb + nh], in_=ov[:, :, 128:129].rearrange("p h o -> p (h o)")), reads=[pkb], writes=['rs8'])
                    S.op('dve', lambda e, hb=hb, ov=ov, nh=nh: e.tensor_tensor(out=osb[:P, 3 * hb:3 * hb + nh, :], in0=ov[:, :, 0:128],
                                                                        in1=rs8[:P, 3 * hb:3 * hb + nh].unsqueeze(2).to_broadcast([P, nh, 128]), op=ALU.mult), reads=[pkb, 'rs8'], writes=['osb'])
                S.op('dve', lambda e: e.scalar_tensor_tensor(out=o4[:P], in0=osb[:P, 1::2, :], scalar=neglam[:P, 0:1], in1=osb[:P, 0::2, :], op0=ALU.mult, op1=ALU.add),
                     reads=['osb', 'neglam'], writes=['o4'])
                S.op('pool', lambda e: e.tensor_tensor(out=osq[:P], in0=o4[:P], in1=o4[:P], op=ALU.mult), reads=['o4'], writes=['osq'])
                S.op('dve', lambda e: e.tensor_reduce(out=rs8[:P, 8:12], in_=osq[:P], axis=AX.X, op=ALU.add), reads=['osq'], writes=['rs8b'])
                rsqrt(rs8[:P, 12:16], rs8[:P, 8:12], ['rs8b'], ['rs8c'], scale=1.0 / 128)
                S.op('dve', lambda e: e.tensor_tensor(out=o4[:P], in0=o4[:P], in1=rs8[:P, 12:16].unsqueeze(2).to_broadcast([P, 4, 128]), op=ALU.mult), reads=['o4', 'rs8c'], writes=['o4'])
                S.op('dve', lambda e: e.tensor_tensor(out=o4[:P], in0=o4[:P], in1=sg8row[:P, :].unsqueeze(1).to_broadcast([P, 4, 128]), op=ALU.mult), reads=['o4', 'sg8row'], writes=['o4'])
                transpose_to(P, o4[:P].rearrange("p a e -> p (a e)"), ['o4'], 4, lambda c0, n: mixT[:, c0:c0 + n, col0:col0 + P], ['mixT'], [3], evac=('act',))

            def conv_ln_silu(P, cv_fn, cv_keys, bA=1, bB=0):
                pk = ('ps', bA)
                for c in range(4):
                    S.op('pe', lambda e, c=c: e.transpose(out=PS[bA][:P, c * 128:(c + 1) * 128], in_=cv_fn(c), identity=ident[:]),
                         reads=cv_keys + ['ident'], writes=[pk], inc=(c == 3))
                S.op('act', lambda e: e.copy(out=cvt[:P, :], in_=PS[bA][:P, :]), reads=[pk], writes=['cvt'])
                st6 = small[:P, 16:22]
                mv = small[:P, 22:24]
                rstd = small[:P, 24:25]
                nmr = small[:P, 25:26]
                k = ('small', 16)
                S.op('dve', lambda e: e.bn_stats(out=st6, in_=cvt[:P, :]), reads=['cvt'], writes=[k])
                S.op('dve', lambda e: e.bn_aggr(out=mv, in_=st6), reads=[k], writes=[k])
                rsqrt(rstd, mv[:, 1:2], [k], [k])
                S.op('dve', lambda e: e.scalar_tensor_tensor(out=nmr, in0=mv[:, 0:1], scalar=-1.0, in1=rstd, op0=ALU.mult, op1=ALU.mult), reads=[k], writes=[k])
                S.op('act', lambda e: e.activation(out=cvn[:P, :], in_=cvt[:P, :], func=AF.Identity, bias=nmr, scale=rstd), reads=['cvt', k], writes=['cvn'])
                pk0 = ('ps', bB)
                for c in range(4):
                    S.op('pe', lambda e, c=c: e.transpose(out=PS[bB][:, c * 128:c * 128 + P], in_=cvn[:P, c * 128:(c + 1) * 128], identity=ident[:P, :P]),
                         reads=['cvn', 'ident'], writes=[pk0], inc=(c == 3))
                for c in range(4):
                    S.op('act', lambda e, c=c: e.activation(out=mixT[:, 4 + c, 0:P], in_=PS[bB][:, c * 128:c * 128 + P], func=AF.Silu, bias=clb[:, c:c + 1], scale=clg[:, c:c + 1]),
                         reads=[pk0, 'clg', 'clb'], writes=['mixT'])

            def out_ln1(P, row0, par=0, rbuf=None, x1o=None, b0=2):
                xp = xpb[par]
                xk = 'xp%d' % par
                rk, ok = ('rbuf', 'x1o') if rbuf is not None else ('htmp', 'xin')
                if rbuf is None:
                    rbuf, x1o = htmp, xin
                for hf in range(2):
                    bank = b0 + hf
                    pk = ('ps', bank)
                    for k in range(8):
                        S.op('pe', lambda e, k=k, hf=hf, bank=bank: e.matmul(PS[bank][:P, :], lhsT=mixT[:, k, 0:P], rhs=W_out[:, k, hf * 512:(hf + 1) * 512], start=(k == 0), stop=(k == 7)),
                             reads=['mixT', 'W_out'], writes=[pk], inc=(k == 7))
                    S.op('dve', lambda e, hf=hf, bank=bank: e.tensor_tensor(out=rbuf[:P, hf * 512:(hf + 1) * 512], in0=PS[bank][:P, :], in1=GT[:P, hf * 512:(hf + 1) * 512], op=ALU.mult),
                         reads=[pk, 'GT'], writes=[rk])
                S.op('dve', lambda e: e.scalar_tensor_tensor(out=rbuf[:P, :], in0=xp[:P, :], scalar=ALPHA, in1=rbuf[:P, :], op0=ALU.mult, op1=ALU.add), reads=[xk, rk], writes=[rk])
                layer_norm_rows(P, rbuf[:P, :], x1o[:P, :], [rk], [ok], 32)
                S.op('dve', lambda e: e.tensor_tensor(out=x1o[:P, :], in0=x1o[:P, :], in1=g1[:P, :], op=ALU.mult), reads=[ok, 'g1'], writes=[ok])
                S.op('dve', lambda e: e.tensor_tensor(out=x1o[:P, :], in0=x1o[:P, :], in1=b1[:P, :], op=ALU.add), reads=[ok, 'b1'], writes=[ok])
                S.dma('sp', lambda e: e.dma_start(out=x1_scr[row0:row0 + P, :], in_=x1o[:P, :]), reads=[ok], writes=['x1scr'])

            with ExitStack() as e1:
                modS = sb("modS", [17, 3 * D], stack=e1)
                S.dma('sp', lambda e: e.dma_start(out=modS[:, :], in_=mod_scr[:, 0:3 * D]), writes=['mod'])
                kall = sb("kall", [128, 16, 512], BF16, stack=e1)
                vall = sb("vall", [128, 16, 512], BF16, stack=e1)
                ktp = [sb("ktp%d" % i, [128, 4, 128], BF16, stack=e1) for i in range(3)]
                PTs = sb("PTs", [128, 512], BF16, stack=e1)
                Us = sb("Us", [128, 4, NS, 34], stack=e1)
                KTs = sb("KTs", [128, 4, 64], BF16, stack=e1)
                Vsb = sb("Vsb", [64, 512], BF16, stack=e1)
                auglsb = sb("auglsb", [128, 128], BF16, stack=e1); augrsb = sb("augrsb", [128, 256], BF16, stack=e1)
                ucont = kall[:].rearrange("p a f -> p (a f)").bitcast(F32)[:, 0:4 * NS * 30].rearrange("p (c b t) -> p c b t", c=4, t=30)
                bnew = sb("bnew", [64, 256], stack=e1); mnew = sb("mnew", [64, 256], stack=e1)
                pnewf = sb("pnewf", [64, 512], stack=e1); pnewT = sb("pnewT", [64, 512], BF16, stack=e1)
                ptl = sb("ptl", [128, NS], I32, stack=e1); ptf = sb("ptf", [128, NS], stack=e1)
                pm8 = sb("pm8", [128, 1], stack=e1); idx = sb("idx", [128, NS], I32, stack=e1)
                stc = sb("stc", [120, 512], stack=e1)
                osall = sb("osall", [128, 4, 64], stack=e1); ossq = sb("ossq", [128, 4, 64], stack=e1)
                rsum = sb("rsum", [128, 32], stack=e1); on32 = sb("on32", [128, 32], stack=e1)
                rstd_s = sb("rstd_s", [128, 256], stack=e1)
                cvs = sb("cvs", [128, 4, 64], stack=e1)
                acc = [sb("acc%d" % i, [128, 64], stack=e1) for i in range(2)]
                osq_s = sb("osq_s", [128, 128], stack=e1)

                for r0 in (0, 64):
                    S.dma('pool', lambda e, r0=r0: e.dma_start(out=auglsb[r0:r0 + 3, :], in_=c_augls), writes=['auglsb'])
                    S.dma('pool', lambda e, r0=r0: e.dma_start(out=augrsb[r0:r0 + 3, :], in_=c_augrs), writes=['augrsb'])
                S.dma('sp', lambda e: e.dma_start(out=bnew[:], in_=c_bnew), writes=['bnew'])
                S.dma('sp', lambda e: e.dma_start(out=mnew[:], in_=c_mnew), writes=['mnew'])
                S.dma('sp', lambda e: e.dma_start(out=ptl[:], in_=pt_lay), writes=['ptl'])
                S.dma('sp', lambda e: e.dma_start(out=pm8[:], in_=c_pm8), writes=['pm8'])
                S.op('dve', lambda e: e.tensor_copy(out=ptf[:], in_=ptl[:]), reads=['ptl'], writes=['ptf'])
                S.op('dve', lambda e: e.tensor_scalar(out=ptf[:], in0=ptf[:], scalar1=8.0, scalar2=pm8[:, 0:1], op0=ALU.mult, op1=ALU.add), reads=['ptf', 'pm8'], writes=['ptf'])
                S.op('dve', lambda e: e.tensor_copy(out=idx[:], in_=ptf[:]), reads=['ptf'], writes=['idx'])

                def gather(b, which):
                    if which == 0:
                        S.dma('pool', lambda e: e.indirect_dma_start(out=kall[:].rearrange("p a f -> p (a f)"), out_offset=None, in_=cache_k,
                                                                    in_offset=bass.IndirectOffsetOnAxis(ap=idx[:, b:b + 1], axis=0)), reads=['idx'], writes=['kall'])
                    else:
                        S.dma('pool', lambda e: e.indirect_dma_start(out=vall[:].rearrange("p a f -> p (a f)"), out_offset=None, in_=cache_v,
                                                                    in_offset=bass.IndirectOffsetOnAxis(ap=idx[:, b:b + 1], axis=0)), reads=['idx'], writes=['vall'])
                if STOP_AFTER == 0.1:
                    S.barrier(); S.finish()
                    return nc
                gather(0, 0)
                gather(0, 1)
                if STOP_AFTER == 0.2:
                    S.barrier(); S.finish()
                    return nc

                for g in range(4):
                    S.dma('sp', lambda e, g=g: e.dma_start(out=stc[:, :], in_=st_conv[g * 120:(g + 1) * 120, :]), writes=['stc'])
                    pk = ('ps', 0)
                    for c in range(4):
                        S.op('pe', lambda e, c=c: e.transpose(out=PS[0][:, c * 128:c * 128 + 120], in_=stc[:, c * 128:(c + 1) * 128], identity=ident[:120, :120]),
                             reads=['stc', 'ident'], writes=[pk], inc=(c == 3))
                    for c in range(4):
                        S.op('dve', lambda e, c=c, g=g: e.tensor_copy(out=Us[:, c, 4 * g:4 * g + 4, 0:30], in_=PS[0][:, c * 128:c * 128 + 120].rearrange("p (b t) -> p b t", t=30)),
                             reads=[pk], writes=['Us'])

                if STOP_AFTER == 0.3:
                    S.barrier(); S.finish()
                    return nc
                set_mod_tiles(64, sels, modS, 1024, 0, 2048)
                if STOP_AFTER == 0.4:
                    S.barrier(); S.finish()
                    return nc

                def s_vbf(st, sk):
                    S.op('dve', lambda e: e.tensor_copy(out=Vsb[:, :], in_=st[:64, :]), reads=[sk], writes=['Vsb'])

                def s_kT(pr, bank, pk):
                    S.op('dve', lambda e: e.tensor_copy(out=KTs[:, pr, :], in_=PS[bank][:, 0:64]), reads=[pk], writes=['KTs'])

                def s_u(c, pa):
                    S.op('dve', lambda e: e.tensor_tensor(out=Us[:, c, :, 30:34], in0=PS[0][:, 0:64].rearrange("p (b t) -> p b t", t=4),
                                                          in1=sig[:, 0:64].rearrange("p (b t) -> p b t", t=4), op=ALU.mult), reads=[pa, 'sig'], writes=['Us'])

                for _ in front(64, x_s, k_s, v_s, s_kT, s_vbf, s_u):
                    pass

                if STOP_AFTER == 0.5:
                    S.barrier(); S.finish()
                    return nc
                for hh in range(2):
                    bank = 4 if hh == 0 else 2
                    pk = ('ps', bank)
                    r0 = 64 * hh
                    for pr in range(4):
                        S.op('pe', lambda e, pr=pr, r0=r0, bank=bank: e.matmul(PS[bank][:64, pr * 64:(pr + 1) * 64], lhsT=KTs[r0:r0 + 64, pr, :], rhs=QT[r0:r0 + 64, pr, 0:64], start=True, stop=True),
                             reads=['KTs', 'QT0'], writes=[pk], inc=(pr == 3))
                    S.op('dve', lambda e, hh=hh, bank=bank: e.scalar_tensor_tensor(out=pnewf[:, hh * 256:(hh + 1) * 256], in0=PS[bank][:64, 0:256], scalar=0.125, in1=bnew[:, :], op0=ALU.mult, op1=ALU.add),
                         reads=[pk, 'bnew'], writes=['pnewf'])
                S.op('act', lambda e: e.activation(out=pnewf[:], in_=pnewf[:], func=AF.Exp), reads=['pnewf'], writes=['pnewf'])
                pnv = pnewT[:].rearrange("p (b hh pr t) -> p b hh pr t", hh=2, pr=4, t=4)
                for hh in range(2):
                    S.op('dve', lambda e, hh=hh: e.tensor_tensor(out=pnv[:, :, hh, :, :].rearrange("p b pr t -> p pr b t"), in0=pnewf[:, hh * 256:(hh + 1) * 256].rearrange("p (pr b t) -> p pr b t", pr=4, t=4),
                                                                in1=mnew[:, :].rearrange("p (pr b t) -> p pr b t", pr=4, t=4), op=ALU.mult), reads=['pnewf', 'mnew'], writes=['pnewT'])
                if STOP_AFTER == 0.6:
                    S.barrier(); S.finish()
                    return nc

                SB = [5, 3]
                for b in range(NS):
                    for t16 in range(16):
                        tb = 6 + (t16 % 2)
                        tpk = ('ps', tb)
                        for pr in range(4):
                            S.op('pe', lambda e, pr=pr, t16=t16, tb=tb: e.transpose(out=PSB[tb][:, pr * 128:(pr + 1) * 128], in_=kall[:, t16, pr * 128:(pr + 1) * 128], identity=identb[:]),
                                 reads=['kall', 'identb'], writes=[tpk], inc=(pr == 3))
                        kt = ktp[t16 % 3]
                        kk = ('ktp', t16 % 3)
                        if t16 % 2 == 0:
                            S.op('act', lambda e, kt=kt, tb=tb: e.copy(out=kt[:].rearrange("p a k -> p (a k)"), in_=PSB[tb][:, 0:512]), reads=[tpk], writes=[kk])
                        else:
                            S.op('dve', lambda e, kt=kt, tb=tb: e.tensor_copy(out=kt[:].rearrange("p a k -> p (a k)"), in_=PSB[tb][:, 0:512]), reads=[tpk], writes=[kk])
                        for hh in range(2):
                            r0 = 64 * hh
                            for pr in range(4):
                                S.op('pe', lambda e, pr=pr, hh=hh, r0=r0, kt=kt, t16=t16, b=b: e.matmul(PS[SB[hh]][:, t16 * 16 + pr * 4:t16 * 16 + pr * 4 + 4], lhsT=kt[r0:r0 + 64, pr, :],
                                                                                               rhs=QT[r0:r0 + 64, pr, 4 * b:4 * b + 4], start=(t16 == 0 and pr == 0), stop=False, skip_group_check=True),
                                     reads=[kk, 'QT0'], writes=[('ps', SB[hh])], inc=False)
                    if b + 1 < NS:
                        gather(b + 1, 0)
                    for hh in range(2):
                        r0 = 64 * hh
                        S.op('pe', lambda e, hh=hh, r0=r0: e.matmul(PS[SB[hh]][:, 0:256], lhsT=auglsb[r0:r0 + 3, :], rhs=augrsb[r0:r0 + 3, :], start=False, stop=True, skip_group_check=True),
                             reads=['auglsb', 'augrsb'], writes=[('ps', SB[hh])])
                    for hh in range(2):
                        S.op('act', lambda e, hh=hh: e.activation(out=PTs[:, hh * 256:(hh + 1) * 256], in_=PS[SB[hh]][:, 0:256], func=AF.Exp, scale=0.125), reads=[('ps', SB[hh])], writes=['PTs'])
                    pk4 = ('ps', 4)
                    for hh in range(2):
                        for t16 in range(16):
                            S.op('pe', lambda e, t16=t16, hh=hh: e.matmul(PS[4][:, hh * 16:(hh + 1) * 16], lhsT=onesb[:, :], rhs=PTs[:, hh * 256 + t16 * 16:hh * 256 + (t16 + 1) * 16], start=(t16 == 0), stop=False),
                                 reads=['PTs', 'onesb'], writes=[pk4], inc=False)
                        S.op('pe', lambda e, b=b, hh=hh: e.matmul(PS[4][:, hh * 16:(hh + 1) * 16], lhsT=onesb[0:64, :], rhs=pnv[:, b, hh, :, :], start=False, stop=True),
                             reads=['pnewT', 'onesb'], writes=[pk4], inc=False)
                    for dh in range(4):
                        for hh in range(2):
                            oc = 64 + dh * 8 + hh * 4
                            for t16 in range(16):
                                S.op('pe', lambda e, dh=dh, t16=t16, hh=hh, oc=oc: e.matmul(PS[4][:, oc:oc + 4], lhsT=vall[:, t16, dh * 128:(dh + 1) * 128],
                                                                                         rhs=PTs[:, hh * 256 + t16 * 16 + dh * 4:hh * 256 + t16 * 16 + dh * 4 + 4], start=(t16 == 0), stop=False),
                                     reads=['PTs', 'vall'], writes=[pk4], inc=False)
                            S.op('pe', lambda e, dh=dh, b=b, hh=hh, oc=oc: e.matmul(PS[4][:, oc:oc + 4], lhsT=Vsb[0:64, dh * 128:(dh + 1) * 128], rhs=pnv[:, b, hh, dh, :], start=False, stop=True),
                                 reads=['pnewT', 'Vsb'], writes=[pk4], inc=(dh == 3 and hh == 1))
                    if b + 1 < NS:
                        gather(b + 1, 1)
                    rsv = rsum[:].rearrange("p (d h q) -> p d h q", h=2, q=4)
                    for hh in range(2):
                        S.op('dve', lambda e, hh=hh: e.reciprocal(out=rsv[:, :, hh, :], in_=PS[4][:, hh * 16:(hh + 1) * 16].rearrange("p (d q) -> p d q", q=4)), reads=[pk4], writes=['rsum'])
                    S.op('dve', lambda e: e.tensor_tensor(out=on32[:], in0=PS[4][:, 64:96], in1=rsum[:], op=ALU.mult), reads=[pk4, 'rsum'], writes=['on32'])
                    onv = on32[:].rearrange("p (d h q) -> p d h q", h=2, q=4)
                    S.op('dve', lambda e, b=b: e.scalar_tensor_tensor(out=osall[:, :, 4 * b:4 * b + 4], in0=onv[:, :, 1, :], scalar=neglam[:, 0:1], in1=onv[:, :, 0, :], op0=ALU.mult, op1=ALU.add),
                         reads=['on32', 'neglam'], writes=['osall'])

                if STOP_AFTER == 0.7:
                    S.barrier(); S.finish()
                    return nc
                S.op('dve', lambda e: e.tensor_tensor(out=ossq[:], in0=osall[:], in1=osall[:], op=ALU.mult), reads=['osall'], writes=['ossq'])
                pk = ('ps', 6)
                S.op('pe', lambda e: e.matmul(PS[6][:, 0:256], lhsT=onesf[:, :], rhs=ossq[:].rearrange("p a t -> p (a t)"), start=True, stop=True), reads=['ossq', 'onesf'], writes=[pk])
                rsqrt(rstd_s[:], PS[6][:, 0:256], [pk], ['rstd_s'], scale=1.0 / 128)
                S.op('dve', lambda e: e.scalar_tensor_tensor(out=mixT[:, 0:4, 0:64], in0=osall[:], scalar=sg8[:, 0:1], in1=rstd_s[:].rearrange("p (a t) -> p a t", t=64), op0=ALU.mult, op1=ALU.mult),
                     reads=['osall', 'sg8', 'rstd_s'], writes=['mixT'])

                if STOP_AFTER == 0.8:
                    S.barrier(); S.finish()
                    return nc
                acc4 = [acc[0][:, 0:64], acc[1][:, 0:64], osq_s[:, 0:64], osq_s[:, 64:128]]
                accv = [a.rearrange("p (b t) -> p b t", t=4) for a in acc4]
                for c in range(4):
                    S.op('dve', lambda e, c=c: e.tensor_scalar(out=accv[c], in0=Us[:, c, :, 0:4], scalar1=cw[:, c, 0:1], scalar2=cb[:, c:c + 1], op0=ALU.mult, op1=ALU.add),
                         reads=['Us', 'cw', 'cb'], writes=[('acc', c)])
                for j in range(1, 31):
                    for c in range(4):
                        dst = accv[c] if j < 30 else cvs[:, c, :].rearrange("p (b t) -> p b t", t=4)
                        S.op('dve', lambda e, c=c, j=j, dst=dst: e.scalar_tensor_tensor(out=dst, in0=Us[:, c, :, j:j + 4], scalar=cw[:, c, j:j + 1], in1=accv[c], op0=ALU.mult, op1=ALU.add),
                             reads=['Us', ('acc', c)], writes=[('acc', c)] if j < 30 else ['cvs'])
                conv_ln_silu(64, lambda c: cvs[:, c, :], ['cvs'])
                for c in range(4):
                    S.op('act', lambda e, c=c: e.copy(out=ucont[:, c, :, :], in_=Us[:, c, :, 4:34]), reads=['Us'], writes=['kall'])
                for g in range(4):
                    pk = ('ps', 1)
                    for c in range(4):
                        S.op('pe', lambda e, c=c, g=g: e.transpose(out=PS[1][:120, c * 128:(c + 1) * 128], in_=ucont[:, c, 4 * g:4 * g + 4, :], identity=ident[:]),
                             reads=['kall', 'ident'], writes=[pk], inc=(c == 3))
                    S.op('act', lambda e: e.copy(out=stc[:, :], in_=PS[1][:120, :]), reads=[pk], writes=['stc'])
                    S.dma('sp', lambda e, g=g: e.dma_start(out=conv_s[g * 120:(g + 1) * 120, :], in_=stc[:, :]), reads=['stc'], writes=[])
                if STOP_AFTER == 0.9:
                    S.barrier(); S.finish()
                    return nc
                out_ln1(64, T)
            S.barrier()
            if STOP_AFTER == 1:
                S.op('dve', lambda e: e.tensor_copy(out=htmp[:, :], in_=mixT[:].rearrange("p a t -> p (a t)")), writes=['htmp'])
                S.dma('sp', lambda e: e.dma_start(out=y_p[0:128, :], in_=htmp[:, :]), reads=['htmp'], writes=[])
                S.finish()
                return nc

            with ExitStack() as e2:
                with ExitStack() as et:
                    modP = sb("modP", [17, 3 * D], stack=et)
                    S.dma('sp', lambda e: e.dma_start(out=modP[:, :], in_=mod_scr[:, 0:3 * D]), writes=['mod'])
                    set_mod_tiles(128, selp, modP, 1024, 0, 2048)
                    S.barrier()
                KT = sb("KT", [128, 4, T], BF16, stack=e2)
                Vext = sb("Vext", [128, NBLK, 4, 130], BF16, stack=e2)
                auglpb = sb("auglpb", [128, 512], BF16, stack=e2); augrpb = sb("augrpb", [128, 4096], BF16, stack=e2)
                osb = sb("osb", [128, 8, 128], stack=e2); o4 = sb("o4", [128, 4, 128], stack=e2); osq = sb("osq", [128, 4, 128], stack=e2)
                rs8 = sb("rs8", [128, 16], stack=e2)
                Ub = [sb("U%d" % i, [128, 4, 30 + 128], stack=e2) for i in range(3)]
                PT = [sb("PT%d" % i, [128, 512], BF16, stack=e2) for i in range(3)]
                cva = [sb("cva%d" % i, [128, 128], stack=e2) for i in range(4)]
                cpo = sb("cpo", [30, 512], stack=e2)
                rbuf_p = sb("rbuf", [128, D], stack=e2); x1o_p = sb("x1o", [128, D], stack=e2)

                for r0 in (0, 64):
                    S.dma('pool', lambda e, r0=r0: e.dma_start(out=auglpb[r0:r0 + 3, :], in_=c_auglp), writes=['auglpb'])
                    S.dma('pool', lambda e, r0=r0: e.dma_start(out=augrpb[r0:r0 + 3, :], in_=c_augrp), writes=['augrpb'])
                S.op('dve', lambda e: e.memset(Ub[0][:], 0.0), writes=['U'])
                S.op('dve', lambda e: e.memset(Ub[1][:], 0.0), writes=['U'])
                S.op('dve', lambda e: e.memset(Ub[2][:], 0.0), writes=['U'])
                S.op('pool', lambda e: e.memset(Vext[:].rearrange("p a b c -> p (a b c)"), 1.0), writes=['Vext'])

                ptcnt = [0]

                def p_front(i):
                    par = i % 2
                    up_i = i % 3
                    un_i = (i + 1) % 3
                    Up = Ub[up_i]
                    Un = Ub[un_i]

                    def p_vbf(st, sk):
                        S.op('pool', lambda e: e.tensor_copy(out=Vext[:, i, :, 0:128], in_=st[:, :].rearrange("p (a e) -> p a e", e=128)), reads=[sk], writes=['Vext'])

                    def p_kT(pr, bank, pk):
                        S.op('dve', lambda e: e.tensor_copy(out=KT[:, pr, i * 128:(i + 1) * 128], in_=PS[bank][:, 0:128]), reads=[pk], writes=['KT'])

                    def p_u(c, pa):
                        S.op('dve', lambda e: e.tensor_tensor(out=Up[:, c, 30:158], in0=PS[0][:, 0:128], in1=sig[:, 0:128], op=ALU.mult), reads=[pa, 'sig'], writes=[('U', up_i, c)])
                        S.op('act', lambda e: e.copy(out=Un[:, c, 0:30], in_=Up[:, c, 128:158]), reads=[('U', up_i, c)], writes=[('U', un_i, c)])

                    yield from front(128, x_p[i * 128:(i + 1) * 128, :], k_p[i * 128:(i + 1) * 128, :], v_p[i * 128:(i + 1) * 128, :], p_kT, p_vbf, p_u, par=par)

                def p_back(i):
                    par = i % 2
                    ui = i % 3
                    U = Ub[ui]
                    QT = QTb[par]
                    qk = 'QT%d' % par
                    for c in range(4):
                        S.op('dve', lambda e, c=c: e.tensor_scalar(out=cva[c][:, :], in0=U[:, c, 0:128], scalar1=cw[:, c, 0:1], scalar2=cb[:, c:c + 1], op0=ALU.mult, op1=ALU.add),
                             reads=[('U', ui, c), 'U', 'cw', 'cb'], writes=[('cva', c)])
                    taps_left = list(range(1, 31))

                    def emit_taps(n):
                        for _ in range(n):
                            if not taps_left:
                                return
                            j = taps_left.pop(0)
                            for c in range(4):
                                S.op('dve', lambda e, c=c, j=j: e.scalar_tensor_tensor(out=cva[c][:, :], in0=U[:, c, j:j + 128], scalar=cw[:, c, j:j + 1], in1=cva[c][:, :], op0=ALU.mult, op1=ALU.add),
                                     reads=[('U', ui, c), ('cva', c)], writes=[('cva', c)])
                    emit_taps(2)
                    yield
                    if i == NBLK - 1:
                        pk = ('ps', 4)
                        for c in range(4):
                            S.op('pe', lambda e, c=c: e.transpose(out=PS[4][:30, c * 128:(c + 1) * 128], in_=U[:, c, 128:158], identity=ident[:]),
                                 reads=[('U', ui, c), 'U', 'ident'], writes=[pk], inc=(c == 3))
                        S.op('act', lambda e: e.copy(out=cpo[:, :], in_=PS[4][:30, :]), reads=[pk], writes=['cpo'])
                        S.dma('sp', lambda e: e.dma_start(out=conv_p, in_=cpo[:, :]), reads=['cpo'], writes=[])

                    groups = []
                    for h in range(8):
                        for g in range((i + 4) // 4):
                            groups.append((h, g))

                    def emit_scores(h, g):
                        r0 = 64 * (h % 2)
                        pr = h // 2
                        j0 = 4 * g
                        nb = min(4, i + 1 - j0)
                        sbank = 3 + (ptcnt[0] % 2)
                        spk = ('ps', sbank)
                        for jj in range(nb):
                            S.op('pe', lambda e, jj=jj: e.matmul(PS[sbank][:, jj * 128:(jj + 1) * 128], lhsT=KT[r0:r0 + 64, pr, (j0 + jj) * 128:(j0 + jj + 1) * 128],
                                                                 rhs=QT[r0:r0 + 64, pr, 0:128], start=(jj == 0), stop=False, skip_group_check=True),
                                 reads=['KT', qk], writes=[spk], inc=False)
                        g0 = j0 - i + 16
                        S.op('pe', lambda e: e.matmul(PS[sbank][:, 0:nb * 128], lhsT=auglpb[r0:r0 + 3, pr * 128:(pr + 1) * 128], rhs=augrpb[r0:r0 + 3, g0 * 128:(g0 + nb) * 128], start=False, stop=True, skip_group_check=True),
                             reads=['auglpb', 'augrpb'], writes=[spk])
                        pt = PT[ptcnt[0] % 3]
                        ptk = ('PT', ptcnt[0] % 3)
                        ptcnt[0] += 1
                        S.op('act', lambda e: e.activation(out=pt[:, 0:nb * 128], in_=PS[sbank][:, 0:nb * 128], func=AF.Exp, scale=0.125), reads=[spk], writes=[ptk])
                        if j0 + nb - 1 == i:
                            S.op('pool', lambda e: e.tensor_tensor(out=pt[:, (nb - 1) * 128:nb * 128], in0=pt[:, (nb - 1) * 128:nb * 128], in1=maskTb[:, :], op=ALU.mult),
                                 reads=[ptk, 'maskTb'], writes=[ptk])
                        return (pt, ptk, j0, nb)

                    def emit_pv(h, st):
                        pt, ptk, j0, nb = st
                        pr = h // 2
                        ob = 5 + h // 3
                        ocol = (h % 3) * 160
                        opk = ('ps', ob)
                        for jj in range(nb):
                            j = j0 + jj
                            S.op('pe', lambda e, jj=jj, j=j: e.matmul(PS[ob][:, ocol:ocol + 129], lhsT=pt[:, jj * 128:(jj + 1) * 128], rhs=Vext[:, j, pr, 0:129], start=(j == 0), stop=(j == i)),
                                 reads=[ptk, 'Vext'], writes=[opk], inc=(j == i))

                    st_prev = emit_scores(*groups[0])
                    for gi, (h, g) in enumerate(groups):
                        st_next = emit_scores(*groups[gi + 1]) if gi + 1 < len(groups) else None
                        emit_pv(h, st_prev)
                        st_prev = st_next
                        if g == (i + 4) // 4 - 1:
                            emit_taps(4)
                            yield
                    emit_taps(31)
                    yield
                    attn_epilogue_tok(128, 5, 0)
                    yield
                    conv_ln_silu(128, lambda c: cva[c][:, :], [('cva', c) for c in range(4)], bA=4, bB=3)
                    yield
                    out_ln1(128, i * 128, par=par, rbuf=rbuf_p, x1o=x1o_p, b0=3)

                def run_interleaved(gens):
                    gens = list(gens)
                    while gens:
                        for g in list(gens):
                            try:
                                next(g)
                            except StopIteration:
                                gens.remove(g)

                run_interleaved([p_front(0)])
                for i in range(NBLK):
                    gl = [p_back(i)]
                    if i + 1 < NBLK:
                        gl.append(p_front(i + 1))
                    run_interleaved(gl)
            S.barrier()
            if STOP_AFTER == 2:
                S.finish()
                return nc

        with ExitStack() as ef:
            W_up = sb("W_up", [128, 8, 2 * DFF], BF16, stack=ef)
            W_dn = sb("W_dn", [128, NCH, D], BF16, stack=ef)
            modB = sb("modBf", [17, 3 * D], stack=ef)
            S.dma('sp', lambda e: e.dma_start(out=modB[:, :], in_=mod_scr[:, 3 * D:6 * D]), writes=['mod'])
            g2 = sb("g2", [128, D], stack=ef); b2 = sb("b2", [128, D], stack=ef)
            SC2 = sb("SC2", [128, D], stack=ef); SH2 = sb("SH2", [128, D], stack=ef); GT2 = sb("GT2", [128, D], stack=ef)
            x1t = sb("x1t", [128, D], stack=ef); ht2 = sb("ht2", [128, D], stack=ef)
            h2T = sb("h2T", [128, 8, 128], BF16, stack=ef)
            gT = sb("gT", [128, NCH, 128], BF16, stack=ef)
            carry = sb("carry", [128, 44, 2], stack=ef)
            carrys = sb("carrys", [128, 44, NS, 2], stack=ef)
            ua = [sb("ua%d" % i, [128, 130], stack=ef) for i in range(4)]
            tt = [sb("tt%d" % i, [128, 128], stack=ef) for i in range(8)]
            uas = [sb("uas%d" % i, [128, NS, 6], stack=ef) for i in range(2)]
            sfc = [sb("sfc%d" % i, [32, 512], stack=ef) for i in range(2)]

            w_up_v = w_up.rearrange("(k p) n -> p k n", p=128)
            for i in range(8):
                S.dma('pool', lambda e, i=i: e.dma_start(out=W_up[:, i:i + 1, :], in_=w_up_v[:, i:i + 1, :]), writes=['W_up'])
            w_dn_v = w_down.rearrange("(c p) n -> p c n", p=128)
            for i in range(2):
                S.dma('pool', lambda e, i=i: e.dma_start(out=W_dn[:, 11 * i:11 * i + 11, :], in_=w_dn_v[:, 11 * i:11 * i + 11, :]), writes=['W_dn'])
            S.dma('sp', lambda e: e.dma_start(out=g2[:], in_=ln2_g.partition_broadcast(128)), writes=['g2'])
            S.dma('sp', lambda e: e.dma_start(out=b2[:], in_=ln2_b.partition_broadcast(128)), writes=['b2'])
            S.op('dve', lambda e: e.memset(carry[:], 0.0), writes=['carry'])
            for g in range(11):
                pk = ('ps', 0)
                sf = sfc[g % 2]
                sfk = ('sfc', g % 2)
                S.dma('sp', lambda e, g=g, sf=sf: e.dma_start(out=sf[:, :], in_=st_ffn[:, g * 512:(g + 1) * 512]), writes=[sfk])
                for c in range(4):
                    S.op('pe', lambda e, c=c, sf=sf: e.transpose(out=PS[0][:, c * 128:c * 128 + 32], in_=sf[:, c * 128:(c + 1) * 128], identity=ident[:32, :32]),
                         reads=[sfk, 'ident'], writes=[pk], inc=(c == 3))
                S.op('dve', lambda e, g=g: e.tensor_copy(out=carrys[:, 4 * g:4 * g + 4, :, :], in_=PS[0][:, :].rearrange("p (c x) -> p c x", x=128)[:, :, 0:32].rearrange("p c (b t) -> p c b t", t=2)),
                     reads=[pk], writes=['carrys'])

            def set_mod2(P, sel):
                for (dst, key, off, one, bank) in [(SH2, 'SH2', 0, False, 4), (SC2, 'SC2', 1024, True, 5), (GT2, 'GT2', 2048, False, 4)]:
                    bcast_rows(dst, P, sel, modB, off, one, key, bank)

            def ffn_block(P, row0, y_dst, sample, last):
                S.dma('sp', lambda e: e.dma_start(out=x1t[:P, :], in_=x1_scr[row0:row0 + P, :]), reads=['x1scr'], writes=['x1t'])
                S.op('dve', lambda e: e.tensor_tensor(out=ht2[:P, :], in0=x1t[:P, :], in1=SC2[:P, :], op=ALU.mult), reads=['x1t', 'SC2'], writes=['ht2'])
                S.op('dve', lambda e: e.tensor_tensor(out=ht2[:P, :], in0=ht2[:P, :], in1=SH2[:P, :], op=ALU.add), reads=['ht2', 'SH2'], writes=['ht2'])
                transpose_to(P, ht2[:P, :], ['ht2'], 8, lambda c0, n: h2T[:, c0:c0 + n, 0:P], ['h2T'], [0, 1])
                pend_sm = []

                def emit_sm(c, res):
                    (ca, ka), (cb_, kb) = res
                    S.op('act', lambda e: e.activation(out=ca, in_=ca, func=AF.Silu), reads=[ka], writes=[ka])
                    S.op('dve', lambda e: e.tensor_tensor(out=gT[:, c, 0:P], in0=ca, in1=cb_, op=ALU.mult), reads=[ka, kb], writes=['gT'])

                for c in range(NCH):
                    res = []
                    taps = []
                    for half in range(2):
                        ch = c + NCH * half
                        bank = 2 + ((2 * c + half) % 4)
                        pk = ('ps', bank)
                        col = ch * 128
                        for k in range(8):
                            S.op('pe', lambda e, k=k, col=col, bank=bank: e.matmul(PS[bank][:, 0:P], lhsT=W_up[:, k, col:col + 128], rhs=h2T[:, k, 0:P], start=(k == 0), stop=(k == 7)),
                                 reads=['h2T', 'W_up'], writes=[pk], inc=(k == 7))
                        ti = (2 * c + half) % 4
                        t0 = tt[ti]; t1 = tt[4 + ti]
                        k0 = ('tt', ti); k1 = ('tt', 4 + ti)
                        if not sample:
                            u = ua[ti]
                            uk = ('ua', ti)
                            S.op('act', lambda e, u=u, bank=bank: e.copy(out=u[:, 2:130], in_=PS[bank][:, 0:128]), reads=[pk], writes=[uk])
                            S.op('pool', lambda e, u=u, ch=ch: e.tensor_copy(out=u[:, 0:2], in_=carry[:, ch, :]), reads=['carry'], writes=[uk])
                            S.op('act', lambda e, t0=t0, bank=bank, ch=ch: e.activation(out=t0[:, :], in_=PS[bank][:, 0:128], func=AF.Identity, bias=fcb[:, ch:ch + 1], scale=fcw[:, ch, 2:3]),
                                 reads=[pk, 'fcw', 'fcb'], writes=[k0])
                            taps.append((t0, t1, u, ch, uk, k0, k1))
                            res.append((t0[:, 0:P], k0))
                        else:
                            u = uas[half]
                            uk = ('uas', half)
                            S.op('act', lambda e, u=u, bank=bank: e.copy(out=u[:, :, 2:6], in_=PS[bank][:, 0:64].rearrange("p (b t) -> p b t", t=4)), reads=[pk], writes=[uk])
                            S.op('pool', lambda e, u=u, ch=ch: e.tensor_copy(out=u[:, :, 0:2], in_=carrys[:, ch, :, :]), reads=['carrys'], writes=[uk])
                            t0v = t0[:, 0:64].rearrange("p (b t) -> p b t", t=4)
                            t1v = t1[:, 0:64].rearrange("p (b t) -> p b t", t=4)
                            S.op('act', lambda e, t0=t0, bank=bank, ch=ch: e.activation(out=t0[:, 0:64], in_=PS[bank][:, 0:64], func=AF.Identity, bias=fcb[:, ch:ch + 1], scale=fcw[:, ch, 2:3]),
                                 reads=[pk, 'fcw', 'fcb'], writes=[k0])
                            S.op('dve', lambda e, t0v=t0v, t1v=t1v, u=u, ch=ch: e.scalar_tensor_tensor(out=t1v, in0=u[:, :, 1:5], scalar=fcw[:, ch, 1:2], in1=t0v, op0=ALU.mult, op1=ALU.add),
                                 reads=[uk, k0, 'fcw'], writes=[k1])
                            S.op('dve', lambda e, t0v=t0v, t1v=t1v, u=u, ch=ch: e.scalar_tensor_tensor(out=t0v, in0=u[:, :, 0:4], scalar=fcw[:, ch, 0:1], in1=t1v, op0=ALU.mult, op1=ALU.add),
                                 reads=[uk, k1, 'fcw'], writes=[k0])
                            S.op('pool', lambda e, u=u, ch=ch: e.tensor_copy(out=carrys[:, ch, :, :], in_=u[:, :, 4:6]), reads=[uk], writes=['carrys'])
                            res.append((t0[:, 0:P], k0))
                    for (t0, t1, u, ch, uk, k0, k1) in taps:
                        S.op('dve', lambda e, t0=t0, t1=t1, u=u, ch=ch: e.scalar_tensor_tensor(out=t1[:, :], in0=u[:, 1:129], scalar=fcw[:, ch, 1:2], in1=t0[:, :], op0=ALU.mult, op1=ALU.add),
                             reads=[uk, k0, 'fcw'], writes=[k1])
                    for (t0, t1, u, ch, uk, k0, k1) in taps:
                        S.op('dve', lambda e, t0=t0, t1=t1, u=u, ch=ch: e.scalar_tensor_tensor(out=t0[:, :], in0=u[:, 0:128], scalar=fcw[:, ch, 0:1], in1=t1[:, :], op0=ALU.mult, op1=ALU.add),
                             reads=[uk, k1, 'fcw'], writes=[k0])
                        S.op('pool', lambda e, u=u, ch=ch: e.tensor_copy(out=carry[:, ch, :], in_=u[:, 128:130]), reads=[uk], writes=['carry'])
                    if pend_sm:
                        emit_sm(*pend_sm.pop())
                    pend_sm.append((c, res))
                emit_sm(*pend_sm.pop())
                for hf in range(2):
                    bank = 6 + hf
                    pk = ('ps', bank)
                    for c in range(NCH):
                        S.op('pe', lambda e, c=c, hf=hf, bank=bank: e.matmul(PS[bank][:P, :], lhsT=gT[:, c, 0:P], rhs=W_dn[:, c, hf * 512:(hf + 1) * 512], start=(c == 0), stop=(c == NCH - 1)),
                             reads=['gT', 'W_dn'], writes=[pk], inc=(c == NCH - 1))
                    S.op('dve', lambda e, hf=hf, bank=bank: e.tensor_tensor(out=ht2[:P, hf * 512:(hf + 1) * 512], in0=PS[bank][:P, :], in1=GT2[:P, hf * 512:(hf + 1) * 512], op=ALU.mult),
                         reads=[pk, 'GT2'], writes=['ht2'])
                S.op('dve', lambda e: e.scalar_tensor_tensor(out=ht2[:P, :], in0=x1t[:P, :], scalar=ALPHA, in1=ht2[:P, :], op0=ALU.mult, op1=ALU.add), reads=['x1t', 'ht2'], writes=['ht2'])
                layer_norm_rows(P, ht2[:P, :], x1t[:P, :], ['ht2'], ['x1t'], 48)
                S.op('pool', lambda e: e.tensor_tensor(out=x1t[:P, :], in0=x1t[:P, :], in1=g2[:P, :], op=ALU.mult), reads=['x1t', 'g2'], writes=['x1t'])
                S.op('pool', lambda e: e.tensor_tensor(out=x1t[:P, :], in0=x1t[:P, :], in1=b2[:P, :], op=ALU.add), reads=['x1t', 'b2'], writes=['x1t'])
                S.dma('sp', lambda e: e.dma_start(out=y_dst, in_=x1t[:P, :]), reads=['x1t'], writes=[])

            def state_out(src_fn, nrow, dst):
                for g in range(11):
                    pk = ('ps', 1)
                    for c in range(4):
                        ch = 4 * g + c
                        S.op('pe', lambda e, c=c, ch=ch: e.transpose(out=PS[1][:nrow, c * 128:(c + 1) * 128], in_=src_fn(ch), identity=ident[:]),
                             reads=['carry', 'carrys', 'ident'], writes=[pk], inc=(c == 3))
                    sf = sfc[g % 2]
                    sfk = ('sfc', g % 2)
                    S.op('act', lambda e, sf=sf: e.copy(out=sf[:nrow, :], in_=PS[1][:nrow, :]), reads=[pk], writes=[sfk])
                    S.dma('sp', lambda e, g=g, sf=sf: e.dma_start(out=dst[:, g * 512:(g + 1) * 512], in_=sf[:nrow, :]), reads=[sfk], writes=[])

            set_mod2(64, sels)
            ffn_block(64, T, y_s, True, True)
            state_out(lambda ch: carrys[:, ch, :, :], 32, ffn_s)
            set_mod2(128, selp)
            for i in range(NBLK):
                ffn_block(128, i * 128, y_p[i * 128:(i + 1) * 128, :], False, i == NBLK - 1)
            state_out(lambda ch: carry[:, ch, :], 2, ffn_p)
            S.finish()
    return nc


def _consts():
    c = {}
    c["c_ident"] = np.eye(128, dtype=np.float32)
    k = np.arange(128)
    c["c_maskT"] = (k[:, None] <= k[None, :]).astype(np.float32)
    auglp = np.zeros((3, 4, 128), np.float32)
    for s, m in enumerate(SLOPES):
        auglp[0, s] = 8 * m * k
        auglp[1, s] = 8 * m
        auglp[2, s] = 1024 * m
    c["c_auglp"] = auglp.reshape(3, 512)
    augls = np.ones((3, 128), np.float32)
    augls[0] = k
    c["c_augls"] = augls
    augrp = np.zeros((3, 32, 128), np.float32)
    augrp[0] = 1.0
    augrp[1] = -k[None, :]
    augrp[2] = (np.arange(32) - 16)[:, None]
    c["c_augrp"] = augrp.reshape(3, 4096)
    augrs = np.zeros((3, 16, 4, 4), np.float32)
    mp = np.array(SLOPES, np.float32)
    augrs[0] = 128.0 * mp[None, :, None]
    augrs[1] = 8.0 * mp[None, :, None] * (np.arange(16)[:, None, None] - np.arange(4)[None, None, :])
    augrs[2] = -16384.0 * mp[None, :, None]
    c["c_augrs"] = augrs.reshape(3, 256)
    bnew = np.zeros((16, 4, 4, 16, 4), np.float32)
    mnew = np.zeros((16, 4, 4, 16, 4), np.float32)
    for b in range(16):
        for t1 in range(4):
            for t in range(t1, 4):
                for pr in range(4):
                    bnew[b, t1, pr, b, t] = -SLOPES[pr] * (t - t1)
                    mnew[b, t1, pr, b, t] = 1.0
    c["c_bnew"] = bnew.reshape(64, 256)
    c["c_mnew"] = mnew.reshape(64, 256)
    selp = np.zeros((17, 128), np.float32)
    selp[0] = 1.0
    sels = np.zeros((17, 64), np.float32)
    for b in range(16):
        sels[1 + b, 4 * b:4 * b + 4] = 1.0
    c["c_selp"] = selp
    c["c_sels"] = sels
    c["c_pm8"] = (k % 8).astype(np.float32).reshape(128, 1)
    return c


_NC = None
_LAST = None


def kernel(x_prompt, x_sample, c_prompt, c_sample, cache_k, cache_v, page_table, state_conv, state_ffn,
           ln_emb_g, ln_emb_b, w_ada, b_ada, w_in, lambda_q1, lambda_k1, lambda_q2, lambda_k2, subln_g,
           conv_w, conv_b, conv_ln_g, conv_ln_b, w_out, ln1_g, ln1_b,
           w_up, ffn_conv_w, ffn_conv_b, w_down, ln2_g, ln2_b):
    global _NC
    f = lambda a: np.ascontiguousarray(np.asarray(a, dtype=np.float32))
    if _NC is None:
        _NC = build_nc()
    nc = _NC
    shared = dict(_consts())
    shared["cache_k"] = f(cache_k).reshape(NPHYS * 8, 16 * 512)
    shared["cache_v"] = f(cache_v).reshape(NPHYS * 8, 16 * 512)
    shared["ln_emb_g"] = f(ln_emb_g).reshape(1, D); shared["ln_emb_b"] = f(ln_emb_b).reshape(1, D)
    shared["w_ada"] = f(w_ada)[0]; shared["b_ada"] = f(b_ada).reshape(1, 6 * D)
    shared["w_in"] = f(w_in)[0]
    shared["lam_in"] = np.concatenate([f(lambda_q1)[0], f(lambda_k1)[0], f(lambda_q2)[0], f(lambda_k2)[0]]).reshape(1, 256)
    shared["subln_g"] = f(subln_g).reshape(128, 1)
    shared["conv_w"] = np.ascontiguousarray(f(conv_w)[0].reshape(31, 4, 128).transpose(2, 1, 0))
    shared["conv_b"] = np.ascontiguousarray(f(conv_b)[0].reshape(4, 128).T)
    shared["conv_ln_g"] = np.ascontiguousarray(f(conv_ln_g)[0].reshape(4, 128).T)
    shared["conv_ln_b"] = np.ascontiguousarray(f(conv_ln_b)[0].reshape(4, 128).T)
    shared["w_out"] = f(w_out)[0]
    shared["ln1_g"] = f(ln1_g).reshape(1, D); shared["ln1_b"] = f(ln1_b).reshape(1, D)
    shared["w_up"] = f(w_up)[0]
    shared["ffn_cw"] = np.ascontiguousarray(f(ffn_conv_w)[0].reshape(3, 44, 128).transpose(2, 1, 0))
    shared["ffn_cb"] = np.ascontiguousarray(f(ffn_conv_b)[0].reshape(44, 128).T)
    shared["w_down"] = f(w_down)[0]
    shared["ln2_g"] = f(ln2_g).reshape(1, D); shared["ln2_b"] = f(ln2_b).reshape(1, D)
    xp = f(x_prompt); xs = f(x_sample); cp = f(c_prompt); cs = f(c_sample)
    pt = np.asarray(page_table, dtype=np.int32)
    sc = f(state_conv)[0]; sf = f(state_ffn)[0]
    in_maps = []
    for c in range(NCORES):
        m = dict(shared)
        m["x_p"] = xp[c]
        m["x_s"] = xs[NS * c:NS * (c + 1)].reshape(ST, D)
        m["c_all"] = np.concatenate([cp[c:c + 1], cs[NS * c:NS * (c + 1)]], axis=0)
        ptc = pt[NS * c:NS * (c + 1)]
        m["pt_lay"] = np.ascontiguousarray(np.repeat(ptc.T, 8, axis=0))
        m["st_conv"] = sc[NS * c:NS * (c + 1)].reshape(NS * 30, 512)
        m["st_ffn"] = sf[NS * c:NS * (c + 1)].reshape(NS * 2, 2 * DFF)
        in_maps.append(m)
    res = run_bass_kernel_spmd(nc, in_maps, core_ids=list(range(NCORES)))
    global _LAST
    _LAST = res
    R = res.results
    cat = lambda name: np.stack([R[c][name] for c in range(NCORES)], axis=0)
    y_prompt = cat("y_p")
    y_sample = cat("y_s").reshape(NCORES * NS, 4, D)
    k_prompt = cat("k_p").reshape(1, NCORES, T, 8, 64)
    v_prompt = cat("v_p").reshape(1, NCORES, T, 4, 128)
    conv_prompt = cat("conv_p").reshape(1, NCORES, 30, 512)
    ffn_prompt = cat("ffn_p").reshape(1, NCORES, 2, 2 * DFF)
    k_sample = cat("k_s").reshape(1, NCORES * NS, 4, 8, 64)
    v_sample = cat("v_s").reshape(1, NCORES * NS, 4, 4, 128)
    conv_sample = cat("conv_s").reshape(1, NCORES * NS, 30, 512)
    ffn_sample = cat("ffn_s").reshape(1, NCORES * NS, 2, 2 * DFF)
    return (y_prompt, y_sample, k_prompt, v_prompt, conv_prompt, ffn_prompt, k_sample, v_sample, conv_sample, ffn_sample)
```

```python
import math
from contextlib import ExitStack

import numpy as np
import concourse.bass as bass
import concourse.mybir as mybir
from concourse.bass_utils import run_bass_kernel_spmd

F32 = mybir.dt.float32
BF16 = mybir.dt.bfloat16
I32 = mybir.dt.int32
AF = mybir.ActivationFunctionType
ALU = mybir.AluOpType
AX = mybir.AxisListType

D = 1024
T = 2048
NBLK = 16
NS = 16
ST = 64
DFF = 2816
NCH = 22
EPS = 1e-5
ALPHA = 2.0 ** 0.25
LAMBDA_INIT = 0.8 - 0.6 * math.exp(0.0)
SLOPES = [2.0 ** (-8.0 * (i + 1) / 4) for i in range(4)]
NCORES = 8
NPHYS = 2560
STOP_AFTER = None
DEBUG_X1 = False


class Sched:
    NDMA = 12

    def __init__(self, nc, es):
        self.nc = nc
        self.engs = {'pe': nc.tensor, 'act': nc.scalar, 'dve': nc.vector, 'pool': nc.gpsimd, 'sp': nc.sync}
        self.sem = {}
        self.cnt = {}
        for e in ['pe', 'act', 'dve', 'pool']:
            self.sem[e] = es.enter_context(nc.semaphore("s_" + e))
            self.cnt[e] = 0
        self.dsem = {}
        self.dcnt = {}
        self.dnext = {}
        for q in ['sp', 'pool']:
            self.dsem[q] = [es.enter_context(nc.semaphore("d_%s%d" % (q, i))) for i in range(self.NDMA)]
            self.dcnt[q] = [0] * self.NDMA
            self.dnext[q] = 0
        self.seen = {e: {} for e in self.engs}
        self.reg = {}
        self.pend = {e: ([], []) for e in self.engs}
        self.semobj = {}
        for e in self.sem:
            self.semobj[('c', e)] = self.sem[e]
        for q in self.dsem:
            for i, s in enumerate(self.dsem[q]):
                self.semobj[('d', q, i)] = s

    def _r(self, k):
        if k not in self.reg:
            self.reg[k] = [None, []]
        return self.reg[k]

    def _wait(self, e, tok):
        sk, val = tok
        if self.seen[e].get(sk, 0) >= val:
            return
        self.engs[e].wait_ge(self.semobj[sk], val)
        self.seen[e][sk] = val

    def _deps(self, e, reads, writes):
        own = ('c', e)
        deps = {}

        def add(tok, same_ok):
            if tok is None:
                return
            if tok[0] == own and same_ok:
                return
            if deps.get(tok[0], 0) < tok[1]:
                deps[tok[0]] = tok[1]
        for k in reads:
            add(self._r(k)[0], False)
        for k in writes:
            r = self._r(k)
            add(r[0], True)
            for t in r[1]:
                add(t, True)
        for sk, v in deps.items():
            self._wait(e, (sk, v))

    def op(self, e, fn, reads=(), writes=(), inc=True):
        reads = list(reads)
        writes = list(writes)
        self._deps(e, reads, writes)
        ins = fn(self.engs[e])
        pr, pw = self.pend[e]
        if not inc:
            pr.extend(reads)
            pw.extend(writes)
            return ins
        self.cnt[e] += 1
        ins.then_inc(self.sem[e], 1)
        tok = (('c', e), self.cnt[e])
        for k in reads + pr:
            self._r(k)[1].append(tok)
        for k in writes + pw:
            r = self._r(k)
            r[0] = tok
            r[1] = []
        self.pend[e] = ([], [])
        return ins

    def dma(self, q, fn, reads=(), writes=()):
        reads = list(reads)
        writes = list(writes)
        self._deps(q, reads, writes)
        i = self.dnext[q]
        self.dnext[q] = (i + 1) % self.NDMA
        sk = ('d', q, i)
        if self.dcnt[q][i] > 0:
            self._wait(q, (sk, self.dcnt[q][i]))
        ins = fn(self.engs[q])
        self.dcnt[q][i] += 16
        ins.then_inc(self.dsem[q][i], 16)
        tok = (sk, self.dcnt[q][i])
        for k in reads:
            self._r(k)[1].append(tok)
        for k in writes:
            r = self._r(k)
            r[0] = tok
            r[1] = []
        return tok

    def barrier(self):
        toks = [(('c', e), self.cnt[e]) for e in self.sem if self.cnt[e] > 0]
        for q in self.dsem:
            for i in range(self.NDMA):
                if self.dcnt[q][i] > 0:
                    toks.append((('d', q, i), self.dcnt[q][i]))
        for e in self.engs:
            for t in toks:
                if t[0] == ('c', e):
                    continue
                self._wait(e, t)
        self.reg = {}

    def finish(self):
        for q in self.dsem:
            for i in range(self.NDMA):
                if self.dcnt[q][i] > 0:
                    self._wait('sp', (('d', q, i), self.dcnt[q][i]))


def build_nc():
    nc = bass.Bass("TRN2", target_bir_lowering=False)

    def din(name, shape, dt=F32):
        return nc.dram_tensor(name, list(shape), dt, kind="ExternalInput").ap()

    def dout(name, shape, dt=F32):
        return nc.dram_tensor(name, list(shape), dt, kind="ExternalOutput").ap()

    x_p = din("x_p", [T, D])
    x_s = din("x_s", [ST, D])
    c_all = din("c_all", [17, D])
    cache_k = din("cache_k", [NPHYS * 8, 16 * 512])
    cache_v = din("cache_v", [NPHYS * 8, 16 * 512])
    pt_lay = din("pt_lay", [128, NS], I32)
    st_conv = din("st_conv", [NS * 30, 512])
    st_ffn = din("st_ffn", [NS * 2, 2 * DFF])
    ln_emb_g = din("ln_emb_g", [1, D]); ln_emb_b = din("ln_emb_b", [1, D])
    w_ada = din("w_ada", [D, 6 * D]); b_ada = din("b_ada", [1, 6 * D])
    w_in = din("w_in", [D, 2560])
    lam_in = din("lam_in", [1, 256])
    subln_g = din("subln_g", [128, 1])
    conv_w = din("conv_w", [128, 4, 31])
    conv_b = din("conv_b", [128, 4])
    conv_ln_g = din("conv_ln_g", [128, 4]); conv_ln_b = din("conv_ln_b", [128, 4])
    w_out = din("w_out", [D, D])
    ln1_g = din("ln1_g", [1, D]); ln1_b = din("ln1_b", [1, D])
    w_up = din("w_up", [D, 2 * DFF])
    ffn_cw = din("ffn_cw", [128, 44, 3]); ffn_cb = din("ffn_cb", [128, 44])
    w_down = din("w_down", [DFF, D])
    ln2_g = din("ln2_g", [1, D]); ln2_b = din("ln2_b", [1, D])
    c_ident = din("c_ident", [128, 128])
    c_maskT = din("c_maskT", [128, 128])
    c_auglp = din("c_auglp", [3, 4 * 128]); c_augrp = din("c_augrp", [3, 4096])
    c_augls = din("c_augls", [3, 128]); c_augrs = din("c_augrs", [3, 256])
    c_bnew = din("c_bnew", [64, 256]); c_mnew = din("c_mnew", [64, 256])
    c_selp = din("c_selp", [17, 128]); c_sels = din("c_sels", [17, 64])
    c_pm8 = din("c_pm8", [128, 1])

    y_p = dout("y_p", [T, D]); y_s = dout("y_s", [ST, D])
    k_p = dout("k_p", [T, 512]); v_p = dout("v_p", [T, 512])
    conv_p = dout("conv_p", [30, 512]); ffn_p = dout("ffn_p", [2, 2 * DFF])
    k_s = dout("k_s", [ST, 512]); v_s = dout("v_s", [ST, 512])
    conv_s = dout("conv_s", [NS * 30, 512]); ffn_s = dout("ffn_s", [NS * 2, 2 * DFF])
    x1_scr = nc.dram_tensor("x1_scr", [T + ST, D], F32, kind=("ExternalOutput" if DEBUG_X1 else "Internal")).ap()

    mod_scr = nc.dram_tensor("mod_scr", [17, 6 * D], F32, kind="Internal").ap()

    es_outer = ExitStack()
    with es_outer as es:
        S = Sched(nc, es)

        def sb(name, shape, dt=F32, stack=None):
            return (stack or es).enter_context(nc.sbuf_tensor(name, list(shape), dt))

        def ps(name, shape, dt=F32, stack=None):
            return (stack or es).enter_context(nc.psum_tensor(name, list(shape), dt))

        PS = [ps("psb%d" % i, [128, 512]) for i in range(8)]
        PSB = [p[:].bitcast(BF16) for p in PS]

        ident = sb("ident", [128, 128]); identb = sb("identb", [128, 128], BF16)
        onesb = sb("onesb", [128, 128], BF16); onesf = sb("onesf", [128, 128])
        maskT = sb("maskT", [128, 128]); maskTb = sb("maskTb", [128, 128], BF16)
        selp = sb("selp", [17, 128]); sels = sb("sels", [17, 64])
        neglam = sb("neglam", [128, 1]); sg8 = sb("sg8", [128, 1])
        sg8row = sb("sg8row", [128, 128])
        cw = sb("cw", [128, 4, 31]); cb = sb("cb", [128, 4]); clg = sb("clg", [128, 4]); clb = sb("clb", [128, 4])
        fcw = sb("fcw", [128, 44, 3]); fcb = sb("fcb", [128, 44])
        small = sb("small", [128, 64])
        epsc = sb("epsc", [128, 1])

        S.dma('sp', lambda e: e.dma_start(out=ident[:], in_=c_ident), writes=['ident'])
        S.dma('sp', lambda e: e.dma_start(out=maskT[:], in_=c_maskT), writes=['maskT'])
        S.dma('sp', lambda e: e.dma_start(out=selp[:], in_=c_selp), writes=['selp'])
        S.dma('sp', lambda e: e.dma_start(out=sels[:], in_=c_sels), writes=['sels'])
        S.dma('sp', lambda e: e.dma_start(out=sg8[:], in_=subln_g), writes=['sg8'])
        S.dma('sp', lambda e: e.dma_start(out=sg8row[:], in_=subln_g.rearrange("p o -> o p").partition_broadcast(128)), writes=['sg8row'])
        S.dma('sp', lambda e: e.dma_start(out=cw[:], in_=conv_w), writes=['cw'])
        S.dma('sp', lambda e: e.dma_start(out=cb[:], in_=conv_b), writes=['cb'])
        S.dma('sp', lambda e: e.dma_start(out=clg[:], in_=conv_ln_g), writes=['clg'])
        S.dma('sp', lambda e: e.dma_start(out=clb[:], in_=conv_ln_b), writes=['clb'])
        S.dma('sp', lambda e: e.dma_start(out=fcw[:], in_=ffn_cw), writes=['fcw'])
        S.dma('sp', lambda e: e.dma_start(out=fcb[:], in_=ffn_cb), writes=['fcb'])
        S.op('dve', lambda e: e.tensor_copy(out=identb[:], in_=ident[:]), reads=['ident'], writes=['identb'])
        S.op('dve', lambda e: e.tensor_copy(out=maskTb[:], in_=maskT[:]), reads=['maskT'], writes=['maskTb'])
        S.op('dve', lambda e: e.memset(onesb[:], 1.0), writes=['onesb'])
        S.op('dve', lambda e: e.memset(onesf[:], 1.0), writes=['onesf'])
        S.op('dve', lambda e: e.memset(epsc[:], EPS), writes=['epsc'])
        S.op('dve', lambda e: e.tensor_scalar(out=sg8[:], in0=sg8[:], scalar1=1.0 - LAMBDA_INIT, scalar2=None, op0=ALU.mult), reads=['sg8'], writes=['sg8'])
        S.op('dve', lambda e: e.tensor_scalar(out=sg8row[:], in0=sg8row[:], scalar1=1.0 - LAMBDA_INIT, scalar2=None, op0=ALU.mult), reads=['sg8row'], writes=['sg8row'])

        def rsqrt(out_ap, in_ap, reads, writes, scale=1.0):
            S.op('act', lambda e: e.activation(out=out_ap, in_=in_ap, func=AF.Ln, bias=epsc[:in_ap.shape[0], 0:1], scale=scale), reads=list(reads) + ['epsc'], writes=writes)
            S.op('act', lambda e: e.activation(out=out_ap, in_=out_ap, func=AF.Exp, scale=-0.5), reads=writes, writes=writes)

        def layer_norm_rows(P, src_ap, dst_ap, src_keys, dst_keys, col):
            st6 = small[:P, col:col + 12].rearrange("p (c s) -> p c s", s=6)
            mv = small[:P, col + 12:col + 14]
            rstd = small[:P, col + 14:col + 15]
            nmr = small[:P, col + 15:col + 16]
            k = ('small', col)
            for c in range(2):
                S.op('dve', lambda e, c=c: e.bn_stats(out=st6[:, c, :], in_=src_ap[:, c * 512:(c + 1) * 512]),
                     reads=src_keys, writes=[k], inc=(c == 1))
            S.op('dve', lambda e: e.bn_aggr(out=mv, in_=st6), reads=[k], writes=[k])
            rsqrt(rstd, mv[:, 1:2], [k], [k])
            S.op('dve', lambda e: e.scalar_tensor_tensor(out=nmr, in0=mv[:, 0:1], scalar=-1.0, in1=rstd, op0=ALU.mult, op1=ALU.mult),
                 reads=[k], writes=[k])
            S.op('act', lambda e: e.activation(out=dst_ap, in_=src_ap, func=AF.Identity, bias=nmr, scale=rstd),
                 reads=src_keys + [k], writes=dst_keys)

        def transpose_to(P, src_ap, src_keys, nchunk, dst_fn, dst_keys, banks, evac=('act', 'dve')):
            done = 0
            gi = 0
            while done < nchunk:
                n = min(4, nchunk - done)
                bank = banks[gi % len(banks)]
                pk = ('ps', bank)
                for j in range(n):
                    c = done + j
                    S.op('pe', lambda e, c=c, j=j: e.transpose(out=PS[bank][:, j * 128:j * 128 + P], in_=src_ap[:, c * 128:(c + 1) * 128], identity=ident[:P, :P]),
                         reads=src_keys + ['ident'], writes=[pk], inc=(j == n - 1))
                eng = evac[gi % len(evac)]
                src = PS[bank][:, 0:n * 128].rearrange("p (n t) -> p n t", t=128)[:, :, 0:P]
                dst = dst_fn(done, n)
                if eng == 'act':
                    S.op('act', lambda e, src=src, dst=dst: e.copy(out=dst, in_=src), reads=[pk], writes=dst_keys)
                else:
                    S.op('dve', lambda e, src=src, dst=dst: e.tensor_copy(out=dst, in_=src), reads=[pk], writes=dst_keys)
                done += n
                gi += 1

        def bcast_rows(dst, P, sel, modt, c0, add_one, key, bank):
            selk = 'selp' if sel is selp else 'sels'
            for hf in range(2):
                pk = ('ps', bank)
                S.op('pe', lambda e, hf=hf: e.matmul(PS[bank][:P, :], lhsT=sel[:, :P], rhs=modt[:, c0 + hf * 512:c0 + (hf + 1) * 512], start=True, stop=True),
                     reads=[selk, 'mod'], writes=[pk])
                if add_one:
                    S.op('dve', lambda e, hf=hf: e.tensor_scalar(out=dst[:P, hf * 512:(hf + 1) * 512], in0=PS[bank][:P, :], scalar1=1.0, scalar2=None, op0=ALU.add),
                         reads=[pk], writes=[key])
                else:
                    S.op('dve', lambda e, hf=hf: e.tensor_copy(out=dst[:P, hf * 512:(hf + 1) * 512], in_=PS[bank][:P, :]), reads=[pk], writes=[key])

        es_mix = ExitStack()
        with es_mix as em:
            with ExitStack() as e0:
                modA = sb("modA", [17, 3 * D], stack=e0)
                modB = sb("modB", [17, 3 * D], stack=e0)
                ct = sb("ct", [17, D], stack=e0)
                cT = sb("cT", [128, 8, 17], BF16, stack=e0)
                bada = sb("bada", [17, 6 * D], stack=e0)
                wada = [sb("wada%d" % i, [128, 8, 512], BF16, stack=e0) for i in range(3)]
                lamt = sb("lamt", [128, 256], stack=e0)
                lamp = sb("lamp", [128, 128], stack=e0)
                lams = sb("lams", [128, 4], stack=e0)
                S.dma('sp', lambda e: e.dma_start(out=ct[:], in_=c_all), writes=['ct'])
                S.dma('sp', lambda e: e.dma_start(out=bada[:], in_=b_ada.partition_broadcast(17)), writes=['bada'])
                S.dma('sp', lambda e: e.dma_start(out=lamt[:], in_=lam_in.partition_broadcast(128)), writes=['lamt'])
                S.op('dve', lambda e: e.tensor_tensor(out=lamp[:].rearrange("p (a d) -> p a d", d=64), in0=lamt[:].rearrange("p (a d) -> p a d", d=64)[:, 0::2, :],
                                                      in1=lamt[:].rearrange("p (a d) -> p a d", d=64)[:, 1::2, :], op=ALU.mult), reads=['lamt'], writes=['lamp'])
                S.op('dve', lambda e: e.tensor_reduce(out=lams[:, 0:2], in_=lamp[:].rearrange("p (a d) -> p a d", d=64), axis=AX.X, op=ALU.add), reads=['lamp'], writes=['lams'])
                S.op('act', lambda e: e.activation(out=lams[:, 2:4], in_=lams[:, 0:2], func=AF.Exp), reads=['lams'], writes=['lams2'])
                S.op('dve', lambda e: e.tensor_tensor(out=neglam[:], in0=lams[:, 3:4], in1=lams[:, 2:3], op=ALU.subtract), reads=['lams2'], writes=['neglam'])
                S.op('dve', lambda e: e.tensor_scalar(out=neglam[:], in0=neglam[:], scalar1=-LAMBDA_INIT, scalar2=None, op0=ALU.add), reads=['neglam'], writes=['neglam'])
                S.op('act', lambda e: e.activation(out=ct[:], in_=ct[:], func=AF.Silu), reads=['ct'], writes=['ct'])
                transpose_to(17, ct[:], ['ct'], 8, lambda c0, n: cT[:, c0:c0 + n, :], ['cT'], [0, 1])
                wv = w_ada.rearrange("(k p) n -> p k n", p=128)
                for n in range(12):
                    wb = wada[n % 3]
                    wk = ('wada', n % 3)
                    S.dma('pool', lambda e, n=n, wb=wb: e.dma_start(out=wb[:], in_=wv[:, :, n * 512:(n + 1) * 512]), writes=[wk])
                    bank = 2 + (n % 2)
                    pk = ('ps', bank)
                    for k in range(8):
                        S.op('pe', lambda e, k=k, wb=wb, bank=bank: e.matmul(PS[bank][:17, :], lhsT=cT[:, k, :], rhs=wb[:, k, :], start=(k == 0), stop=(k == 7)),
                             reads=['cT', wk], writes=[pk], inc=(k == 7))
                    mt = modA if n < 6 else modB
                    off = (n % 6) * 512
                    S.op('dve', lambda e, mt=mt, off=off, bank=bank, n=n: e.tensor_tensor(out=mt[:, off:off + 512], in0=PS[bank][:17, :], in1=bada[:, n * 512:(n + 1) * 512], op=ALU.add),
                         reads=[pk, 'bada'], writes=['mod'])
                S.dma('sp', lambda e: e.dma_start(out=mod_scr[:, 0:3 * D], in_=modA[:, :]), reads=['mod'], writes=['mod_scr'])
                S.dma('sp', lambda e: e.dma_start(out=mod_scr[:, 3 * D:6 * D], in_=modB[:, :]), reads=['mod'], writes=['mod_scr'])
            S.barrier()

            W_in = sb("W_in", [128, 8, 2560], BF16, stack=em)
            W_out = sb("W_out", [128, 8, D], BF16, stack=em)
            gE = sb("gE", [128, D], stack=em); bE = sb("bE", [128, D], stack=em)
            g1 = sb("g1", [128, D], stack=em); b1 = sb("b1", [128, D], stack=em)
            SC = sb("SC", [128, D], stack=em); SH = sb("SH", [128, D], stack=em); GT = sb("GT", [128, D], stack=em)
            xin = sb("xin", [128, D], stack=em); xpb = [sb("xp%d" % i, [128, D], stack=em) for i in range(2)]; htmp = sb("htmp", [128, D], stack=em)
            hT = sb("hT", [128, 8, 128], BF16, stack=em)
            QTb = [sb("QT%d" % i, [128, 4, 128], BF16, stack=em) for i in range(2)]
            QT = QTb[0]
            kvst = [sb("kvst%d" % i, [128, 512], stack=em) for i in range(2)]
            sig = sb("sig", [128, 128], stack=em)
            mixT = sb("mixT", [128, 8, 128], BF16, stack=em)
            cvt = sb("cvt", [128, 512], stack=em); cvn = sb("cvn", [128, 512], stack=em)

            w_in_v = w_in.rearrange("(k p) n -> p k n", p=128)
            for i in range(4):
                S.dma('pool', lambda e, i=i: e.dma_start(out=W_in[:, 2 * i:2 * i + 2, :], in_=w_in_v[:, 2 * i:2 * i + 2, :]), writes=['W_in'])
            S.dma('pool', lambda e: e.dma_start(out=W_out[:], in_=w_out.rearrange("(k p) n -> p k n", p=128)), writes=['W_out'])
            for (tl, src, key) in [(gE, ln_emb_g, 'gE'), (bE, ln_emb_b, 'bE'), (g1, ln1_g, 'g1'), (b1, ln1_b, 'b1')]:
                S.dma('sp', lambda e, tl=tl, src=src: e.dma_start(out=tl[:], in_=src.partition_broadcast(128)), writes=[key])

            def set_mod_tiles(P, sel, modt, scaleoff, shiftoff, gateoff):
                bcast_rows(SH, P, sel, modt, shiftoff, False, 'SH', 4)
                bcast_rows(SC, P, sel, modt, scaleoff, True, 'SC', 5)
                bcast_rows(GT, P, sel, modt, gateoff, False, 'GT', 4)
                S.op('dve', lambda e: e.tensor_tensor(out=htmp[:P, :], in0=bE[:P, :], in1=SC[:P, :], op=ALU.mult), reads=['bE', 'SC'], writes=['htmp'])
                S.op('dve', lambda e: e.tensor_tensor(out=SH[:P, :], in0=SH[:P, :], in1=htmp[:P, :], op=ALU.add), reads=['SH', 'htmp'], writes=['SH'])
                S.op('dve', lambda e: e.tensor_tensor(out=SC[:P, :], in0=SC[:P, :], in1=gE[:P, :], op=ALU.mult), reads=['SC', 'gE'], writes=['SC'])

            def front(P, x_src, kout, vout, kT_dst, vbf_dst, u_dst, par=0):
                xp = xpb[par]
                xk = 'xp%d' % par
                QT = QTb[par]
                qk = 'QT%d' % par
                S.dma('sp', lambda e: e.dma_start(out=xin[:P, :], in_=x_src), writes=['xin'])
                layer_norm_rows(P, xin[:P, :], xp[:P, :], ['xin'], [xk], 0)
                S.op('dve', lambda e: e.tensor_tensor(out=htmp[:P, :], in0=xp[:P, :], in1=SC[:P, :], op=ALU.mult), reads=[xk, 'SC'], writes=['htmp'])
                S.op('dve', lambda e: e.tensor_tensor(out=htmp[:P, :], in0=htmp[:P, :], in1=SH[:P, :], op=ALU.add), reads=['htmp', 'SH'], writes=['htmp'])
                yield
                transpose_to(P, htmp[:P, :], ['htmp'], 8, lambda c0, n: hT[:, c0:c0 + n, 0:P], ['hT'], [0, 1])
                yield
                for wi, (c0, dst) in enumerate([(512, kout), (1024, vout)]):
                    bank = 2 if wi == 0 else 0
                    pk = ('ps', bank)
                    for k in range(8):
                        S.op('pe', lambda e, k=k, c0=c0, bank=bank: e.matmul(PS[bank][:P, :], lhsT=hT[:, k, 0:P], rhs=W_in[:, k, c0:c0 + 512], start=(k == 0), stop=(k == 7)),
                             reads=['hT', 'W_in'], writes=[pk], inc=(k == 7))
                    st = kvst[wi]
                    sk = ('kvst', wi)
                    S.op('act', lambda e, st=st, bank=bank: e.copy(out=st[:P, :], in_=PS[bank][:P, :]), reads=[pk], writes=[sk])
                    S.dma('sp', lambda e, st=st, dst=dst: e.dma_start(out=dst, in_=st[:P, :]), reads=[sk], writes=[])
                    if wi == 1:
                        vbf_dst(st, sk)
                    yield
                for pr in range(4):
                    for wi, c0 in enumerate([0, 512]):
                        bank = 1 + ((2 * pr + wi) % 2)
                        pk = ('ps', bank)
                        for k in range(8):
                            S.op('pe', lambda e, k=k, c0=c0, pr=pr, bank=bank: e.matmul(PS[bank][:, 0:P], lhsT=W_in[:, k, c0 + 128 * pr:c0 + 128 * pr + 128], rhs=hT[:, k, 0:P], start=(k == 0), stop=(k == 7)),
                                 reads=['hT', 'W_in'], writes=[pk], inc=(k == 7))
                        if wi == 0:
                            S.op('act', lambda e, pr=pr, bank=bank: e.copy(out=QT[:, pr, 0:P], in_=PS[bank][:, 0:P]), reads=[pk], writes=[qk])
                        else:
                            kT_dst(pr, bank, pk)
                    yield
                for c in range(4):
                    pa = ('ps', 0)
                    pg = ('ps', 1)
                    for k in range(8):
                        S.op('pe', lambda e, k=k, c=c: e.matmul(PS[0][:, 0:P], lhsT=W_in[:, k, 1536 + 128 * c:1536 + 128 * c + 128], rhs=hT[:, k, 0:P], start=(k == 0), stop=(k == 7)),
                             reads=['hT', 'W_in'], writes=[pa], inc=(k == 7))
                    for k in range(8):
                        S.op('pe', lambda e, k=k, c=c: e.matmul(PS[1][:, 0:P], lhsT=W_in[:, k, 2048 + 128 * c:2048 + 128 * c + 128], rhs=hT[:, k, 0:P], start=(k == 0), stop=(k == 7)),
                             reads=['hT', 'W_in'], writes=[pg], inc=(k == 7))
                    S.op('act', lambda e: e.activation(out=sig[:, 0:P], in_=PS[1][:, 0:P], func=AF.Sigmoid), reads=[pg], writes=['sig'])
                    u_dst(c, pa)
                    yield
                S.op('pool', lambda e: e.tensor_tensor(out=xp[:P, :], in0=xp[:P, :], in1=gE[:P, :], op=ALU.mult), reads=[xk, 'gE'], writes=[xk])
                S.op('pool', lambda e: e.tensor_tensor(out=xp[:P, :], in0=xp[:P, :], in1=bE[:P, :], op=ALU.add), reads=[xk, 'bE'], writes=[xk])
                yield

            def attn_epilogue_tok(P, obank0, col0):
                for hb in range(3):
                    nh = 3 if hb < 2 else 2
                    pkb = ('ps', obank0 + hb)
                    ov = PS[obank0 + hb][:P, 0:480].rearrange("p (h w) -> p h w", w=160)[:, 0:nh, :]
                    S.op('dve', lambda e, hb=hb, ov=ov, nh=nh: e.reciprocal(out=rs8[:P, 3 * hb:3 * hb + nh], in_=ov[:, :, 128:129].rearrange("p h o -> p (h o)")), reads=[pkb], writes=['rs8'])
                    S.op('dve', lambda e, hb=hb, ov=ov, nh=nh: e.tensor_tensor(out=osb[:P, 3 * hb:3 * hb + nh, :], in0=ov[:, :, 0:128],
                                                                        in1=rs8[:P, 3 * hb:3 * hb + nh].unsqueeze(2).to_broadcast([P, nh, 128]), op=ALU.mult), reads=[pkb, 'rs8'], writes=['osb'])
                S.op('dve', lambda e: e.scalar_tensor_tensor(out=o4[:P], in0=osb[:P, 1::2, :], scalar=neglam[:P, 0:1], in1=osb[:P, 0::2, :], op0=ALU.mult, op1=ALU.add),
                     reads=['osb', 'neglam'], writes=['o4'])
                S.op('pool', lambda e: e.tensor_tensor(out=osq[:P], in0=o4[:P], in1=o4[:P], op=ALU.mult), reads=['o4'], writes=['osq'])
                S.op('dve', lambda e: e.tensor_reduce(out=rs8[:P, 8:12], in_=osq[:P], axis=AX.X, op=ALU.add), reads=['osq'], writes=['rs8b'])
                rsqrt(rs8[:P, 12:16], rs8[:P, 8:12], ['rs8b'], ['rs8c'], scale=1.0 / 128)
                S.op('dve', lambda e: e.tensor_tensor(out=o4[:P], in0=o4[:P], in1=rs8[:P, 12:16].unsqueeze(2).to_broadcast([P, 4, 128]), op=ALU.mult), reads=['o4', 'rs8c'], writes=['o4'])
                S.op('dve', lambda e: e.tensor_tensor(out=o4[:P], in0=o4[:P], in1=sg8row[:P, :].unsqueeze(1).to_broadcast([P, 4, 128]), op=ALU.mult), reads=['o4', 'sg8row'], writes=['o4'])
                transpose_to(P, o4[:P].rearrange("p a e -> p (a e)"), ['o4'], 4, lambda c0, n: mixT[:, c0:c0 + n, col0:col0 + P], ['mixT'], [3], evac=('act',))

            def conv_ln_silu(P, cv_fn, cv_keys, bA=1, bB=0):
                pk = ('ps', bA)
                for c in range(4):
                    S.op('pe', lambda e, c=c: e.transpose(out=PS[bA][:P, c * 128:(c + 1) * 128], in_=cv_fn(c), identity=ident[:]),
                         reads=cv_keys + ['ident'], writes=[pk], inc=(c == 3))
                S.op('act', lambda e: e.copy(out=cvt[:P, :], in_=PS[bA][:P, :]), reads=[pk], writes=['cvt'])
                st6 = small[:P, 16:22]
                mv = small[:P, 22:24]
                rstd = small[:P, 24:25]
                nmr = small[:P, 25:26]
                k = ('small', 16)
                S.op('dve', lambda e: e.bn_stats(out=st6, in_=cvt[:P, :]), reads=['cvt'], writes=[k])
                S.op('dve', lambda e: e.bn_aggr(out=mv, in_=st6), reads=[k], writes=[k])
                rsqrt(rstd, mv[:, 1:2], [k], [k])
                S.op('dve', lambda e: e.scalar_tensor_tensor(out=nmr, in0=mv[:, 0:1], scalar=-1.0, in1=rstd, op0=ALU.mult, op1=ALU.mult), reads=[k], writes=[k])
                S.op('act', lambda e: e.activation(out=cvn[:P, :], in_=cvt[:P, :], func=AF.Identity, bias=nmr, scale=rstd), reads=['cvt', k], writes=['cvn'])
                pk0 = ('ps', bB)
                for c in range(4):
                    S.op('pe', lambda e, c=c: e.transpose(out=PS[bB][:, c * 128:c * 128 + P], in_=cvn[:P, c * 128:(c + 1) * 128], identity=ident[:P, :P]),
                         reads=['cvn', 'ident'], writes=[pk0], inc=(c == 3))
                for c in range(4):
                    S.op('act', lambda e, c=c: e.activation(out=mixT[:, 4 + c, 0:P], in_=PS[bB][:, c * 128:c * 128 + P], func=AF.Silu, bias=clb[:, c:c + 1], scale=clg[:, c:c + 1]),
                         reads=[pk0, 'clg', 'clb'], writes=['mixT'])

            def out_ln1(P, row0, par=0, rbuf=None, x1o=None, b0=2):
                xp = xpb[par]
                xk = 'xp%d' % par
                rk, ok = ('rbuf', 'x1o') if rbuf is not None else ('htmp', 'xin')
                if rbuf is None:
                    rbuf, x1o = htmp, xin
                for hf in range(2):
                    bank = b0 + hf
                    pk = ('ps', bank)
                    for k in range(8):
                        S.op('pe', lambda e, k=k, hf=hf, bank=bank: e.matmul(PS[bank][:P, :], lhsT=mixT[:, k, 0:P], rhs=W_out[:, k, hf * 512:(hf + 1) * 512], start=(k == 0), stop=(k == 7)),
                             reads=['mixT', 'W_out'], writes=[pk], inc=(k == 7))
                    S.op('dve', lambda e, hf=hf, bank=bank: e.tensor_tensor(out=rbuf[:P, hf * 512:(hf + 1) * 512], in0=PS[bank][:P, :], in1=GT[:P, hf * 512:(hf + 1) * 512], op=ALU.mult),
                         reads=[pk, 'GT'], writes=[rk])
                S.op('dve', lambda e: e.scalar_tensor_tensor(out=rbuf[:P, :], in0=xp[:P, :], scalar=ALPHA, in1=rbuf[:P, :], op0=ALU.mult, op1=ALU.add), reads=[xk, rk], writes=[rk])
                layer_norm_rows(P, rbuf[:P, :], x1o[:P, :], [rk], [ok], 32)
                S.op('dve', lambda e: e.tensor_tensor(out=x1o[:P, :], in0=x1o[:P, :], in1=g1[:P, :], op=ALU.mult), reads=[ok, 'g1'], writes=[ok])
                S.op('dve', lambda e: e.tensor_tensor(out=x1o[:P, :], in0=x1o[:P, :], in1=b1[:P, :], op=ALU.add), reads=[ok, 'b1'], writes=[ok])
                S.dma('sp', lambda e: e.dma_start(out=x1_scr[row0:row0 + P, :], in_=x1o[:P, :]), reads=[ok], writes=['x1scr'])

            with ExitStack() as e1:
                modS = sb("modS", [17, 3 * D], stack=e1)
                S.dma('sp', lambda e: e.dma_start(out=modS[:, :], in_=mod_scr[:, 0:3 * D]), writes=['mod'])
                kall = sb("kall", [128, 16, 512], BF16, stack=e1)
                vall = sb("vall", [128, 16, 512], BF16, stack=e1)
                ktp = [sb("ktp%d" % i, [128, 4, 128], BF16, stack=e1) for i in range(3)]
                PTs = sb("PTs", [128, 512], BF16, stack=e1)
                Us = sb("Us", [128, 4, NS, 34], stack=e1)
                KTs = sb("KTs", [128, 4, 64], BF16, stack=e1)
                Vsb = sb("Vsb", [64, 512], BF16, stack=e1)
                auglsb = sb("auglsb", [128, 128], BF16, stack=e1); augrsb = sb("augrsb", [128, 256], BF16, stack=e1)
                ucont = kall[:].rearrange("p a f -> p (a f)").bitcast(F32)[:, 0:4 * NS * 30].rearrange("p (c b t) -> p c b t", c=4, t=30)
                bnew = sb("bnew", [64, 256], stack=e1); mnew = sb("mnew", [64, 256], stack=e1)
                pnewf = sb("pnewf", [64, 512], stack=e1); pnewT = sb("pnewT", [64, 512], BF16, stack=e1)
                ptl = sb("ptl", [128, NS], I32, stack=e1); ptf = sb("ptf", [128, NS], stack=e1)
                pm8 = sb("pm8", [128, 1], stack=e1); idx = sb("idx", [128, NS], I32, stack=e1)
                stc = sb("stc", [120, 512], stack=e1)
                osall = sb("osall", [128, 4, 64], stack=e1); ossq = sb("ossq", [128, 4, 64], stack=e1)
                rsum = sb("rsum", [128, 32], stack=e1); on32 = sb("on32", [128, 32], stack=e1)
                rstd_s = sb("rstd_s", [128, 256], stack=e1)
                cvs = sb("cvs", [128, 4, 64], stack=e1)
                acc = [sb("acc%d" % i, [128, 64], stack=e1) for i in range(2)]
                osq_s = sb("osq_s", [128, 128], stack=e1)

                for r0 in (0, 64):
                    S.dma('pool', lambda e, r0=r0: e.dma_start(out=auglsb[r0:r0 + 3, :], in_=c_augls), writes=['auglsb'])
                    S.dma('pool', lambda e, r0=r0: e.dma_start(out=augrsb[r0:r0 + 3, :], in_=c_augrs), writes=['augrsb'])
                S.dma('sp', lambda e: e.dma_start(out=bnew[:], in_=c_bnew), writes=['bnew'])
                S.dma('sp', lambda e: e.dma_start(out=mnew[:], in_=c_mnew), writes=['mnew'])
                S.dma('sp', lambda e: e.dma_start(out=ptl[:], in_=pt_lay), writes=['ptl'])
                S.dma('sp', lambda e: e.dma_start(out=pm8[:], in_=c_pm8), writes=['pm8'])
                S.op('dve', lambda e: e.tensor_copy(out=ptf[:], in_=ptl[:]), reads=['ptl'], writes=['ptf'])
                S.op('dve', lambda e: e.tensor_scalar(out=ptf[:], in0=ptf[:], scalar1=8.0, scalar2=pm8[:, 0:1], op0=ALU.mult, op1=ALU.add), reads=['ptf', 'pm8'], writes=['ptf'])
                S.op('dve', lambda e: e.tensor_copy(out=idx[:], in_=ptf[:]), reads=['ptf'], writes=['idx'])

                def gather(b, which):
                    if which == 0:
                        S.dma('pool', lambda e: e.indirect_dma_start(out=kall[:].rearrange("p a f -> p (a f)"), out_offset=None, in_=cache_k,
                                                                    in_offset=bass.IndirectOffsetOnAxis(ap=idx[:, b:b + 1], axis=0)), reads=['idx'], writes=['kall'])
                    else:
                        S.dma('pool', lambda e: e.indirect_dma_start(out=vall[:].rearrange("p a f -> p (a f)"), out_offset=None, in_=cache_v,
                                                                    in_offset=bass.IndirectOffsetOnAxis(ap=idx[:, b:b + 1], axis=0)), reads=['idx'], writes=['vall'])
                if STOP_AFTER == 0.1:
                    S.barrier(); S.finish()
                    return nc
                gather(0, 0)
                gather(0, 1)
                if STOP_AFTER == 0.2:
                    S.barrier(); S.finish()
                    return nc

                for g in range(4):
                    S.dma('sp', lambda e, g=g: e.dma_start(out=stc[:, :], in_=st_conv[g * 120:(g + 1) * 120, :]), writes=['stc'])
                    pk = ('ps', 0)
                    for c in range(4):
                        S.op('pe', lambda e, c=c: e.transpose(out=PS[0][:, c * 128:c * 128 + 120], in_=stc[:, c * 128:(c + 1) * 128], identity=ident[:120, :120]),
                             reads=['stc', 'ident'], writes=[pk], inc=(c == 3))
                    for c in range(4):
                        S.op('dve', lambda e, c=c, g=g: e.tensor_copy(out=Us[:, c, 4 * g:4 * g + 4, 0:30], in_=PS[0][:, c * 128:c * 128 + 120].rearrange("p (b t) -> p b t", t=30)),
                             reads=[pk], writes=['Us'])

                if STOP_AFTER == 0.3:
                    S.barrier(); S.finish()
                    return nc
                set_mod_tiles(64, sels, modS, 1024, 0, 2048)
                if STOP_AFTER == 0.4:
                    S.barrier(); S.finish()
                    return nc

                def s_vbf(st, sk):
                    S.op('dve', lambda e: e.tensor_copy(out=Vsb[:, :], in_=st[:64, :]), reads=[sk], writes=['Vsb'])

                def s_kT(pr, bank, pk):
                    S.op('dve', lambda e: e.tensor_copy(out=KTs[:, pr, :], in_=PS[bank][:, 0:64]), reads=[pk], writes=['KTs'])

                def s_u(c, pa):
                    S.op('dve', lambda e: e.tensor_tensor(out=Us[:, c, :, 30:34], in0=PS[0][:, 0:64].rearrange("p (b t) -> p b t", t=4),
                                                          in1=sig[:, 0:64].rearrange("p (b t) -> p b t", t=4), op=ALU.mult), reads=[pa, 'sig'], writes=['Us'])

                for _ in front(64, x_s, k_s, v_s, s_kT, s_vbf, s_u):
                    pass

                if STOP_AFTER == 0.5:
                    S.barrier(); S.finish()
                    return nc
                for hh in range(2):
                    bank = 4 if hh == 0 else 2
                    pk = ('ps', bank)
                    r0 = 64 * hh
                    for pr in range(4):
                        S.op('pe', lambda e, pr=pr, r0=r0, bank=bank: e.matmul(PS[bank][:64, pr * 64:(pr + 1) * 64], lhsT=KTs[r0:r0 + 64, pr, :], rhs=QT[r0:r0 + 64, pr, 0:64], start=True, stop=True),
                             reads=['KTs', 'QT0'], writes=[pk], inc=(pr == 3))
                    S.op('dve', lambda e, hh=hh, bank=bank: e.scalar_tensor_tensor(out=pnewf[:, hh * 256:(hh + 1) * 256], in0=PS[bank][:64, 0:256], scalar=0.125, in1=bnew[:, :], op0=ALU.mult, op1=ALU.add),
                         reads=[pk, 'bnew'], writes=['pnewf'])
                S.op('act', lambda e: e.activation(out=pnewf[:], in_=pnewf[:], func=AF.Exp), reads=['pnewf'], writes=['pnewf'])
                pnv = pnewT[:].rearrange("p (b hh pr t) -> p b hh pr t", hh=2, pr=4, t=4)
                for hh in range(2):
                    S.op('dve', lambda e, hh=hh: e.tensor_tensor(out=pnv[:, :, hh, :, :].rearrange("p b pr t -> p pr b t"), in0=pnewf[:, hh * 256:(hh + 1) * 256].rearrange("p (pr b t) -> p pr b t", pr=4, t=4),
                                                                in1=mnew[:, :].rearrange("p (pr b t) -> p pr b t", pr=4, t=4), op=ALU.mult), reads=['pnewf', 'mnew'], writes=['pnewT'])
                if STOP_AFTER == 0.6:
                    S.barrier(); S.finish()
                    return nc

                SB = [5, 3]
                for b in range(NS):
                    for t16 in range(16):
                        tb = 6 + (t16 % 2)
                        tpk = ('ps', tb)
                        for pr in range(4):
                            S.op('pe', lambda e, pr=pr, t16=t16, tb=tb: e.transpose(out=PSB[tb][:, pr * 128:(pr + 1) * 128], in_=kall[:, t16, pr * 128:(pr + 1) * 128], identity=identb[:]),
                                 reads=['kall', 'identb'], writes=[tpk], inc=(pr == 3))
                        kt = ktp[t16 % 3]
                        kk = ('ktp', t16 % 3)
                        if t16 % 2 == 0:
                            S.op('act', lambda e, kt=kt, tb=tb: e.copy(out=kt[:].rearrange("p a k -> p (a k)"), in_=PSB[tb][:, 0:512]), reads=[tpk], writes=[kk])
                        else:
                            S.op('dve', lambda e, kt=kt, tb=tb: e.tensor_copy(out=kt[:].rearrange("p a k -> p (a k)"), in_=PSB[tb][:, 0:512]), reads=[tpk], writes=[kk])
                        for hh in range(2):
                            r0 = 64 * hh
                            for pr in range(4):
                                S.op('pe', lambda e, pr=pr, hh=hh, r0=r0, kt=kt, t16=t16, b=b: e.matmul(PS[SB[hh]][:, t16 * 16 + pr * 4:t16 * 16 + pr * 4 + 4], lhsT=kt[r0:r0 + 64, pr, :],
                                                                                               rhs=QT[r0:r0 + 64, pr, 4 * b:4 * b + 4], start=(t16 == 0 and pr == 0), stop=False, skip_group_check=True),
                                     reads=[kk, 'QT0'], writes=[('ps', SB[hh])], inc=False)
                    if b + 1 < NS:
                        gather(b + 1, 0)
                    for hh in range(2):
                        r0 = 64 * hh
                        S.op('pe', lambda e, hh=hh, r0=r0: e.matmul(PS[SB[hh]][:, 0:256], lhsT=auglsb[r0:r0 + 3, :], rhs=augrsb[r0:r0 + 3, :], start=False, stop=True, skip_group_check=True),
                             reads=['auglsb', 'augrsb'], writes=[('ps', SB[hh])])
                    for hh in range(2):
                        S.op('act', lambda e, hh=hh: e.activation(out=PTs[:, hh * 256:(hh + 1) * 256], in_=PS[SB[hh]][:, 0:256], func=AF.Exp, scale=0.125), reads=[('ps', SB[hh])], writes=['PTs'])
                    pk4 = ('ps', 4)
                    for hh in range(2):
                        for t16 in range(16):
                            S.op('pe', lambda e, t16=t16, hh=hh: e.matmul(PS[4][:, hh * 16:(hh + 1) * 16], lhsT=onesb[:, :], rhs=PTs[:, hh * 256 + t16 * 16:hh * 256 + (t16 + 1) * 16], start=(t16 == 0), stop=False),
                                 reads=['PTs', 'onesb'], writes=[pk4], inc=False)
                        S.op('pe', lambda e, b=b, hh=hh: e.matmul(PS[4][:, hh * 16:(hh + 1) * 16], lhsT=onesb[0:64, :], rhs=pnv[:, b, hh, :, :], start=False, stop=True),
                             reads=['pnewT', 'onesb'], writes=[pk4], inc=False)
                    for dh in range(4):
                        for hh in range(2):
                            oc = 64 + dh * 8 + hh * 4
                            for t16 in range(16):
                                S.op('pe', lambda e, dh=dh, t16=t16, hh=hh, oc=oc: e.matmul(PS[4][:, oc:oc + 4], lhsT=vall[:, t16, dh * 128:(dh + 1) * 128],
                                                                                         rhs=PTs[:, hh * 256 + t16 * 16 + dh * 4:hh * 256 + t16 * 16 + dh * 4 + 4], start=(t16 == 0), stop=False),
                                     reads=['PTs', 'vall'], writes=[pk4], inc=False)
                            S.op('pe', lambda e, dh=dh, b=b, hh=hh, oc=oc: e.matmul(PS[4][:, oc:oc + 4], lhsT=Vsb[0:64, dh * 128:(dh + 1) * 128], rhs=pnv[:, b, hh, dh, :], start=False, stop=True),
                                 reads=['pnewT', 'Vsb'], writes=[pk4], inc=(dh == 3 and hh == 1))
                    if b + 1 < NS:
                        gather(b + 1, 1)
                    rsv = rsum[:].rearrange("p (d h q) -> p d h q", h=2, q=4)
                    for hh in range(2):
                        S.op('dve', lambda e, hh=hh: e.reciprocal(out=rsv[:, :, hh, :], in_=PS[4][:, hh * 16:(hh + 1) * 16].rearrange("p (d q) -> p d q", q=4)), reads=[pk4], writes=['rsum'])
                    S.op('dve', lambda e: e.tensor_tensor(out=on32[:], in0=PS[4][:, 64:96], in1=rsum[:], op=ALU.mult), reads=[pk4, 'rsum'], writes=['on32'])
                    onv = on32[:].rearrange("p (d h q) -> p d h q", h=2, q=4)
                    S.op('dve', lambda e, b=b: e.scalar_tensor_tensor(out=osall[:, :, 4 * b:4 * b + 4], in0=onv[:, :, 1, :], scalar=neglam[:, 0:1], in1=onv[:, :, 0, :], op0=ALU.mult, op1=ALU.add),
                         reads=['on32', 'neglam'], writes=['osall'])

                if STOP_AFTER == 0.7:
                    S.barrier(); S.finish()
                    return nc
                S.op('dve', lambda e: e.tensor_tensor(out=ossq[:], in0=osall[:], in1=osall[:], op=ALU.mult), reads=['osall'], writes=['ossq'])
                pk = ('ps', 6)
                S.op('pe', lambda e: e.matmul(PS[6][:, 0:256], lhsT=onesf[:, :], rhs=ossq[:].rearrange("p a t -> p (a t)"), start=True, stop=True), reads=['ossq', 'onesf'], writes=[pk])
                rsqrt(rstd_s[:], PS[6][:, 0:256], [pk], ['rstd_s'], scale=1.0 / 128)
                S.op('dve', lambda e: e.scalar_tensor_tensor(out=mixT[:, 0:4, 0:64], in0=osall[:], scalar=sg8[:, 0:1], in1=rstd_s[:].rearrange("p (a t) -> p a t", t=64), op0=ALU.mult, op1=ALU.mult),
                     reads=['osall', 'sg8', 'rstd_s'], writes=['mixT'])

                if STOP_AFTER == 0.8:
                    S.barrier(); S.finish()
                    return nc
                acc4 = [acc[0][:, 0:64], acc[1][:, 0:64], osq_s[:, 0:64], osq_s[:, 64:128]]
                accv = [a.rearrange("p (b t) -> p b t", t=4) for a in acc4]
                for c in range(4):
                    S.op('dve', lambda e, c=c: e.tensor_scalar(out=accv[c], in0=Us[:, c, :, 0:4], scalar1=cw[:, c, 0:1], scalar2=cb[:, c:c + 1], op0=ALU.mult, op1=ALU.add),
                         reads=['Us', 'cw', 'cb'], writes=[('acc', c)])
                for j in range(1, 31):
                    for c in range(4):
                        dst = accv[c] if j < 30 else cvs[:, c, :].rearrange("p (b t) -> p b t", t=4)
                        S.op('dve', lambda e, c=c, j=j, dst=dst: e.scalar_tensor_tensor(out=dst, in0=Us[:, c, :, j:j + 4], scalar=cw[:, c, j:j + 1], in1=accv[c], op0=ALU.mult, op1=ALU.add),
                             reads=['Us', ('acc', c)], writes=[('acc', c)] if j < 30 else ['cvs'])
                conv_ln_silu(64, lambda c: cvs[:, c, :], ['cvs'])
                for c in range(4):
                    S.op('act', lambda e, c=c: e.copy(out=ucont[:, c, :, :], in_=Us[:, c, :, 4:34]), reads=['Us'], writes=['kall'])
                for g in range(4):
                    pk = ('ps', 1)
                    for c in range(4):
                        S.op('pe', lambda e, c=c, g=g: e.transpose(out=PS[1][:120, c * 128:(c + 1) * 128], in_=ucont[:, c, 4 * g:4 * g + 4, :], identity=ident[:]),
                             reads=['kall', 'ident'], writes=[pk], inc=(c == 3))
                    S.op('act', lambda e: e.copy(out=stc[:, :], in_=PS[1][:120, :]), reads=[pk], writes=['stc'])
                    S.dma('sp', lambda e, g=g: e.dma_start(out=conv_s[g * 120:(g + 1) * 120, :], in_=stc[:, :]), reads=['stc'], writes=[])
                if STOP_AFTER == 0.9:
                    S.barrier(); S.finish()
                    return nc
                out_ln1(64, T)
            S.barrier()
            if STOP_AFTER == 1:
                S.op('dve', lambda e: e.tensor_copy(out=htmp[:, :], in_=mixT[:].rearrange("p a t -> p (a t)")), writes=['htmp'])
                S.dma('sp', lambda e: e.dma_start(out=y_p[0:128, :], in_=htmp[:, :]), reads=['htmp'], writes=[])
                S.finish()
                return nc

            with ExitStack() as e2:
                with ExitStack() as et:
                    modP = sb("modP", [17, 3 * D], stack=et)
                    S.dma('sp', lambda e: e.dma_start(out=modP[:, :], in_=mod_scr[:, 0:3 * D]), writes=['mod'])
                    set_mod_tiles(128, selp, modP, 1024, 0, 2048)
                    S.barrier()
                KT = sb("KT", [128, 4, T], BF16, stack=e2)
                Vext = sb("Vext", [128, NBLK, 4, 130], BF16, stack=e2)
                auglpb = sb("auglpb", [128, 512], BF16, stack=e2); augrpb = sb("augrpb", [128, 4096], BF16, stack=e2)
                osb = sb("osb", [128, 8, 128], stack=e2); o4 = sb("o4", [128, 4, 128], stack=e2); osq = sb("osq", [128, 4, 128], stack=e2)
                rs8 = sb("rs8", [128, 16], stack=e2)
                Ub = [sb("U%d" % i, [128, 4, 30 + 128], stack=e2) for i in range(3)]
                PT = [sb("PT%d" % i, [128, 512], BF16, stack=e2) for i in range(3)]
                cva = [sb("cva%d" % i, [128, 128], stack=e2) for i in range(4)]
                cpo = sb("cpo", [30, 512], stack=e2)
                rbuf_p = sb("rbuf", [128, D], stack=e2); x1o_p = sb("x1o", [128, D], stack=e2)

                for r0 in (0, 64):
                    S.dma('pool', lambda e, r0=r0: e.dma_start(out=auglpb[r0:r0 + 3, :], in_=c_auglp), writes=['auglpb'])
                    S.dma('pool', lambda e, r0=r0: e.dma_start(out=augrpb[r0:r0 + 3, :], in_=c_augrp), writes=['augrpb'])
                S.op('dve', lambda e: e.memset(Ub[0][:], 0.0), writes=['U'])
                S.op('dve', lambda e: e.memset(Ub[1][:], 0.0), writes=['U'])
                S.op('dve', lambda e: e.memset(Ub[2][:], 0.0), writes=['U'])
                S.op('pool', lambda e: e.memset(Vext[:].rearrange("p a b c -> p (a b c)"), 1.0), writes=['Vext'])

                ptcnt = [0]

                def p_front(i):
                    par = i % 2
                    up_i = i % 3
                    un_i = (i + 1) % 3
                    Up = Ub[up_i]
                    Un = Ub[un_i]

                    def p_vbf(st, sk):
                        S.op('pool', lambda e: e.tensor_copy(out=Vext[:, i, :, 0:128], in_=st[:, :].rearrange("p (a e) -> p a e", e=128)), reads=[sk], writes=['Vext'])

                    def p_kT(pr, bank, pk):
                        S.op('dve', lambda e: e.tensor_copy(out=KT[:, pr, i * 128:(i + 1) * 128], in_=PS[bank][:, 0:128]), reads=[pk], writes=['KT'])

                    def p_u(c, pa):
                        S.op('dve', lambda e: e.tensor_tensor(out=Up[:, c, 30:158], in0=PS[0][:, 0:128], in1=sig[:, 0:128], op=ALU.mult), reads=[pa, 'sig'], writes=[('U', up_i, c)])
                        S.op('act', lambda e: e.copy(out=Un[:, c, 0:30], in_=Up[:, c, 128:158]), reads=[('U', up_i, c)], writes=[('U', un_i, c)])

                    yield from front(128, x_p[i * 128:(i + 1) * 128, :], k_p[i * 128:(i + 1) * 128, :], v_p[i * 128:(i + 1) * 128, :], p_kT, p_vbf, p_u, par=par)

                def p_back(i):
                    par = i % 2
                    ui = i % 3
                    U = Ub[ui]
                    QT = QTb[par]
                    qk = 'QT%d' % par
                    for c in range(4):
                        S.op('dve', lambda e, c=c: e.tensor_scalar(out=cva[c][:, :], in0=U[:, c, 0:128], scalar1=cw[:, c, 0:1], scalar2=cb[:, c:c + 1], op0=ALU.mult, op1=ALU.add),
                             reads=[('U', ui, c), 'U', 'cw', 'cb'], writes=[('cva', c)])
                    taps_left = list(range(1, 31))

                    def emit_taps(n):
                        for _ in range(n):
                            if not taps_left:
                                return
                            j = taps_left.pop(0)
                            for c in range(4):
                                S.op('dve', lambda e, c=c, j=j: e.scalar_tensor_tensor(out=cva[c][:, :], in0=U[:, c, j:j + 128], scalar=cw[:, c, j:j + 1], in1=cva[c][:, :], op0=ALU.mult, op1=ALU.add),
                                     reads=[('U', ui, c), ('cva', c)], writes=[('cva', c)])
                    emit_taps(2)
                    yield
                    if i == NBLK - 1:
                        pk = ('ps', 4)
                        for c in range(4):
                            S.op('pe', lambda e, c=c: e.transpose(out=PS[4][:30, c * 128:(c + 1) * 128], in_=U[:, c, 128:158], identity=ident[:]),
                                 reads=[('U', ui, c), 'U', 'ident'], writes=[pk], inc=(c == 3))
                        S.op('act', lambda e: e.copy(out=cpo[:, :], in_=PS[4][:30, :]), reads=[pk], writes=['cpo'])
                        S.dma('sp', lambda e: e.dma_start(out=conv_p, in_=cpo[:, :]), reads=['cpo'], writes=[])

                    groups = []
                    for h in range(8):
                        for g in range((i + 4) // 4):
                            groups.append((h, g))

                    def emit_scores(h, g):
                        r0 = 64 * (h % 2)
                        pr = h // 2
                        j0 = 4 * g
                        nb = min(4, i + 1 - j0)
                        sbank = 3 + (ptcnt[0] % 2)
                        spk = ('ps', sbank)
                        for jj in range(nb):
                            S.op('pe', lambda e, jj=jj: e.matmul(PS[sbank][:, jj * 128:(jj + 1) * 128], lhsT=KT[r0:r0 + 64, pr, (j0 + jj) * 128:(j0 + jj + 1) * 128],
                                                                 rhs=QT[r0:r0 + 64, pr, 0:128], start=(jj == 0), stop=False, skip_group_check=True),
                                 reads=['KT', qk], writes=[spk], inc=False)
                        g0 = j0 - i + 16
                        S.op('pe', lambda e: e.matmul(PS[sbank][:, 0:nb * 128], lhsT=auglpb[r0:r0 + 3, pr * 128:(pr + 1) * 128], rhs=augrpb[r0:r0 + 3, g0 * 128:(g0 + nb) * 128], start=False, stop=True, skip_group_check=True),
                             reads=['auglpb', 'augrpb'], writes=[spk])
                        pt = PT[ptcnt[0] % 3]
                        ptk = ('PT', ptcnt[0] % 3)
                        ptcnt[0] += 1
                        S.op('act', lambda e: e.activation(out=pt[:, 0:nb * 128], in_=PS[sbank][:, 0:nb * 128], func=AF.Exp, scale=0.125), reads=[spk], writes=[ptk])
                        if j0 + nb - 1 == i:
                            S.op('pool', lambda e: e.tensor_tensor(out=pt[:, (nb - 1) * 128:nb * 128], in0=pt[:, (nb - 1) * 128:nb * 128], in1=maskTb[:, :], op=ALU.mult),
                                 reads=[ptk, 'maskTb'], writes=[ptk])
                        return (pt, ptk, j0, nb)

                    def emit_pv(h, st):
                        pt, ptk, j0, nb = st
                        pr = h // 2
                        ob = 5 + h // 3
                        ocol = (h % 3) * 160
                        opk = ('ps', ob)
                        for jj in range(nb):
                            j = j0 + jj
                            S.op('pe', lambda e, jj=jj, j=j: e.matmul(PS[ob][:, ocol:ocol + 129], lhsT=pt[:, jj * 128:(jj + 1) * 128], rhs=Vext[:, j, pr, 0:129], start=(j == 0), stop=(j == i)),
                                 reads=[ptk, 'Vext'], writes=[opk], inc=(j == i))

                    st_prev = emit_scores(*groups[0])
                    for gi, (h, g) in enumerate(groups):
                        st_next = emit_scores(*groups[gi + 1]) if gi + 1 < len(groups) else None
                        emit_pv(h, st_prev)
                        st_prev = st_next
                        if g == (i + 4) // 4 - 1:
                            emit_taps(4)
                            yield
                    emit_taps(31)
                    yield
                    attn_epilogue_tok(128, 5, 0)
                    yield
                    conv_ln_silu(128, lambda c: cva[c][:, :], [('cva', c) for c in range(4)], bA=4, bB=3)
                    yield
                    out_ln1(128, i * 128, par=par, rbuf=rbuf_p, x1o=x1o_p, b0=3)

                def run_interleaved(gens):
                    gens = list(gens)
                    while gens:
                        for g in list(gens):
                            try:
                                next(g)
                            except StopIteration:
                                gens.remove(g)

                run_interleaved([p_front(0)])
                for i in range(NBLK):
                    gl = [p_back(i)]
                    if i + 1 < NBLK:
                        gl.append(p_front(i + 1))
                    run_interleaved(gl)
            S.barrier()
            if STOP_AFTER == 2:
                S.finish()
                return nc

        with ExitStack() as ef:
            W_up = sb("W_up", [128, 8, 2 * DFF], BF16, stack=ef)
            W_dn = sb("W_dn", [128, NCH, D], BF16, stack=ef)
            modst = [sb("modst%d" % i, [17, 512], stack=ef) for i in range(2)]
            g2 = sb("g2", [128, D], stack=ef); b2 = sb("b2", [128, D], stack=ef)
            SC2 = sb("SC2", [128, D], stack=ef); SH2 = sb("SH2", [128, D], stack=ef); GT2 = sb("GT2", [128, D], stack=ef)
            x1b = [sb("x1t%d" % i, [128, D], stack=ef) for i in range(2)]; ht2 = sb("ht2", [128, D], stack=ef); rb2 = sb("rb2", [128, D], stack=ef)
            h2T = sb("h2T", [128, 8, 128], BF16, stack=ef)
            gTb = [sb("gT%d" % i, [128, NCH, 128], BF16, stack=ef) for i in range(2)]
            carry = sb("carry", [128, 44, 2], stack=ef)
            carrys = sb("carrys", [128, 44, NS, 2], stack=ef)
            ua = [sb("ua%d" % i, [128, 130], stack=ef) for i in range(4)]
            tt = [sb("tt%d" % i, [128, 128], stack=ef) for i in range(8)]
            uas = [sb("uas%d" % i, [128, NS, 6], stack=ef) for i in range(2)]
            sfc = [sb("sfc%d" % i, [32, 512], stack=ef) for i in range(2)]

            w_up_v = w_up.rearrange("(k p) n -> p k n", p=128)
            for i in range(8):
                S.dma('pool', lambda e, i=i: e.dma_start(out=W_up[:, i:i + 1, :], in_=w_up_v[:, i:i + 1, :]), writes=['W_up'])
            w_dn_v = w_down.rearrange("(c p) n -> p c n", p=128)
            for i in range(2):
                S.dma('pool', lambda e, i=i: e.dma_start(out=W_dn[:, 11 * i:11 * i + 11, :], in_=w_dn_v[:, 11 * i:11 * i + 11, :]), writes=['W_dn'])
            S.dma('sp', lambda e: e.dma_start(out=g2[:], in_=ln2_g.partition_broadcast(128)), writes=['g2'])
            S.dma('sp', lambda e: e.dma_start(out=b2[:], in_=ln2_b.partition_broadcast(128)), writes=['b2'])
            S.op('dve', lambda e: e.memset(carry[:], 0.0), writes=['carry'])
            for g in range(11):
                pk = ('ps', 0)
                sf = sfc[g % 2]
                sfk = ('sfc', g % 2)
                S.dma('sp', lambda e, g=g, sf=sf: e.dma_start(out=sf[:, :], in_=st_ffn[:, g * 512:(g + 1) * 512]), writes=[sfk])
                for c in range(4):
                    S.op('pe', lambda e, c=c, sf=sf: e.transpose(out=PS[0][:, c * 128:c * 128 + 32], in_=sf[:, c * 128:(c + 1) * 128], identity=ident[:32, :32]),
                         reads=[sfk, 'ident'], writes=[pk], inc=(c == 3))
                S.op('dve', lambda e, g=g: e.tensor_copy(out=carrys[:, 4 * g:4 * g + 4, :, :], in_=PS[0][:, :].rearrange("p (c x) -> p c x", x=128)[:, :, 0:32].rearrange("p c (b t) -> p c b t", t=2)),
                     reads=[pk], writes=['carrys'])

            def set_mod2(P, sel):
                selk = 'selp' if sel is selp else 'sels'
                n = 0
                for (dst, key, off, one, bank) in [(SH2, 'SH2', 0, False, 4), (SC2, 'SC2', 1024, True, 5), (GT2, 'GT2', 2048, False, 4)]:
                    for hf in range(2):
                        mt = modst[n % 2]
                        mk = ('modst', n % 2)
                        n += 1
                        c0 = 3 * D + off + hf * 512
                        S.dma('sp', lambda e, mt=mt, c0=c0: e.dma_start(out=mt[:, :], in_=mod_scr[:, c0:c0 + 512]), writes=[mk])
                        pk = ('ps', bank)
                        S.op('pe', lambda e, mt=mt, bank=bank: e.matmul(PS[bank][:P, :], lhsT=sel[:, :P], rhs=mt[:, :], start=True, stop=True), reads=[selk, mk], writes=[pk])
                        if one:
                            S.op('dve', lambda e, hf=hf, dst=dst, bank=bank: e.tensor_scalar(out=dst[:P, hf * 512:(hf + 1) * 512], in0=PS[bank][:P, :], scalar1=1.0, scalar2=None, op0=ALU.add),
                                 reads=[pk], writes=[key])
                        else:
                            S.op('dve', lambda e, hf=hf, dst=dst, bank=bank: e.tensor_copy(out=dst[:P, hf * 512:(hf + 1) * 512], in_=PS[bank][:P, :]), reads=[pk], writes=[key])

            def ffn_s1a(P, row0, par):
                x1t = x1b[par]
                xk = 'x1t%d' % par
                S.dma('sp', lambda e: e.dma_start(out=x1t[:P, :], in_=x1_scr[row0:row0 + P, :]), reads=['x1scr'], writes=[xk])
                S.op('dve', lambda e: e.tensor_tensor(out=ht2[:P, :], in0=x1t[:P, :], in1=SC2[:P, :], op=ALU.mult), reads=[xk, 'SC2'], writes=['ht2'])
                S.op('dve', lambda e: e.tensor_tensor(out=ht2[:P, :], in0=ht2[:P, :], in1=SH2[:P, :], op=ALU.add), reads=['ht2', 'SH2'], writes=['ht2'])
                yield

            def ffn_s1b(P):
                transpose_to(P, ht2[:P, :], ['ht2'], 8, lambda c0, n: h2T[:, c0:c0 + n, 0:P], ['h2T'], [0, 1])

            def ffn_s2(P, sample, gpar):
                gT = gTb[gpar]
                gk = 'gT%d' % gpar
                pend_sm = []

                def emit_sm(c, res):
                    (ca, ka), (cb_, kb) = res
                    S.op('act', lambda e: e.activation(out=ca, in_=ca, func=AF.Silu), reads=[ka], writes=[ka])
                    S.op('pool', lambda e: e.tensor_tensor(out=gT[:, c, 0:P], in0=ca, in1=cb_, op=ALU.mult), reads=[ka, kb], writes=[gk])

                for c in range(NCH):
                    res = []
                    taps = []
                    for half in range(2):
                        ch = c + NCH * half
                        bank = 2 + ((2 * c + half) % 4)
                        pk = ('ps', bank)
                        col = ch * 128
                        for k in range(8):
                            S.op('pe', lambda e, k=k, col=col, bank=bank: e.matmul(PS[bank][:, 0:P], lhsT=W_up[:, k, col:col + 128], rhs=h2T[:, k, 0:P], start=(k == 0), stop=(k == 7)),
                                 reads=['h2T', 'W_up'], writes=[pk], inc=(k == 7))
                        ti = (2 * c + half) % 4
                        t0 = tt[ti]; t1 = tt[4 + ti]
                        k0 = ('tt', ti); k1 = ('tt', 4 + ti)
                        if not sample:
                            u = ua[ti]
                            uk = ('ua', ti)
                            S.op('act', lambda e, u=u, bank=bank: e.copy(out=u[:, 2:130], in_=PS[bank][:, 0:128]), reads=[pk], writes=[uk])
                            S.op('pool', lambda e, u=u, ch=ch: e.tensor_copy(out=u[:, 0:2], in_=carry[:, ch, :]), reads=['carry'], writes=[uk])
                            S.op('act', lambda e, t0=t0, bank=bank, ch=ch: e.activation(out=t0[:, :], in_=PS[bank][:, 0:128], func=AF.Identity, bias=fcb[:, ch:ch + 1], scale=fcw[:, ch, 2:3]),
                                 reads=[pk, 'fcw', 'fcb'], writes=[k0])
                            taps.append((t0, t1, u, ch, uk, k0, k1))
                            res.append((t0[:, 0:P], k0))
                        else:
                            u = uas[half]
                            uk = ('uas', half)
                            S.op('act', lambda e, u=u, bank=bank: e.copy(out=u[:, :, 2:6], in_=PS[bank][:, 0:64].rearrange("p (b t) -> p b t", t=4)), reads=[pk], writes=[uk])
                            S.op('pool', lambda e, u=u, ch=ch: e.tensor_copy(out=u[:, :, 0:2], in_=carrys[:, ch, :, :]), reads=['carrys'], writes=[uk])
                            t0v = t0[:, 0:64].rearrange("p (b t) -> p b t", t=4)
                            t1v = t1[:, 0:64].rearrange("p (b t) -> p b t", t=4)
                            S.op('act', lambda e, t0=t0, bank=bank, ch=ch: e.activation(out=t0[:, 0:64], in_=PS[bank][:, 0:64], func=AF.Identity, bias=fcb[:, ch:ch + 1], scale=fcw[:, ch, 2:3]),
                                 reads=[pk, 'fcw', 'fcb'], writes=[k0])
                            S.op('dve', lambda e, t0v=t0v, t1v=t1v, u=u, ch=ch: e.scalar_tensor_tensor(out=t1v, in0=u[:, :, 1:5], scalar=fcw[:, ch, 1:2], in1=t0v, op0=ALU.mult, op1=ALU.add),
                                 reads=[uk, k0, 'fcw'], writes=[k1])
                            S.op('dve', lambda e, t0v=t0v, t1v=t1v, u=u, ch=ch: e.scalar_tensor_tensor(out=t0v, in0=u[:, :, 0:4], scalar=fcw[:, ch, 0:1], in1=t1v, op0=ALU.mult, op1=ALU.add),
                                 reads=[uk, k1, 'fcw'], writes=[k0])
                            S.op('pool', lambda e, u=u, ch=ch: e.tensor_copy(out=carrys[:, ch, :, :], in_=u[:, :, 4:6]), reads=[uk], writes=['carrys'])
                            res.append((t0[:, 0:P], k0))
                    for (t0, t1, u, ch, uk, k0, k1) in taps:
                        S.op('dve', lambda e, t0=t0, t1=t1, u=u, ch=ch: e.scalar_tensor_tensor(out=t1[:, :], in0=u[:, 1:129], scalar=fcw[:, ch, 1:2], in1=t0[:, :], op0=ALU.mult, op1=ALU.add),
                             reads=[uk, k0, 'fcw'], writes=[k1])
                    for (t0, t1, u, ch, uk, k0, k1) in taps:
                        S.op('dve', lambda e, t0=t0, t1=t1, u=u, ch=ch: e.scalar_tensor_tensor(out=t0[:, :], in0=u[:, 0:128], scalar=fcw[:, ch, 0:1], in1=t1[:, :], op0=ALU.mult, op1=ALU.add),
                             reads=[uk, k1, 'fcw'], writes=[k0])
                        S.op('pool', lambda e, u=u, ch=ch: e.tensor_copy(out=carry[:, ch, :], in_=u[:, 128:130]), reads=[uk], writes=['carry'])
                    if pend_sm:
                        emit_sm(*pend_sm.pop())
                    pend_sm.append((c, res))
                    yield
                emit_sm(*pend_sm.pop())
                yield
            def ffn_s3(P, par, gpar, y_dst):
                x1t = x1b[par]
                xk = 'x1t%d' % par
                gT = gTb[gpar]
                gk = 'gT%d' % gpar
                for hf in range(2):
                    bank = 6 + hf
                    pk = ('ps', bank)
                    for c in range(NCH):
                        S.op('pe', lambda e, c=c, hf=hf, bank=bank: e.matmul(PS[bank][:P, :], lhsT=gT[:, c, 0:P], rhs=W_dn[:, c, hf * 512:(hf + 1) * 512], start=(c == 0), stop=(c == NCH - 1)),
                             reads=[gk, 'W_dn'], writes=[pk], inc=(c == NCH - 1))
                        if c % 4 == 3:
                            yield
                    S.op('dve', lambda e, hf=hf, bank=bank: e.tensor_tensor(out=rb2[:P, hf * 512:(hf + 1) * 512], in0=PS[bank][:P, :], in1=GT2[:P, hf * 512:(hf + 1) * 512], op=ALU.mult),
                         reads=[pk, 'GT2'], writes=['rb2'])
                    yield
                S.op('dve', lambda e: e.scalar_tensor_tensor(out=rb2[:P, :], in0=x1t[:P, :], scalar=ALPHA, in1=rb2[:P, :], op0=ALU.mult, op1=ALU.add), reads=[xk, 'rb2'], writes=['rb2'])
                yield
                layer_norm_rows(P, rb2[:P, :], x1t[:P, :], ['rb2'], [xk], 48)
                yield
                S.op('pool', lambda e: e.tensor_tensor(out=x1t[:P, :], in0=x1t[:P, :], in1=g2[:P, :], op=ALU.mult), reads=[xk, 'g2'], writes=[xk])
                S.op('pool', lambda e: e.tensor_tensor(out=x1t[:P, :], in0=x1t[:P, :], in1=b2[:P, :], op=ALU.add), reads=[xk, 'b2'], writes=[xk])
                S.dma('sp', lambda e: e.dma_start(out=y_dst, in_=x1t[:P, :]), reads=[xk], writes=[])
                yield

            def state_out(src_fn, nrow, dst):
                for g in range(11):
                    pk = ('ps', 1)
                    for c in range(4):
                        ch = 4 * g + c
                        S.op('pe', lambda e, c=c, ch=ch: e.transpose(out=PS[1][:nrow, c * 128:(c + 1) * 128], in_=src_fn(ch), identity=ident[:]),
                             reads=['carry', 'carrys', 'ident'], writes=[pk], inc=(c == 3))
                    sf = sfc[g % 2]
                    sfk = ('sfc', g % 2)
                    S.op('act', lambda e, sf=sf: e.copy(out=sf[:nrow, :], in_=PS[1][:nrow, :]), reads=[pk], writes=[sfk])
                    S.dma('sp', lambda e, g=g, sf=sf: e.dma_start(out=dst[:, g * 512:(g + 1) * 512], in_=sf[:nrow, :]), reads=[sfk], writes=[])

            def run_il(gens):
                gens = list(gens)
                while gens:
                    for g in list(gens):
                        try:
                            next(g)
                        except StopIteration:
                            gens.remove(g)

            def chain(*gs):
                for g in gs:
                    yield from g

            def delayed(n, g):
                for _ in range(n):
                    yield
                yield from g

            set_mod2(64, sels)
            run_il([ffn_s1a(64, T, 0)])
            ffn_s1b(64)
            run_il([ffn_s2(64, True, 0)])
            run_il([ffn_s3(64, 0, 0, y_s)])
            state_out(lambda ch: carrys[:, ch, :, :], 32, ffn_s)
            set_mod2(128, selp)
            run_il([ffn_s1a(128, 0, 0)])
            ffn_s1b(128)
            for i in range(NBLK):
                gens = [ffn_s2(128, False, i % 2)]
                tailg = []
                if i > 0:
                    tailg.append(ffn_s3(128, (i - 1) % 2, (i - 1) % 2, y_p[(i - 1) * 128:i * 128, :]))
                if i + 1 < NBLK:
                    tailg.append(ffn_s1a(128, (i + 1) * 128, (i + 1) % 2))
                if tailg:
                    gens.append(delayed(1, chain(*tailg)))
                run_il(gens)
                if i + 1 < NBLK:
                    ffn_s1b(128)
            run_il([ffn_s3(128, (NBLK - 1) % 2, (NBLK - 1) % 2, y_p[(NBLK - 1) * 128:NBLK * 128, :])])
            state_out(lambda ch: carry[:, ch, :], 2, ffn_p)
            S.finish()
    return nc


def _consts():
    c = {}
    c["c_ident"] = np.eye(128, dtype=np.float32)
    k = np.arange(128)
    c["c_maskT"] = (k[:, None] <= k[None, :]).astype(np.float32)
    auglp = np.zeros((3, 4, 128), np.float32)
    for s, m in enumerate(SLOPES):
        auglp[0, s] = 8 * m * k
        auglp[1, s] = 8 * m
        auglp[2, s] = 1024 * m
    c["c_auglp"] = auglp.reshape(3, 512)
    augls = np.ones((3, 128), np.float32)
    augls[0] = k
    c["c_augls"] = augls
    augrp = np.zeros((3, 32, 128), np.float32)
    augrp[0] = 1.0
    augrp[1] = -k[None, :]
    augrp[2] = (np.arange(32) - 16)[:, None]
    c["c_augrp"] = augrp.reshape(3, 4096)
    augrs = np.zeros((3, 16, 4, 4), np.float32)
    mp = np.array(SLOPES, np.float32)
    augrs[0] = 128.0 * mp[None, :, None]
    augrs[1] = 8.0 * mp[None, :, None] * (np.arange(16)[:, None, None] - np.arange(4)[None, None, :])
    augrs[2] = -16384.0 * mp[None, :, None]
    c["c_augrs"] = augrs.reshape(3, 256)
    bnew = np.zeros((16, 4, 4, 16, 4), np.float32)
    mnew = np.zeros((16, 4, 4, 16, 4), np.float32)
    for b in range(16):
        for t1 in range(4):
            for t in range(t1, 4):
                for pr in range(4):
                    bnew[b, t1, pr, b, t] = -SLOPES[pr] * (t - t1)
                    mnew[b, t1, pr, b, t] = 1.0
    c["c_bnew"] = bnew.reshape(64, 256)
    c["c_mnew"] = mnew.reshape(64, 256)
    selp = np.zeros((17, 128), np.float32)
    selp[0] = 1.0
    sels = np.zeros((17, 64), np.float32)
    for b in range(16):
        sels[1 + b, 4 * b:4 * b + 4] = 1.0
    c["c_selp"] = selp
    c["c_sels"] = sels
    c["c_pm8"] = (k % 8).astype(np.float32).reshape(128, 1)
    return c


_NC = None
_LAST = None


def kernel(x_prompt, x_sample, c_prompt, c_sample, cache_k, cache_v, page_table, state_conv, state_ffn,
           ln_emb_g, ln_emb_b, w_ada, b_ada, w_in, lambda_q1, lambda_k1, lambda_q2, lambda_k2, subln_g,
           conv_w, conv_b, conv_ln_g, conv_ln_b, w_out, ln1_g, ln1_b,
           w_up, ffn_conv_w, ffn_conv_b, w_down, ln2_g, ln2_b):
    global _NC
    f = lambda a: np.ascontiguousarray(np.asarray(a, dtype=np.float32))
    if _NC is None:
        _NC = build_nc()
    nc = _NC
    shared = dict(_consts())
    shared["cache_k"] = f(cache_k).reshape(NPHYS * 8, 16 * 512)
    shared["cache_v"] = f(cache_v).reshape(NPHYS * 8, 16 * 512)
    shared["ln_emb_g"] = f(ln_emb_g).reshape(1, D); shared["ln_emb_b"] = f(ln_emb_b).reshape(1, D)
    shared["w_ada"] = f(w_ada)[0]; shared["b_ada"] = f(b_ada).reshape(1, 6 * D)
    shared["w_in"] = f(w_in)[0]
    shared["lam_in"] = np.concatenate([f(lambda_q1)[0], f(lambda_k1)[0], f(lambda_q2)[0], f(lambda_k2)[0]]).reshape(1, 256)
    shared["subln_g"] = f(subln_g).reshape(128, 1)
    shared["conv_w"] = np.ascontiguousarray(f(conv_w)[0].reshape(31, 4, 128).transpose(2, 1, 0))
    shared["conv_b"] = np.ascontiguousarray(f(conv_b)[0].reshape(4, 128).T)
    shared["conv_ln_g"] = np.ascontiguousarray(f(conv_ln_g)[0].reshape(4, 128).T)
    shared["conv_ln_b"] = np.ascontiguousarray(f(conv_ln_b)[0].reshape(4, 128).T)
    shared["w_out"] = f(w_out)[0]
    shared["ln1_g"] = f(ln1_g).reshape(1, D); shared["ln1_b"] = f(ln1_b).reshape(1, D)
    shared["w_up"] = f(w_up)[0]
    shared["ffn_cw"] = np.ascontiguousarray(f(ffn_conv_w)[0].reshape(3, 44, 128).transpose(2, 1, 0))
    shared["ffn_cb"] = np.ascontiguousarray(f(ffn_conv_b)[0].reshape(44, 128).T)
    shared["w_down"] = f(w_down)[0]
    shared["ln2_g"] = f(ln2_g).reshape(1, D); shared["ln2_b"] = f(ln2_b).reshape(1, D)
    xp = f(x_prompt); xs = f(x_sample); cp = f(c_prompt); cs = f(c_sample)
    pt = np.asarray(page_table, dtype=np.int32)
    sc = f(state_conv)[0]; sf = f(state_ffn)[0]
    in_maps = []
    for c in range(NCORES):
        m = dict(shared)
        m["x_p"] = xp[c]
        m["x_s"] = xs[NS * c:NS * (c + 1)].reshape(ST, D)
        m["c_all"] = np.concatenate([cp[c:c + 1], cs[NS * c:NS * (c + 1)]], axis=0)
        ptc = pt[NS * c:NS * (c + 1)]
        m["pt_lay"] = np.ascontiguousarray(np.repeat(ptc.T, 8, axis=0))
        m["st_conv"] = sc[NS * c:NS * (c + 1)].reshape(NS * 30, 512)
        m["st_ffn"] = sf[NS * c:NS * (c + 1)].reshape(NS * 2, 2 * DFF)
        in_maps.append(m)
    res = run_bass_kernel_spmd(nc, in_maps, core_ids=list(range(NCORES)))
    global _LAST
    _LAST = res
    R = res.results
    cat = lambda name: np.stack([R[c][name] for c in range(NCORES)], axis=0)
    y_prompt = cat("y_p")
    y_sample = cat("y_s").reshape(NCORES * NS, 4, D)
    k_prompt = cat("k_p").reshape(1, NCORES, T, 8, 64)
    v_prompt = cat("v_p").reshape(1, NCORES, T, 4, 128)
    conv_prompt = cat("conv_p").reshape(1, NCORES, 30, 512)
    ffn_prompt = cat("ffn_p").reshape(1, NCORES, 2, 2 * DFF)
    k_sample = cat("k_s").reshape(1, NCORES * NS, 4, 8, 64)
    v_sample = cat("v_s").reshape(1, NCORES * NS, 4, 4, 128)
    conv_sample = cat("conv_s").reshape(1, NCORES * NS, 30, 512)
    ffn_sample = cat("ffn_s").reshape(1, NCORES * NS, 2, 2 * DFF)
    return (y_prompt, y_sample, k_prompt, v_prompt, conv_prompt, ffn_prompt, k_sample, v_sample, conv_sample, ffn_sample)
```

```python
import math
from contextlib import ExitStack

import numpy as np
import concourse.bass as bass
import concourse.mybir as mybir
from concourse.bass_utils import run_bass_kernel_spmd

F32 = mybir.dt.float32
BF16 = mybir.dt.bfloat16
I32 = mybir.dt.int32
AF = mybir.ActivationFunctionType
ALU = mybir.AluOpType
AX = mybir.AxisListType

D = 1024
T = 2048
NBLK = 16
NS = 16
ST = 64
DFF = 2816
NCH = 22
EPS = 1e-5
ALPHA = 2.0 ** 0.25
LAMBDA_INIT = 0.8 - 0.6 * math.exp(0.0)
SLOPES = [2.0 ** (-8.0 * (i + 1) / 4) for i in range(4)]
NCORES = 8
NPHYS = 2560
STOP_AFTER = None
DEBUG_X1 = False


class Sched:
    NDMA = 12

    def __init__(self, nc, es):
        self.nc = nc
        self.engs = {'pe': nc.tensor, 'act': nc.scalar, 'dve': nc.vector, 'pool': nc.gpsimd, 'sp': nc.sync}
        self.sem = {}
        self.cnt = {}
        for e in ['pe', 'act', 'dve', 'pool']:
            self.sem[e] = es.enter_context(nc.semaphore("s_" + e))
            self.cnt[e] = 0
        self.dsem = {}
        self.dcnt = {}
        self.dnext = {}
        for q in ['sp', 'pool']:
            self.dsem[q] = [es.enter_context(nc.semaphore("d_%s%d" % (q, i))) for i in range(self.NDMA)]
            self.dcnt[q] = [0] * self.NDMA
            self.dnext[q] = 0
        self.seen = {e: {} for e in self.engs}
        self.reg = {}
        self.pend = {e: ([], []) for e in self.engs}
        self.semobj = {}
        for e in self.sem:
            self.semobj[('c', e)] = self.sem[e]
        for q in self.dsem:
            for i, s in enumerate(self.dsem[q]):
                self.semobj[('d', q, i)] = s

    def _r(self, k):
        if k not in self.reg:
            self.reg[k] = [None, []]
        return self.reg[k]

    def _wait(self, e, tok):
        sk, val = tok
        if self.seen[e].get(sk, 0) >= val:
            return
        self.engs[e].wait_ge(self.semobj[sk], val)
        self.seen[e][sk] = val

    def _deps(self, e, reads, writes):
        own = ('c', e)
        deps = {}

        def add(tok, same_ok):
            if tok is None:
                return
            if tok[0] == own and same_ok:
                return
            if deps.get(tok[0], 0) < tok[1]:
                deps[tok[0]] = tok[1]
        for k in reads:
            add(self._r(k)[0], False)
        for k in writes:
            r = self._r(k)
            add(r[0], True)
            for t in r[1]:
                add(t, True)
        for sk, v in deps.items():
            self._wait(e, (sk, v))

    def op(self, e, fn, reads=(), writes=(), inc=True):
        reads = list(reads)
        writes = list(writes)
        self._deps(e, reads, writes)
        ins = fn(self.engs[e])
        pr, pw = self.pend[e]
        if not inc:
            pr.extend(reads)
            pw.extend(writes)
            return ins
        self.cnt[e] += 1
        ins.then_inc(self.sem[e], 1)
        tok = (('c', e), self.cnt[e])
        for k in reads + pr:
            self._r(k)[1].append(tok)
        for k in writes + pw:
            r = self._r(k)
            r[0] = tok
            r[1] = []
        self.pend[e] = ([], [])
        return ins

    def dma(self, q, fn, reads=(), writes=()):
        reads = list(reads)
        writes = list(writes)
        self._deps(q, reads, writes)
        i = self.dnext[q]
        self.dnext[q] = (i + 1) % self.NDMA
        sk = ('d', q, i)
        if self.dcnt[q][i] > 0:
            self._wait(q, (sk, self.dcnt[q][i]))
        ins = fn(self.engs[q])
        self.dcnt[q][i] += 16
        ins.then_inc(self.dsem[q][i], 16)
        tok = (sk, self.dcnt[q][i])
        for k in reads:
            self._r(k)[1].append(tok)
        for k in writes:
            r = self._r(k)
            r[0] = tok
            r[1] = []
        return tok

    def barrier(self):
        toks = [(('c', e), self.cnt[e]) for e in self.sem if self.cnt[e] > 0]
        for q in self.dsem:
            for i in range(self.NDMA):
                if self.dcnt[q][i] > 0:
                    toks.append((('d', q, i), self.dcnt[q][i]))
        for e in self.engs:
            for t in toks:
                if t[0] == ('c', e):
                    continue
                self._wait(e, t)
        self.reg = {}

    def finish(self):
        for q in self.dsem:
            for i in range(self.NDMA):
                if self.dcnt[q][i] > 0:
                    self._wait('sp', (('d', q, i), self.dcnt[q][i]))


def build_nc():
    nc = bass.Bass("TRN2", target_bir_lowering=False)

    def din(name, shape, dt=F32):
        return nc.dram_tensor(name, list(shape), dt, kind="ExternalInput").ap()

    def dout(name, shape, dt=F32):
        return nc.dram_tensor(name, list(shape), dt, kind="ExternalOutput").ap()

    x_p = din("x_p", [T, D])
    x_s = din("x_s", [ST, D])
    c_all = din("c_all", [17, D])
    cache_k = din("cache_k", [NPHYS * 8, 16 * 512])
    cache_v = din("cache_v", [NPHYS * 8, 16 * 512])
    pt_lay = din("pt_lay", [128, NS], I32)
    st_conv = din("st_conv", [NS * 30, 512])
    st_ffn = din("st_ffn", [NS * 2, 2 * DFF])
    ln_emb_g = din("ln_emb_g", [1, D]); ln_emb_b = din("ln_emb_b", [1, D])
    w_ada = din("w_ada", [D, 6 * D]); b_ada = din("b_ada", [1, 6 * D])
    w_in = din("w_in", [D, 2560])
    lam_in = din("lam_in", [1, 256])
    subln_g = din("subln_g", [128, 1])
    conv_w = din("conv_w", [128, 4, 31])
    conv_b = din("conv_b", [128, 4])
    conv_ln_g = din("conv_ln_g", [128, 4]); conv_ln_b = din("conv_ln_b", [128, 4])
    w_out = din("w_out", [D, D])
    ln1_g = din("ln1_g", [1, D]); ln1_b = din("ln1_b", [1, D])
    w_up = din("w_up", [D, 2 * DFF])
    ffn_cw = din("ffn_cw", [128, 44, 3]); ffn_cb = din("ffn_cb", [128, 44])
    w_down = din("w_down", [DFF, D])
    ln2_g = din("ln2_g", [1, D]); ln2_b = din("ln2_b", [1, D])
    c_ident = din("c_ident", [128, 128])
    c_maskT = din("c_maskT", [128, 128])
    c_auglp = din("c_auglp", [3, 4 * 128]); c_augrp = din("c_augrp", [3, 4096])
    c_augls = din("c_augls", [3, 128]); c_augrs = din("c_augrs", [3, 256])
    c_bnew = din("c_bnew", [64, 256]); c_mnew = din("c_mnew", [64, 256])
    c_selp = din("c_selp", [17, 128]); c_sels = din("c_sels", [17, 64])
    c_pm8 = din("c_pm8", [128, 1])

    y_p = dout("y_p", [T, D]); y_s = dout("y_s", [ST, D])
    k_p = dout("k_p", [T, 512]); v_p = dout("v_p", [T, 512])
    conv_p = dout("conv_p", [30, 512]); ffn_p = dout("ffn_p", [2, 2 * DFF])
    k_s = dout("k_s", [ST, 512]); v_s = dout("v_s", [ST, 512])
    conv_s = dout("conv_s", [NS * 30, 512]); ffn_s = dout("ffn_s", [NS * 2, 2 * DFF])
    x1_scr = nc.dram_tensor("x1_scr", [T + ST, D], F32, kind=("ExternalOutput" if DEBUG_X1 else "Internal")).ap()

    mod_scr = nc.dram_tensor("mod_scr", [17, 6 * D], F32, kind="Internal").ap()

    es_outer = ExitStack()
    with es_outer as es:
        S = Sched(nc, es)

        def sb(name, shape, dt=F32, stack=None):
            return (stack or es).enter_context(nc.sbuf_tensor(name, list(shape), dt))

        def ps(name, shape, dt=F32, stack=None):
            return (stack or es).enter_context(nc.psum_tensor(name, list(shape), dt))

        PS = [ps("psb%d" % i, [128, 512]) for i in range(8)]
        PSB = [p[:].bitcast(BF16) for p in PS]

        ident = sb("ident", [128, 128]); identb = sb("identb", [128, 128], BF16)
        onesb = sb("onesb", [128, 128], BF16); onesf = sb("onesf", [128, 128])
        maskT = sb("maskT", [128, 128]); maskTb = sb("maskTb", [128, 128], BF16)
        selp = sb("selp", [17, 128]); sels = sb("sels", [17, 64])
        neglam = sb("neglam", [128, 1]); sg8 = sb("sg8", [128, 1])
        sg8row = sb("sg8row", [128, 128])
        cw = sb("cw", [128, 4, 31]); cb = sb("cb", [128, 4]); clg = sb("clg", [128, 4]); clb = sb("clb", [128, 4])
        fcw = sb("fcw", [128, 44, 3]); fcb = sb("fcb", [128, 44])
        small = sb("small", [128, 64])
        epsc = sb("epsc", [128, 1])

        S.dma('sp', lambda e: e.dma_start(out=ident[:], in_=c_ident), writes=['ident'])
        S.dma('sp', lambda e: e.dma_start(out=maskT[:], in_=c_maskT), writes=['maskT'])
        S.dma('sp', lambda e: e.dma_start(out=selp[:], in_=c_selp), writes=['selp'])
        S.dma('sp', lambda e: e.dma_start(out=sels[:], in_=c_sels), writes=['sels'])
        S.dma('sp', lambda e: e.dma_start(out=sg8[:], in_=subln_g), writes=['sg8'])
        S.dma('sp', lambda e: e.dma_start(out=sg8row[:], in_=subln_g.rearrange("p o -> o p").partition_broadcast(128)), writes=['sg8row'])
        S.dma('sp', lambda e: e.dma_start(out=cw[:], in_=conv_w), writes=['cw'])
        S.dma('sp', lambda e: e.dma_start(out=cb[:], in_=conv_b), writes=['cb'])
        S.dma('sp', lambda e: e.dma_start(out=clg[:], in_=conv_ln_g), writes=['clg'])
        S.dma('sp', lambda e: e.dma_start(out=clb[:], in_=conv_ln_b), writes=['clb'])
        S.dma('sp', lambda e: e.dma_start(out=fcw[:], in_=ffn_cw), writes=['fcw'])
        S.dma('sp', lambda e: e.dma_start(out=fcb[:], in_=ffn_cb), writes=['fcb'])
        S.op('dve', lambda e: e.tensor_copy(out=identb[:], in_=ident[:]), reads=['ident'], writes=['identb'])
        S.op('dve', lambda e: e.tensor_copy(out=maskTb[:], in_=maskT[:]), reads=['maskT'], writes=['maskTb'])
        S.op('dve', lambda e: e.memset(onesb[:], 1.0), writes=['onesb'])
        S.op('dve', lambda e: e.memset(onesf[:], 1.0), writes=['onesf'])
        S.op('dve', lambda e: e.memset(epsc[:], EPS), writes=['epsc'])
        S.op('dve', lambda e: e.tensor_scalar(out=sg8[:], in0=sg8[:], scalar1=1.0 - LAMBDA_INIT, scalar2=None, op0=ALU.mult), reads=['sg8'], writes=['sg8'])
        S.op('dve', lambda e: e.tensor_scalar(out=sg8row[:], in0=sg8row[:], scalar1=1.0 - LAMBDA_INIT, scalar2=None, op0=ALU.mult), reads=['sg8row'], writes=['sg8row'])

        def rsqrt(out_ap, in_ap, reads, writes, scale=1.0):
            S.op('act', lambda e: e.activation(out=out_ap, in_=in_ap, func=AF.Ln, bias=epsc[:in_ap.shape[0], 0:1], scale=scale), reads=list(reads) + ['epsc'], writes=writes)
            S.op('act', lambda e: e.activation(out=out_ap, in_=out_ap, func=AF.Exp, scale=-0.5), reads=writes, writes=writes)

        def layer_norm_rows(P, src_ap, dst_ap, src_keys, dst_keys, col):
            st6 = small[:P, col:col + 12].rearrange("p (c s) -> p c s", s=6)
            mv = small[:P, col + 12:col + 14]
            rstd = small[:P, col + 14:col + 15]
            nmr = small[:P, col + 15:col + 16]
            k = ('small', col)
            for c in range(2):
                S.op('dve', lambda e, c=c: e.bn_stats(out=st6[:, c, :], in_=src_ap[:, c * 512:(c + 1) * 512]),
                     reads=src_keys, writes=[k], inc=(c == 1))
            S.op('dve', lambda e: e.bn_aggr(out=mv, in_=st6), reads=[k], writes=[k])
            rsqrt(rstd, mv[:, 1:2], [k], [k])
            S.op('dve', lambda e: e.scalar_tensor_tensor(out=nmr, in0=mv[:, 0:1], scalar=-1.0, in1=rstd, op0=ALU.mult, op1=ALU.mult),
                 reads=[k], writes=[k])
            S.op('act', lambda e: e.activation(out=dst_ap, in_=src_ap, func=AF.Identity, bias=nmr, scale=rstd),
                 reads=src_keys + [k], writes=dst_keys)

        def transpose_to(P, src_ap, src_keys, nchunk, dst_fn, dst_keys, banks, evac=('act', 'dve')):
            done = 0
            gi = 0
            while done < nchunk:
                n = min(4, nchunk - done)
                bank = banks[gi % len(banks)]
                pk = ('ps', bank)
                for j in range(n):
                    c = done + j
                    S.op('pe', lambda e, c=c, j=j: e.transpose(out=PS[bank][:, j * 128:j * 128 + P], in_=src_ap[:, c * 128:(c + 1) * 128], identity=ident[:P, :P]),
                         reads=src_keys + ['ident'], writes=[pk], inc=(j == n - 1))
                eng = evac[gi % len(evac)]
                src = PS[bank][:, 0:n * 128].rearrange("p (n t) -> p n t", t=128)[:, :, 0:P]
                dst = dst_fn(done, n)
                if eng == 'act':
                    S.op('act', lambda e, src=src, dst=dst: e.copy(out=dst, in_=src), reads=[pk], writes=dst_keys)
                else:
                    S.op('dve', lambda e, src=src, dst=dst: e.tensor_copy(out=dst, in_=src), reads=[pk], writes=dst_keys)
                done += n
                gi += 1

        def bcast_rows(dst, P, sel, modt, c0, add_one, key, bank):
            selk = 'selp' if sel is selp else 'sels'
            for hf in range(2):
                pk = ('ps', bank)
                S.op('pe', lambda e, hf=hf: e.matmul(PS[bank][:P, :], lhsT=sel[:, :P], rhs=modt[:, c0 + hf * 512:c0 + (hf + 1) * 512], start=True, stop=True),
                     reads=[selk, 'mod'], writes=[pk])
                if add_one:
                    S.op('dve', lambda e, hf=hf: e.tensor_scalar(out=dst[:P, hf * 512:(hf + 1) * 512], in0=PS[bank][:P, :], scalar1=1.0, scalar2=None, op0=ALU.add),
                         reads=[pk], writes=[key])
                else:
                    S.op('dve', lambda e, hf=hf: e.tensor_copy(out=dst[:P, hf * 512:(hf + 1) * 512], in_=PS[bank][:P, :]), reads=[pk], writes=[key])

        es_mix = ExitStack()
        with es_mix as em:
            W_in = sb("W_in", [128, 8, 2560], BF16, stack=em)
            W_out = sb("W_out", [128, 8, D], BF16, stack=em)
            with ExitStack() as e0:
                modA = sb("modA", [17, 3 * D], stack=e0)
                modB = sb("modB", [17, 3 * D], stack=e0)
                ct = sb("ct", [17, D], stack=e0)
                cT = sb("cT", [128, 8, 17], BF16, stack=e0)
                bada = sb("bada", [17, 6 * D], stack=e0)
                wada = [sb("wada%d" % i, [128, 8, 512], BF16, stack=e0) for i in range(3)]
                lamt = sb("lamt", [128, 256], stack=e0)
                lamp = sb("lamp", [128, 128], stack=e0)
                lams = sb("lams", [128, 4], stack=e0)
                S.dma('sp', lambda e: e.dma_start(out=ct[:], in_=c_all), writes=['ct'])
                S.dma('sp', lambda e: e.dma_start(out=bada[:], in_=b_ada.partition_broadcast(17)), writes=['bada'])
                S.dma('sp', lambda e: e.dma_start(out=lamt[:], in_=lam_in.partition_broadcast(128)), writes=['lamt'])
                S.op('dve', lambda e: e.tensor_tensor(out=lamp[:].rearrange("p (a d) -> p a d", d=64), in0=lamt[:].rearrange("p (a d) -> p a d", d=64)[:, 0::2, :],
                                                      in1=lamt[:].rearrange("p (a d) -> p a d", d=64)[:, 1::2, :], op=ALU.mult), reads=['lamt'], writes=['lamp'])
                S.op('dve', lambda e: e.tensor_reduce(out=lams[:, 0:2], in_=lamp[:].rearrange("p (a d) -> p a d", d=64), axis=AX.X, op=ALU.add), reads=['lamp'], writes=['lams'])
                S.op('act', lambda e: e.activation(out=lams[:, 2:4], in_=lams[:, 0:2], func=AF.Exp), reads=['lams'], writes=['lams2'])
                S.op('dve', lambda e: e.tensor_tensor(out=neglam[:], in0=lams[:, 3:4], in1=lams[:, 2:3], op=ALU.subtract), reads=['lams2'], writes=['neglam'])
                S.op('dve', lambda e: e.tensor_scalar(out=neglam[:], in0=neglam[:], scalar1=-LAMBDA_INIT, scalar2=None, op0=ALU.add), reads=['neglam'], writes=['neglam'])
                S.op('act', lambda e: e.activation(out=ct[:], in_=ct[:], func=AF.Silu), reads=['ct'], writes=['ct'])
                transpose_to(17, ct[:], ['ct'], 8, lambda c0, n: cT[:, c0:c0 + n, :], ['cT'], [0, 1])
                wv = w_ada.rearrange("(k p) n -> p k n", p=128)
                for n in range(12):
                    wb = wada[n % 3]
                    wk = ('wada', n % 3)
                    S.dma('pool', lambda e, n=n, wb=wb: e.dma_start(out=wb[:], in_=wv[:, :, n * 512:(n + 1) * 512]), writes=[wk])
                    bank = 2 + (n % 2)
                    pk = ('ps', bank)
                    for k in range(8):
                        S.op('pe', lambda e, k=k, wb=wb, bank=bank: e.matmul(PS[bank][:17, :], lhsT=cT[:, k, :], rhs=wb[:, k, :], start=(k == 0), stop=(k == 7)),
                             reads=['cT', wk], writes=[pk], inc=(k == 7))
                    mt = modA if n < 6 else modB
                    off = (n % 6) * 512
                    S.op('dve', lambda e, mt=mt, off=off, bank=bank, n=n: e.tensor_tensor(out=mt[:, off:off + 512], in0=PS[bank][:17, :], in1=bada[:, n * 512:(n + 1) * 512], op=ALU.add),
                         reads=[pk, 'bada'], writes=['mod'])
                w_in_v = w_in.rearrange("(k p) n -> p k n", p=128)
                for i in range(4):
                    S.dma('pool', lambda e, i=i: e.dma_start(out=W_in[:, 2 * i:2 * i + 2, :], in_=w_in_v[:, 2 * i:2 * i + 2, :]), writes=['W_in'])
                S.dma('pool', lambda e: e.dma_start(out=W_out[:], in_=w_out.rearrange("(k p) n -> p k n", p=128)), writes=['W_out'])
                S.dma('sp', lambda e: e.dma_start(out=mod_scr[:, 0:3 * D], in_=modA[:, :]), reads=['mod'], writes=['mod_scr'])
                S.dma('sp', lambda e: e.dma_start(out=mod_scr[:, 3 * D:6 * D], in_=modB[:, :]), reads=['mod'], writes=['mod_scr'])
            S.barrier()

            gE = sb("gE", [128, D], stack=em); bE = sb("bE", [128, D], stack=em)
            g1 = sb("g1", [128, D], stack=em); b1 = sb("b1", [128, D], stack=em)
            SC = sb("SC", [128, D], stack=em); SH = sb("SH", [128, D], stack=em); GT = sb("GT", [128, D], stack=em)
            xin = sb("xin", [128, D], stack=em); xpb = [sb("xp%d" % i, [128, D], stack=em) for i in range(2)]; htmp = sb("htmp", [128, D], stack=em)
            hT = sb("hT", [128, 8, 128], BF16, stack=em)
            QTb = [sb("QT%d" % i, [128, 4, 128], BF16, stack=em) for i in range(2)]
            QT = QTb[0]
            kvst = [sb("kvst%d" % i, [128, 512], stack=em) for i in range(2)]
            sig = sb("sig", [128, 128], stack=em)
            mixT = sb("mixT", [128, 8, 128], BF16, stack=em)
            cvt = sb("cvt", [128, 512], stack=em); cvn = sb("cvn", [128, 512], stack=em)

            for (tl, src, key) in [(gE, ln_emb_g, 'gE'), (bE, ln_emb_b, 'bE'), (g1, ln1_g, 'g1'), (b1, ln1_b, 'b1')]:
                S.dma('sp', lambda e, tl=tl, src=src: e.dma_start(out=tl[:], in_=src.partition_broadcast(128)), writes=[key])

            def set_mod_tiles(P, sel, modt, scaleoff, shiftoff, gateoff):
                bcast_rows(SH, P, sel, modt, shiftoff, False, 'SH', 4)
                bcast_rows(SC, P, sel, modt, scaleoff, True, 'SC', 5)
                bcast_rows(GT, P, sel, modt, gateoff, False, 'GT', 4)
                S.op('dve', lambda e: e.tensor_tensor(out=htmp[:P, :], in0=bE[:P, :], in1=SC[:P, :], op=ALU.mult), reads=['bE', 'SC'], writes=['htmp'])
                S.op('dve', lambda e: e.tensor_tensor(out=SH[:P, :], in0=SH[:P, :], in1=htmp[:P, :], op=ALU.add), reads=['SH', 'htmp'], writes=['SH'])
                S.op('dve', lambda e: e.tensor_tensor(out=SC[:P, :], in0=SC[:P, :], in1=gE[:P, :], op=ALU.mult), reads=['SC', 'gE'], writes=['SC'])

            def front(P, x_src, kout, vout, kT_dst, vbf_dst, u_dst, par=0):
                xp = xpb[par]
                xk = 'xp%d' % par
                QT = QTb[par]
                qk = 'QT%d' % par
                S.dma('sp', lambda e: e.dma_start(out=xin[:P, :], in_=x_src), writes=['xin'])
                layer_norm_rows(P, xin[:P, :], xp[:P, :], ['xin'], [xk], 0)
                S.op('dve', lambda e: e.tensor_tensor(out=htmp[:P, :], in0=xp[:P, :], in1=SC[:P, :], op=ALU.mult), reads=[xk, 'SC'], writes=['htmp'])
                S.op('dve', lambda e: e.tensor_tensor(out=htmp[:P, :], in0=htmp[:P, :], in1=SH[:P, :], op=ALU.add), reads=['htmp', 'SH'], writes=['htmp'])
                yield
                transpose_to(P, htmp[:P, :], ['htmp'], 8, lambda c0, n: hT[:, c0:c0 + n, 0:P], ['hT'], [0, 1])
                yield
                for wi, (c0, dst) in enumerate([(512, kout), (1024, vout)]):
                    bank = 2 if wi == 0 else 0
                    pk = ('ps', bank)
                    for k in range(8):
                        S.op('pe', lambda e, k=k, c0=c0, bank=bank: e.matmul(PS[bank][:P, :], lhsT=hT[:, k, 0:P], rhs=W_in[:, k, c0:c0 + 512], start=(k == 0), stop=(k == 7)),
                             reads=['hT', 'W_in'], writes=[pk], inc=(k == 7))
                    st = kvst[wi]
                    sk = ('kvst', wi)
                    S.op('act', lambda e, st=st, bank=bank: e.copy(out=st[:P, :], in_=PS[bank][:P, :]), reads=[pk], writes=[sk])
                    S.dma('sp', lambda e, st=st, dst=dst: e.dma_start(out=dst, in_=st[:P, :]), reads=[sk], writes=[])
                    if wi == 1:
                        vbf_dst(st, sk)
                    yield
                for pr in range(4):
                    for wi, c0 in enumerate([0, 512]):
                        bank = 1 + ((2 * pr + wi) % 2)
                        pk = ('ps', bank)
                        for k in range(8):
                            S.op('pe', lambda e, k=k, c0=c0, pr=pr, bank=bank: e.matmul(PS[bank][:, 0:P], lhsT=W_in[:, k, c0 + 128 * pr:c0 + 128 * pr + 128], rhs=hT[:, k, 0:P], start=(k == 0), stop=(k == 7)),
                                 reads=['hT', 'W_in'], writes=[pk], inc=(k == 7))
                        if wi == 0:
                            S.op('act', lambda e, pr=pr, bank=bank: e.copy(out=QT[:, pr, 0:P], in_=PS[bank][:, 0:P]), reads=[pk], writes=[qk])
                        else:
                            kT_dst(pr, bank, pk)
                    yield
                for c in range(4):
                    pa = ('ps', 0)
                    pg = ('ps', 1)
                    for k in range(8):
                        S.op('pe', lambda e, k=k, c=c: e.matmul(PS[0][:, 0:P], lhsT=W_in[:, k, 1536 + 128 * c:1536 + 128 * c + 128], rhs=hT[:, k, 0:P], start=(k == 0), stop=(k == 7)),
                             reads=['hT', 'W_in'], writes=[pa], inc=(k == 7))
                    for k in range(8):
                        S.op('pe', lambda e, k=k, c=c: e.matmul(PS[1][:, 0:P], lhsT=W_in[:, k, 2048 + 128 * c:2048 + 128 * c + 128], rhs=hT[:, k, 0:P], start=(k == 0), stop=(k == 7)),
                             reads=['hT', 'W_in'], writes=[pg], inc=(k == 7))
                    S.op('act', lambda e: e.activation(out=sig[:, 0:P], in_=PS[1][:, 0:P], func=AF.Sigmoid), reads=[pg], writes=['sig'])
                    u_dst(c, pa)
                    yield
                S.op('pool', lambda e: e.tensor_tensor(out=xp[:P, :], in0=xp[:P, :], in1=gE[:P, :], op=ALU.mult), reads=[xk, 'gE'], writes=[xk])
                S.op('pool', lambda e: e.tensor_tensor(out=xp[:P, :], in0=xp[:P, :], in1=bE[:P, :], op=ALU.add), reads=[xk, 'bE'], writes=[xk])
                yield

            def attn_epilogue_tok(P, obank0, col0):
                for hb in range(3):
                    nh = 3 if hb < 2 else 2
                    pkb = ('ps', obank0 + hb)
                    ov = PS[obank0 + hb][:P, 0:480].rearrange("p (h w) -> p h w", w=160)[:, 0:nh, :]
                    S.op('dve', lambda e, hb=hb, ov=ov, nh=nh: e.reciprocal(out=rs8[:P, 3 * hb:3 * hb + nh], in_=ov[:, :, 128:129].rearrange("p h o -> p (h o)")), reads=[pkb], writes=['rs8'])
                    S.op('dve', lambda e, hb=hb, ov=ov, nh=nh: e.tensor_tensor(out=osb[:P, 3 * hb:3 * hb + nh, :], in0=ov[:, :, 0:128],
                                                                        in1=rs8[:P, 3 * hb:3 * hb + nh].unsqueeze(2).to_broadcast([P, nh, 128]), op=ALU.mult), reads=[pkb, 'rs8'], writes=['osb'])
                S.op('dve', lambda e: e.scalar_tensor_tensor(out=o4[:P], in0=osb[:P, 1::2, :], scalar=neglam[:P, 0:1], in1=osb[:P, 0::2, :], op0=ALU.mult, op1=ALU.add),
                     reads=['osb', 'neglam'], writes=['o4'])
                S.op('pool', lambda e: e.tensor_tensor(out=osq[:P], in0=o4[:P], in1=o4[:P], op=ALU.mult), reads=['o4'], writes=['osq'])
                S.op('dve', lambda e: e.tensor_reduce(out=rs8[:P, 8:12], in_=osq[:P], axis=AX.X, op=ALU.add), reads=['osq'], writes=['rs8b'])
                rsqrt(rs8[:P, 12:16], rs8[:P, 8:12], ['rs8b'], ['rs8c'], scale=1.0 / 128)
                S.op('dve', lambda e: e.tensor_tensor(out=o4[:P], in0=o4[:P], in1=rs8[:P, 12:16].unsqueeze(2).to_broadcast([P, 4, 128]), op=ALU.mult), reads=['o4', 'rs8c'], writes=['o4'])
                S.op('dve', lambda e: e.tensor_tensor(out=o4[:P], in0=o4[:P], in1=sg8row[:P, :].unsqueeze(1).to_broadcast([P, 4, 128]), op=ALU.mult), reads=['o4', 'sg8row'], writes=['o4'])
                transpose_to(P, o4[:P].rearrange("p a e -> p (a e)"), ['o4'], 4, lambda c0, n: mixT[:, c0:c0 + n, col0:col0 + P], ['mixT'], [3], evac=('act',))

            def conv_ln_silu(P, cv_fn, cv_keys, bA=1, bB=0):
                pk = ('ps', bA)
                for c in range(4):
                    S.op('pe', lambda e, c=c: e.transpose(out=PS[bA][:P, c * 128:(c + 1) * 128], in_=cv_fn(c), identity=ident[:]),
                         reads=cv_keys + ['ident'], writes=[pk], inc=(c == 3))
                S.op('act', lambda e: e.copy(out=cvt[:P, :], in_=PS[bA][:P, :]), reads=[pk], writes=['cvt'])
                st6 = small[:P, 16:22]
                mv = small[:P, 22:24]
                rstd = small[:P, 24:25]
                nmr = small[:P, 25:26]
                k = ('small', 16)
                S.op('dve', lambda e: e.bn_stats(out=st6, in_=cvt[:P, :]), reads=['cvt'], writes=[k])
                S.op('dve', lambda e: e.bn_aggr(out=mv, in_=st6), reads=[k], writes=[k])
                rsqrt(rstd, mv[:, 1:2], [k], [k])
                S.op('dve', lambda e: e.scalar_tensor_tensor(out=nmr, in0=mv[:, 0:1], scalar=-1.0, in1=rstd, op0=ALU.mult, op1=ALU.mult), reads=[k], writes=[k])
                S.op('act', lambda e: e.activation(out=cvn[:P, :], in_=cvt[:P, :], func=AF.Identity, bias=nmr, scale=rstd), reads=['cvt', k], writes=['cvn'])
                pk0 = ('ps', bB)
                for c in range(4):
                    S.op('pe', lambda e, c=c: e.transpose(out=PS[bB][:, c * 128:c * 128 + P], in_=cvn[:P, c * 128:(c + 1) * 128], identity=ident[:P, :P]),
                         reads=['cvn', 'ident'], writes=[pk0], inc=(c == 3))
                for c in range(4):
                    S.op('act', lambda e, c=c: e.activation(out=mixT[:, 4 + c, 0:P], in_=PS[bB][:, c * 128:c * 128 + P], func=AF.Silu, bias=clb[:, c:c + 1], scale=clg[:, c:c + 1]),
                         reads=[pk0, 'clg', 'clb'], writes=['mixT'])

            def out_ln1(P, row0, par=0, rbuf=None, x1o=None, b0=2):
                xp = xpb[par]
                xk = 'xp%d' % par
                rk, ok = ('rbuf', 'x1o') if rbuf is not None else ('htmp', 'xin')
                if rbuf is None:
                    rbuf, x1o = htmp, xin
                for hf in range(2):
                    bank = b0 + hf
                    pk = ('ps', bank)
                    for k in range(8):
                        S.op('pe', lambda e, k=k, hf=hf, bank=bank: e.matmul(PS[bank][:P, :], lhsT=mixT[:, k, 0:P], rhs=W_out[:, k, hf * 512:(hf + 1) * 512], start=(k == 0), stop=(k == 7)),
                             reads=['mixT', 'W_out'], writes=[pk], inc=(k == 7))
                    S.op('dve', lambda e, hf=hf, bank=bank: e.tensor_tensor(out=rbuf[:P, hf * 512:(hf + 1) * 512], in0=PS[bank][:P, :], in1=GT[:P, hf * 512:(hf + 1) * 512], op=ALU.mult),
                         reads=[pk, 'GT'], writes=[rk])
                S.op('dve', lambda e: e.scalar_tensor_tensor(out=rbuf[:P, :], in0=xp[:P, :], scalar=ALPHA, in1=rbuf[:P, :], op0=ALU.mult, op1=ALU.add), reads=[xk, rk], writes=[rk])
                layer_norm_rows(P, rbuf[:P, :], x1o[:P, :], [rk], [ok], 32)
                S.op('dve', lambda e: e.tensor_tensor(out=x1o[:P, :], in0=x1o[:P, :], in1=g1[:P, :], op=ALU.mult), reads=[ok, 'g1'], writes=[ok])
                S.op('dve', lambda e: e.tensor_tensor(out=x1o[:P, :], in0=x1o[:P, :], in1=b1[:P, :], op=ALU.add), reads=[ok, 'b1'], writes=[ok])
                S.dma('sp', lambda e: e.dma_start(out=x1_scr[row0:row0 + P, :], in_=x1o[:P, :]), reads=[ok], writes=['x1scr'])

            with ExitStack() as e1:
                modS = sb("modS", [17, 3 * D], stack=e1)
                S.dma('sp', lambda e: e.dma_start(out=modS[:, :], in_=mod_scr[:, 0:3 * D]), writes=['mod'])
                kall = sb("kall", [128, 16, 512], BF16, stack=e1)
                vall = sb("vall", [128, 16, 512], BF16, stack=e1)
                ktp = [sb("ktp%d" % i, [128, 4, 128], BF16, stack=e1) for i in range(3)]
                PTs = sb("PTs", [128, 512], BF16, stack=e1)
                Us = sb("Us", [128, 4, NS, 34], stack=e1)
                KTs = sb("KTs", [128, 4, 64], BF16, stack=e1)
                Vsb = sb("Vsb", [64, 512], BF16, stack=e1)
                auglsb = sb("auglsb", [128, 128], BF16, stack=e1); augrsb = sb("augrsb", [128, 256], BF16, stack=e1)
                ucont = kall[:].rearrange("p a f -> p (a f)").bitcast(F32)[:, 0:4 * NS * 30].rearrange("p (c b t) -> p c b t", c=4, t=30)
                bnew = sb("bnew", [64, 256], stack=e1); mnew = sb("mnew", [64, 256], stack=e1)
                pnewf = sb("pnewf", [64, 512], stack=e1); pnewT = sb("pnewT", [64, 512], BF16, stack=e1)
                ptl = sb("ptl", [128, NS], I32, stack=e1); ptf = sb("ptf", [128, NS], stack=e1)
                pm8 = sb("pm8", [128, 1], stack=e1); idx = sb("idx", [128, NS], I32, stack=e1)
                stc = sb("stc", [120, 512], stack=e1)
                osall = sb("osall", [128, 4, 64], stack=e1); ossq = sb("ossq", [128, 4, 64], stack=e1)
                rsum = sb("rsum", [128, 32], stack=e1); on32 = sb("on32", [128, 32], stack=e1)
                rstd_s = sb("rstd_s", [128, 256], stack=e1)
                cvs = sb("cvs", [128, 4, 64], stack=e1)
                acc = [sb("acc%d" % i, [128, 64], stack=e1) for i in range(2)]
                osq_s = sb("osq_s", [128, 128], stack=e1)

                for r0 in (0, 64):
                    S.dma('pool', lambda e, r0=r0: e.dma_start(out=auglsb[r0:r0 + 3, :], in_=c_augls), writes=['auglsb'])
                    S.dma('pool', lambda e, r0=r0: e.dma_start(out=augrsb[r0:r0 + 3, :], in_=c_augrs), writes=['augrsb'])
                S.dma('sp', lambda e: e.dma_start(out=bnew[:], in_=c_bnew), writes=['bnew'])
                S.dma('sp', lambda e: e.dma_start(out=mnew[:], in_=c_mnew), writes=['mnew'])
                S.dma('sp', lambda e: e.dma_start(out=ptl[:], in_=pt_lay), writes=['ptl'])
                S.dma('sp', lambda e: e.dma_start(out=pm8[:], in_=c_pm8), writes=['pm8'])
                S.op('dve', lambda e: e.tensor_copy(out=ptf[:], in_=ptl[:]), reads=['ptl'], writes=['ptf'])
                S.op('dve', lambda e: e.tensor_scalar(out=ptf[:], in0=ptf[:], scalar1=8.0, scalar2=pm8[:, 0:1], op0=ALU.mult, op1=ALU.add), reads=['ptf', 'pm8'], writes=['ptf'])
                S.op('dve', lambda e: e.tensor_copy(out=idx[:], in_=ptf[:]), reads=['ptf'], writes=['idx'])

                def gather(b, which):
                    if which == 0:
                        S.dma('pool', lambda e: e.indirect_dma_start(out=kall[:].rearrange("p a f -> p (a f)"), out_offset=None, in_=cache_k,
                                                                    in_offset=bass.IndirectOffsetOnAxis(ap=idx[:, b:b + 1], axis=0)), reads=['idx'], writes=['kall'])
                    else:
                        S.dma('pool', lambda e: e.indirect_dma_start(out=vall[:].rearrange("p a f -> p (a f)"), out_offset=None, in_=cache_v,
                                                                    in_offset=bass.IndirectOffsetOnAxis(ap=idx[:, b:b + 1], axis=0)), reads=['idx'], writes=['vall'])
                if STOP_AFTER == 0.1:
                    S.barrier(); S.finish()
                    return nc
                gather(0, 0)
                gather(0, 1)
                if STOP_AFTER == 0.2:
                    S.barrier(); S.finish()
                    return nc

                for g in range(4):
                    S.dma('sp', lambda e, g=g: e.dma_start(out=stc[:, :], in_=st_conv[g * 120:(g + 1) * 120, :]), writes=['stc'])
                    pk = ('ps', 0)
                    for c in range(4):
                        S.op('pe', lambda e, c=c: e.transpose(out=PS[0][:, c * 128:c * 128 + 120], in_=stc[:, c * 128:(c + 1) * 128], identity=ident[:120, :120]),
                             reads=['stc', 'ident'], writes=[pk], inc=(c == 3))
                    for c in range(4):
                        S.op('dve', lambda e, c=c, g=g: e.tensor_copy(out=Us[:, c, 4 * g:4 * g + 4, 0:30], in_=PS[0][:, c * 128:c * 128 + 120].rearrange("p (b t) -> p b t", t=30)),
                             reads=[pk], writes=['Us'])

                if STOP_AFTER == 0.3:
                    S.barrier(); S.finish()
                    return nc
                set_mod_tiles(64, sels, modS, 1024, 0, 2048)
                if STOP_AFTER == 0.4:
                    S.barrier(); S.finish()
                    return nc

                def s_vbf(st, sk):
                    S.op('dve', lambda e: e.tensor_copy(out=Vsb[:, :], in_=st[:64, :]), reads=[sk], writes=['Vsb'])

                def s_kT(pr, bank, pk):
                    S.op('dve', lambda e: e.tensor_copy(out=KTs[:, pr, :], in_=PS[bank][:, 0:64]), reads=[pk], writes=['KTs'])

                def s_u(c, pa):
                    S.op('dve', lambda e: e.tensor_tensor(out=Us[:, c, :, 30:34], in0=PS[0][:, 0:64].rearrange("p (b t) -> p b t", t=4),
                                                          in1=sig[:, 0:64].rearrange("p (b t) -> p b t", t=4), op=ALU.mult), reads=[pa, 'sig'], writes=['Us'])

                for _ in front(64, x_s, k_s, v_s, s_kT, s_vbf, s_u):
                    pass

                if STOP_AFTER == 0.5:
                    S.barrier(); S.finish()
                    return nc
                for hh in range(2):
                    bank = 4 if hh == 0 else 2
                    pk = ('ps', bank)
                    r0 = 64 * hh
                    for pr in range(4):
                        S.op('pe', lambda e, pr=pr, r0=r0, bank=bank: e.matmul(PS[bank][:64, pr * 64:(pr + 1) * 64], lhsT=KTs[r0:r0 + 64, pr, :], rhs=QT[r0:r0 + 64, pr, 0:64], start=True, stop=True),
                             reads=['KTs', 'QT0'], writes=[pk], inc=(pr == 3))
                    S.op('dve', lambda e, hh=hh, bank=bank: e.scalar_tensor_tensor(out=pnewf[:, hh * 256:(hh + 1) * 256], in0=PS[bank][:64, 0:256], scalar=0.125, in1=bnew[:, :], op0=ALU.mult, op1=ALU.add),
                         reads=[pk, 'bnew'], writes=['pnewf'])
                S.op('act', lambda e: e.activation(out=pnewf[:], in_=pnewf[:], func=AF.Exp), reads=['pnewf'], writes=['pnewf'])
                pnv = pnewT[:].rearrange("p (b hh pr t) -> p b hh pr t", hh=2, pr=4, t=4)
                for hh in range(2):
                    S.op('dve', lambda e, hh=hh: e.tensor_tensor(out=pnv[:, :, hh, :, :].rearrange("p b pr t -> p pr b t"), in0=pnewf[:, hh * 256:(hh + 1) * 256].rearrange("p (pr b t) -> p pr b t", pr=4, t=4),
                                                                in1=mnew[:, :].rearrange("p (pr b t) -> p pr b t", pr=4, t=4), op=ALU.mult), reads=['pnewf', 'mnew'], writes=['pnewT'])
                if STOP_AFTER == 0.6:
                    S.barrier(); S.finish()
                    return nc

                SB = [5, 3]
                for b in range(NS):
                    for t16 in range(16):
                        tb = 6 + (t16 % 2)
                        tpk = ('ps', tb)
                        for pr in range(4):
                            S.op('pe', lambda e, pr=pr, t16=t16, tb=tb: e.transpose(out=PSB[tb][:, pr * 128:(pr + 1) * 128], in_=kall[:, t16, pr * 128:(pr + 1) * 128], identity=identb[:]),
                                 reads=['kall', 'identb'], writes=[tpk], inc=(pr == 3))
                        kt = ktp[t16 % 3]
                        kk = ('ktp', t16 % 3)
                        if t16 % 2 == 0:
                            S.op('act', lambda e, kt=kt, tb=tb: e.copy(out=kt[:].rearrange("p a k -> p (a k)"), in_=PSB[tb][:, 0:512]), reads=[tpk], writes=[kk])
                        else:
                            S.op('dve', lambda e, kt=kt, tb=tb: e.tensor_copy(out=kt[:].rearrange("p a k -> p (a k)"), in_=PSB[tb][:, 0:512]), reads=[tpk], writes=[kk])
                        for hh in range(2):
                            r0 = 64 * hh
                            for pr in range(4):
                                S.op('pe', lambda e, pr=pr, hh=hh, r0=r0, kt=kt, t16=t16, b=b: e.matmul(PS[SB[hh]][:, t16 * 16 + pr * 4:t16 * 16 + pr * 4 + 4], lhsT=kt[r0:r0 + 64, pr, :],
                                                                                               rhs=QT[r0:r0 + 64, pr, 4 * b:4 * b + 4], start=(t16 == 0 and pr == 0), stop=False, skip_group_check=True),
                                     reads=[kk, 'QT0'], writes=[('ps', SB[hh])], inc=False)
                    if b + 1 < NS:
                        gather(b + 1, 0)
                    for hh in range(2):
                        r0 = 64 * hh
                        S.op('pe', lambda e, hh=hh, r0=r0: e.matmul(PS[SB[hh]][:, 0:256], lhsT=auglsb[r0:r0 + 3, :], rhs=augrsb[r0:r0 + 3, :], start=False, stop=True, skip_group_check=True),
                             reads=['auglsb', 'augrsb'], writes=[('ps', SB[hh])])
                    for hh in range(2):
                        S.op('act', lambda e, hh=hh: e.activation(out=PTs[:, hh * 256:(hh + 1) * 256], in_=PS[SB[hh]][:, 0:256], func=AF.Exp, scale=0.125), reads=[('ps', SB[hh])], writes=['PTs'])
                    pk4 = ('ps', 4)
                    for hh in range(2):
                        for t16 in range(16):
                            S.op('pe', lambda e, t16=t16, hh=hh: e.matmul(PS[4][:, hh * 16:(hh + 1) * 16], lhsT=onesb[:, :], rhs=PTs[:, hh * 256 + t16 * 16:hh * 256 + (t16 + 1) * 16], start=(t16 == 0), stop=False),
                                 reads=['PTs', 'onesb'], writes=[pk4], inc=False)
                        S.op('pe', lambda e, b=b, hh=hh: e.matmul(PS[4][:, hh * 16:(hh + 1) * 16], lhsT=onesb[0:64, :], rhs=pnv[:, b, hh, :, :], start=False, stop=True),
                             reads=['pnewT', 'onesb'], writes=[pk4], inc=False)
                    for dh in range(4):
                        for hh in range(2):
                            oc = 64 + dh * 8 + hh * 4
                            for t16 in range(16):
                                S.op('pe', lambda e, dh=dh, t16=t16, hh=hh, oc=oc: e.matmul(PS[4][:, oc:oc + 4], lhsT=vall[:, t16, dh * 128:(dh + 1) * 128],
                                                                                         rhs=PTs[:, hh * 256 + t16 * 16 + dh * 4:hh * 256 + t16 * 16 + dh * 4 + 4], start=(t16 == 0), stop=False),
                                     reads=['PTs', 'vall'], writes=[pk4], inc=False)
                            S.op('pe', lambda e, dh=dh, b=b, hh=hh, oc=oc: e.matmul(PS[4][:, oc:oc + 4], lhsT=Vsb[0:64, dh * 128:(dh + 1) * 128], rhs=pnv[:, b, hh, dh, :], start=False, stop=True),
                                 reads=['pnewT', 'Vsb'], writes=[pk4], inc=(dh == 3 and hh == 1))
                    if b + 1 < NS:
                        gather(b + 1, 1)
                    rsv = rsum[:].rearrange("p (d h q) -> p d h q", h=2, q=4)
                    for hh in range(2):
                        S.op('dve', lambda e, hh=hh: e.reciprocal(out=rsv[:, :, hh, :], in_=PS[4][:, hh * 16:(hh + 1) * 16].rearrange("p (d q) -> p d q", q=4)), reads=[pk4], writes=['rsum'])
                    S.op('dve', lambda e: e.tensor_tensor(out=on32[:], in0=PS[4][:, 64:96], in1=rsum[:], op=ALU.mult), reads=[pk4, 'rsum'], writes=['on32'])
                    onv = on32[:].rearrange("p (d h q) -> p d h q", h=2, q=4)
                    S.op('dve', lambda e, b=b: e.scalar_tensor_tensor(out=osall[:, :, 4 * b:4 * b + 4], in0=onv[:, :, 1, :], scalar=neglam[:, 0:1], in1=onv[:, :, 0, :], op0=ALU.mult, op1=ALU.add),
                         reads=['on32', 'neglam'], writes=['osall'])

                if STOP_AFTER == 0.7:
                    S.barrier(); S.finish()
                    return nc
                S.op('dve', lambda e: e.tensor_tensor(out=ossq[:], in0=osall[:], in1=osall[:], op=ALU.mult), reads=['osall'], writes=['ossq'])
                pk = ('ps', 6)
                S.op('pe', lambda e: e.matmul(PS[6][:, 0:256], lhsT=onesf[:, :], rhs=ossq[:].rearrange("p a t -> p (a t)"), start=True, stop=True), reads=['ossq', 'onesf'], writes=[pk])
                rsqrt(rstd_s[:], PS[6][:, 0:256], [pk], ['rstd_s'], scale=1.0 / 128)
                S.op('dve', lambda e: e.scalar_tensor_tensor(out=mixT[:, 0:4, 0:64], in0=osall[:], scalar=sg8[:, 0:1], in1=rstd_s[:].rearrange("p (a t) -> p a t", t=64), op0=ALU.mult, op1=ALU.mult),
                     reads=['osall', 'sg8', 'rstd_s'], writes=['mixT'])

                if STOP_AFTER == 0.8:
                    S.barrier(); S.finish()
                    return nc
                acc4 = [acc[0][:, 0:64], acc[1][:, 0:64], osq_s[:, 0:64], osq_s[:, 64:128]]
                accv = [a.rearrange("p (b t) -> p b t", t=4) for a in acc4]
                for c in range(4):
                    S.op('dve', lambda e, c=c: e.tensor_scalar(out=accv[c], in0=Us[:, c, :, 0:4], scalar1=cw[:, c, 0:1], scalar2=cb[:, c:c + 1], op0=ALU.mult, op1=ALU.add),
                         reads=['Us', 'cw', 'cb'], writes=[('acc', c)])
                for j in range(1, 31):
                    for c in range(4):
                        dst = accv[c] if j < 30 else cvs[:, c, :].rearrange("p (b t) -> p b t", t=4)
                        S.op('dve', lambda e, c=c, j=j, dst=dst: e.scalar_tensor_tensor(out=dst, in0=Us[:, c, :, j:j + 4], scalar=cw[:, c, j:j + 1], in1=accv[c], op0=ALU.mult, op1=ALU.add),
                             reads=['Us', ('acc', c)], writes=[('acc', c)] if j < 30 else ['cvs'])
                conv_ln_silu(64, lambda c: cvs[:, c, :], ['cvs'])
                for c in range(4):
                    S.op('act', lambda e, c=c: e.copy(out=ucont[:, c, :, :], in_=Us[:, c, :, 4:34]), reads=['Us'], writes=['kall'])
                for g in range(4):
                    pk = ('ps', 1)
                    for c in range(4):
                        S.op('pe', lambda e, c=c, g=g: e.transpose(out=PS[1][:120, c * 128:(c + 1) * 128], in_=ucont[:, c, 4 * g:4 * g + 4, :], identity=ident[:]),
                             reads=['kall', 'ident'], writes=[pk], inc=(c == 3))
                    S.op('act', lambda e: e.copy(out=stc[:, :], in_=PS[1][:120, :]), reads=[pk], writes=['stc'])
                    S.dma('sp', lambda e, g=g: e.dma_start(out=conv_s[g * 120:(g + 1) * 120, :], in_=stc[:, :]), reads=['stc'], writes=[])
                if STOP_AFTER == 0.9:
                    S.barrier(); S.finish()
                    return nc
                out_ln1(64, T)
            S.barrier()
            if STOP_AFTER == 1:
                S.op('dve', lambda e: e.tensor_copy(out=htmp[:, :], in_=mixT[:].rearrange("p a t -> p (a t)")), writes=['htmp'])
                S.dma('sp', lambda e: e.dma_start(out=y_p[0:128, :], in_=htmp[:, :]), reads=['htmp'], writes=[])
                S.finish()
                return nc

            with ExitStack() as e2:
                with ExitStack() as et:
                    modP = sb("modP", [17, 3 * D], stack=et)
                    S.dma('sp', lambda e: e.dma_start(out=modP[:, :], in_=mod_scr[:, 0:3 * D]), writes=['mod'])
                    set_mod_tiles(128, selp, modP, 1024, 0, 2048)
                    S.barrier()
                KT = sb("KT", [128, 4, T], BF16, stack=e2)
                Vext = sb("Vext", [128, NBLK, 4, 130], BF16, stack=e2)
                auglpb = sb("auglpb", [128, 512], BF16, stack=e2); augrpb = sb("augrpb", [128, 4096], BF16, stack=e2)
                osb = sb("osb", [128, 8, 128], stack=e2); o4 = sb("o4", [128, 4, 128], stack=e2); osq = sb("osq", [128, 4, 128], stack=e2)
                rs8 = sb("rs8", [128, 16], stack=e2)
                Ub = [sb("U%d" % i, [128, 4, 30 + 128], stack=e2) for i in range(3)]
                PT = [sb("PT%d" % i, [128, 512], BF16, stack=e2) for i in range(3)]
                cva = [sb("cva%d" % i, [128, 128], stack=e2) for i in range(4)]
                cpo = sb("cpo", [30, 512], stack=e2)
                rbuf_p = sb("rbuf", [128, D], stack=e2); x1o_p = sb("x1o", [128, D], stack=e2)

                for r0 in (0, 64):
                    S.dma('pool', lambda e, r0=r0: e.dma_start(out=auglpb[r0:r0 + 3, :], in_=c_auglp), writes=['auglpb'])
                    S.dma('pool', lambda e, r0=r0: e.dma_start(out=augrpb[r0:r0 + 3, :], in_=c_augrp), writes=['augrpb'])
                S.op('dve', lambda e: e.memset(Ub[0][:], 0.0), writes=['U'])
                S.op('dve', lambda e: e.memset(Ub[1][:], 0.0), writes=['U'])
                S.op('dve', lambda e: e.memset(Ub[2][:], 0.0), writes=['U'])
                S.op('pool', lambda e: e.memset(Vext[:].rearrange("p a b c -> p (a b c)"), 1.0), writes=['Vext'])

                ptcnt = [0]

                def p_front(i):
                    par = i % 2
                    up_i = i % 3
                    un_i = (i + 1) % 3
                    Up = Ub[up_i]
                    Un = Ub[un_i]

                    def p_vbf(st, sk):
                        S.op('pool', lambda e: e.tensor_copy(out=Vext[:, i, :, 0:128], in_=st[:, :].rearrange("p (a e) -> p a e", e=128)), reads=[sk], writes=['Vext'])

                    def p_kT(pr, bank, pk):
                        S.op('dve', lambda e: e.tensor_copy(out=KT[:, pr, i * 128:(i + 1) * 128], in_=PS[bank][:, 0:128]), reads=[pk], writes=['KT'])

                    def p_u(c, pa):
                        S.op('dve', lambda e: e.tensor_tensor(out=Up[:, c, 30:158], in0=PS[0][:, 0:128], in1=sig[:, 0:128], op=ALU.mult), reads=[pa, 'sig'], writes=[('U', up_i, c)])
                        S.op('act', lambda e: e.copy(out=Un[:, c, 0:30], in_=Up[:, c, 128:158]), reads=[('U', up_i, c)], writes=[('U', un_i, c)])

                    yield from front(128, x_p[i * 128:(i + 1) * 128, :], k_p[i * 128:(i + 1) * 128, :], v_p[i * 128:(i + 1) * 128, :], p_kT, p_vbf, p_u, par=par)

                def p_back(i):
                    par = i % 2
                    ui = i % 3
                    U = Ub[ui]
                    QT = QTb[par]
                    qk = 'QT%d' % par
                    for c in range(4):
                        S.op('dve', lambda e, c=c: e.tensor_scalar(out=cva[c][:, :], in0=U[:, c, 0:128], scalar1=cw[:, c, 0:1], scalar2=cb[:, c:c + 1], op0=ALU.mult, op1=ALU.add),
                             reads=[('U', ui, c), 'U', 'cw', 'cb'], writes=[('cva', c)])
                    taps_left = list(range(1, 31))

                    def emit_taps(n):
                        for _ in range(n):
                            if not taps_left:
                                return
                            j = taps_left.pop(0)
                            for c in range(4):
                                S.op('dve', lambda e, c=c, j=j: e.scalar_tensor_tensor(out=cva[c][:, :], in0=U[:, c, j:j + 128], scalar=cw[:, c, j:j + 1], in1=cva[c][:, :], op0=ALU.mult, op1=ALU.add),
                                     reads=[('U', ui, c), ('cva', c)], writes=[('cva', c)])
                    emit_taps(2)
                    yield
                    if i == NBLK - 1:
                        pk = ('ps', 4)
                        for c in range(4):
                            S.op('pe', lambda e, c=c: e.transpose(out=PS[4][:30, c * 128:(c + 1) * 128], in_=U[:, c, 128:158], identity=ident[:]),
                                 reads=[('U', ui, c), 'U', 'ident'], writes=[pk], inc=(c == 3))
                        S.op('act', lambda e: e.copy(out=cpo[:, :], in_=PS[4][:30, :]), reads=[pk], writes=['cpo'])
                        S.dma('sp', lambda e: e.dma_start(out=conv_p, in_=cpo[:, :]), reads=['cpo'], writes=[])

                    groups = []
                    for h in range(8):
                        for g in range((i + 4) // 4):
                            groups.append((h, g))

                    def emit_scores(h, g):
                        r0 = 64 * (h % 2)
                        pr = h // 2
                        j0 = 4 * g
                        nb = min(4, i + 1 - j0)
                        sbank = 3 + (ptcnt[0] % 2)
                        spk = ('ps', sbank)
                        for jj in range(nb):
                            S.op('pe', lambda e, jj=jj: e.matmul(PS[sbank][:, jj * 128:(jj + 1) * 128], lhsT=KT[r0:r0 + 64, pr, (j0 + jj) * 128:(j0 + jj + 1) * 128],
                                                                 rhs=QT[r0:r0 + 64, pr, 0:128], start=(jj == 0), stop=False, skip_group_check=True),
                                 reads=['KT', qk], writes=[spk], inc=False)
                        g0 = j0 - i + 16
                        S.op('pe', lambda e: e.matmul(PS[sbank][:, 0:nb * 128], lhsT=auglpb[r0:r0 + 3, pr * 128:(pr + 1) * 128], rhs=augrpb[r0:r0 + 3, g0 * 128:(g0 + nb) * 128], start=False, stop=True, skip_group_check=True),
                             reads=['auglpb', 'augrpb'], writes=[spk])
                        pt = PT[ptcnt[0] % 3]
                        ptk = ('PT', ptcnt[0] % 3)
                        ptcnt[0] += 1
                        S.op('act', lambda e: e.activation(out=pt[:, 0:nb * 128], in_=PS[sbank][:, 0:nb * 128], func=AF.Exp, scale=0.125), reads=[spk], writes=[ptk])
                        if j0 + nb - 1 == i:
                            S.op('pool', lambda e: e.tensor_tensor(out=pt[:, (nb - 1) * 128:nb * 128], in0=pt[:, (nb - 1) * 128:nb * 128], in1=maskTb[:, :], op=ALU.mult),
                                 reads=[ptk, 'maskTb'], writes=[ptk])
                        return (pt, ptk, j0, nb)

                    def emit_pv(h, st):
                        pt, ptk, j0, nb = st
                        pr = h // 2
                        ob = 5 + h // 3
                        ocol = (h % 3) * 160
                        opk = ('ps', ob)
                        for jj in range(nb):
                            j = j0 + jj
                            S.op('pe', lambda e, jj=jj, j=j: e.matmul(PS[ob][:, ocol:ocol + 129], lhsT=pt[:, jj * 128:(jj + 1) * 128], rhs=Vext[:, j, pr, 0:129], start=(j == 0), stop=(j == i)),
                                 reads=[ptk, 'Vext'], writes=[opk], inc=(j == i))

                    st_prev = emit_scores(*groups[0])
                    for gi, (h, g) in enumerate(groups):
                        st_next = emit_scores(*groups[gi + 1]) if gi + 1 < len(groups) else None
                        emit_pv(h, st_prev)
                        st_prev = st_next
                        if g == (i + 4) // 4 - 1:
                            emit_taps(4)
                            yield
                    emit_taps(31)
                    yield
                    attn_epilogue_tok(128, 5, 0)
                    yield
                    conv_ln_silu(128, lambda c: cva[c][:, :], [('cva', c) for c in range(4)], bA=4, bB=3)
                    yield
                    out_ln1(128, i * 128, par=par, rbuf=rbuf_p, x1o=x1o_p, b0=3)

                def run_interleaved(gens):
                    gens = list(gens)
                    while gens:
                        for g in list(gens):
                            try:
                                next(g)
                            except StopIteration:
                                gens.remove(g)

                run_interleaved([p_front(0)])
                for i in range(NBLK):
                    gl = [p_back(i)]
                    if i + 1 < NBLK:
                        gl.append(p_front(i + 1))
                    run_interleaved(gl)
            S.barrier()
            if STOP_AFTER == 2:
                S.finish()
                return nc

        with ExitStack() as ef:
            W_up = sb("W_up", [128, 8, 2 * DFF], BF16, stack=ef)
            W_dn = sb("W_dn", [128, NCH, D], BF16, stack=ef)
            modst = [sb("modst%d" % i, [17, 512], stack=ef) for i in range(2)]
            g2 = sb("g2", [128, D], stack=ef); b2 = sb("b2", [128, D], stack=ef)
            SC2 = sb("SC2", [128, D], stack=ef); SH2 = sb("SH2", [128, D], stack=ef); GT2 = sb("GT2", [128, D], stack=ef)
            x1b = [sb("x1t%d" % i, [128, D], stack=ef) for i in range(2)]; ht2 = sb("ht2", [128, D], stack=ef); rb2 = sb("rb2", [128, D], stack=ef)
            h2T = sb("h2T", [128, 8, 128], BF16, stack=ef)
            gTb = [sb("gT%d" % i, [128, NCH, 128], BF16, stack=ef) for i in range(2)]
            carry = sb("carry", [128, 44, 2], stack=ef)
            carrys = sb("carrys", [128, 44, NS, 2], stack=ef)
            ua = [sb("ua%d" % i, [128, 130], stack=ef) for i in range(4)]
            tt = [sb("tt%d" % i, [128, 128], stack=ef) for i in range(8)]
            uas = [sb("uas%d" % i, [128, NS, 6], stack=ef) for i in range(2)]
            sfc = [sb("sfc%d" % i, [32, 512], stack=ef) for i in range(2)]

            w_up_v = w_up.rearrange("(k p) n -> p k n", p=128)
            for g in range(11):
                for half in range(2):
                    c0 = half * DFF + g * 256
                    S.dma('pool', lambda e, c0=c0: e.dma_start(out=W_up[:, :, c0:c0 + 256], in_=w_up_v[:, :, c0:c0 + 256]), writes=[('W_up', half, g)])
            w_dn_v = w_down.rearrange("(c p) n -> p c n", p=128)
            for i in range(2):
                S.dma('pool', lambda e, i=i: e.dma_start(out=W_dn[:, 11 * i:11 * i + 11, :], in_=w_dn_v[:, 11 * i:11 * i + 11, :]), writes=['W_dn'])
            S.dma('sp', lambda e: e.dma_start(out=g2[:], in_=ln2_g.partition_broadcast(128)), writes=['g2'])
            S.dma('sp', lambda e: e.dma_start(out=b2[:], in_=ln2_b.partition_broadcast(128)), writes=['b2'])
            S.op('dve', lambda e: e.memset(carry[:], 0.0), writes=['carry'])
            for g in range(11):
                pk = ('ps', 0)
                sf = sfc[g % 2]
                sfk = ('sfc', g % 2)
                S.dma('sp', lambda e, g=g, sf=sf: e.dma_start(out=sf[:, :], in_=st_ffn[:, g * 512:(g + 1) * 512]), writes=[sfk])
                for c in range(4):
                    S.op('pe', lambda e, c=c, sf=sf: e.transpose(out=PS[0][:, c * 128:c * 128 + 32], in_=sf[:, c * 128:(c + 1) * 128], identity=ident[:32, :32]),
                         reads=[sfk, 'ident'], writes=[pk], inc=(c == 3))
                S.op('dve', lambda e, g=g: e.tensor_copy(out=carrys[:, 4 * g:4 * g + 4, :, :], in_=PS[0][:, :].rearrange("p (c x) -> p c x", x=128)[:, :, 0:32].rearrange("p c (b t) -> p c b t", t=2)),
                     reads=[pk], writes=['carrys'])

            def set_mod2(P, sel):
                selk = 'selp' if sel is selp else 'sels'
                n = 0
                for (dst, key, off, one, bank) in [(SH2, 'SH2', 0, False, 4), (SC2, 'SC2', 1024, True, 5), (GT2, 'GT2', 2048, False, 4)]:
                    for hf in range(2):
                        mt = modst[n % 2]
                        mk = ('modst', n % 2)
                        n += 1
                        c0 = 3 * D + off + hf * 512
                        S.dma('sp', lambda e, mt=mt, c0=c0: e.dma_start(out=mt[:, :], in_=mod_scr[:, c0:c0 + 512]), writes=[mk])
                        pk = ('ps', bank)
                        S.op('pe', lambda e, mt=mt, bank=bank: e.matmul(PS[bank][:P, :], lhsT=sel[:, :P], rhs=mt[:, :], start=True, stop=True), reads=[selk, mk], writes=[pk])
                        if one:
                            S.op('dve', lambda e, hf=hf, dst=dst, bank=bank: e.tensor_scalar(out=dst[:P, hf * 512:(hf + 1) * 512], in0=PS[bank][:P, :], scalar1=1.0, scalar2=None, op0=ALU.add),
                                 reads=[pk], writes=[key])
                        else:
                            S.op('dve', lambda e, hf=hf, dst=dst, bank=bank: e.tensor_copy(out=dst[:P, hf * 512:(hf + 1) * 512], in_=PS[bank][:P, :]), reads=[pk], writes=[key])

            def ffn_s1a(P, row0, par):
                x1t = x1b[par]
                xk = 'x1t%d' % par
                S.dma('sp', lambda e: e.dma_start(out=x1t[:P, :], in_=x1_scr[row0:row0 + P, :]), reads=['x1scr'], writes=[xk])
                S.op('dve', lambda e: e.tensor_tensor(out=ht2[:P, :], in0=x1t[:P, :], in1=SC2[:P, :], op=ALU.mult), reads=[xk, 'SC2'], writes=['ht2'])
                S.op('dve', lambda e: e.tensor_tensor(out=ht2[:P, :], in0=ht2[:P, :], in1=SH2[:P, :], op=ALU.add), reads=['ht2', 'SH2'], writes=['ht2'])
                yield

            def ffn_s1b(P):
                transpose_to(P, ht2[:P, :], ['ht2'], 8, lambda c0, n: h2T[:, c0:c0 + n, 0:P], ['h2T'], [0, 1])

            def ffn_s2(P, sample, gpar):
                gT = gTb[gpar]
                gk = 'gT%d' % gpar
                pend_sm = []

                def emit_sm(c, res):
                    (ca, ka), (cb_, kb) = res
                    S.op('act', lambda e: e.activation(out=ca, in_=ca, func=AF.Silu), reads=[ka], writes=[ka])
                    S.op('pool', lambda e: e.tensor_tensor(out=gT[:, c, 0:P], in0=ca, in1=cb_, op=ALU.mult), reads=[ka, kb], writes=[gk])

                for c in range(NCH):
                    res = []
                    taps = []
                    for half in range(2):
                        ch = c + NCH * half
                        bank = 2 + ((2 * c + half) % 4)
                        pk = ('ps', bank)
                        col = ch * 128
                        for k in range(8):
                            S.op('pe', lambda e, k=k, col=col, bank=bank: e.matmul(PS[bank][:, 0:P], lhsT=W_up[:, k, col:col + 128], rhs=h2T[:, k, 0:P], start=(k == 0), stop=(k == 7)),
                                 reads=['h2T', ('W_up', half, c // 2)], writes=[pk], inc=(k == 7))
                        ti = (2 * c + half) % 4
                        t0 = tt[ti]; t1 = tt[4 + ti]
                        k0 = ('tt', ti); k1 = ('tt', 4 + ti)
                        if not sample:
                            u = ua[ti]
                            uk = ('ua', ti)
                            S.op('act', lambda e, u=u, bank=bank: e.copy(out=u[:, 2:130], in_=PS[bank][:, 0:128]), reads=[pk], writes=[uk])
                            S.op('pool', lambda e, u=u, ch=ch: e.tensor_copy(out=u[:, 0:2], in_=carry[:, ch, :]), reads=['carry'], writes=[uk])
                            S.op('act', lambda e, t0=t0, bank=bank, ch=ch: e.activation(out=t0[:, :], in_=PS[bank][:, 0:128], func=AF.Identity, bias=fcb[:, ch:ch + 1], scale=fcw[:, ch, 2:3]),
                                 reads=[pk, 'fcw', 'fcb'], writes=[k0])
                            taps.append((t0, t1, u, ch, uk, k0, k1))
                            res.append((t0[:, 0:P], k0))
                        else:
                            u = uas[half]
                            uk = ('uas', half)
                            S.op('act', lambda e, u=u, bank=bank: e.copy(out=u[:, :, 2:6], in_=PS[bank][:, 0:64].rearrange("p (b t) -> p b t", t=4)), reads=[pk], writes=[uk])
                            S.op('pool', lambda e, u=u, ch=ch: e.tensor_copy(out=u[:, :, 0:2], in_=carrys[:, ch, :, :]), reads=['carrys'], writes=[uk])
                            t0v = t0[:, 0:64].rearrange("p (b t) -> p b t", t=4)
                            t1v = t1[:, 0:64].rearrange("p (b t) -> p b t", t=4)
                            S.op('act', lambda e, t0=t0, bank=bank, ch=ch: e.activation(out=t0[:, 0:64], in_=PS[bank][:, 0:64], func=AF.Identity, bias=fcb[:, ch:ch + 1], scale=fcw[:, ch, 2:3]),
                                 reads=[pk, 'fcw', 'fcb'], writes=[k0])
                            S.op('dve', lambda e, t0v=t0v, t1v=t1v, u=u, ch=ch: e.scalar_tensor_tensor(out=t1v, in0=u[:, :, 1:5], scalar=fcw[:, ch, 1:2], in1=t0v, op0=ALU.mult, op1=ALU.add),
                                 reads=[uk, k0, 'fcw'], writes=[k1])
                            S.op('dve', lambda e, t0v=t0v, t1v=t1v, u=u, ch=ch: e.scalar_tensor_tensor(out=t0v, in0=u[:, :, 0:4], scalar=fcw[:, ch, 0:1], in1=t1v, op0=ALU.mult, op1=ALU.add),
                                 reads=[uk, k1, 'fcw'], writes=[k0])
                            S.op('pool', lambda e, u=u, ch=ch: e.tensor_copy(out=carrys[:, ch, :, :], in_=u[:, :, 4:6]), reads=[uk], writes=['carrys'])
                            res.append((t0[:, 0:P], k0))
                    for (t0, t1, u, ch, uk, k0, k1) in taps:
                        S.op('dve', lambda e, t0=t0, t1=t1, u=u, ch=ch: e.scalar_tensor_tensor(out=t1[:, :], in0=u[:, 1:129], scalar=fcw[:, ch, 1:2], in1=t0[:, :], op0=ALU.mult, op1=ALU.add),
                             reads=[uk, k0, 'fcw'], writes=[k1])
                    for (t0, t1, u, ch, uk, k0, k1) in taps:
                        S.op('dve', lambda e, t0=t0, t1=t1, u=u, ch=ch: e.scalar_tensor_tensor(out=t0[:, :], in0=u[:, 0:128], scalar=fcw[:, ch, 0:1], in1=t1[:, :], op0=ALU.mult, op1=ALU.add),
                             reads=[uk, k1, 'fcw'], writes=[k0])
                        S.op('pool', lambda e, u=u, ch=ch: e.tensor_copy(out=carry[:, ch, :], in_=u[:, 128:130]), reads=[uk], writes=['carry'])
                    if pend_sm:
                        emit_sm(*pend_sm.pop())
                    pend_sm.append((c, res))
                    yield
                emit_sm(*pend_sm.pop())
                yield
            def ffn_s3(P, par, gpar, y_dst):
                x1t = x1b[par]
                xk = 'x1t%d' % par
                gT = gTb[gpar]
                gk = 'gT%d' % gpar
                for hf in range(2):
                    bank = 6 + hf
                    pk = ('ps', bank)
                    for c in range(NCH):
                        S.op('pe', lambda e, c=c, hf=hf, bank=bank: e.matmul(PS[bank][:P, :], lhsT=gT[:, c, 0:P], rhs=W_dn[:, c, hf * 512:(hf + 1) * 512], start=(c == 0), stop=(c == NCH - 1)),
                             reads=[gk, 'W_dn'], writes=[pk], inc=(c == NCH - 1))
                        if c % 4 == 3:
                            yield
                    S.op('dve', lambda e, hf=hf, bank=bank: e.tensor_tensor(out=rb2[:P, hf * 512:(hf + 1) * 512], in0=PS[bank][:P, :], in1=GT2[:P, hf * 512:(hf + 1) * 512], op=ALU.mult),
                         reads=[pk, 'GT2'], writes=['rb2'])
                    yield
                S.op('dve', lambda e: e.scalar_tensor_tensor(out=rb2[:P, :], in0=x1t[:P, :], scalar=ALPHA, in1=rb2[:P, :], op0=ALU.mult, op1=ALU.add), reads=[xk, 'rb2'], writes=['rb2'])
                yield
                layer_norm_rows(P, rb2[:P, :], x1t[:P, :], ['rb2'], [xk], 48)
                yield
                S.op('pool', lambda e: e.tensor_tensor(out=x1t[:P, :], in0=x1t[:P, :], in1=g2[:P, :], op=ALU.mult), reads=[xk, 'g2'], writes=[xk])
                S.op('pool', lambda e: e.tensor_tensor(out=x1t[:P, :], in0=x1t[:P, :], in1=b2[:P, :], op=ALU.add), reads=[xk, 'b2'], writes=[xk])
                S.dma('sp', lambda e: e.dma_start(out=y_dst, in_=x1t[:P, :]), reads=[xk], writes=[])
                yield

            def state_out(src_fn, nrow, dst):
                for g in range(11):
                    pk = ('ps', 1)
                    for c in range(4):
                        ch = 4 * g + c
                        S.op('pe', lambda e, c=c, ch=ch: e.transpose(out=PS[1][:nrow, c * 128:(c + 1) * 128], in_=src_fn(ch), identity=ident[:]),
                             reads=['carry', 'carrys', 'ident'], writes=[pk], inc=(c == 3))
                    sf = sfc[g % 2]
                    sfk = ('sfc', g % 2)
                    S.op('act', lambda e, sf=sf: e.copy(out=sf[:nrow, :], in_=PS[1][:nrow, :]), reads=[pk], writes=[sfk])
                    S.dma('sp', lambda e, g=g, sf=sf: e.dma_start(out=dst[:, g * 512:(g + 1) * 512], in_=sf[:nrow, :]), reads=[sfk], writes=[])

            def run_il(gens):
                gens = list(gens)
                while gens:
                    for g in list(gens):
                        try:
                            next(g)
                        except StopIteration:
                            gens.remove(g)

            def chain(*gs):
                for g in gs:
                    yield from g

            def delayed(n, g):
                for _ in range(n):
                    yield
                yield from g

            set_mod2(64, sels)
            run_il([ffn_s1a(64, T, 0)])
            ffn_s1b(64)
            run_il([ffn_s2(64, True, 0)])
            run_il([ffn_s3(64, 0, 0, y_s)])
            state_out(lambda ch: carrys[:, ch, :, :], 32, ffn_s)
            set_mod2(128, selp)
            run_il([ffn_s1a(128, 0, 0)])
            ffn_s1b(128)
            for i in range(NBLK):
                gens = [ffn_s2(128, False, i % 2)]
                tailg = []
                if i > 0:
                    tailg.append(ffn_s3(128, (i - 1) % 2, (i - 1) % 2, y_p[(i - 1) * 128:i * 128, :]))
                if i + 1 < NBLK:
                    tailg.append(ffn_s1a(128, (i + 1) * 128, (i + 1) % 2))
                if tailg:
                    gens.append(delayed(1, chain(*tailg)))
                run_il(gens)
                if i + 1 < NBLK:
                    ffn_s1b(128)
            run_il([ffn_s3(128, (NBLK - 1) % 2, (NBLK - 1) % 2, y_p[(NBLK - 1) * 128:NBLK * 128, :])])
            state_out(lambda ch: carry[:, ch, :], 2, ffn_p)
            S.finish()
    return nc


def _consts():
    c = {}
    c["c_ident"] = np.eye(128, dtype=np.float32)
    k = np.arange(128)
    c["c_maskT"] = (k[:, None] <= k[None, :]).astype(np.float32)
    auglp = np.zeros((3, 4, 128), np.float32)
    for s, m in enumerate(SLOPES):
        auglp[0, s] = 8 * m * k
        auglp[1, s] = 8 * m
        auglp[2, s] = 1024 * m
    c["c_auglp"] = auglp.reshape(3, 512)
    augls = np.ones((3, 128), np.float32)
    augls[0] = k
    c["c_augls"] = augls
    augrp = np.zeros((3, 32, 128), np.float32)
    augrp[0] = 1.0
    augrp[1] = -k[None, :]
    augrp[2] = (np.arange(32) - 16)[:, None]
    c["c_augrp"] = augrp.reshape(3, 4096)
    augrs = np.zeros((3, 16, 4, 4), np.float32)
    mp = np.array(SLOPES, np.float32)
    augrs[0] = 128.0 * mp[None, :, None]
    augrs[1] = 8.0 * mp[None, :, None] * (np.arange(16)[:, None, None] - np.arange(4)[None, None, :])
    augrs[2] = -16384.0 * mp[None, :, None]
    c["c_augrs"] = augrs.reshape(3, 256)
    bnew = np.zeros((16, 4, 4, 16, 4), np.float32)
    mnew = np.zeros((16, 4, 4, 16, 4), np.float32)
    for b in range(16):
        for t1 in range(4):
            for t in range(t1, 4):
                for pr in range(4):
                    bnew[b, t1, pr, b, t] = -SLOPES[pr] * (t - t1)
                    mnew[b, t1, pr, b, t] = 1.0
    c["c_bnew"] = bnew.reshape(64, 256)
    c["c_mnew"] = mnew.reshape(64, 256)
    selp = np.zeros((17, 128), np.float32)
    selp[0] = 1.0
    sels = np.zeros((17, 64), np.float32)
    for b in range(16):
        sels[1 + b, 4 * b:4 * b + 4] = 1.0
    c["c_selp"] = selp
    c["c_sels"] = sels
    c["c_pm8"] = (k % 8).astype(np.float32).reshape(128, 1)
    return c


_NC = None
_LAST = None


def kernel(x_prompt, x_sample, c_prompt, c_sample, cache_k, cache_v, page_table, state_conv, state_ffn,
           ln_emb_g, ln_emb_b, w_ada, b_ada, w_in, lambda_q1, lambda_k1, lambda_q2, lambda_k2, subln_g,
           conv_w, conv_b, conv_ln_g, conv_ln_b, w_out, ln1_g, ln1_b,
           w_up, ffn_conv_w, ffn_conv_b, w_down, ln2_g, ln2_b):
    global _NC
    f = lambda a: np.ascontiguousarray(np.asarray(a, dtype=np.float32))
    if _NC is None:
        _NC = build_nc()
    nc = _NC
    shared = dict(_consts())
    shared["cache_k"] = f(cache_k).reshape(NPHYS * 8, 16 * 512)
    shared["cache_v"] = f(cache_v).reshape(NPHYS * 8, 16 * 512)
    shared["ln_emb_g"] = f(ln_emb_g).reshape(1, D); shared["ln_emb_b"] = f(ln_emb_b).reshape(1, D)
    shared["w_ada"] = f(w_ada)[0]; shared["b_ada"] = f(b_ada).reshape(1, 6 * D)
    shared["w_in"] = f(w_in)[0]
    shared["lam_in"] = np.concatenate([f(lambda_q1)[0], f(lambda_k1)[0], f(lambda_q2)[0], f(lambda_k2)[0]]).reshape(1, 256)
    shared["subln_g"] = f(subln_g).reshape(128, 1)
    shared["conv_w"] = np.ascontiguousarray(f(conv_w)[0].reshape(31, 4, 128).transpose(2, 1, 0))
    shared["conv_b"] = np.ascontiguousarray(f(conv_b)[0].reshape(4, 128).T)
    shared["conv_ln_g"] = np.ascontiguousarray(f(conv_ln_g)[0].reshape(4, 128).T)
    shared["conv_ln_b"] = np.ascontiguousarray(f(conv_ln_b)[0].reshape(4, 128).T)
    shared["w_out"] = f(w_out)[0]
    shared["ln1_g"] = f(ln1_g).reshape(1, D); shared["ln1_b"] = f(ln1_b).reshape(1, D)
    shared["w_up"] = f(w_up)[0]
    shared["ffn_cw"] = np.ascontiguousarray(f(ffn_conv_w)[0].reshape(3, 44, 128).transpose(2, 1, 0))
    shared["ffn_cb"] = np.ascontiguousarray(f(ffn_conv_b)[0].reshape(44, 128).T)
    shared["w_down"] = f(w_down)[0]
    shared["ln2_g"] = f(ln2_g).reshape(1, D); shared["ln2_b"] = f(ln2_b).reshape(1, D)
    xp = f(x_prompt); xs = f(x_sample); cp = f(c_prompt); cs = f(c_sample)
    pt = np.asarray(page_table, dtype=np.int32)
    sc = f(state_conv)[0]; sf = f(state_ffn)[0]
    in_maps = []
    for c in range(NCORES):
        m = dict(shared)
        m["x_p"] = xp[c]
        m["x_s"] = xs[NS * c:NS * (c + 1)].reshape(ST, D)
        m["c_all"] = np.concatenate([cp[c:c + 1], cs[NS * c:NS * (c + 1)]], axis=0)
        ptc = pt[NS * c:NS * (c + 1)]
        m["pt_lay"] = np.ascontiguousarray(np.repeat(ptc.T, 8, axis=0))
        m["st_conv"] = sc[NS * c:NS * (c + 1)].reshape(NS * 30, 512)
        m["st_ffn"] = sf[NS * c:NS * (c + 1)].reshape(NS * 2, 2 * DFF)
        in_maps.append(m)
    res = run_bass_kernel_spmd(nc, in_maps, core_ids=list(range(NCORES)))
    global _LAST
    _LAST = res
    R = res.results
    cat = lambda name: np.stack([R[c][name] for c in range(NCORES)], axis=0)
    y_prompt = cat("y_p")
    y_sample = cat("y_s").reshape(NCORES * NS, 4, D)
    k_prompt = cat("k_p").reshape(1, NCORES, T, 8, 64)
    v_prompt = cat("v_p").reshape(1, NCORES, T, 4, 128)
    conv_prompt = cat("conv_p").reshape(1, NCORES, 30, 512)
    ffn_prompt = cat("ffn_p").reshape(1, NCORES, 2, 2 * DFF)
    k_sample = cat("k_s").reshape(1, NCORES * NS, 4, 8, 64)
    v_sample = cat("v_s").reshape(1, NCORES * NS, 4, 4, 128)
    conv_sample = cat("conv_s").reshape(1, NCORES * NS, 30, 512)
    ffn_sample = cat("ffn_s").reshape(1, NCORES * NS, 2, 2 * DFF)
    return (y_prompt, y_sample, k_prompt, v_prompt, conv_prompt, ffn_prompt, k_sample, v_sample, conv_sample, ffn_sample)
```

```python
import math
from contextlib import ExitStack

import numpy as np
import concourse.bass as bass
import concourse.mybir as mybir
from concourse.bass_utils import run_bass_kernel_spmd

F32 = mybir.dt.float32
BF16 = mybir.dt.bfloat16
I32 = mybir.dt.int32
AF = mybir.ActivationFunctionType
ALU = mybir.AluOpType
AX = mybir.AxisListType

D = 1024
T = 2048
NBLK = 16
NS = 16
ST = 64
DFF = 2816
NCH = 22
EPS = 1e-5
ALPHA = 2.0 ** 0.25
LAMBDA_INIT = 0.8 - 0.6 * math.exp(0.0)
SLOPES = [2.0 ** (-8.0 * (i + 1) / 4) for i in range(4)]
NCORES = 8
NPHYS = 2560
STOP_AFTER = None
DEBUG_X1 = False
SIDE_W = 2


class Sched:
    NDMA = 12

    def __init__(self, nc, es):
        self.nc = nc
        self.engs = {'pe': nc.tensor, 'act': nc.scalar, 'dve': nc.vector, 'pool': nc.gpsimd, 'sp': nc.sync}
        self.sem = {}
        self.cnt = {}
        for e in ['pe', 'act', 'dve', 'pool']:
            self.sem[e] = es.enter_context(nc.semaphore("s_" + e))
            self.cnt[e] = 0
        self.dsem = {}
        self.dcnt = {}
        self.dnext = {}
        for q in ['sp', 'pool']:
            self.dsem[q] = [es.enter_context(nc.semaphore("d_%s%d" % (q, i))) for i in range(self.NDMA)]
            self.dcnt[q] = [0] * self.NDMA
            self.dnext[q] = 0
        self.seen = {e: {} for e in self.engs}
        self.reg = {}
        self.pend = {e: ([], []) for e in self.engs}
        self.semobj = {}
        for e in self.sem:
            self.semobj[('c', e)] = self.sem[e]
        for q in self.dsem:
            for i, s in enumerate(self.dsem[q]):
                self.semobj[('d', q, i)] = s

    def _r(self, k):
        if k not in self.reg:
            self.reg[k] = [None, []]
        return self.reg[k]

    def _wait(self, e, tok):
        sk, val = tok
        if self.seen[e].get(sk, 0) >= val:
            return
        self.engs[e].wait_ge(self.semobj[sk], val)
        self.seen[e][sk] = val

    def _deps(self, e, reads, writes):
        own = ('c', e)
        deps = {}

        def add(tok, same_ok):
            if tok is None:
                return
            if tok[0] == own and same_ok:
                return
            if deps.get(tok[0], 0) < tok[1]:
                deps[tok[0]] = tok[1]
        for k in reads:
            add(self._r(k)[0], False)
        for k in writes:
            r = self._r(k)
            add(r[0], True)
            for t in r[1]:
                add(t, True)
        for sk, v in deps.items():
            self._wait(e, (sk, v))

    def op(self, e, fn, reads=(), writes=(), inc=True):
        reads = list(reads)
        writes = list(writes)
        self._deps(e, reads, writes)
        ins = fn(self.engs[e])
        pr, pw = self.pend[e]
        if not inc:
            pr.extend(reads)
            pw.extend(writes)
            return ins
        self.cnt[e] += 1
        ins.then_inc(self.sem[e], 1)
        tok = (('c', e), self.cnt[e])
        for k in reads + pr:
            self._r(k)[1].append(tok)
        for k in writes + pw:
            r = self._r(k)
            r[0] = tok
            r[1] = []
        self.pend[e] = ([], [])
        return ins

    def dma(self, q, fn, reads=(), writes=()):
        reads = list(reads)
        writes = list(writes)
        self._deps(q, reads, writes)
        i = self.dnext[q]
        self.dnext[q] = (i + 1) % self.NDMA
        sk = ('d', q, i)
        if self.dcnt[q][i] > 0:
            self._wait(q, (sk, self.dcnt[q][i]))
        ins = fn(self.engs[q])
        self.dcnt[q][i] += 16
        ins.then_inc(self.dsem[q][i], 16)
        tok = (sk, self.dcnt[q][i])
        for k in reads:
            self._r(k)[1].append(tok)
        for k in writes:
            r = self._r(k)
            r[0] = tok
            r[1] = []
        return tok

    def barrier(self):
        toks = [(('c', e), self.cnt[e]) for e in self.sem if self.cnt[e] > 0]
        for q in self.dsem:
            for i in range(self.NDMA):
                if self.dcnt[q][i] > 0:
                    toks.append((('d', q, i), self.dcnt[q][i]))
        for e in self.engs:
            for t in toks:
                if t[0] == ('c', e):
                    continue
                self._wait(e, t)
        self.reg = {}

    def finish(self):
        for q in self.dsem:
            for i in range(self.NDMA):
                if self.dcnt[q][i] > 0:
                    self._wait('sp', (('d', q, i), self.dcnt[q][i]))


def build_nc():
    nc = bass.Bass("TRN2", target_bir_lowering=False)

    def din(name, shape, dt=F32):
        return nc.dram_tensor(name, list(shape), dt, kind="ExternalInput").ap()

    def dout(name, shape, dt=F32):
        return nc.dram_tensor(name, list(shape), dt, kind="ExternalOutput").ap()

    x_p = din("x_p", [T, D])
    x_s = din("x_s", [ST, D])
    c_all = din("c_all", [17, D])
    cache_k = din("cache_k", [NPHYS * 8, 16 * 512])
    cache_v = din("cache_v", [NPHYS * 8, 16 * 512])
    pt_lay = din("pt_lay", [128, NS], I32)
    st_conv = din("st_conv", [NS * 30, 512])
    st_ffn = din("st_ffn", [NS * 2, 2 * DFF])
    ln_emb_g = din("ln_emb_g", [1, D]); ln_emb_b = din("ln_emb_b", [1, D])
    w_ada = din("w_ada", [D, 6 * D]); b_ada = din("b_ada", [1, 6 * D])
    w_in = din("w_in", [D, 2560])
    lam_in = din("lam_in", [1, 256])
    subln_g = din("subln_g", [128, 1])
    conv_w = din("conv_w", [128, 4, 31])
    conv_b = din("conv_b", [128, 4])
    conv_ln_g = din("conv_ln_g", [128, 4]); conv_ln_b = din("conv_ln_b", [128, 4])
    w_out = din("w_out", [D, D])
    ln1_g = din("ln1_g", [1, D]); ln1_b = din("ln1_b", [1, D])
    w_up = din("w_up", [D, 2 * DFF])
    ffn_cw = din("ffn_cw", [128, 44, 3]); ffn_cb = din("ffn_cb", [128, 44])
    w_down = din("w_down", [DFF, D])
    ln2_g = din("ln2_g", [1, D]); ln2_b = din("ln2_b", [1, D])
    c_ident = din("c_ident", [128, 128])
    c_maskT = din("c_maskT", [128, 128])
    c_auglp = din("c_auglp", [3, 4 * 128]); c_augrp = din("c_augrp", [3, 4096])
    c_augls = din("c_augls", [3, 128]); c_augrs = din("c_augrs", [3, 256])
    c_bnew = din("c_bnew", [64, 256]); c_mnew = din("c_mnew", [64, 256])
    c_selp = din("c_selp", [17, 128]); c_sels = din("c_sels", [17, 64])
    c_pm8 = din("c_pm8", [128, 1])

    y_p = dout("y_p", [T, D]); y_s = dout("y_s", [ST, D])
    k_p = dout("k_p", [T, 512]); v_p = dout("v_p", [T, 512])
    conv_p = dout("conv_p", [30, 512]); ffn_p = dout("ffn_p", [2, 2 * DFF])
    k_s = dout("k_s", [ST, 512]); v_s = dout("v_s", [ST, 512])
    conv_s = dout("conv_s", [NS * 30, 512]); ffn_s = dout("ffn_s", [NS * 2, 2 * DFF])
    x1_scr = nc.dram_tensor("x1_scr", [T + ST, D], F32, kind=("ExternalOutput" if DEBUG_X1 else "Internal")).ap()

    mod_scr = nc.dram_tensor("mod_scr", [17, 6 * D], F32, kind="Internal").ap()

    es_outer = ExitStack()
    with es_outer as es:
        S = Sched(nc, es)

        def sb(name, shape, dt=F32, stack=None):
            return (stack or es).enter_context(nc.sbuf_tensor(name, list(shape), dt))

        def ps(name, shape, dt=F32, stack=None):
            return (stack or es).enter_context(nc.psum_tensor(name, list(shape), dt))

        PS = [ps("psb%d" % i, [128, 512]) for i in range(8)]
        PSB = [p[:].bitcast(BF16) for p in PS]

        ident = sb("ident", [128, 128]); identb = sb("identb", [128, 128], BF16)
        onesb = sb("onesb", [128, 128], BF16); onesf = sb("onesf", [128, 128])
        maskT = sb("maskT", [128, 128]); maskTb = sb("maskTb", [128, 128], BF16)
        selp = sb("selp", [17, 128]); sels = sb("sels", [17, 64])
        neglam = sb("neglam", [128, 1]); sg8 = sb("sg8", [128, 1])
        sg8row = sb("sg8row", [128, 128])
        cw = sb("cw", [128, 4, 31]); cb = sb("cb", [128, 4]); clg = sb("clg", [128, 4]); clb = sb("clb", [128, 4])
        fcw = sb("fcw", [128, 44, 3]); fcb = sb("fcb", [128, 44])
        small = sb("small", [128, 64])
        epsc = sb("epsc", [128, 1])

        S.dma('sp', lambda e: e.dma_start(out=ident[:], in_=c_ident), writes=['ident'])
        S.dma('sp', lambda e: e.dma_start(out=maskT[:], in_=c_maskT), writes=['maskT'])
        S.dma('sp', lambda e: e.dma_start(out=selp[:], in_=c_selp), writes=['selp'])
        S.dma('sp', lambda e: e.dma_start(out=sels[:], in_=c_sels), writes=['sels'])
        S.dma('sp', lambda e: e.dma_start(out=sg8[:], in_=subln_g), writes=['sg8'])
        S.dma('sp', lambda e: e.dma_start(out=sg8row[:], in_=subln_g.rearrange("p o -> o p").partition_broadcast(128)), writes=['sg8row'])
        S.dma('sp', lambda e: e.dma_start(out=cw[:], in_=conv_w), writes=['cw'])
        S.dma('sp', lambda e: e.dma_start(out=cb[:], in_=conv_b), writes=['cb'])
        S.dma('sp', lambda e: e.dma_start(out=clg[:], in_=conv_ln_g), writes=['clg'])
        S.dma('sp', lambda e: e.dma_start(out=clb[:], in_=conv_ln_b), writes=['clb'])
        S.dma('sp', lambda e: e.dma_start(out=fcw[:], in_=ffn_cw), writes=['fcw'])
        S.dma('sp', lambda e: e.dma_start(out=fcb[:], in_=ffn_cb), writes=['fcb'])
        S.op('dve', lambda e: e.tensor_copy(out=identb[:], in_=ident[:]), reads=['ident'], writes=['identb'])
        S.op('dve', lambda e: e.tensor_copy(out=maskTb[:], in_=maskT[:]), reads=['maskT'], writes=['maskTb'])
        S.op('dve', lambda e: e.memset(onesb[:], 1.0), writes=['onesb'])
        S.op('dve', lambda e: e.memset(onesf[:], 1.0), writes=['onesf'])
        S.op('dve', lambda e: e.memset(epsc[:], EPS), writes=['epsc'])
        S.op('dve', lambda e: e.tensor_scalar(out=sg8[:], in0=sg8[:], scalar1=1.0 - LAMBDA_INIT, scalar2=None, op0=ALU.mult), reads=['sg8'], writes=['sg8'])
        S.op('dve', lambda e: e.tensor_scalar(out=sg8row[:], in0=sg8row[:], scalar1=1.0 - LAMBDA_INIT, scalar2=None, op0=ALU.mult), reads=['sg8row'], writes=['sg8row'])

        def rsqrt(out_ap, in_ap, reads, writes, scale=1.0):
            S.op('act', lambda e: e.activation(out=out_ap, in_=in_ap, func=AF.Ln, bias=epsc[:in_ap.shape[0], 0:1], scale=scale), reads=list(reads) + ['epsc'], writes=writes)
            S.op('act', lambda e: e.activation(out=out_ap, in_=out_ap, func=AF.Exp, scale=-0.5), reads=writes, writes=writes)

        def layer_norm_rows(P, src_ap, dst_ap, src_keys, dst_keys, col):
            st6 = small[:P, col:col + 12].rearrange("p (c s) -> p c s", s=6)
            mv = small[:P, col + 12:col + 14]
            rstd = small[:P, col + 14:col + 15]
            nmr = small[:P, col + 15:col + 16]
            k = ('small', col)
            for c in range(2):
                S.op('dve', lambda e, c=c: e.bn_stats(out=st6[:, c, :], in_=src_ap[:, c * 512:(c + 1) * 512]),
                     reads=src_keys, writes=[k], inc=(c == 1))
            S.op('dve', lambda e: e.bn_aggr(out=mv, in_=st6), reads=[k], writes=[k])
            rsqrt(rstd, mv[:, 1:2], [k], [k])
            S.op('dve', lambda e: e.scalar_tensor_tensor(out=nmr, in0=mv[:, 0:1], scalar=-1.0, in1=rstd, op0=ALU.mult, op1=ALU.mult),
                 reads=[k], writes=[k])
            S.op('act', lambda e: e.activation(out=dst_ap, in_=src_ap, func=AF.Identity, bias=nmr, scale=rstd),
                 reads=src_keys + [k], writes=dst_keys)

        def transpose_to(P, src_ap, src_keys, nchunk, dst_fn, dst_keys, banks, evac=('act', 'dve')):
            done = 0
            gi = 0
            while done < nchunk:
                n = min(4, nchunk - done)
                bank = banks[gi % len(banks)]
                pk = ('ps', bank)
                for j in range(n):
                    c = done + j
                    S.op('pe', lambda e, c=c, j=j: e.transpose(out=PS[bank][:, j * 128:j * 128 + P], in_=src_ap[:, c * 128:(c + 1) * 128], identity=ident[:P, :P]),
                         reads=src_keys + ['ident'], writes=[pk], inc=(j == n - 1))
                eng = evac[gi % len(evac)]
                src = PS[bank][:, 0:n * 128].rearrange("p (n t) -> p n t", t=128)[:, :, 0:P]
                dst = dst_fn(done, n)
                if eng == 'act':
                    S.op('act', lambda e, src=src, dst=dst: e.copy(out=dst, in_=src), reads=[pk], writes=dst_keys)
                else:
                    S.op('dve', lambda e, src=src, dst=dst: e.tensor_copy(out=dst, in_=src), reads=[pk], writes=dst_keys)
                done += n
                gi += 1

        def bcast_rows(dst, P, sel, modt, c0, add_one, key, bank):
            selk = 'selp' if sel is selp else 'sels'
            for hf in range(2):
                pk = ('ps', bank)
                S.op('pe', lambda e, hf=hf: e.matmul(PS[bank][:P, :], lhsT=sel[:, :P], rhs=modt[:, c0 + hf * 512:c0 + (hf + 1) * 512], start=True, stop=True),
                     reads=[selk, 'mod'], writes=[pk])
                if add_one:
                    S.op('dve', lambda e, hf=hf: e.tensor_scalar(out=dst[:P, hf * 512:(hf + 1) * 512], in0=PS[bank][:P, :], scalar1=1.0, scalar2=None, op0=ALU.add),
                         reads=[pk], writes=[key])
                else:
                    S.op('dve', lambda e, hf=hf: e.tensor_copy(out=dst[:P, hf * 512:(hf + 1) * 512], in_=PS[bank][:P, :]), reads=[pk], writes=[key])

        es_mix = ExitStack()
        with es_mix as em:
            W_in = sb("W_in", [128, 8, 2560], BF16, stack=em)
            W_out = sb("W_out", [128, 8, D], BF16, stack=em)
            with ExitStack() as e0:
                modA = sb("modA", [17, 3 * D], stack=e0)
                modB = sb("modB", [17, 3 * D], stack=e0)
                ct = sb("ct", [17, D], stack=e0)
                cT = sb("cT", [128, 8, 17], BF16, stack=e0)
                bada = sb("bada", [17, 6 * D], stack=e0)
                wada = [sb("wada%d" % i, [128, 8, 512], BF16, stack=e0) for i in range(3)]
                lamt = sb("lamt", [128, 256], stack=e0)
                lamp = sb("lamp", [128, 128], stack=e0)
                lams = sb("lams", [128, 4], stack=e0)
                S.dma('sp', lambda e: e.dma_start(out=ct[:], in_=c_all), writes=['ct'])
                S.dma('sp', lambda e: e.dma_start(out=bada[:], in_=b_ada.partition_broadcast(17)), writes=['bada'])
                S.dma('sp', lambda e: e.dma_start(out=lamt[:], in_=lam_in.partition_broadcast(128)), writes=['lamt'])
                S.op('dve', lambda e: e.tensor_tensor(out=lamp[:].rearrange("p (a d) -> p a d", d=64), in0=lamt[:].rearrange("p (a d) -> p a d", d=64)[:, 0::2, :],
                                                      in1=lamt[:].rearrange("p (a d) -> p a d", d=64)[:, 1::2, :], op=ALU.mult), reads=['lamt'], writes=['lamp'])
                S.op('dve', lambda e: e.tensor_reduce(out=lams[:, 0:2], in_=lamp[:].rearrange("p (a d) -> p a d", d=64), axis=AX.X, op=ALU.add), reads=['lamp'], writes=['lams'])
                S.op('act', lambda e: e.activation(out=lams[:, 2:4], in_=lams[:, 0:2], func=AF.Exp), reads=['lams'], writes=['lams2'])
                S.op('dve', lambda e: e.tensor_tensor(out=neglam[:], in0=lams[:, 3:4], in1=lams[:, 2:3], op=ALU.subtract), reads=['lams2'], writes=['neglam'])
                S.op('dve', lambda e: e.tensor_scalar(out=neglam[:], in0=neglam[:], scalar1=-LAMBDA_INIT, scalar2=None, op0=ALU.add), reads=['neglam'], writes=['neglam'])
                S.op('act', lambda e: e.activation(out=ct[:], in_=ct[:], func=AF.Silu), reads=['ct'], writes=['ct'])
                transpose_to(17, ct[:], ['ct'], 8, lambda c0, n: cT[:, c0:c0 + n, :], ['cT'], [0, 1])
                wv = w_ada.rearrange("(k p) n -> p k n", p=128)
                for n in range(12):
                    wb = wada[n % 3]
                    wk = ('wada', n % 3)
                    S.dma('pool', lambda e, n=n, wb=wb: e.dma_start(out=wb[:], in_=wv[:, :, n * 512:(n + 1) * 512]), writes=[wk])
                    bank = 2 + (n % 2)
                    pk = ('ps', bank)
                    for k in range(8):
                        S.op('pe', lambda e, k=k, wb=wb, bank=bank: e.matmul(PS[bank][:17, :], lhsT=cT[:, k, :], rhs=wb[:, k, :], start=(k == 0), stop=(k == 7)),
                             reads=['cT', wk], writes=[pk], inc=(k == 7))
                    mt = modA if n < 6 else modB
                    off = (n % 6) * 512
                    S.op('dve', lambda e, mt=mt, off=off, bank=bank, n=n: e.tensor_tensor(out=mt[:, off:off + 512], in0=PS[bank][:17, :], in1=bada[:, n * 512:(n + 1) * 512], op=ALU.add),
                         reads=[pk, 'bada'], writes=['mod'])
                w_in_v = w_in.rearrange("(k p) n -> p k n", p=128)
                for i in range(4):
                    S.dma('pool', lambda e, i=i: e.dma_start(out=W_in[:, 2 * i:2 * i + 2, :], in_=w_in_v[:, 2 * i:2 * i + 2, :]), writes=['W_in'])
                S.dma('pool', lambda e: e.dma_start(out=W_out[:], in_=w_out.rearrange("(k p) n -> p k n", p=128)), writes=['W_out'])
                S.dma('sp', lambda e: e.dma_start(out=mod_scr[:, 0:3 * D], in_=modA[:, :]), reads=['mod'], writes=['mod_scr'])
                S.dma('sp', lambda e: e.dma_start(out=mod_scr[:, 3 * D:6 * D], in_=modB[:, :]), reads=['mod'], writes=['mod_scr'])
            S.barrier()

            gE = sb("gE", [128, D], stack=em); bE = sb("bE", [128, D], stack=em)
            g1 = sb("g1", [128, D], stack=em); b1 = sb("b1", [128, D], stack=em)
            SC = sb("SC", [128, D], stack=em); SH = sb("SH", [128, D], stack=em); GT = sb("GT", [128, D], stack=em)
            xin = sb("xin", [128, D], stack=em); xpb = [sb("xp%d" % i, [128, D], stack=em) for i in range(2)]; htmp = sb("htmp", [128, D], stack=em)
            hT = sb("hT", [128, 8, 128], BF16, stack=em)
            QTb = [sb("QT%d" % i, [128, 4, 128], BF16, stack=em) for i in range(2)]
            QT = QTb[0]
            kvst = [sb("kvst%d" % i, [128, 512], stack=em) for i in range(2)]
            sig = sb("sig", [128, 128], stack=em)
            mixTb = [sb("mixT%d" % i, [128, 8, 128], BF16, stack=em) for i in range(2)]
            mixT = mixTb[0]
            cvt = sb("cvt", [128, 512], stack=em); cvn = sb("cvn", [128, 512], stack=em)

            for (tl, src, key) in [(gE, ln_emb_g, 'gE'), (bE, ln_emb_b, 'bE'), (g1, ln1_g, 'g1'), (b1, ln1_b, 'b1')]:
                S.dma('sp', lambda e, tl=tl, src=src: e.dma_start(out=tl[:], in_=src.partition_broadcast(128)), writes=[key])

            def set_mod_tiles(P, sel, modt, scaleoff, shiftoff, gateoff):
                bcast_rows(SH, P, sel, modt, shiftoff, False, 'SH', 4)
                bcast_rows(SC, P, sel, modt, scaleoff, True, 'SC', 5)
                bcast_rows(GT, P, sel, modt, gateoff, False, 'GT', 4)
                S.op('dve', lambda e: e.tensor_tensor(out=htmp[:P, :], in0=bE[:P, :], in1=SC[:P, :], op=ALU.mult), reads=['bE', 'SC'], writes=['htmp'])
                S.op('dve', lambda e: e.tensor_tensor(out=SH[:P, :], in0=SH[:P, :], in1=htmp[:P, :], op=ALU.add), reads=['SH', 'htmp'], writes=['SH'])
                S.op('dve', lambda e: e.tensor_tensor(out=SC[:P, :], in0=SC[:P, :], in1=gE[:P, :], op=ALU.mult), reads=['SC', 'gE'], writes=['SC'])

            def front(P, x_src, kout, vout, kT_dst, vbf_dst, u_dst, par=0):
                xp = xpb[par]
                xk = 'xp%d' % par
                QT = QTb[par]
                qk = 'QT%d' % par
                S.dma('sp', lambda e: e.dma_start(out=xin[:P, :], in_=x_src), writes=['xin'])
                layer_norm_rows(P, xin[:P, :], xp[:P, :], ['xin'], [xk], 0)
                S.op('dve', lambda e: e.tensor_tensor(out=htmp[:P, :], in0=xp[:P, :], in1=SC[:P, :], op=ALU.mult), reads=[xk, 'SC'], writes=['htmp'])
                S.op('dve', lambda e: e.tensor_tensor(out=htmp[:P, :], in0=htmp[:P, :], in1=SH[:P, :], op=ALU.add), reads=['htmp', 'SH'], writes=['htmp'])
                yield
                transpose_to(P, htmp[:P, :], ['htmp'], 8, lambda c0, n: hT[:, c0:c0 + n, 0:P], ['hT'], [0, 1])
                yield
                for wi, (c0, dst) in enumerate([(512, kout), (1024, vout)]):
                    bank = 2 if wi == 0 else 0
                    pk = ('ps', bank)
                    for k in range(8):
                        S.op('pe', lambda e, k=k, c0=c0, bank=bank: e.matmul(PS[bank][:P, :], lhsT=hT[:, k, 0:P], rhs=W_in[:, k, c0:c0 + 512], start=(k == 0), stop=(k == 7)),
                             reads=['hT', 'W_in'], writes=[pk], inc=(k == 7))
                    st = kvst[wi]
                    sk = ('kvst', wi)
                    S.op('act', lambda e, st=st, bank=bank: e.copy(out=st[:P, :], in_=PS[bank][:P, :]), reads=[pk], writes=[sk])
                    S.dma('sp', lambda e, st=st, dst=dst: e.dma_start(out=dst, in_=st[:P, :]), reads=[sk], writes=[])
                    if wi == 1:
                        vbf_dst(st, sk)
                    yield
                for pr in range(4):
                    for wi, c0 in enumerate([0, 512]):
                        bank = 1 + ((2 * pr + wi) % 2)
                        pk = ('ps', bank)
                        for k in range(8):
                            S.op('pe', lambda e, k=k, c0=c0, pr=pr, bank=bank: e.matmul(PS[bank][:, 0:P], lhsT=W_in[:, k, c0 + 128 * pr:c0 + 128 * pr + 128], rhs=hT[:, k, 0:P], start=(k == 0), stop=(k == 7)),
                                 reads=['hT', 'W_in'], writes=[pk], inc=(k == 7))
                        if wi == 0:
                            S.op('act', lambda e, pr=pr, bank=bank: e.copy(out=QT[:, pr, 0:P], in_=PS[bank][:, 0:P]), reads=[pk], writes=[qk])
                        else:
                            kT_dst(pr, bank, pk)
                    yield
                for c in range(4):
                    pa = ('ps', 0)
                    pg = ('ps', 1)
                    for k in range(8):
                        S.op('pe', lambda e, k=k, c=c: e.matmul(PS[0][:, 0:P], lhsT=W_in[:, k, 1536 + 128 * c:1536 + 128 * c + 128], rhs=hT[:, k, 0:P], start=(k == 0), stop=(k == 7)),
                             reads=['hT', 'W_in'], writes=[pa], inc=(k == 7))
                    for k in range(8):
                        S.op('pe', lambda e, k=k, c=c: e.matmul(PS[1][:, 0:P], lhsT=W_in[:, k, 2048 + 128 * c:2048 + 128 * c + 128], rhs=hT[:, k, 0:P], start=(k == 0), stop=(k == 7)),
                             reads=['hT', 'W_in'], writes=[pg], inc=(k == 7))
                    S.op('act', lambda e: e.activation(out=sig[:, 0:P], in_=PS[1][:, 0:P], func=AF.Sigmoid), reads=[pg], writes=['sig'])
                    u_dst(c, pa)
                    yield
                S.op('pool', lambda e: e.tensor_tensor(out=xp[:P, :], in0=xp[:P, :], in1=gE[:P, :], op=ALU.mult), reads=[xk, 'gE'], writes=[xk])
                S.op('pool', lambda e: e.tensor_tensor(out=xp[:P, :], in0=xp[:P, :], in1=bE[:P, :], op=ALU.add), reads=[xk, 'bE'], writes=[xk])
                yield

            def attn_epilogue_tok(P, obank0, col0, mp=0):
                for hb in range(3):
                    nh = 3 if hb < 2 else 2
                    pkb = ('ps', obank0 + hb)
                    ov = PS[obank0 + hb][:P, 0:480].rearrange("p (h w) -> p h w", w=160)[:, 0:nh, :]
                    S.op('dve', lambda e, hb=hb, ov=ov, nh=nh: e.reciprocal(out=rs8[:P, 3 * hb:3 * hb + nh], in_=ov[:, :, 128:129].rearrange("p h o -> p (h o)")), reads=[pkb], writes=['rs8'])
                    S.op('dve', lambda e, hb=hb, ov=ov, nh=nh: e.tensor_tensor(out=osb[:P, 3 * hb:3 * hb + nh, :], in0=ov[:, :, 0:128],
                                                                        in1=rs8[:P, 3 * hb:3 * hb + nh].unsqueeze(2).to_broadcast([P, nh, 128]), op=ALU.mult), reads=[pkb, 'rs8'], writes=['osb'])
                S.op('dve', lambda e: e.scalar_tensor_tensor(out=o4[:P], in0=osb[:P, 1::2, :], scalar=neglam[:P, 0:1], in1=osb[:P, 0::2, :], op0=ALU.mult, op1=ALU.add),
                     reads=['osb', 'neglam'], writes=['o4'])
                S.op('pool', lambda e: e.tensor_tensor(out=osq[:P], in0=o4[:P], in1=o4[:P], op=ALU.mult), reads=['o4'], writes=['osq'])
                S.op('dve', lambda e: e.tensor_reduce(out=rs8[:P, 8:12], in_=osq[:P], axis=AX.X, op=ALU.add), reads=['osq'], writes=['rs8b'])
                rsqrt(rs8[:P, 12:16], rs8[:P, 8:12], ['rs8b'], ['rs8c'], scale=1.0 / 128)
                S.op('dve', lambda e: e.tensor_tensor(out=o4[:P], in0=o4[:P], in1=rs8[:P, 12:16].unsqueeze(2).to_broadcast([P, 4, 128]), op=ALU.mult), reads=['o4', 'rs8c'], writes=['o4'])
                S.op('dve', lambda e: e.tensor_tensor(out=o4[:P], in0=o4[:P], in1=sg8row[:P, :].unsqueeze(1).to_broadcast([P, 4, 128]), op=ALU.mult), reads=['o4', 'sg8row'], writes=['o4'])
                transpose_to(P, o4[:P].rearrange("p a e -> p (a e)"), ['o4'], 4, lambda c0, n: mixTb[mp][:, c0:c0 + n, col0:col0 + P], [('mixT', mp)], [3], evac=('act',))

            def conv_ln_silu(P, cv_fn, cv_keys, bA=1, bB=0, mp=0):
                pk = ('ps', bA)
                for c in range(4):
                    S.op('pe', lambda e, c=c: e.transpose(out=PS[bA][:P, c * 128:(c + 1) * 128], in_=cv_fn(c), identity=ident[:]),
                         reads=cv_keys + ['ident'], writes=[pk], inc=(c == 3))
                S.op('act', lambda e: e.copy(out=cvt[:P, :], in_=PS[bA][:P, :]), reads=[pk], writes=['cvt'])
                st6 = small[:P, 16:22]
                mv = small[:P, 22:24]
                rstd = small[:P, 24:25]
                nmr = small[:P, 25:26]
                k = ('small', 16)
                S.op('dve', lambda e: e.bn_stats(out=st6, in_=cvt[:P, :]), reads=['cvt'], writes=[k])
                S.op('dve', lambda e: e.bn_aggr(out=mv, in_=st6), reads=[k], writes=[k])
                rsqrt(rstd, mv[:, 1:2], [k], [k])
                S.op('dve', lambda e: e.scalar_tensor_tensor(out=nmr, in0=mv[:, 0:1], scalar=-1.0, in1=rstd, op0=ALU.mult, op1=ALU.mult), reads=[k], writes=[k])
                S.op('act', lambda e: e.activation(out=cvn[:P, :], in_=cvt[:P, :], func=AF.Identity, bias=nmr, scale=rstd), reads=['cvt', k], writes=['cvn'])
                pk0 = ('ps', bB)
                for c in range(4):
                    S.op('pe', lambda e, c=c: e.transpose(out=PS[bB][:, c * 128:c * 128 + P], in_=cvn[:P, c * 128:(c + 1) * 128], identity=ident[:P, :P]),
                         reads=['cvn', 'ident'], writes=[pk0], inc=(c == 3))
                for c in range(4):
                    S.op('act', lambda e, c=c: e.activation(out=mixTb[mp][:, 4 + c, 0:P], in_=PS[bB][:, c * 128:c * 128 + P], func=AF.Silu, bias=clb[:, c:c + 1], scale=clg[:, c:c + 1]),
                         reads=[pk0, 'clg', 'clb'], writes=[('mixT', mp)])

            def out_ln1(P, row0, par=0, rbuf=None, x1o=None, b0=2, mp=0):
                xp = xpb[par]
                xk = 'xp%d' % par
                rk, ok = ('rbuf', 'x1o') if rbuf is not None else ('htmp', 'xin')
                if rbuf is None:
                    rbuf, x1o = htmp, xin
                for hf in range(2):
                    bank = b0 + hf
                    pk = ('ps', bank)
                    for k in range(8):
                        S.op('pe', lambda e, k=k, hf=hf, bank=bank: e.matmul(PS[bank][:P, :], lhsT=mixTb[mp][:, k, 0:P], rhs=W_out[:, k, hf * 512:(hf + 1) * 512], start=(k == 0), stop=(k == 7)),
                             reads=[('mixT', mp), 'W_out'], writes=[pk], inc=(k == 7))
                    S.op('dve', lambda e, hf=hf, bank=bank: e.tensor_tensor(out=rbuf[:P, hf * 512:(hf + 1) * 512], in0=PS[bank][:P, :], in1=GT[:P, hf * 512:(hf + 1) * 512], op=ALU.mult),
                         reads=[pk, 'GT'], writes=[rk])
                S.op('dve', lambda e: e.scalar_tensor_tensor(out=rbuf[:P, :], in0=xp[:P, :], scalar=ALPHA, in1=rbuf[:P, :], op0=ALU.mult, op1=ALU.add), reads=[xk, rk], writes=[rk])
                layer_norm_rows(P, rbuf[:P, :], x1o[:P, :], [rk], [ok], 32)
                S.op('dve', lambda e: e.tensor_tensor(out=x1o[:P, :], in0=x1o[:P, :], in1=g1[:P, :], op=ALU.mult), reads=[ok, 'g1'], writes=[ok])
                S.op('dve', lambda e: e.tensor_tensor(out=x1o[:P, :], in0=x1o[:P, :], in1=b1[:P, :], op=ALU.add), reads=[ok, 'b1'], writes=[ok])
                S.dma('sp', lambda e: e.dma_start(out=x1_scr[row0:row0 + P, :], in_=x1o[:P, :]), reads=[ok], writes=['x1scr'])

            with ExitStack() as e1:
                modS = sb("modS", [17, 3 * D], stack=e1)
                S.dma('sp', lambda e: e.dma_start(out=modS[:, :], in_=mod_scr[:, 0:3 * D]), writes=['mod'])
                kall = sb("kall", [128, 16, 512], BF16, stack=e1)
                vall = sb("vall", [128, 16, 512], BF16, stack=e1)
                ktp = [sb("ktp%d" % i, [128, 4, 128], BF16, stack=e1) for i in range(3)]
                PTs = sb("PTs", [128, 512], BF16, stack=e1)
                Us = sb("Us", [128, 4, NS, 34], stack=e1)
                KTs = sb("KTs", [128, 4, 64], BF16, stack=e1)
                Vsb = sb("Vsb", [64, 512], BF16, stack=e1)
                auglsb = sb("auglsb", [128, 128], BF16, stack=e1); augrsb = sb("augrsb", [128, 256], BF16, stack=e1)
                ucont = kall[:].rearrange("p a f -> p (a f)").bitcast(F32)[:, 0:4 * NS * 30].rearrange("p (c b t) -> p c b t", c=4, t=30)
                bnew = sb("bnew", [64, 256], stack=e1); mnew = sb("mnew", [64, 256], stack=e1)
                pnewf = sb("pnewf", [64, 512], stack=e1); pnewT = sb("pnewT", [64, 512], BF16, stack=e1)
                ptl = sb("ptl", [128, NS], I32, stack=e1); ptf = sb("ptf", [128, NS], stack=e1)
                pm8 = sb("pm8", [128, 1], stack=e1); idx = sb("idx", [128, NS], I32, stack=e1)
                stc = sb("stc", [120, 512], stack=e1)
                osall = sb("osall", [128, 4, 64], stack=e1); ossq = sb("ossq", [128, 4, 64], stack=e1)
                rsum = sb("rsum", [128, 32], stack=e1); on32 = sb("on32", [128, 32], stack=e1)
                rstd_s = sb("rstd_s", [128, 256], stack=e1)
                cvs = sb("cvs", [128, 4, 64], stack=e1)
                acc = [sb("acc%d" % i, [128, 64], stack=e1) for i in range(2)]
                osq_s = sb("osq_s", [128, 128], stack=e1)

                for r0 in (0, 64):
                    S.dma('pool', lambda e, r0=r0: e.dma_start(out=auglsb[r0:r0 + 3, :], in_=c_augls), writes=['auglsb'])
                    S.dma('pool', lambda e, r0=r0: e.dma_start(out=augrsb[r0:r0 + 3, :], in_=c_augrs), writes=['augrsb'])
                S.dma('sp', lambda e: e.dma_start(out=bnew[:], in_=c_bnew), writes=['bnew'])
                S.dma('sp', lambda e: e.dma_start(out=mnew[:], in_=c_mnew), writes=['mnew'])
                S.dma('sp', lambda e: e.dma_start(out=ptl[:], in_=pt_lay), writes=['ptl'])
                S.dma('sp', lambda e: e.dma_start(out=pm8[:], in_=c_pm8), writes=['pm8'])
                S.op('dve', lambda e: e.tensor_copy(out=ptf[:], in_=ptl[:]), reads=['ptl'], writes=['ptf'])
                S.op('dve', lambda e: e.tensor_scalar(out=ptf[:], in0=ptf[:], scalar1=8.0, scalar2=pm8[:, 0:1], op0=ALU.mult, op1=ALU.add), reads=['ptf', 'pm8'], writes=['ptf'])
                S.op('dve', lambda e: e.tensor_copy(out=idx[:], in_=ptf[:]), reads=['ptf'], writes=['idx'])

                def gather(b, which):
                    if which == 0:
                        S.dma('pool', lambda e: e.indirect_dma_start(out=kall[:].rearrange("p a f -> p (a f)"), out_offset=None, in_=cache_k,
                                                                    in_offset=bass.IndirectOffsetOnAxis(ap=idx[:, b:b + 1], axis=0)), reads=['idx'], writes=['kall'])
                    else:
                        S.dma('pool', lambda e: e.indirect_dma_start(out=vall[:].rearrange("p a f -> p (a f)"), out_offset=None, in_=cache_v,
                                                                    in_offset=bass.IndirectOffsetOnAxis(ap=idx[:, b:b + 1], axis=0)), reads=['idx'], writes=['vall'])
                if STOP_AFTER == 0.1:
                    S.barrier(); S.finish()
                    return nc
                gather(0, 0)
                gather(0, 1)
                if STOP_AFTER == 0.2:
                    S.barrier(); S.finish()
                    return nc

                for g in range(4):
                    S.dma('sp', lambda e, g=g: e.dma_start(out=stc[:, :], in_=st_conv[g * 120:(g + 1) * 120, :]), writes=['stc'])
                    pk = ('ps', 0)
                    for c in range(4):
                        S.op('pe', lambda e, c=c: e.transpose(out=PS[0][:, c * 128:c * 128 + 120], in_=stc[:, c * 128:(c + 1) * 128], identity=ident[:120, :120]),
                             reads=['stc', 'ident'], writes=[pk], inc=(c == 3))
                    for c in range(4):
                        S.op('dve', lambda e, c=c, g=g: e.tensor_copy(out=Us[:, c, 4 * g:4 * g + 4, 0:30], in_=PS[0][:, c * 128:c * 128 + 120].rearrange("p (b t) -> p b t", t=30)),
                             reads=[pk], writes=['Us'])

                if STOP_AFTER == 0.3:
                    S.barrier(); S.finish()
                    return nc
                set_mod_tiles(64, sels, modS, 1024, 0, 2048)
                if STOP_AFTER == 0.4:
                    S.barrier(); S.finish()
                    return nc

                def s_vbf(st, sk):
                    S.op('dve', lambda e: e.tensor_copy(out=Vsb[:, :], in_=st[:64, :]), reads=[sk], writes=['Vsb'])

                def s_kT(pr, bank, pk):
                    S.op('dve', lambda e: e.tensor_copy(out=KTs[:, pr, :], in_=PS[bank][:, 0:64]), reads=[pk], writes=['KTs'])

                def s_u(c, pa):
                    S.op('dve', lambda e: e.tensor_tensor(out=Us[:, c, :, 30:34], in0=PS[0][:, 0:64].rearrange("p (b t) -> p b t", t=4),
                                                          in1=sig[:, 0:64].rearrange("p (b t) -> p b t", t=4), op=ALU.mult), reads=[pa, 'sig'], writes=['Us'])

                for _ in front(64, x_s, k_s, v_s, s_kT, s_vbf, s_u):
                    pass

                if STOP_AFTER == 0.5:
                    S.barrier(); S.finish()
                    return nc
                for hh in range(2):
                    bank = 4 if hh == 0 else 2
                    pk = ('ps', bank)
                    r0 = 64 * hh
                    for pr in range(4):
                        S.op('pe', lambda e, pr=pr, r0=r0, bank=bank: e.matmul(PS[bank][:64, pr * 64:(pr + 1) * 64], lhsT=KTs[r0:r0 + 64, pr, :], rhs=QT[r0:r0 + 64, pr, 0:64], start=True, stop=True),
                             reads=['KTs', 'QT0'], writes=[pk], inc=(pr == 3))
                    S.op('dve', lambda e, hh=hh, bank=bank: e.scalar_tensor_tensor(out=pnewf[:, hh * 256:(hh + 1) * 256], in0=PS[bank][:64, 0:256], scalar=0.125, in1=bnew[:, :], op0=ALU.mult, op1=ALU.add),
                         reads=[pk, 'bnew'], writes=['pnewf'])
                S.op('act', lambda e: e.activation(out=pnewf[:], in_=pnewf[:], func=AF.Exp), reads=['pnewf'], writes=['pnewf'])
                pnv = pnewT[:].rearrange("p (b hh pr t) -> p b hh pr t", hh=2, pr=4, t=4)
                for hh in range(2):
                    S.op('dve', lambda e, hh=hh: e.tensor_tensor(out=pnv[:, :, hh, :, :].rearrange("p b pr t -> p pr b t"), in0=pnewf[:, hh * 256:(hh + 1) * 256].rearrange("p (pr b t) -> p pr b t", pr=4, t=4),
                                                                in1=mnew[:, :].rearrange("p (pr b t) -> p pr b t", pr=4, t=4), op=ALU.mult), reads=['pnewf', 'mnew'], writes=['pnewT'])
                if STOP_AFTER == 0.6:
                    S.barrier(); S.finish()
                    return nc

                SB = [5, 3]
                for b in range(NS):
                    for t16 in range(16):
                        tb = 6 + (t16 % 2)
                        tpk = ('ps', tb)
                        for pr in range(4):
                            S.op('pe', lambda e, pr=pr, t16=t16, tb=tb: e.transpose(out=PSB[tb][:, pr * 128:(pr + 1) * 128], in_=kall[:, t16, pr * 128:(pr + 1) * 128], identity=identb[:]),
                                 reads=['kall', 'identb'], writes=[tpk], inc=(pr == 3))
                        kt = ktp[t16 % 3]
                        kk = ('ktp', t16 % 3)
                        if t16 % 2 == 0:
                            S.op('act', lambda e, kt=kt, tb=tb: e.copy(out=kt[:].rearrange("p a k -> p (a k)"), in_=PSB[tb][:, 0:512]), reads=[tpk], writes=[kk])
                        else:
                            S.op('dve', lambda e, kt=kt, tb=tb: e.tensor_copy(out=kt[:].rearrange("p a k -> p (a k)"), in_=PSB[tb][:, 0:512]), reads=[tpk], writes=[kk])
                        for hh in range(2):
                            r0 = 64 * hh
                            for pr in range(4):
                                S.op('pe', lambda e, pr=pr, hh=hh, r0=r0, kt=kt, t16=t16, b=b: e.matmul(PS[SB[hh]][:, t16 * 16 + pr * 4:t16 * 16 + pr * 4 + 4], lhsT=kt[r0:r0 + 64, pr, :],
                                                                                               rhs=QT[r0:r0 + 64, pr, 4 * b:4 * b + 4], start=(t16 == 0 and pr == 0), stop=False, skip_group_check=True),
                                     reads=[kk, 'QT0'], writes=[('ps', SB[hh])], inc=False)
                    if b + 1 < NS:
                        gather(b + 1, 0)
                    for hh in range(2):
                        r0 = 64 * hh
                        S.op('pe', lambda e, hh=hh, r0=r0: e.matmul(PS[SB[hh]][:, 0:256], lhsT=auglsb[r0:r0 + 3, :], rhs=augrsb[r0:r0 + 3, :], start=False, stop=True, skip_group_check=True),
                             reads=['auglsb', 'augrsb'], writes=[('ps', SB[hh])])
                    for hh in range(2):
                        S.op('act', lambda e, hh=hh: e.activation(out=PTs[:, hh * 256:(hh + 1) * 256], in_=PS[SB[hh]][:, 0:256], func=AF.Exp, scale=0.125), reads=[('ps', SB[hh])], writes=['PTs'])
                    pk4 = ('ps', 4)
                    for hh in range(2):
                        for t16 in range(16):
                            S.op('pe', lambda e, t16=t16, hh=hh: e.matmul(PS[4][:, hh * 16:(hh + 1) * 16], lhsT=onesb[:, :], rhs=PTs[:, hh * 256 + t16 * 16:hh * 256 + (t16 + 1) * 16], start=(t16 == 0), stop=False),
                                 reads=['PTs', 'onesb'], writes=[pk4], inc=False)
                        S.op('pe', lambda e, b=b, hh=hh: e.matmul(PS[4][:, hh * 16:(hh + 1) * 16], lhsT=onesb[0:64, :], rhs=pnv[:, b, hh, :, :], start=False, stop=True),
                             reads=['pnewT', 'onesb'], writes=[pk4], inc=False)
                    for dh in range(4):
                        for hh in range(2):
                            oc = 64 + dh * 8 + hh * 4
                            for t16 in range(16):
                                S.op('pe', lambda e, dh=dh, t16=t16, hh=hh, oc=oc: e.matmul(PS[4][:, oc:oc + 4], lhsT=vall[:, t16, dh * 128:(dh + 1) * 128],
                                                                                         rhs=PTs[:, hh * 256 + t16 * 16 + dh * 4:hh * 256 + t16 * 16 + dh * 4 + 4], start=(t16 == 0), stop=False),
                                     reads=['PTs', 'vall'], writes=[pk4], inc=False)
                            S.op('pe', lambda e, dh=dh, b=b, hh=hh, oc=oc: e.matmul(PS[4][:, oc:oc + 4], lhsT=Vsb[0:64, dh * 128:(dh + 1) * 128], rhs=pnv[:, b, hh, dh, :], start=False, stop=True),
                                 reads=['pnewT', 'Vsb'], writes=[pk4], inc=(dh == 3 and hh == 1))
                    if b + 1 < NS:
                        gather(b + 1, 1)
                    rsv = rsum[:].rearrange("p (d h q) -> p d h q", h=2, q=4)
                    for hh in range(2):
                        S.op('dve', lambda e, hh=hh: e.reciprocal(out=rsv[:, :, hh, :], in_=PS[4][:, hh * 16:(hh + 1) * 16].rearrange("p (d q) -> p d q", q=4)), reads=[pk4], writes=['rsum'])
                    S.op('dve', lambda e: e.tensor_tensor(out=on32[:], in0=PS[4][:, 64:96], in1=rsum[:], op=ALU.mult), reads=[pk4, 'rsum'], writes=['on32'])
                    onv = on32[:].rearrange("p (d h q) -> p d h q", h=2, q=4)
                    S.op('dve', lambda e, b=b: e.scalar_tensor_tensor(out=osall[:, :, 4 * b:4 * b + 4], in0=onv[:, :, 1, :], scalar=neglam[:, 0:1], in1=onv[:, :, 0, :], op0=ALU.mult, op1=ALU.add),
                         reads=['on32', 'neglam'], writes=['osall'])

                if STOP_AFTER == 0.7:
                    S.barrier(); S.finish()
                    return nc
                S.op('dve', lambda e: e.tensor_tensor(out=ossq[:], in0=osall[:], in1=osall[:], op=ALU.mult), reads=['osall'], writes=['ossq'])
                pk = ('ps', 6)
                S.op('pe', lambda e: e.matmul(PS[6][:, 0:256], lhsT=onesf[:, :], rhs=ossq[:].rearrange("p a t -> p (a t)"), start=True, stop=True), reads=['ossq', 'onesf'], writes=[pk])
                rsqrt(rstd_s[:], PS[6][:, 0:256], [pk], ['rstd_s'], scale=1.0 / 128)
                S.op('dve', lambda e: e.scalar_tensor_tensor(out=mixT[:, 0:4, 0:64], in0=osall[:], scalar=sg8[:, 0:1], in1=rstd_s[:].rearrange("p (a t) -> p a t", t=64), op0=ALU.mult, op1=ALU.mult),
                     reads=['osall', 'sg8', 'rstd_s'], writes=[('mixT', 0)])

                if STOP_AFTER == 0.8:
                    S.barrier(); S.finish()
                    return nc
                acc4 = [acc[0][:, 0:64], acc[1][:, 0:64], osq_s[:, 0:64], osq_s[:, 64:128]]
                accv = [a.rearrange("p (b t) -> p b t", t=4) for a in acc4]
                for c in range(4):
                    S.op('dve', lambda e, c=c: e.tensor_scalar(out=accv[c], in0=Us[:, c, :, 0:4], scalar1=cw[:, c, 0:1], scalar2=cb[:, c:c + 1], op0=ALU.mult, op1=ALU.add),
                         reads=['Us', 'cw', 'cb'], writes=[('acc', c)])
                for j in range(1, 31):
                    for c in range(4):
                        dst = accv[c] if j < 30 else cvs[:, c, :].rearrange("p (b t) -> p b t", t=4)
                        S.op('dve', lambda e, c=c, j=j, dst=dst: e.scalar_tensor_tensor(out=dst, in0=Us[:, c, :, j:j + 4], scalar=cw[:, c, j:j + 1], in1=accv[c], op0=ALU.mult, op1=ALU.add),
                             reads=['Us', ('acc', c)], writes=[('acc', c)] if j < 30 else ['cvs'])
                conv_ln_silu(64, lambda c: cvs[:, c, :], ['cvs'])
                for c in range(4):
                    S.op('act', lambda e, c=c: e.copy(out=ucont[:, c, :, :], in_=Us[:, c, :, 4:34]), reads=['Us'], writes=['kall'])
                for g in range(4):
                    pk = ('ps', 1)
                    for c in range(4):
                        S.op('pe', lambda e, c=c, g=g: e.transpose(out=PS[1][:120, c * 128:(c + 1) * 128], in_=ucont[:, c, 4 * g:4 * g + 4, :], identity=ident[:]),
                             reads=['kall', 'ident'], writes=[pk], inc=(c == 3))
                    S.op('act', lambda e: e.copy(out=stc[:, :], in_=PS[1][:120, :]), reads=[pk], writes=['stc'])
                    S.dma('sp', lambda e, g=g: e.dma_start(out=conv_s[g * 120:(g + 1) * 120, :], in_=stc[:, :]), reads=['stc'], writes=[])
                if STOP_AFTER == 0.9:
                    S.barrier(); S.finish()
                    return nc
                out_ln1(64, T)
            S.barrier()
            if STOP_AFTER == 1:
                S.op('dve', lambda e: e.tensor_copy(out=htmp[:, :], in_=mixT[:].rearrange("p a t -> p (a t)")), writes=['htmp'])
                S.dma('sp', lambda e: e.dma_start(out=y_p[0:128, :], in_=htmp[:, :]), reads=['htmp'], writes=[])
                S.finish()
                return nc

            with ExitStack() as e2:
                with ExitStack() as et:
                    modP = sb("modP", [17, 3 * D], stack=et)
                    S.dma('sp', lambda e: e.dma_start(out=modP[:, :], in_=mod_scr[:, 0:3 * D]), writes=['mod'])
                    set_mod_tiles(128, selp, modP, 1024, 0, 2048)
                    S.barrier()
                KT = sb("KT", [128, 4, T], BF16, stack=e2)
                Vext = sb("Vext", [128, NBLK, 4, 130], BF16, stack=e2)
                auglpb = sb("auglpb", [128, 512], BF16, stack=e2); augrpb = sb("augrpb", [128, 4096], BF16, stack=e2)
                osb = sb("osb", [128, 8, 128], stack=e2); o4 = sb("o4", [128, 4, 128], stack=e2); osq = sb("osq", [128, 4, 128], stack=e2)
                rs8 = sb("rs8", [128, 16], stack=e2)
                Ub = [sb("U%d" % i, [128, 4, 30 + 128], stack=e2) for i in range(3)]
                PT = [sb("PT%d" % i, [128, 512], BF16, stack=e2) for i in range(3)]
                cvab = [[sb("cva%d_%d" % (p_, i), [128, 128], stack=e2) for i in range(4)] for p_ in range(2)]
                cpo = sb("cpo", [30, 512], stack=e2)
                rbuf_p = sb("rbuf", [128, D], stack=e2); x1o_p = sb("x1o", [128, D], stack=e2)

                for r0 in (0, 64):
                    S.dma('pool', lambda e, r0=r0: e.dma_start(out=auglpb[r0:r0 + 3, :], in_=c_auglp), writes=['auglpb'])
                    S.dma('pool', lambda e, r0=r0: e.dma_start(out=augrpb[r0:r0 + 3, :], in_=c_augrp), writes=['augrpb'])
                S.op('dve', lambda e: e.memset(Ub[0][:], 0.0), writes=['U'])
                S.op('dve', lambda e: e.memset(Ub[1][:], 0.0), writes=['U'])
                S.op('dve', lambda e: e.memset(Ub[2][:], 0.0), writes=['U'])
                S.op('pool', lambda e: e.memset(Vext[:].rearrange("p a b c -> p (a b c)"), 1.0), writes=['Vext'])

                ptcnt = [0]

                def p_front(i):
                    par = i % 2
                    up_i = i % 3
                    un_i = (i + 1) % 3
                    Up = Ub[up_i]
                    Un = Ub[un_i]

                    def p_vbf(st, sk):
                        S.op('pool', lambda e: e.tensor_copy(out=Vext[:, i, :, 0:128], in_=st[:, :].rearrange("p (a e) -> p a e", e=128)), reads=[sk], writes=['Vext'])

                    def p_kT(pr, bank, pk):
                        S.op('dve', lambda e: e.tensor_copy(out=KT[:, pr, i * 128:(i + 1) * 128], in_=PS[bank][:, 0:128]), reads=[pk], writes=['KT'])

                    def p_u(c, pa):
                        S.op('dve', lambda e: e.tensor_tensor(out=Up[:, c, 30:158], in0=PS[0][:, 0:128], in1=sig[:, 0:128], op=ALU.mult), reads=[pa, 'sig'], writes=[('U', up_i, c)])
                        S.op('act', lambda e: e.copy(out=Un[:, c, 0:30], in_=Up[:, c, 128:158]), reads=[('U', up_i, c)], writes=[('U', un_i, c)])

                    yield from front(128, x_p[i * 128:(i + 1) * 128, :], k_p[i * 128:(i + 1) * 128, :], v_p[i * 128:(i + 1) * 128, :], p_kT, p_vbf, p_u, par=par)

                def p_back(i):
                    par = i % 2
                    ui = i % 3
                    U = Ub[ui]
                    cva = cvab[par]
                    QT = QTb[par]
                    qk = 'QT%d' % par
                    for c in range(4):
                        S.op('dve', lambda e, c=c: e.tensor_scalar(out=cva[c][:, :], in0=U[:, c, 0:128], scalar1=cw[:, c, 0:1], scalar2=cb[:, c:c + 1], op0=ALU.mult, op1=ALU.add),
                             reads=[('U', ui, c), 'U', 'cw', 'cb'], writes=[('cva', par, c)])
                    taps_left = list(range(1, 31))

                    def emit_taps(n):
                        for _ in range(n):
                            if not taps_left:
                                return
                            j = taps_left.pop(0)
                            for c in range(4):
                                S.op('dve', lambda e, c=c, j=j: e.scalar_tensor_tensor(out=cva[c][:, :], in0=U[:, c, j:j + 128], scalar=cw[:, c, j:j + 1], in1=cva[c][:, :], op0=ALU.mult, op1=ALU.add),
                                     reads=[('U', ui, c), ('cva', par, c)], writes=[('cva', par, c)])
                    emit_taps(2)
                    yield
                    if i == NBLK - 1:
                        pk = ('ps', 4)
                        for c in range(4):
                            S.op('pe', lambda e, c=c: e.transpose(out=PS[4][:30, c * 128:(c + 1) * 128], in_=U[:, c, 128:158], identity=ident[:]),
                                 reads=[('U', ui, c), 'U', 'ident'], writes=[pk], inc=(c == 3))
                        S.op('act', lambda e: e.copy(out=cpo[:, :], in_=PS[4][:30, :]), reads=[pk], writes=['cpo'])
                        S.dma('sp', lambda e: e.dma_start(out=conv_p, in_=cpo[:, :]), reads=['cpo'], writes=[])

                    groups = []
                    for h in range(8):
                        for g in range((i + 4) // 4):
                            groups.append((h, g))

                    def emit_scores(h, g):
                        r0 = 64 * (h % 2)
                        pr = h // 2
                        j0 = 4 * g
                        nb = min(4, i + 1 - j0)
                        sbank = 3 + (ptcnt[0] % 2)
                        spk = ('ps', sbank)
                        for jj in range(nb):
                            S.op('pe', lambda e, jj=jj: e.matmul(PS[sbank][:, jj * 128:(jj + 1) * 128], lhsT=KT[r0:r0 + 64, pr, (j0 + jj) * 128:(j0 + jj + 1) * 128],
                                                                 rhs=QT[r0:r0 + 64, pr, 0:128], start=(jj == 0), stop=False, skip_group_check=True),
                                 reads=['KT', qk], writes=[spk], inc=False)
                        g0 = j0 - i + 16
                        S.op('pe', lambda e: e.matmul(PS[sbank][:, 0:nb * 128], lhsT=auglpb[r0:r0 + 3, pr * 128:(pr + 1) * 128], rhs=augrpb[r0:r0 + 3, g0 * 128:(g0 + nb) * 128], start=False, stop=True, skip_group_check=True),
                             reads=['auglpb', 'augrpb'], writes=[spk])
                        pt = PT[ptcnt[0] % 3]
                        ptk = ('PT', ptcnt[0] % 3)
                        ptcnt[0] += 1
                        S.op('act', lambda e: e.activation(out=pt[:, 0:nb * 128], in_=PS[sbank][:, 0:nb * 128], func=AF.Exp, scale=0.125), reads=[spk], writes=[ptk])
                        if j0 + nb - 1 == i:
                            S.op('pool', lambda e: e.tensor_tensor(out=pt[:, (nb - 1) * 128:nb * 128], in0=pt[:, (nb - 1) * 128:nb * 128], in1=maskTb[:, :], op=ALU.mult),
                                 reads=[ptk, 'maskTb'], writes=[ptk])
                        return (pt, ptk, j0, nb)

                    def emit_pv(h, st):
                        pt, ptk, j0, nb = st
                        pr = h // 2
                        ob = 5 + h // 3
                        ocol = (h % 3) * 160
                        opk = ('ps', ob)
                        for jj in range(nb):
                            j = j0 + jj
                            S.op('pe', lambda e, jj=jj, j=j: e.matmul(PS[ob][:, ocol:ocol + 129], lhsT=pt[:, jj * 128:(jj + 1) * 128], rhs=Vext[:, j, pr, 0:129], start=(j == 0), stop=(j == i)),
                                 reads=[ptk, 'Vext'], writes=[opk], inc=(j == i))

                    st_prev = emit_scores(*groups[0])
                    for gi, (h, g) in enumerate(groups):
                        st_next = emit_scores(*groups[gi + 1]) if gi + 1 < len(groups) else None
                        emit_pv(h, st_prev)
                        st_prev = st_next
                        if g == (i + 4) // 4 - 1:
                            emit_taps(4)
                            yield
                    emit_taps(31)
                    yield
                    attn_epilogue_tok(128, 5, 0, mp=par)
                    yield

                def p_tail(i):
                    par = i % 2
                    cva = cvab[par]
                    conv_ln_silu(128, lambda c: cva[c][:, :], [('cva', par, c) for c in range(4)], bA=1, bB=0, mp=par)
                    yield
                    out_ln1(128, i * 128, par=par, rbuf=rbuf_p, x1o=x1o_p, b0=1, mp=par)
                    yield

                def chain2(*gs):
                    for g in gs:
                        yield from g

                def run_interleaved(gens, weights=None):
                    gens = list(gens)
                    weights = list(weights or [1] * len(gens))
                    wmap = {id(g): w for g, w in zip(gens, weights)}
                    while gens:
                        for g in list(gens):
                            for _ in range(wmap[id(g)]):
                                try:
                                    next(g)
                                except StopIteration:
                                    gens.remove(g)
                                    break

                run_interleaved([p_front(0)])
                for i in range(NBLK):
                    gl = [p_back(i)]
                    side = []
                    if i > 0:
                        side.append(p_tail(i - 1))
                    if i + 1 < NBLK:
                        side.append(p_front(i + 1))
                    wl = [1]
                    if side:
                        gl.append(chain2(*side))
                        wl.append(SIDE_W)
                    run_interleaved(gl, wl)
                run_interleaved([p_tail(NBLK - 1)])
            S.barrier()
            if STOP_AFTER == 2:
                S.finish()
                return nc

        with ExitStack() as ef:
            W_up = sb("W_up", [128, 8, 2 * DFF], BF16, stack=ef)
            W_dn = sb("W_dn", [128, NCH, D], BF16, stack=ef)
            modst = [sb("modst%d" % i, [17, 512], stack=ef) for i in range(2)]
            g2 = sb("g2", [128, D], stack=ef); b2 = sb("b2", [128, D], stack=ef)
            SC2 = sb("SC2", [128, D], stack=ef); SH2 = sb("SH2", [128, D], stack=ef); GT2 = sb("GT2", [128, D], stack=ef)
            x1b = [sb("x1t%d" % i, [128, D], stack=ef) for i in range(2)]; ht2 = sb("ht2", [128, D], stack=ef); rb2 = sb("rb2", [128, D], stack=ef)
            h2T = sb("h2T", [128, 8, 128], BF16, stack=ef)
            gTb = [sb("gT%d" % i, [128, NCH, 128], BF16, stack=ef) for i in range(2)]
            carry = sb("carry", [128, 44, 2], stack=ef)
            carrys = sb("carrys", [128, 44, NS, 2], stack=ef)
            ua = [sb("ua%d" % i, [128, 130], stack=ef) for i in range(4)]
            tt = [sb("tt%d" % i, [128, 128], stack=ef) for i in range(8)]
            uas = [sb("uas%d" % i, [128, NS, 6], stack=ef) for i in range(2)]
            sfc = [sb("sfc%d" % i, [32, 512], stack=ef) for i in range(2)]

            w_up_v = w_up.rearrange("(k p) n -> p k n", p=128)
            for g in range(11):
                for half in range(2):
                    c0 = half * DFF + g * 256
                    S.dma('pool', lambda e, c0=c0: e.dma_start(out=W_up[:, :, c0:c0 + 256], in_=w_up_v[:, :, c0:c0 + 256]), writes=[('W_up', half, g)])
            w_dn_v = w_down.rearrange("(c p) n -> p c n", p=128)
            for i in range(2):
                S.dma('pool', lambda e, i=i: e.dma_start(out=W_dn[:, 11 * i:11 * i + 11, :], in_=w_dn_v[:, 11 * i:11 * i + 11, :]), writes=['W_dn'])
            S.dma('sp', lambda e: e.dma_start(out=g2[:], in_=ln2_g.partition_broadcast(128)), writes=['g2'])
            S.dma('sp', lambda e: e.dma_start(out=b2[:], in_=ln2_b.partition_broadcast(128)), writes=['b2'])
            S.op('dve', lambda e: e.memset(carry[:], 0.0), writes=['carry'])
            for g in range(11):
                pk = ('ps', 0)
                sf = sfc[g % 2]
                sfk = ('sfc', g % 2)
                S.dma('sp', lambda e, g=g, sf=sf: e.dma_start(out=sf[:, :], in_=st_ffn[:, g * 512:(g + 1) * 512]), writes=[sfk])
                for c in range(4):
                    S.op('pe', lambda e, c=c, sf=sf: e.transpose(out=PS[0][:, c * 128:c * 128 + 32], in_=sf[:, c * 128:(c + 1) * 128], identity=ident[:32, :32]),
                         reads=[sfk, 'ident'], writes=[pk], inc=(c == 3))
                S.op('dve', lambda e, g=g: e.tensor_copy(out=carrys[:, 4 * g:4 * g + 4, :, :], in_=PS[0][:, :].rearrange("p (c x) -> p c x", x=128)[:, :, 0:32].rearrange("p c (b t) -> p c b t", t=2)),
                     reads=[pk], writes=['carrys'])

            def set_mod2(P, sel):
                selk = 'selp' if sel is selp else 'sels'
                n = 0
                for (dst, key, off, one, bank) in [(SH2, 'SH2', 0, False, 4), (SC2, 'SC2', 1024, True, 5), (GT2, 'GT2', 2048, False, 4)]:
                    for hf in range(2):
                        mt = modst[n % 2]
                        mk = ('modst', n % 2)
                        n += 1
                        c0 = 3 * D + off + hf * 512
                        S.dma('sp', lambda e, mt=mt, c0=c0: e.dma_start(out=mt[:, :], in_=mod_scr[:, c0:c0 + 512]), writes=[mk])
                        pk = ('ps', bank)
                        S.op('pe', lambda e, mt=mt, bank=bank: e.matmul(PS[bank][:P, :], lhsT=sel[:, :P], rhs=mt[:, :], start=True, stop=True), reads=[selk, mk], writes=[pk])
                        if one:
                            S.op('dve', lambda e, hf=hf, dst=dst, bank=bank: e.tensor_scalar(out=dst[:P, hf * 512:(hf + 1) * 512], in0=PS[bank][:P, :], scalar1=1.0, scalar2=None, op0=ALU.add),
                                 reads=[pk], writes=[key])
                        else:
                            S.op('dve', lambda e, hf=hf, dst=dst, bank=bank: e.tensor_copy(out=dst[:P, hf * 512:(hf + 1) * 512], in_=PS[bank][:P, :]), reads=[pk], writes=[key])

            def ffn_s1a(P, row0, par):
                x1t = x1b[par]
                xk = 'x1t%d' % par
                S.dma('sp', lambda e: e.dma_start(out=x1t[:P, :], in_=x1_scr[row0:row0 + P, :]), reads=['x1scr'], writes=[xk])
                S.op('dve', lambda e: e.tensor_tensor(out=ht2[:P, :], in0=x1t[:P, :], in1=SC2[:P, :], op=ALU.mult), reads=[xk, 'SC2'], writes=['ht2'])
                S.op('dve', lambda e: e.tensor_tensor(out=ht2[:P, :], in0=ht2[:P, :], in1=SH2[:P, :], op=ALU.add), reads=['ht2', 'SH2'], writes=['ht2'])
                yield

            def ffn_s1b(P):
                transpose_to(P, ht2[:P, :], ['ht2'], 8, lambda c0, n: h2T[:, c0:c0 + n, 0:P], ['h2T'], [0, 1])

            def ffn_s2(P, sample, gpar):
                gT = gTb[gpar]
                gk = 'gT%d' % gpar
                pend_sm = []

                def emit_sm(c, res):
                    (ca, ka), (cb_, kb) = res
                    S.op('act', lambda e: e.activation(out=ca, in_=ca, func=AF.Silu), reads=[ka], writes=[ka])
                    S.op('pool', lambda e: e.tensor_tensor(out=gT[:, c, 0:P], in0=ca, in1=cb_, op=ALU.mult), reads=[ka, kb], writes=[gk])

                for c in range(NCH):
                    res = []
                    taps = []
                    for half in range(2):
                        ch = c + NCH * half
                        bank = 2 + ((2 * c + half) % 4)
                        pk = ('ps', bank)
                        col = ch * 128
                        for k in range(8):
                            S.op('pe', lambda e, k=k, col=col, bank=bank: e.matmul(PS[bank][:, 0:P], lhsT=W_up[:, k, col:col + 128], rhs=h2T[:, k, 0:P], start=(k == 0), stop=(k == 7)),
                                 reads=['h2T', ('W_up', half, c // 2)], writes=[pk], inc=(k == 7))
                        ti = (2 * c + half) % 4
                        t0 = tt[ti]; t1 = tt[4 + ti]
                        k0 = ('tt', ti); k1 = ('tt', 4 + ti)
                        if not sample:
                            u = ua[ti]
                            uk = ('ua', ti)
                            S.op('act', lambda e, u=u, bank=bank: e.copy(out=u[:, 2:130], in_=PS[bank][:, 0:128]), reads=[pk], writes=[uk])
                            S.op('pool', lambda e, u=u, ch=ch: e.tensor_copy(out=u[:, 0:2], in_=carry[:, ch, :]), reads=['carry'], writes=[uk])
                            S.op('act', lambda e, t0=t0, bank=bank, ch=ch: e.activation(out=t0[:, :], in_=PS[bank][:, 0:128], func=AF.Identity, bias=fcb[:, ch:ch + 1], scale=fcw[:, ch, 2:3]),
                                 reads=[pk, 'fcw', 'fcb'], writes=[k0])
                            taps.append((t0, t1, u, ch, uk, k0, k1))
                            res.append((t0[:, 0:P], k0))
                        else:
                            u = uas[half]
                            uk = ('uas', half)
                            S.op('act', lambda e, u=u, bank=bank: e.copy(out=u[:, :, 2:6], in_=PS[bank][:, 0:64].rearrange("p (b t) -> p b t", t=4)), reads=[pk], writes=[uk])
                            S.op('pool', lambda e, u=u, ch=ch: e.tensor_copy(out=u[:, :, 0:2], in_=carrys[:, ch, :, :]), reads=['carrys'], writes=[uk])
                            t0v = t0[:, 0:64].rearrange("p (b t) -> p b t", t=4)
                            t1v = t1[:, 0:64].rearrange("p (b t) -> p b t", t=4)
                            S.op('act', lambda e, t0=t0, bank=bank, ch=ch: e.activation(out=t0[:, 0:64], in_=PS[bank][:, 0:64], func=AF.Identity, bias=fcb[:, ch:ch + 1], scale=fcw[:, ch, 2:3]),
                                 reads=[pk, 'fcw', 'fcb'], writes=[k0])
                            S.op('dve', lambda e, t0v=t0v, t1v=t1v, u=u, ch=ch: e.scalar_tensor_tensor(out=t1v, in0=u[:, :, 1:5], scalar=fcw[:, ch, 1:2], in1=t0v, op0=ALU.mult, op1=ALU.add),
                                 reads=[uk, k0, 'fcw'], writes=[k1])
                            S.op('dve', lambda e, t0v=t0v, t1v=t1v, u=u, ch=ch: e.scalar_tensor_tensor(out=t0v, in0=u[:, :, 0:4], scalar=fcw[:, ch, 0:1], in1=t1v, op0=ALU.mult, op1=ALU.add),
                                 reads=[uk, k1, 'fcw'], writes=[k0])
                            S.op('pool', lambda e, u=u, ch=ch: e.tensor_copy(out=carrys[:, ch, :, :], in_=u[:, :, 4:6]), reads=[uk], writes=['carrys'])
                            res.append((t0[:, 0:P], k0))
                    for (t0, t1, u, ch, uk, k0, k1) in taps:
                        S.op('dve', lambda e, t0=t0, t1=t1, u=u, ch=ch: e.scalar_tensor_tensor(out=t1[:, :], in0=u[:, 1:129], scalar=fcw[:, ch, 1:2], in1=t0[:, :], op0=ALU.mult, op1=ALU.add),
                             reads=[uk, k0, 'fcw'], writes=[k1])
                    for (t0, t1, u, ch, uk, k0, k1) in taps:
                        S.op('dve', lambda e, t0=t0, t1=t1, u=u, ch=ch: e.scalar_tensor_tensor(out=t0[:, :], in0=u[:, 0:128], scalar=fcw[:, ch, 0:1], in1=t1[:, :], op0=ALU.mult, op1=ALU.add),
                             reads=[uk, k1, 'fcw'], writes=[k0])
                        S.op('pool', lambda e, u=u, ch=ch: e.tensor_copy(out=carry[:, ch, :], in_=u[:, 128:130]), reads=[uk], writes=['carry'])
                    if pend_sm:
                        emit_sm(*pend_sm.pop())
                    pend_sm.append((c, res))
                    yield
                emit_sm(*pend_sm.pop())
                yield
            def ffn_s3(P, par, gpar, y_dst):
                x1t = x1b[par]
                xk = 'x1t%d' % par
                gT = gTb[gpar]
                gk = 'gT%d' % gpar
                for hf in range(2):
                    bank = 6 + hf
                    pk = ('ps', bank)
                    for c in range(NCH):
                        S.op('pe', lambda e, c=c, hf=hf, bank=bank: e.matmul(PS[bank][:P, :], lhsT=gT[:, c, 0:P], rhs=W_dn[:, c, hf * 512:(hf + 1) * 512], start=(c == 0), stop=(c == NCH - 1)),
                             reads=[gk, 'W_dn'], writes=[pk], inc=(c == NCH - 1))
                        if c % 4 == 3:
                            yield
                    S.op('dve', lambda e, hf=hf, bank=bank: e.tensor_tensor(out=rb2[:P, hf * 512:(hf + 1) * 512], in0=PS[bank][:P, :], in1=GT2[:P, hf * 512:(hf + 1) * 512], op=ALU.mult),
                         reads=[pk, 'GT2'], writes=['rb2'])
                    yield
                S.op('dve', lambda e: e.scalar_tensor_tensor(out=rb2[:P, :], in0=x1t[:P, :], scalar=ALPHA, in1=rb2[:P, :], op0=ALU.mult, op1=ALU.add), reads=[xk, 'rb2'], writes=['rb2'])
                yield
                layer_norm_rows(P, rb2[:P, :], x1t[:P, :], ['rb2'], [xk], 48)
                yield
                S.op('pool', lambda e: e.tensor_tensor(out=x1t[:P, :], in0=x1t[:P, :], in1=g2[:P, :], op=ALU.mult), reads=[xk, 'g2'], writes=[xk])
                S.op('pool', lambda e: e.tensor_tensor(out=x1t[:P, :], in0=x1t[:P, :], in1=b2[:P, :], op=ALU.add), reads=[xk, 'b2'], writes=[xk])
                S.dma('sp', lambda e: e.dma_start(out=y_dst, in_=x1t[:P, :]), reads=[xk], writes=[])
                yield

            def state_out(src_fn, nrow, dst):
                for g in range(11):
                    pk = ('ps', 1)
                    for c in range(4):
                        ch = 4 * g + c
                        S.op('pe', lambda e, c=c, ch=ch: e.transpose(out=PS[1][:nrow, c * 128:(c + 1) * 128], in_=src_fn(ch), identity=ident[:]),
                             reads=['carry', 'carrys', 'ident'], writes=[pk], inc=(c == 3))
                    sf = sfc[g % 2]
                    sfk = ('sfc', g % 2)
                    S.op('act', lambda e, sf=sf: e.copy(out=sf[:nrow, :], in_=PS[1][:nrow, :]), reads=[pk], writes=[sfk])
                    S.dma('sp', lambda e, g=g, sf=sf: e.dma_start(out=dst[:, g * 512:(g + 1) * 512], in_=sf[:nrow, :]), reads=[sfk], writes=[])

            def run_il(gens):
                gens = list(gens)
                while gens:
                    for g in list(gens):
                        try:
                            next(g)
                        except StopIteration:
                            gens.remove(g)

            def chain(*gs):
                for g in gs:
                    yield from g

            def delayed(n, g):
                for _ in range(n):
                    yield
                yield from g

            set_mod2(64, sels)
            run_il([ffn_s1a(64, T, 0)])
            ffn_s1b(64)
            run_il([ffn_s2(64, True, 0)])
            run_il([ffn_s3(64, 0, 0, y_s)])
            state_out(lambda ch: carrys[:, ch, :, :], 32, ffn_s)
            set_mod2(128, selp)
            run_il([ffn_s1a(128, 0, 0)])
            ffn_s1b(128)
            for i in range(NBLK):
                gens = [ffn_s2(128, False, i % 2)]
                tailg = []
                if i > 0:
                    tailg.append(ffn_s3(128, (i - 1) % 2, (i - 1) % 2, y_p[(i - 1) * 128:i * 128, :]))
                if i + 1 < NBLK:
                    tailg.append(ffn_s1a(128, (i + 1) * 128, (i + 1) % 2))
                if tailg:
                    gens.append(delayed(1, chain(*tailg)))
                run_il(gens)
                if i + 1 < NBLK:
                    ffn_s1b(128)
            run_il([ffn_s3(128, (NBLK - 1) % 2, (NBLK - 1) % 2, y_p[(NBLK - 1) * 128:NBLK * 128, :])])
            state_out(lambda ch: carry[:, ch, :], 2, ffn_p)
            S.finish()
    return nc


def _consts():
    c = {}
    c["c_ident"] = np.eye(128, dtype=np.float32)
    k = np.arange(128)
    c["c_maskT"] = (k[:, None] <= k[None, :]).astype(np.float32)
    auglp = np.zeros((3, 4, 128), np.float32)
    for s, m in enumerate(SLOPES):
        auglp[0, s] = 8 * m * k
        auglp[1, s] = 8 * m
        auglp[2, s] = 1024 * m
    c["c_auglp"] = auglp.reshape(3, 512)
    augls = np.ones((3, 128), np.float32)
    augls[0] = k
    c["c_augls"] = augls
    augrp = np.zeros((3, 32, 128), np.float32)
    augrp[0] = 1.0
    augrp[1] = -k[None, :]
    augrp[2] = (np.arange(32) - 16)[:, None]
    c["c_augrp"] = augrp.reshape(3, 4096)
    augrs = np.zeros((3, 16, 4, 4), np.float32)
    mp = np.array(SLOPES, np.float32)
    augrs[0] = 128.0 * mp[None, :, None]
    augrs[1] = 8.0 * mp[None, :, None] * (np.arange(16)[:, None, None] - np.arange(4)[None, None, :])
    augrs[2] = -16384.0 * mp[None, :, None]
    c["c_augrs"] = augrs.reshape(3, 256)
    bnew = np.zeros((16, 4, 4, 16, 4), np.float32)
    mnew = np.zeros((16, 4, 4, 16, 4), np.float32)
    for b in range(16):
        for t1 in range(4):
            for t in range(t1, 4):
                for pr in range(4):
                    bnew[b, t1, pr, b, t] = -SLOPES[pr] * (t - t1)
                    mnew[b, t1, pr, b, t] = 1.0
    c["c_bnew"] = bnew.reshape(64, 256)
    c["c_mnew"] = mnew.reshape(64, 256)
    selp = np.zeros((17, 128), np.float32)
    selp[0] = 1.0
    sels = np.zeros((17, 64), np.float32)
    for b in range(16):
        sels[1 + b, 4 * b:4 * b + 4] = 1.0
    c["c_selp"] = selp
    c["c_sels"] = sels
    c["c_pm8"] = (k % 8).astype(np.float32).reshape(128, 1)
    return c


_NC = None
_LAST = None


def kernel(x_prompt, x_sample, c_prompt, c_sample, cache_k, cache_v, page_table, state_conv, state_ffn,
           ln_emb_g, ln_emb_b, w_ada, b_ada, w_in, lambda_q1, lambda_k1, lambda_q2, lambda_k2, subln_g,
           conv_w, conv_b, conv_ln_g, conv_ln_b, w_out, ln1_g, ln1_b,
           w_up, ffn_conv_w, ffn_conv_b, w_down, ln2_g, ln2_b):
    global _NC
    f = lambda a: np.ascontiguousarray(np.asarray(a, dtype=np.float32))
    if _NC is None:
        _NC = build_nc()
    nc = _NC
    shared = dict(_consts())
    shared["cache_k"] = f(cache_k).reshape(NPHYS * 8, 16 * 512)
    shared["cache_v"] = f(cache_v).reshape(NPHYS * 8, 16 * 512)
    shared["ln_emb_g"] = f(ln_emb_g).reshape(1, D); shared["ln_emb_b"] = f(ln_emb_b).reshape(1, D)
    shared["w_ada"] = f(w_ada)[0]; shared["b_ada"] = f(b_ada).reshape(1, 6 * D)
    shared["w_in"] = f(w_in)[0]
    shared["lam_in"] = np.concatenate([f(lambda_q1)[0], f(lambda_k1)[0], f(lambda_q2)[0], f(lambda_k2)[0]]).reshape(1, 256)
    shared["subln_g"] = f(subln_g).reshape(128, 1)
    shared["conv_w"] = np.ascontiguousarray(f(conv_w)[0].reshape(31, 4, 128).transpose(2, 1, 0))
    shared["conv_b"] = np.ascontiguousarray(f(conv_b)[0].reshape(4, 128).T)
    shared["conv_ln_g"] = np.ascontiguousarray(f(conv_ln_g)[0].reshape(4, 128).T)
    shared["conv_ln_b"] = np.ascontiguousarray(f(conv_ln_b)[0].reshape(4, 128).T)
    shared["w_out"] = f(w_out)[0]
    shared["ln1_g"] = f(ln1_g).reshape(1, D); shared["ln1_b"] = f(ln1_b).reshape(1, D)
    shared["w_up"] = f(w_up)[0]
    shared["ffn_cw"] = np.ascontiguousarray(f(ffn_conv_w)[0].reshape(3, 44, 128).transpose(2, 1, 0))
    shared["ffn_cb"] = np.ascontiguousarray(f(ffn_conv_b)[0].reshape(44, 128).T)
    shared["w_down"] = f(w_down)[0]
    shared["ln2_g"] = f(ln2_g).reshape(1, D); shared["ln2_b"] = f(ln2_b).reshape(1, D)
    xp = f(x_prompt); xs = f(x_sample); cp = f(c_prompt); cs = f(c_sample)
    pt = np.asarray(page_table, dtype=np.int32)
    sc = f(state_conv)[0]; sf = f(state_ffn)[0]
    in_maps = []
    for c in range(NCORES):
        m = dict(shared)
        m["x_p"] = xp[c]
        m["x_s"] = xs[NS * c:NS * (c + 1)].reshape(ST, D)
        m["c_all"] = np.concatenate([cp[c:c + 1], cs[NS * c:NS * (c + 1)]], axis=0)
        ptc = pt[NS * c:NS * (c + 1)]
        m["pt_lay"] = np.ascontiguousarray(np.repeat(ptc.T, 8, axis=0))
        m["st_conv"] = sc[NS * c:NS * (c + 1)].reshape(NS * 30, 512)
        m["st_ffn"] = sf[NS * c:NS * (c + 1)].reshape(NS * 2, 2 * DFF)
        in_maps.append(m)
    res = run_bass_kernel_spmd(nc, in_maps, core_ids=list(range(NCORES)))
    global _LAST
    _LAST = res
    R = res.results
    cat = lambda name: np.stack([R[c][name] for c in range(NCORES)], axis=0)
    y_prompt = cat("y_p")
    y_sample = cat("y_s").reshape(NCORES * NS, 4, D)
    k_prompt = cat("k_p").reshape(1, NCORES, T, 8, 64)
    v_prompt = cat("v_p").reshape(1, NCORES, T, 4, 128)
    conv_prompt = cat("conv_p").reshape(1, NCORES, 30, 512)
    ffn_prompt = cat("ffn_p").reshape(1, NCORES, 2, 2 * DFF)
    k_sample = cat("k_s").reshape(1, NCORES * NS, 4, 8, 64)
    v_sample = cat("v_s").reshape(1, NCORES * NS, 4, 4, 128)
    conv_sample = cat("conv_s").reshape(1, NCORES * NS, 30, 512)
    ffn_sample = cat("ffn_s").reshape(1, NCORES * NS, 2, 2 * DFF)
    return (y_prompt, y_sample, k_prompt, v_prompt, conv_prompt, ffn_prompt, k_sample, v_sample, conv_sample, ffn_sample)
```
